# Optimizing a Trainium2 kernel written in Bass

```python
import jax, jax.numpy as jnp
from jax import lax
import numpy as np

D_MODEL = 1024
BATCH = 8
SEQ = 2048
DEPTH = 1
DEC_BATCH = 32
DEC_SEQ = 4
PAST_LEN = 8192
PAGE_SIZE = 128

N_MEM = 256
MEM_HEADS = 4
MEM_HD = 128
RET_HEADS = 4
RET_DK = 64
RET_DV = 128
RET_CHUNK = 128
SWA_HEADS = 8
SWA_HD = 64
SWA_PATTERNS = ((128, 1), (512, 4), (2048, 16))
SWA_SPAN = 2048
SWA_BLOCK = 128
ROPE_THETA = 10000.0
LN_EPS = 1e-5
GN_EPS = 1e-5
RET_W = RET_HEADS * RET_DV
SWA_W = SWA_HEADS * SWA_HD
MEM_W = MEM_HEADS * MEM_HD
D_MIX = RET_W + SWA_W + MEM_W
IN_SPLITS = (RET_HEADS * RET_DK, RET_HEADS * RET_DK, RET_W, RET_W, SWA_W, SWA_W, SWA_W, SWA_W, MEM_W, MEM_W)
D_IN = sum(IN_SPLITS)
DEEPNORM_ALPHA = (2.0 * DEPTH) ** 0.25
DEEPNORM_BETA = (8.0 * DEPTH) ** -0.25

kernel_name = "hymba_retention_dilated_swa_memory_deepnorm_step"

F32 = jnp.float32


def _rope(x, pos):
    d = x.shape[-1]
    half = d // 2
    inv = ROPE_THETA ** (-jnp.arange(half, dtype=F32) * 2.0 / d)
    ang = pos.astype(F32)[:, None] * inv[None, :]
    cos = jnp.cos(ang)[:, None, :]
    sin = jnp.sin(ang)[:, None, :]
    x32 = x.astype(F32)
    x1, x2 = x32[..., :half], x32[..., half:]
    return jnp.concatenate([x1 * cos - x2 * sin, x2 * cos + x1 * sin], axis=-1).astype(x.dtype)


def _layernorm(z, g, b):
    z32 = z.astype(F32)
    mu = z32.mean(-1, keepdims=True)
    var = jnp.square(z32 - mu).mean(-1, keepdims=True)
    return (z32 - mu) * lax.rsqrt(var + LN_EPS) * g.astype(F32) + b.astype(F32)


def _head_norm(o):
    mu = o.mean(-1, keepdims=True)
    var = jnp.square(o - mu).mean(-1, keepdims=True)
    return (o - mu) * lax.rsqrt(var + GN_EPS)


def _project(x, w_in, pos):
    B, T, _ = x.shape
    h = jnp.einsum('btd,de->bte', x, w_in)
    offsets = np.cumsum(IN_SPLITS)[:-1].tolist()
    rq, rk, rv, rg, sq, sk, sv, sg, mq, mg = jnp.split(h, offsets, axis=-1)
    rq = _rope(rq.reshape(B, T, RET_HEADS, RET_DK), pos)
    rk = _rope(rk.reshape(B, T, RET_HEADS, RET_DK), pos) * (RET_DK ** -0.5)
    rv = rv.reshape(B, T, RET_HEADS, RET_DV)
    sq = _rope(sq.reshape(B, T, SWA_HEADS, SWA_HD), pos)
    sk = _rope(sk.reshape(B, T, SWA_HEADS, SWA_HD), pos)
    sv = sv.reshape(B, T, SWA_HEADS, SWA_HD)
    mq = mq.reshape(B, T, MEM_HEADS, MEM_HD)
    return (rq, rk, rv, sq, sk, sv, mq), (rg, sg, mg)


def _retention(q, k, v, state0, chunk):
    B, T, H, dk = q.shape
    dv = v.shape[-1]
    n = T // chunk
    lg = jnp.log1p(-jnp.exp2(-5.0 - jnp.arange(H, dtype=F32)))
    idx = jnp.arange(chunk, dtype=F32)
    rel = idx[:, None] - idx[None, :]
    dmat = jnp.where(rel >= 0, jnp.exp(jnp.maximum(rel, 0.0)[None] * lg[:, None, None]), 0.0)
    qc = q.astype(F32).reshape(B, n, chunk, H, dk)
    kc = k.astype(F32).reshape(B, n, chunk, H, dk)
    vc = v.astype(F32).reshape(B, n, chunk, H, dv)
    s = jnp.einsum('bnihk,bnjhk->bnhij', qc, kc) * dmat
    intra = jnp.einsum('bnhij,bnjhv->bnihv', s, vc)
    k_dec = kc * jnp.exp((chunk - 1.0 - idx)[:, None] * lg[None, :])[..., None]
    kv = jnp.einsum('bnjhk,bnjhv->nbhkv', k_dec, vc)
    g_chunk = jnp.exp(chunk * lg)[:, None, None]

    def step(r, kv_c):
        return g_chunk * r + kv_c, r

    r_fin, r_prev = lax.scan(step, state0, kv)
    q_dec = qc * jnp.exp((idx + 1.0)[:, None] * lg[None, :])[..., None]
    cross = jnp.einsum('bnihk,nbhkv->bnihv', q_dec, r_prev)
    return (intra + cross).reshape(B, T, H, dv), r_fin


def _dilated_prompt(q, k, v, dil, steps):
    B, S, H, D = q.shape
    L = S // dil
    nblk = -(-L // SWA_BLOCK)
    Lp = nblk * SWA_BLOCK

    def to_res(a):
        a = a.reshape(B, L, dil, H, D).transpose(0, 2, 1, 3, 4).reshape(B * dil, L, H, D)
        return jnp.pad(a, ((0, 0), (0, Lp - L), (0, 0), (0, 0)))

    bd = B * dil
    qb = to_res(q).reshape(bd, nblk, SWA_BLOCK, H, D)
    kb = to_res(k).reshape(bd, nblk, SWA_BLOCK, H, D)
    vb = to_res(v).reshape(bd, nblk, SWA_BLOCK, H, D)

    def with_prev(a):
        prev = jnp.concatenate([jnp.zeros_like(a[:, :1]), a[:, :-1]], axis=1)
        return jnp.concatenate([prev, a], axis=2)

    kk, vv = with_prev(kb), with_prev(vb)
    s = jnp.einsum('bnqhd,bnkhd->bnhqk', qb, kk).astype(F32) * (D ** -0.5)
    qi = jnp.arange(SWA_BLOCK)[:, None] + SWA_BLOCK
    kj = jnp.arange(2 * SWA_BLOCK)[None, :]
    rel = qi - kj
    band = (rel >= 0) & (rel <= steps)
    first = (jnp.arange(nblk)[:, None, None] > 0) | (kj >= SWA_BLOCK)[None]
    mask = band[None] & first
    s = jnp.where(mask[None, :, None], s, -jnp.inf)
    m = s.max(-1)
    p = jnp.exp(s - m[..., None])
    l = p.sum(-1)
    m_t = jnp.swapaxes(m, -1, -2)
    l_t = jnp.swapaxes(l, -1, -2)
    o = jnp.einsum('bnhqk,bnkhd->bnqhd', p, vv.astype(F32)) / l_t[..., None]

    def back(a):
        a = a.reshape(B, dil, Lp, *a.shape[3:])[:, :, :L]
        a = jnp.moveaxis(a, 1, 2)
        return a.reshape(B, S, *a.shape[3:])

    return back(o), back(m_t), back(l_t)


def _dilated_sample(q, keys, vals, wb, dil, steps):
    D = q.shape[-1]
    T = q.shape[1]
    idx = wb + jnp.arange(T)[:, None] - dil * jnp.arange(steps + 1)[None, :]
    valid = idx >= 0
    idx_c = jnp.maximum(idx, 0)
    kg = keys[:, idx_c]
    vg = vals[:, idx_c]
    s = jnp.einsum('bthd,btnhd->bthn', q, kg).astype(F32) * (D ** -0.5)
    s = jnp.where(valid[None, :, None, :], s, -jnp.inf)
    m = s.max(-1)
    p = jnp.exp(s - m[..., None])
    l = p.sum(-1)
    o = jnp.einsum('bthn,btnhd->bthd', p, vg.astype(F32)) / l[..., None]
    return o, m, l


def _combine(parts):
    ms = jnp.stack([p[1] for p in parts])
    ls = jnp.stack([p[2] for p in parts])
    os_ = jnp.stack([p[0] for p in parts])
    w = ls * jnp.exp(ms - ms.max(0))
    return (w[..., None] * os_).sum(0) / w.sum(0)[..., None]


def _mem_attn(q, k, v):
    s = jnp.einsum('bthd,bmhd->bhtm', q, k).astype(F32) * (MEM_HD ** -0.5)
    p = jax.nn.softmax(s, axis=-1)
    return jnp.einsum('bhtm,bmhd->bthd', p, v.astype(F32))


def _finish(x, ret_o, swa_o, mem_o, gates, w_out, g, b):
    B, T, _ = x.shape
    rg, sg, mg = gates
    silu = lambda a: jax.nn.silu(a.astype(F32))
    mix = jnp.concatenate([
        silu(rg) * _head_norm(ret_o).reshape(B, T, RET_W),
        silu(sg) * swa_o.reshape(B, T, SWA_W),
        silu(mg) * mem_o.reshape(B, T, MEM_W)], axis=-1).astype(x.dtype)
    h = jnp.einsum('bte,ed->btd', mix, w_out)
    return _layernorm(DEEPNORM_ALPHA * x + h, g, b).astype(x.dtype)


def _prompt_layer(x, mem, w_in, w_mem_kv, w_out, g, b):
    B, S, _ = x.shape
    pos = jnp.arange(S)
    (rq, rk, rv, sq, sk, sv, mq), gates = _project(x, w_in, pos)
    state0 = jnp.zeros((B, RET_HEADS, RET_DK, RET_DV), F32)
    ret_o, ret_state = _retention(rq, rk, rv, state0, min(RET_CHUNK, S))
    swa_o = _combine([_dilated_prompt(sq, sk, sv, d, w // d) for (w, d) in SWA_PATTERNS])
    mkv = jnp.einsum('bmd,de->bme', mem, w_mem_kv)
    mk, mv = jnp.split(mkv, 2, axis=-1)
    mk = mk.reshape(B, mem.shape[1], MEM_HEADS, MEM_HD)
    mv = mv.reshape(B, mem.shape[1], MEM_HEADS, MEM_HD)
    mem_o = _mem_attn(mq, mk, mv)
    y = _finish(x, ret_o, swa_o, mem_o, gates, w_out, g, b)
    wb = min(SWA_SPAN, S)
    return y, ret_state.astype(x.dtype), sk[:, S - wb:], sv[:, S - wb:], mk, mv


def _sample_layer(x, state_ret, ck, cv, mk, mv, w_in, w_out, g, b):
    B, T, _ = x.shape
    wb = ck.shape[1]
    pos = PAST_LEN + jnp.arange(T)
    (rq, rk, rv, sq, sk, sv, mq), gates = _project(x, w_in, pos)
    ret_o, ret_state = _retention(rq, rk, rv, state_ret.astype(F32), T)
    keys = jnp.concatenate([ck, sk.astype(ck.dtype)], axis=1)
    vals = jnp.concatenate([cv, sv.astype(cv.dtype)], axis=1)
    swa_o = _combine([_dilated_sample(sq, keys, vals, wb, d, w // d) for (w, d) in SWA_PATTERNS])
    mem_o = _mem_attn(mq, mk, mv)
    y = _finish(x, ret_o, swa_o, mem_o, gates, w_out, g, b)
    return y, ret_state.astype(state_ret.dtype), sk, sv


def setup_inputs(seed: int = 0) -> dict:
    key = jax.random.key(seed)
    ks = jax.random.split(key, 13)
    wb = min(SWA_SPAN, PAST_LEN)
    nrm = jax.random.normal
    in_gains = (1.0, 1.0, DEEPNORM_BETA, 1.0, 1.0, 1.0, DEEPNORM_BETA, 1.0, 1.0, 1.0)
    col_scale = jnp.concatenate([jnp.full((w,), s, F32) for w, s in zip(IN_SPLITS, in_gains)])
    mem_scale = jnp.concatenate([jnp.ones((MEM_W,), F32), jnp.full((MEM_W,), DEEPNORM_BETA, F32)])
    return {
        'x_prompt': nrm(ks[0], (BATCH, SEQ, D_MODEL), F32),
        'x_sample': nrm(ks[1], (DEC_BATCH, DEC_SEQ, D_MODEL), F32),
        'state_ret': nrm(ks[2], (DEPTH, DEC_BATCH, RET_HEADS, RET_DK, RET_DV), F32),
        'cache_swa_k': nrm(ks[3], (DEPTH, DEC_BATCH, wb, SWA_HEADS, SWA_HD), F32),
        'cache_swa_v': nrm(ks[4], (DEPTH, DEC_BATCH, wb, SWA_HEADS, SWA_HD), F32),
        'cache_mem_k': nrm(ks[5], (DEPTH, DEC_BATCH, N_MEM, MEM_HEADS, MEM_HD), F32),
        'cache_mem_v': nrm(ks[6], (DEPTH, DEC_BATCH, N_MEM, MEM_HEADS, MEM_HD), F32),
        'mem_prompt': nrm(ks[7], (BATCH, N_MEM, D_MODEL), F32),
        'w_in': nrm(ks[8], (DEPTH, D_MODEL, D_IN), F32) * (D_MODEL ** -0.5) * col_scale,
        'w_mem_kv': nrm(ks[9], (DEPTH, D_MODEL, 2 * MEM_W), F32) * (D_MODEL ** -0.5) * mem_scale,
        'w_out': nrm(ks[10], (DEPTH, D_MIX, D_MODEL), F32) * (D_MIX ** -0.5) * DEEPNORM_BETA,
        'ln_gain': 1.0 + 0.02 * nrm(ks[11], (DEPTH, D_MODEL), F32),
        'ln_bias': 0.02 * nrm(ks[12], (DEPTH, D_MODEL), F32),
    }


def reference(x_prompt, x_sample, state_ret, cache_swa_k, cache_swa_v, cache_mem_k, cache_mem_v,
              mem_prompt, w_in, w_mem_kv, w_out, ln_gain, ln_bias):
    yp, ys = x_prompt, x_sample
    rp_l, rs_l, kp_l, vp_l, ks_l, vs_l, mkp_l, mvp_l = [], [], [], [], [], [], [], []
    for layer in range(DEPTH):
        yp, rp, kp, vp, mkp, mvp = _prompt_layer(yp, mem_prompt, w_in[layer], w_mem_kv[layer],
                                                 w_out[layer], ln_gain[layer], ln_bias[layer])
        ys, rs, ksn, vsn = _sample_layer(ys, state_ret[layer], cache_swa_k[layer], cache_swa_v[layer],
                                         cache_mem_k[layer], cache_mem_v[layer], w_in[layer],
                                         w_out[layer], ln_gain[layer], ln_bias[layer])
        rp_l.append(rp); rs_l.append(rs); kp_l.append(kp); vp_l.append(vp)
        ks_l.append(ksn); vs_l.append(vsn); mkp_l.append(mkp); mvp_l.append(mvp)
    return (yp, ys, jnp.stack(rp_l), jnp.stack(rs_l), jnp.stack(kp_l), jnp.stack(vp_l),
            jnp.stack(ks_l), jnp.stack(vs_l), jnp.stack(mkp_l), jnp.stack(mvp_l))
```

```python
from contextlib import ExitStack

import numpy as np
import concourse.bass as bass
import concourse.mybir as mybir
from concourse.bass_utils import run_bass_kernel_spmd

F32 = mybir.dt.float32
BF16 = mybir.dt.bfloat16
AF = mybir.ActivationFunctionType
ALU = mybir.AluOpType
AX = mybir.AxisListType

NCORES = 8
D = 1024
SEQ = 2048
NT = 16
DIN = 4608
DMIX = 1536
NS = 16
PAST = 8192
ALPHA = 2.0 ** 0.25
EPS = 1e-5
C_RQ, C_RK, C_RV, C_RG, C_SQ, C_SK, C_SV, C_SG, C_MQ, C_MG = 0, 256, 512, 1024, 1536, 2048, 2560, 3072, 3584, 4096

ENGS = ("pe", "act", "dve", "pool", "sp")
SAME_ENGINE_SYNC = True
SAME_ENGINE_WAR = True


class Tk:
    __slots__ = ("name", "w", "r", "excl", "wg")

    def __init__(self, name):
        self.name = name
        self.w = None
        self.r = {}
        self.wg = []
        self.excl = False


class Buf:
    __slots__ = ("ap", "tk")

    def __init__(self, ap, name):
        self.ap = ap
        self.tk = Tk(name)


class Sched:
    def __init__(self, n_dma_sems):
        self.q = {e: [] for e in ENGS}
        self.known = {e: {} for e in ENGS}
        self.dma_val = [0] * n_dma_sems
        self.rr = 0
        self.rrp = 0
        self.rrq = 0
        self.needed = {e: set() for e in ENGS}

    def _collect(self, eng, reads, writes, par=False):
        deps = []
        for t in reads:
            if t.w is not None:
                deps.append((t.w, True))
            for d in t.wg:
                deps.append((d, True))
            if t.excl:
                for d in t.r.values():
                    if not (d[0] == "e" and d[1] == eng):
                        deps.append((d, True))
        for t in writes:
            if t.w is not None and not (par and t.w[0] == "d" and not t.r):
                deps.append((t.w, True))
                for d in t.wg:
                    deps.append((d, True))
            for d in t.r.values():
                deps.append((d, False))
        kn = self.known[eng]
        best = {}
        for d, is_raw in deps:
            if d[0] == "e" and d[1] == eng:
                if eng == "pe" or not (SAME_ENGINE_SYNC and (is_raw or SAME_ENGINE_WAR)):
                    continue
            key = (d[0], d[1])
            if kn.get(key, -1) >= d[2]:
                continue
            if key not in best or best[key][2] < d[2]:
                best[key] = d
        waits = list(best.values())
        for d in waits:
            kn[(d[0], d[1])] = d[2]
            if d[0] == "e":
                self.needed[d[1]].add(d[2])
        return waits

    def op(self, eng, fn, reads=(), writes=()):
        idx = len(self.q[eng])
        waits = self._collect(eng, reads, writes)
        self.q[eng].append({"fn": fn, "waits": waits, "dma": None})
        me = ("e", eng, idx)
        for t in reads:
            t.r[("e", eng)] = me
        for t in writes:
            t.w = me
            t.wg = []
            t.r = {}
        return idx

    NPRE = 12

    def dma(self, eng, fn, reads=(), writes=(), prefetch=False, parallel=False):
        nn = len(self.dma_val) - self.NPRE
        half = nn // 2
        if prefetch:
            si = nn + self.rrp
            self.rrp = (self.rrp + 1) % self.NPRE
        elif eng == "pool":
            si = half + self.rrq
            self.rrq = (self.rrq + 1) % (nn - half)
        else:
            si = self.rr
            self.rr = (self.rr + 1) % half
        prev = self.dma_val[si]
        new = prev + 16
        self.dma_val[si] = new
        waits = self._collect(eng, reads, writes, par=parallel)
        if prev > 0:
            kn = self.known[eng]
            if kn.get(("d", si), -1) < prev:
                kn[("d", si)] = prev
                waits.append(("d", si, prev))
        self.q[eng].append({"fn": fn, "waits": waits, "dma": si})
        me = ("d", si, new)
        for t in reads:
            t.r[("d", si)] = me
        for t in writes:
            if parallel and t.w is not None and t.w[0] == "d" and not t.r:
                t.wg.append(t.w)
            else:
                t.wg = []
            t.w = me
            t.r = {}

    def barrier(self, final=False):
        last = {}
        for e in ENGS:
            for i in range(len(self.q[e]) - 1, -1, -1):
                ent = self.q[e][i]
                if ent["dma"] is None and ent["fn"] is not None:
                    last[e] = i
                    break
        for e in ENGS:
            waits = []
            kn = self.known[e]
            for e2, idx in last.items():
                if e2 == e:
                    continue
                if kn.get(("e", e2), -1) < idx:
                    kn[("e", e2)] = idx
                    waits.append(("e", e2, idx))
                    self.needed[e2].add(idx)
            for si, v in enumerate(self.dma_val):
                if si >= len(self.dma_val) - self.NPRE and not final:
                    continue
                if v > 0 and kn.get(("d", si), -1) < v:
                    kn[("d", si)] = v
                    waits.append(("d", si, v))
            self.q[e].append({"fn": None, "waits": waits, "dma": None})

    def emit(self, block, eng_sems, dma_sems):
        rank = {}
        for e in ENGS:
            for r, idx in enumerate(sorted(self.needed[e])):
                rank[(e, idx)] = r + 1

        def run(ename, eng):
            for idx, ent in enumerate(self.q[ename]):
                for d in ent["waits"]:
                    if d[0] == "e":
                        eng.wait_ge(eng_sems[d[1]], rank[(d[1], d[2])])
                    else:
                        eng.wait_ge(dma_sems[d[1]], d[2])
                if ent["fn"] is None:
                    continue
                ins = ent["fn"](eng)
                if ent["dma"] is not None:
                    ins.then_inc(dma_sems[ent["dma"]], 16)
                elif (ename, idx) in rank:
                    ins.then_inc(eng_sems[ename], 1)

        block.tensor(lambda eng: run("pe", eng))
        block.scalar(lambda eng: run("act", eng))
        block.vector(lambda eng: run("dve", eng))
        block.gpsimd(lambda eng: run("pool", eng))
        block.sync(lambda eng: run("sp", eng))


def _consts():
    f32 = np.float32
    c = {}
    c["ident"] = np.eye(128, dtype=f32)
    inv = (f32(10000.0) ** (-(np.arange(32, dtype=f32) * f32(2.0) / f32(64)))).astype(f32)
    pos = np.arange(SEQ, dtype=f32)
    ang = (pos[:, None] * inv[None, :]).astype(f32).astype(np.float64)
    cs = np.cos(ang).reshape(NT, 128, 32).transpose(1, 0, 2)
    sn = np.sin(ang).reshape(NT, 128, 32).transpose(1, 0, 2)
    c["cosp"] = np.ascontiguousarray(cs).astype(f32)
    c["sinp"] = np.ascontiguousarray(np.stack([-sn, sn], axis=2)).astype(f32)
    poss = (PAST + np.arange(4)).astype(f32)
    angs = (poss[:, None] * inv[None, :]).astype(f32).astype(np.float64)
    c["coss"] = np.tile(np.cos(angs), (4, 1)).astype(f32)
    sns = np.tile(np.sin(angs), (4, 1))
    c["sins"] = np.stack([-sns, sns], axis=1).astype(f32)
    lg = np.log1p(-np.exp2(-5.0 - np.arange(4, dtype=np.float64)))
    p1 = np.arange(1, 129, dtype=np.float64)[:, None]
    dq = np.exp(p1 * lg[None, :])
    dk = np.exp(-p1 * lg[None, :]) * 0.125
    c["dqk"] = np.concatenate([dq, dk], axis=1).astype(f32)
    g128 = np.exp(128.0 * lg)
    G = np.zeros((128, 4, 128), dtype=np.float64)
    for h in range(4):
        G[(h % 2) * 64:(h % 2) * 64 + 64, h, :] = g128[h]
    c["g128"] = G.reshape(128, 512).astype(f32)
    jj = np.arange(128)[:, None]
    ii = np.arange(128)[None, :]
    c["cmask"] = (ii >= jj).astype(f32)
    c["swamask"] = np.concatenate([(ii >= jj), (ii <= jj)], axis=1).astype(f32)
    sel = np.zeros((8, 4, 128), dtype=f32)
    for h in range(8):
        sel[h, h // 2, (h % 2) * 64:(h % 2) * 64 + 64] = 1.0
    c["sel"] = sel.reshape(8, 512)
    c["mhalf"] = np.full((128, 4), -0.5, dtype=f32)
    i4 = np.tile(np.arange(4, dtype=np.float64), 4)[:, None]
    dqs = np.exp((i4 + 1.0) * lg[None, :])
    dks = np.exp(-(i4 + 1.0) * lg[None, :]) * 0.125
    c["dqks"] = np.concatenate([dqs, dks], axis=1).astype(f32)
    r16 = np.arange(16)
    bj, ij = r16 // 4, r16 % 4
    c["rmask"] = ((bj[:, None] == bj[None, :]) & (ij[None, :] >= ij[:, None])).astype(f32)
    c["bmask"] = (bj[:, None] == np.arange(4)[None, :]).astype(f32)
    sm = np.zeros((128, 9, 4), dtype=f32)
    for i in range(4):
        sm[:, i, i] = 1.0
        sm[:, 4 + i, i] = 1.0
        sm[:, 8, i] = (np.arange(128) >= i).astype(f32)
    c["smask"] = sm
    smn = np.zeros((16, 4, 4), dtype=f32)
    for kk in range(16):
        for b in range(4):
            for i in range(4):
                if kk // 4 == b:
                    j = kk % 4
                    smn[kk, b, i] = (1.0 if j <= i else 0.0) + (2.0 if j == i else 0.0)
    c["smaskn"] = smn
    return c


CONST_SHAPES = {
    "ident": (128, 128), "cosp": (128, 16, 32), "sinp": (128, 16, 2, 32), "coss": (16, 32), "sins": (16, 2, 32),
    "dqk": (128, 8), "g128": (128, 512), "cmask": (128, 128), "swamask": (128, 256), "sel": (8, 512),
    "mhalf": (128, 4), "dqks": (16, 8), "rmask": (16, 16), "bmask": (16, 4), "smask": (128, 9, 4), "smaskn": (16, 4, 4),
}
IN_SHAPES = {
    "x": (SEQ, D), "memx": (256, D), "w_in": (D, DIN), "w_mem": (D, 1024), "w_out": (DMIX, D),
    "ln_g": (1, D), "ln_b": (1, D), "xs": (NS, D), "state": (4, 4, 64, 128),
    "ck": (4, 2048, 512), "cv": (4, 2048, 512), "cmk": (4, 256, 512), "cmv": (4, 256, 512),
}
OUT_SHAPES = {
    "y": (SEQ, D), "ys": (NS, D), "retp": (4, 64, 128), "rets": (4, 4, 64, 128),
    "kp": (SEQ, 512), "vp": (SEQ, 512), "ks": (NS, 512), "vs": (NS, 512),
    "mkp": (256, 512), "mvp": (256, 512),
}


def tok_slice(g, b, nblk=1):
    cnt = 128 * nblk
    if g == 1:
        start = 128 * b
    elif g == 4:
        r, n = divmod(b, 4)
        start = 512 * n + r
    else:
        start = b
    return slice(start, start + g * (cnt - 1) + 1, g)


class Prog:
    def __init__(self):
        self.nc = bass.Bass("TRN2", target_bir_lowering=False)
        nc = self.nc
        self.din = {k: nc.dram_tensor(k, list(s), F32, kind="ExternalInput").ap() for k, s in IN_SHAPES.items()}
        self.dco = {k: nc.dram_tensor("c_" + k, list(s), F32, kind="ExternalInput").ap() for k, s in CONST_SHAPES.items()}
        self.dout = {k: nc.dram_tensor(k, list(s), F32, kind="ExternalOutput").ap() for k, s in OUT_SHAPES.items()}
        self.vscr = nc.dram_tensor("vscr", [SEQ, 520], BF16, kind="Internal").ap()
        self.NDMA = 64
        self.S = Sched(self.NDMA)
        self.es = ExitStack()
        self.eng_sems = {e: self.es.enter_context(nc.semaphore("sem_" + e)) for e in ENGS}
        self.dma_sems = [self.es.enter_context(nc.semaphore("dsem%d" % i)) for i in range(self.NDMA)]
        self.uid = 0

    def sb(self, name, shape, dt):
        return self.es.enter_context(self.nc.sbuf_tensor(name, list(shape), dt))

    def buf(self, name, shape, dt):
        return Buf(self.sb(name, shape, dt)[:], name)

    def arena_reset(self):
        self.aoff = 0

    def arena(self, name, shape, dt):
        n = 1
        for s in shape[1:]:
            n *= s
        words = n if dt == F32 else (n + 1) // 2
        words = (words + 15) // 16 * 16
        assert self.aoff + words <= self.AW, (name, self.aoff, words, self.AW)
        v = self.arena_t[:, self.aoff:self.aoff + words]
        self.aoff += words
        if dt != F32:
            v = v.bitcast(dt)
        v = v[:, 0:n]
        if len(shape) > 2:
            names = " ".join("d%d" % i for i in range(len(shape) - 1))
            v = v.rearrange("p (%s) -> p %s" % (names, names), **{"d%d" % i: shape[i + 1] for i in range(len(shape) - 1)})
        if shape[0] < 128:
            v = v[0:shape[0]]
        self.uid += 1
        return Buf(v, "%s_%d" % (name, self.uid))

    def dma_in(self, q, dst, dst_ap, src_ap, parallel=False):
        self.S.dma(q, lambda e: e.dma_start(out=dst_ap, in_=src_ap), reads=[], writes=[dst.tk], parallel=parallel)

    def dma_out(self, q, dst_ap, src, src_ap):
        self.S.dma(q, lambda e: e.dma_start(out=dst_ap, in_=src_ap), reads=[src.tk], writes=[])

    def mm(self, out, out_ap, lhsT, lhsT_ap, rhs, rhs_ap, start=True, stop=True, extra_reads=()):
        self.S.op("pe", lambda e: e.matmul(out_ap, lhsT=lhsT_ap, rhs=rhs_ap, start=start, stop=stop),
                  reads=[lhsT.tk, rhs.tk] + list(extra_reads), writes=[out.tk])

    def tr(self, out, out_ap, in_, in_ap, ident):
        n = in_ap.shape[0]
        self.S.op("pe", lambda e: e.transpose(out=out_ap, in_=in_ap, identity=ident.ap[0:n, 0:n]),
                  reads=[in_.tk, ident.tk], writes=[out.tk])

    def act(self, out, out_ap, in_, in_ap, func, scale=1.0, bias=0.0, extra_reads=()):
        self.S.op("act", lambda e: e.activation(out=out_ap, in_=in_ap, func=func, scale=scale, bias=bias),
                  reads=[in_.tk] + [b.tk for b in extra_reads], writes=[out.tk])

    def tt(self, eng, out, out_ap, a, a_ap, b, b_ap, op, extra_reads=()):
        self.S.op(eng, lambda e: e.tensor_tensor(out=out_ap, in0=a_ap, in1=b_ap, op=op),
                  reads=[a.tk, b.tk] + list(extra_reads), writes=[out.tk])

    def ts(self, eng, out, out_ap, a, a_ap, s1, op0, s2=None, op1=None, extra_reads=()):
        if op1 is None:
            fn = lambda e: e.tensor_scalar(out=out_ap, in0=a_ap, scalar1=s1, scalar2=None, op0=op0)
        else:
            fn = lambda e: e.tensor_scalar(out=out_ap, in0=a_ap, scalar1=s1, scalar2=s2, op0=op0, op1=op1)
        self.S.op(eng, fn, reads=[a.tk] + [b.tk for b in extra_reads], writes=[out.tk])

    def stt(self, out, out_ap, a, a_ap, scalar, op0, b, b_ap, op1, extra_reads=()):
        self.S.op("dve", lambda e: e.scalar_tensor_tensor(out=out_ap, in0=a_ap, scalar=scalar, op0=op0, in1=b_ap, op1=op1),
                  reads=[a.tk, b.tk] + [x.tk for x in extra_reads], writes=[out.tk])

    def cp(self, eng, out, out_ap, in_, in_ap, extra_reads=()):
        if eng == "act":
            fn = lambda e: e.copy(out=out_ap, in_=in_ap)
        else:
            fn = lambda e: e.tensor_copy(out=out_ap, in_=in_ap)
        self.S.op(eng, fn, reads=[in_.tk] + list(extra_reads), writes=[out.tk])

    def memset(self, eng, out, out_ap, val):
        self.S.op(eng, lambda e: e.memset(out_ap, val), reads=[], writes=[out.tk])

    def load_w(self, slot, src, col0, ncols=512, row0=0):
        wb = self.W[slot]
        self.S.dma("pool", (lambda e: e.dma_start(out=wb.ap[:, :, 0:ncols],
                                                  in_=src.rearrange("(k p) c -> p k c", p=128)[:, :, col0:col0 + ncols])),
                   reads=[], writes=[wb.tk], prefetch=True)

    def load_w1(self, slot, src, col0, k, ncols=512):
        wb = self.W[slot]
        self.S.dma("pool", (lambda e: e.dma_start(out=wb.ap[:, k, 0:ncols], in_=src[k * 128:(k + 1) * 128, col0:col0 + ncols])),
                   reads=[], writes=[wb.tk], prefetch=True)

    def proj(self, bank, ntok, xt_reads, xt_ap, wslot, wc0=0, ncols=512):
        wb = self.W[wslot]
        for k in range(8):
            self.S.op("pe", (lambda e, k=k: e.matmul(bank.ap[0:ntok, 0:ncols], lhsT=xt_ap[:, k, :],
                                                     rhs=wb.ap[:, k, wc0:wc0 + ncols], start=(k == 0), stop=(k == 7))),
                      reads=list(xt_reads) + [wb.tk], writes=[bank.tk])

    def rope(self, bank, ntok, cos_b, cos_ap, sin_b, sin_ap, t1, t2, out, out_ap, nh=8):
        X = bank.ap[0:ntok, 0:nh * 64].rearrange("p (h two f) -> p h two f", h=nh, two=2)
        T1 = t1.ap[0:ntok, 0:nh * 64].rearrange("p (h two f) -> p h two f", h=nh, two=2)
        T2 = t2.ap[0:ntok, 0:nh * 64].rearrange("p (h two f) -> p h two f", h=nh, two=2)
        O = out_ap.rearrange("p (h two f) -> p h two f", h=nh, two=2)
        self.tt("dve", t1, T1, bank, X, cos_b, cos_ap, ALU.mult)
        self.tt("dve", t2, T2, bank, X[:, :, ::-1, :], sin_b, sin_ap, ALU.mult)
        self.tt("dve", out, O, t1, T1, t2, T2, ALU.add)

    def build(self, phases):
        self.phases = phases
        nc = self.nc
        S = self.S
        din, dco, dout = self.din, self.dco, self.dout
        self.ps_t = self.es.enter_context(nc.psum_tensor("psum", [128, 4096], F32))
        self.PB = [Buf(self.ps_t[:, i * 512:(i + 1) * 512], "bank%d" % i) for i in range(8)]
        for b_ in self.PB:
            b_.tk.excl = True
        PB = self.PB
        n = NS

        self.xT_t = self.sb("xT", (128, 8, SEQ), BF16)
        self.xT = [Buf(self.xT_t[:, :, t * 128:(t + 1) * 128], "xT%d" % t) for t in range(NT)]
        self.xT_all = [b.tk for b in self.xT]
        self.xsT = self.buf("xsT", (128, 8, NS), BF16)
        self.mixsT = self.buf("mixsT", (128, 12, NS), BF16)
        self.mixS_t = self.sb("mixS", (128, 4, SEQ), BF16)
        self.mixS = Buf(self.mixS_t, "mixS")
        self.NSLOT = 6
        self.W_t = self.sb("W", (128, self.NSLOT * 4096), BF16)
        self.W = [Buf(self.W_t[:, i * 4096:(i + 1) * 4096].rearrange("p (k c) -> p k c", k=8), "W%d" % i)
                  for i in range(self.NSLOT)]
        self.identb = self.buf("identb", (128, 128), BF16)
        self.identf = self.buf("identf", (128, 128), F32)
        self.cosp = self.buf("cosp", (128, 16, 32), F32)
        self.sinp = self.buf("sinp", (128, 16, 2, 32), F32)
        self.mkT = self.buf("mkT", (128, 4, 256), BF16)
        self.mvaug = self.buf("mvaug", (128, 2, 4, 129), BF16)
        self.sqs = self.buf("sqs", (n, 512), BF16)
        self.sks = self.buf("sks", (n, 512), BF16)
        self.Vnew = self.buf("Vnew", (n, 8, 65), BF16)
        self.Gss = self.buf("Gss", (n, 512), BF16)
        self.coss = self.buf("coss", (n, 32), F32)
        self.sins = self.buf("sins", (n, 2, 32), F32)
        self.A_t = self.sb("arenaA", (128, 8, SEQ), BF16)
        self.AW = 15872
        self.arena_t = self.sb("arenaB", (128, self.AW), F32)
        self.arena_reset()
        self.mixR = [Buf(self.A_t[:, 0:4, t * 128:(t + 1) * 128], "mixR%d" % t) for t in range(NT)]
        self.mixM = [Buf(self.A_t[:, 4:8, t * 128:(t + 1) * 128], "mixM%d" % t) for t in range(NT)]

        self.dma_in("pool", self.identb, self.identb.ap, dco["ident"])
        self.load_w(0, din["w_in"], C_SQ)
        self.load_w(1, din["w_in"], C_SK)
        self.load_w(2, din["w_in"], C_SV)
        self.load_w(4, din["w_mem"], 0)
        self.load_w(5, din["w_mem"], 512)
        self.load_w(3, din["w_in"], C_SG)
        self.dma_in("sp", self.identf, self.identf.ap, dco["ident"])
        self.dma_in("sp", self.cosp, self.cosp.ap, dco["cosp"])
        self.dma_in("sp", self.sinp, self.sinp.ap, dco["sinp"])
        self.dma_in("sp", self.coss, self.coss.ap, dco["coss"])
        self.dma_in("sp", self.sins, self.sins.ap, dco["sins"])

        xld = [self.arena("xld", (128, D), F32) for _ in range(3)]
        memT = self.arena("memT", (128, 8, 256), BF16)
        mf = [self.arena("mf", (128, 512), F32) for _ in range(2)]
        xsld = self.arena("xsld", (n, D), BF16)

        def load_T(src_ap, ntok, dst, dst_ap, i):
            xb = xld[i % 3]
            self.dma_in("sp", xb, xb.ap[0:ntok], src_ap)
            b0, b1 = PB[(2 * i) % 8], PB[(2 * i + 1) % 8]
            for c in range(8):
                bk = b0 if c < 4 else b1
                self.tr(bk, bk.ap[:, (c % 4) * ntok:(c % 4 + 1) * ntok], xb, xb.ap[0:ntok, c * 128:(c + 1) * 128], self.identf)
            e0, e1 = ("dve", "act") if i % 2 == 0 else ("act", "dve")
            self.cp(e0, dst, dst_ap[:, 0:4, :], b0, b0.ap[:, 0:4 * ntok].rearrange("p (c t) -> p c t", c=4))
            self.cp(e1, dst, dst_ap[:, 4:8, :], b1, b1.ap[:, 0:4 * ntok].rearrange("p (c t) -> p c t", c=4))

        i = 0
        for mt in range(2):
            load_T(din["memx"][mt * 128:(mt + 1) * 128, :], 128, memT, memT.ap[:, :, mt * 128:(mt + 1) * 128], i)
            i += 1
        for t in range(NT):
            load_T(din["x"][t * 128:(t + 1) * 128, :], 128, self.xT[t], self.xT[t].ap, i)
            i += 1
            if t == 15:
                self.dma_in("pool", xsld, xsld.ap, din["xs"])
                bk = PB[(2 * i) % 8]
                pv = bk.ap.bitcast(BF16)
                for c in range(8):
                    self.tr(bk, pv[:, c * n:(c + 1) * n], xsld, xsld.ap[:, c * 128:(c + 1) * 128], self.identb)
                self.cp("dve", self.xsT, self.xsT.ap, bk, pv[:, 0:8 * n].rearrange("p (c t) -> p c t", c=8))
                i += 1
                self.mem_setup(memT, mf)
        S.barrier()

        if "S" in phases:
            self.phase_S()
            S.barrier()
        if "R" in phases:
            self.phase_R()
            S.barrier()
        if "M" in phases:
            self.phase_M()
            S.barrier()
        if "F" in phases:
            self.phase_F()
            S.barrier()
        S.barrier(final=True)
        with nc.Block() as block:
            S.emit(block, self.eng_sems, self.dma_sems)
        self.es.close()
        return nc

    def mem_setup(self, memT, mf):
        S = self.S
        PB = self.PB
        dout = self.dout
        mkT, mvaug = self.mkT, self.mvaug
        self.memset("pool", mvaug, mvaug.ap[:, :, :, 128:129], 1.0)
        i = 0
        for mt in range(2):
            for which in range(2):
                bank = PB[4 + i % 4]
                self.proj(bank, 128, [memT.tk], memT.ap[:, :, mt * 128:(mt + 1) * 128], 4 + which)
                f = mf[i % 2]
                i += 1
                self.cp("act", f, f.ap, bank, bank.ap)
                self.dma_out("sp", dout["mkp" if which == 0 else "mvp"][mt * 128:(mt + 1) * 128, :], f, f.ap)
                if which == 1:
                    self.cp("pool", mvaug, mvaug.ap[:, mt, :, 0:128], f, f.ap.rearrange("p (h f) -> p h f", h=4))
        w0 = self.W[4]
        for h in range(4):
            bank = PB[4 + h % 4]
            for k in range(8):
                S.op("pe", (lambda e, k=k, h=h, bank=bank: e.matmul(bank.ap[:, 0:256], lhsT=w0.ap[:, k, h * 128:(h + 1) * 128],
                                                                     rhs=memT.ap[:, k, :], start=(k == 0), stop=(k == 7))),
                     reads=[w0.tk, memT.tk], writes=[bank.tk])
            self.cp("act", mkT, mkT.ap[:, h, :], bank, bank.ap[:, 0:256])

    def gate(self, bank, n, th, out, out_ap):
        self.act(th, th.ap[0:n], bank, bank.ap[0:n], AF.Tanh, scale=0.5)
        self.stt(out, out_ap, th, th.ap[0:n], 1.0, ALU.add, bank, bank.ap[0:n], ALU.mult)

    def phase_S(self):
        S = self.S
        PB = self.PB
        din, dco, dout = self.din, self.dco, self.dout
        n = NS
        self.arena_reset()
        QKT_t = self.A_t
        QKT = [Buf(QKT_t[:, :, t * 128:(t + 1) * 128], "QKT%d" % t) for t in range(NT)]
        qkt_all = [b.tk for b in QKT]
        Vaug_t = self.arena("Vaug", (128, 16, 8, 65), BF16)
        Vaug = [Buf(Vaug_t.ap[:, b], "Vaug%d" % b) for b in range(16)]
        PT = [self.arena("PT", (128, 8, 256), BF16) for _ in range(3)]
        LT = self.arena("LT", (8, SEQ), F32)
        swam = self.arena("swam", (128, 256), BF16)
        sel = self.arena("sel", (8, 512), F32)
        t1 = self.arena("t1", (128, 512), F32)
        t2 = self.arena("t2", (128, 512), F32)
        qkb = [self.arena("qkb", (128, 1024), BF16) for _ in range(2)]
        kf = [self.arena("kf", (128, 512), F32) for _ in range(2)]
        vf = [self.arena("vf", (128, 512), F32) for _ in range(2)]
        Ub = [self.arena("Ub", (128, 8, 64), BF16) for _ in range(2)]
        lf = [self.arena("lf", (128, 8), F32) for _ in range(2)]
        tha, ga, gb = [t1, t2], kf, vf

        self.dma_in("pool", swam, swam.ap, dco["swamask"])
        self.dma_in("sp", sel, sel.ap, dco["sel"])
        self.memset("pool", Vaug_t, Vaug_t.ap[:, :, :, 64:65], 1.0)
        for b in Vaug:
            b.tk.w = Vaug_t.tk.w

        vscr_tk = Tk("vscr")
        def s1_front(t):
            xt = self.xT[t]
            bq, bk, bv = PB[(2 * t) % 4], PB[(2 * t + 1) % 4], PB[4 + t % 2]
            cos4 = self.cosp.ap[:, t, :].unsqueeze(1).unsqueeze(1).broadcast_to([128, 8, 2, 32])
            sin4 = self.sinp.ap[:, t, :, :].unsqueeze(1).broadcast_to([128, 8, 2, 32])
            qk = qkb[t % 2]
            self.proj(bq, 128, [xt.tk], xt.ap, 0)
            self.rope(bq, 128, self.cosp, cos4, self.sinp, sin4, t1, t2, qk, qk.ap[:, 0:512])
            self.proj(bk, 128, [xt.tk], xt.ap, 1)
            self.rope(bk, 128, self.cosp, cos4, self.sinp, sin4, t1, t2, kf[t % 2], kf[t % 2].ap)
            self.dma_out("sp", dout["kp"][t * 128:(t + 1) * 128, :], kf[t % 2], kf[t % 2].ap)
            self.cp("pool", qk, qk.ap[:, 512:1024], kf[t % 2], kf[t % 2].ap)
            self.proj(bv, 128, [xt.tk], xt.ap, 2)
            self.cp("act", vf[t % 2], vf[t % 2].ap, bv, bv.ap)
            self.dma_out("sp", dout["vp"][t * 128:(t + 1) * 128, :], vf[t % 2], vf[t % 2].ap)
            self.cp("pool", Vaug[t], Vaug[t].ap[:, :, 0:64], vf[t % 2], vf[t % 2].ap.rearrange("p (h f) -> p h f", h=8))
            S.dma("sp", (lambda e, t=t: e.dma_start(out=self.vscr[t * 128:(t + 1) * 128, :],
                                                    in_=Vaug[t].ap.rearrange("p h f -> p (h f)"))),
                  reads=[Vaug[t].tk], writes=[vscr_tk])

        def s1_back(t):
            bt = PB[6 + t % 2]
            qk = qkb[t % 2]
            btv = bt.ap.bitcast(BF16)
            for c in range(8):
                self.tr(bt, btv[:, c * 128:(c + 1) * 128], qk, qk.ap[:, c * 128:(c + 1) * 128], self.identb)
            self.cp("act", QKT[t], QKT[t].ap, bt, btv.rearrange("p (c t) -> p c t", c=8))

        for t in range(NT):
            s1_front(t)
            if t > 0:
                s1_back(t - 1)
        s1_back(NT - 1)
        cos4 = self.coss.ap.unsqueeze(1).unsqueeze(1).broadcast_to([n, 8, 2, 32])
        sin4 = self.sins.ap.unsqueeze(1).broadcast_to([n, 8, 2, 32])
        self.proj(PB[0], n, [self.xsT.tk], self.xsT.ap, 0)
        self.rope(PB[0], n, self.coss, cos4, self.sins, sin4, t1, t2, self.sqs, self.sqs.ap)
        self.proj(PB[1], n, [self.xsT.tk], self.xsT.ap, 1)
        self.rope(PB[1], n, self.coss, cos4, self.sins, sin4, t1, t2, kf[0], kf[0].ap[0:n])
        self.dma_out("sp", dout["ks"], kf[0], kf[0].ap[0:n])
        self.cp("pool", self.sks, self.sks.ap, kf[0], kf[0].ap[0:n])
        self.proj(PB[4], n, [self.xsT.tk], self.xsT.ap, 2)
        self.cp("act", vf[0], vf[0].ap[0:n], PB[4], PB[4].ap[0:n])
        self.dma_out("sp", dout["vs"], vf[0], vf[0].ap[0:n])
        self.memset("pool", self.Vnew, self.Vnew.ap[:, :, 64:65], 1.0)
        self.cp("pool", self.Vnew, self.Vnew.ap[:, :, 0:64], vf[0], vf[0].ap[0:n].rearrange("p (h f) -> p h f", h=8))
        self.proj(PB[5], n, [self.xsT.tk], self.xsT.ap, 3)
        self.gate(PB[5], n, t1, t2, t2.ap[0:n])
        self.ts("dve", self.Gss, self.Gss.ap, t2, t2.ap[0:n], 0.5, ALU.mult)
        pre = []
        for slot, col in ((4, C_RQ), (5, C_RV), (0, C_RG), (1, C_MQ)):
            for k in range(8):
                pre.append((slot, col, k))

        STB = PB[0:4]
        UB = [PB[4], PB[5]]
        TBa = PB[6]
        VB = PB[7]
        TBl = VB
        entries = []
        for g in (1, 4, 16):
            seqs = {1: [list(range(16))], 4: [[4 * r + nn for nn in range(4)] for r in range(4)],
                    16: [[r] for r in range(16)]}[g]
            for seq in seqs:
                for si, kb in enumerate(seq):
                    entries.append((g, seq, si, kb))
        NE = len(entries)
        vdone = {1: True}

        nextg = {1: 4, 4: 16}

        def vreload(g, b):
            tsl = tok_slice(g, b)
            S.dma("sp", (lambda e, b=b, tsl=tsl: e.dma_start(out=Vaug[b].ap.rearrange("p h f -> p (h f)"),
                                                            in_=self.vscr[tsl, :])),
                  reads=[vscr_tk], writes=[Vaug[b].tk])

        def stage_A(i):
            g, seq, si, kb = entries[i]
            last = (si == len(seq) - 1)
            nq = 128 if last else 256
            ksl = tok_slice(g, kb)
            qsl = tok_slice(g, kb, 1 if last else 2)
            pt = PT[i % 3]
            for h in range(8):
                c, half = h // 2, h % 2
                bank = STB[2 * (h // 4) + (h % 2)]
                slot = (h // 2) % 2
                self.mm(bank, bank.ap[:, slot * 256:slot * 256 + nq],
                        QKT[0], QKT_t[half * 64:(half + 1) * 64, 4 + c, ksl],
                        QKT[0], QKT_t[half * 64:(half + 1) * 64, c, qsl], extra_reads=qkt_all)
            for bi in range(4):
                bank = STB[bi]
                hs = 4 * (bi // 2) + (bi % 2)
                self.act(pt, pt.ap[:, hs:hs + 3:2, 0:nq], bank,
                         bank.ap.rearrange("p (s q) -> p s q", s=2)[:, :, 0:nq], AF.Exp, scale=0.125)
            self.tt("pool", pt, pt.ap[:, :, 0:nq], pt, pt.ap[:, :, 0:nq], swam,
                    swam.ap[:, 0:nq].unsqueeze(1).broadcast_to([128, 8, nq]), ALU.mult)

        def stage_B(i):
            g, seq, si, kb = entries[i]
            pt = PT[i % 3]
            ptp = PT[(i - 1) % 3]
            for h in range(8):
                ub = UB[h // 4]
                o_ap = ub.ap[:, (h % 4) * 65:(h % 4) * 65 + 65]
                if si > 0:
                    self.mm(ub, o_ap, ptp, ptp.ap[:, h, 128:256], Vaug[seq[si - 1]], Vaug[seq[si - 1]].ap[:, h, :],
                            start=True, stop=False)
                    self.mm(ub, o_ap, pt, pt.ap[:, h, 0:128], Vaug[kb], Vaug[kb].ap[:, h, :], start=False, stop=True)
                else:
                    self.mm(ub, o_ap, pt, pt.ap[:, h, 0:128], Vaug[kb], Vaug[kb].ap[:, h, :], start=True, stop=True)
            u = Ub[i % 2]
            l = lf[i % 2]
            for j in range(2):
                uv = UB[j].ap[:, 0:260].rearrange("p (h f) -> p h f", h=4)
                self.cp("dve", u, u.ap[:, 4 * j:4 * j + 4, :], UB[j], uv[:, :, 0:64])
                self.cp("dve", l, l.ap[:, 4 * j:4 * j + 4], UB[j], uv[:, :, 64])
            if g in nextg:
                if si > 0:
                    vreload(nextg[g], seq[si - 1])
                if si == len(seq) - 1:
                    vreload(nextg[g], kb)
            if g == 1 and kb == 0:
                self.load_w(2, din["w_in"], C_MG)
            if pre:
                slot, col, k = pre.pop(0)
                self.load_w1(slot, din["w_in"], col, k)

        def stage_C(i):
            g, seq, si, kb = entries[i]
            ksl = tok_slice(g, kb)
            u = Ub[i % 2]
            l = lf[i % 2]
            tav = TBa.ap.bitcast(BF16)
            for c in range(4):
                self.tr(TBa, tav[:, c * 128:(c + 1) * 128], u,
                        u.ap[:, 2 * c:2 * c + 2, :].rearrange("p h f -> p (h f)"), self.identb)
            self.tr(TBl, TBl.ap[0:8, 256:384], l, l.ap, self.identf)
            dstm = self.mixS_t[:, :, ksl]
            dstl = LT.ap[:, ksl]
            srcm = tav[:, 0:512].rearrange("p (c q) -> p c q", c=4)
            if g == 1:
                self.cp("dve", self.mixS, dstm, TBa, srcm)
                self.cp("dve", LT, dstl, TBl, TBl.ap[0:8, 256:384])
            else:
                self.tt("dve", self.mixS, dstm, TBa, srcm, self.mixS, dstm, ALU.add)
                self.tt("dve", LT, dstl, TBl, TBl.ap[0:8, 256:384], LT, dstl, ALU.add)

        for i in range(NE + 2):
            if i < NE:
                stage_A(i)
            if 0 <= i - 1 < NE:
                stage_B(i - 1)
            if 0 <= i - 2 < NE:
                stage_C(i - 2)

        S.op("dve", lambda e: e.reciprocal(out=LT.ap, in_=LT.ap), reads=[LT.tk], writes=[LT.tk])
        i = 0
        for nn in range(4):
            gsl = slice(nn * 512, (nn + 1) * 512)
            for c in range(4):
                bb, bg = PB[(2 * i) % 8], PB[(2 * i + 1) % 8]
                self.mm(bb, bb.ap, sel, sel.ap[:, c * 128:(c + 1) * 128], LT, LT.ap[:, gsl])
                wb = self.W[3]
                for k in range(8):
                    S.op("pe", (lambda e, k=k, c=c, gsl=gsl, bg=bg, wb=wb: e.matmul(
                        bg.ap, lhsT=wb.ap[:, k, c * 128:(c + 1) * 128], rhs=self.xT_t[:, k, gsl],
                        start=(k == 0), stop=(k == 7))), reads=self.xT_all + [wb.tk], writes=[bg.tk])
                th, a, b = tha[i % 2], ga[i % 2], gb[i % 2]
                self.act(th, th.ap, bg, bg.ap, AF.Tanh, scale=0.5)
                self.stt(a, a.ap, th, th.ap, 1.0, ALU.add, bg, bg.ap, ALU.mult)
                self.stt(b, b.ap, self.mixS, self.mixS_t[:, c, gsl], 0.5, ALU.mult, bb, bb.ap, ALU.mult)
                self.tt("dve", self.mixS, self.mixS_t[:, c, gsl], a, a.ap, b, b.ap, ALU.mult)
                i += 1

    def phase_R(self):
        S = self.S
        PB = self.PB
        din, dco, dout = self.din, self.dco, self.dout
        n = NS
        self.arena_reset()
        A = self.arena
        lgs = np.log1p(-np.exp2(-5.0 - np.arange(4, dtype=np.float64)))
        g128 = [float(np.exp(128.0 * v)) for v in lgs]
        g4 = [float(np.exp(4.0 * v)) for v in lgs]
        dqk = A("dqk", (128, 8), F32)
        dqks = A("dqks", (n, 8), F32)
        cmask = A("cmask", (128, 128), BF16)
        rmask = A("rmask", (n, 16), BF16)
        bmask = A("bmask", (n, 4), F32)
        mhalf = A("mhalf", (128, 4), F32)
        Rst = A("Rst", (128, 4, 128), F32)
        Rb = A("Rb", (128, 4, 128), BF16)
        CS = [A("CS", (128, 8, 2, 32), F32) for _ in range(2)]
        SN = [A("SN", (128, 8, 2, 32), F32) for _ in range(2)]
        t1 = A("t1", (128, 512), F32)
        t2 = A("t2", (128, 512), F32)
        QKr = [A("QKr", (128, 512), BF16) for _ in range(2)]
        Vr = [A("Vr", (128, 512), BF16) for _ in range(2)]
        thr = [A("thr", (128, 512), F32) for _ in range(2)]
        Gr = [A("Gr", (128, 512), F32) for _ in range(2)]
        QKrT = [A("QKrT", (128, 4, 128), BF16) for _ in range(2)]
        QZ = [A("QZ", (128, 4, 128), BF16) for _ in range(2)]
        STm = [A("STm", (128, 4, 128), BF16) for _ in range(2)]
        s12 = A("s12", (128, 2, 4), F32)
        st6 = A("st6", (128, 4, 6), F32)
        mvr = A("mvr", (128, 4, 2), F32)
        mean = A("mean", (128, 4), F32)
        msq = A("msq", (128, 4), F32)
        va = A("va", (128, 4), F32)
        ve = A("ve", (128, 4), F32)
        rstd = A("rstd", (128, 4), F32)
        junk = A("junk", (128, 512), BF16)
        mixRs = A("mixRs", (n, 512), BF16)
        On = A("On", (128, 512), F32)
        mixRt = [A("mixRt", (128, 512), BF16) for _ in range(2)]
        Rs = A("Rs", (128, 4, 4, 128), F32)
        Rsb = A("Rsb", (128, 4, 4, 128), BF16)
        QKsT = A("QKsT", (128, 4, n), BF16)
        QZs = A("QZs", (128, 4, n), BF16)
        QZB = A("QZB", (128, 4, 4, n), BF16)
        STs = A("STs", (n, 4, n), BF16)
        KZ = A("KZ", (n, 4, 256), BF16)

        for b_, k_ in ((dqk, "dqk"), (mhalf, "mhalf"), (dqks, "dqks"), (bmask, "bmask")):
            self.dma_in("sp", b_, b_.ap, dco[k_])
        self.dma_in("pool", cmask, cmask.ap, dco["cmask"])
        self.dma_in("pool", rmask, rmask.ap, dco["rmask"])
        Gz = A("Gz", (128, 4, 128), F32)
        self.dma_in("sp", Gz, Gz.ap, dco["g128"].rearrange("p (h f) -> p h f", h=4))
        self.memset("pool", Rst, Rst.ap, 0.0)
        self.memset("pool", Rb, Rb.ap, 0.0)
        for z in QZ:
            self.memset("pool", z, z.ap, 0.0)
        self.memset("pool", Rs, Rs.ap, 0.0)
        self.memset("pool", QZs, QZs.ap, 0.0)
        self.memset("pool", QZB, QZB.ap, 0.0)
        self.memset("pool", Rsb, Rsb.ap, 0.0)
        st_src = din["state"].rearrange("b h k v -> k (b h) v")
        for hp in range(2):
            rows = slice(hp * 64, hp * 64 + 64)
            self.dma_in("sp", Rs, Rs.ap.rearrange("p b h v -> p (b h) v")[rows, hp:16:2, :], st_src[:, hp:16:2, :], parallel=True)
            self.dma_in("pool", Rsb, Rsb.ap.rearrange("p b h v -> p (b h) v")[rows, hp:16:2, :], st_src[:, hp:16:2, :], parallel=True)
        dq4 = dqk.ap.unsqueeze(2).unsqueeze(3).broadcast_to([128, 8, 2, 32])
        WQK, WV, WG = 4, 5, 0

        def headnorm_gate(bo, nn_, gr, mixrt):
            for h in range(4):
                S.op("dve", (lambda e, h=h: e.bn_stats(out=st6.ap[0:nn_, h, :], in_=bo.ap[0:nn_, h * 128:(h + 1) * 128])),
                     reads=[bo.tk], writes=[st6.tk])
            for h in range(4):
                S.op("dve", (lambda e, h=h: e.bn_aggr(out=mvr.ap[0:nn_, h, :], in_=st6.ap[0:nn_, h, :])),
                     reads=[st6.tk], writes=[mvr.tk])
            self.ts("dve", ve, ve.ap[0:nn_], mvr, mvr.ap[0:nn_, :, 1], 4.0, ALU.mult, 4.0 * EPS, ALU.add)
            self.tt("pool", rstd, rstd.ap[0:nn_], ve, ve.ap[0:nn_], mhalf, mhalf.ap[0:nn_], ALU.pow)
            self.tt("pool", On, On.ap[0:nn_].rearrange("p (h f) -> p h f", h=4), gr, gr.ap[0:nn_].rearrange("p (h f) -> p h f", h=4),
                    rstd, rstd.ap[0:nn_].unsqueeze(2).broadcast_to([nn_, 4, 128]), ALU.mult)
            for h in range(4):
                hs = slice(h * 128, (h + 1) * 128)
                self.stt(mixrt, mixrt.ap[0:nn_, hs], bo, bo.ap[0:nn_, hs], mvr.ap[0:nn_, h, 0:1], ALU.subtract,
                         On, On.ap[0:nn_, hs], ALU.mult, extra_reads=[mvr])

        b0, b1, b2, bt, bs, bo = PB[0], PB[1], PB[2], PB[3], PB[4], PB[5]
        bkv = [PB[6], PB[7]]
        btv = bt.ap.bitcast(BF16)
        bsv = bs.ap.bitcast(BF16)

        def r_front(t):
            xt = self.xT[t]
            cs, sn = CS[t % 2], SN[t % 2]
            cos4 = self.cosp.ap[:, t, :].unsqueeze(1).unsqueeze(1).broadcast_to([128, 8, 2, 32])
            sin4 = self.sinp.ap[:, t, :, :].unsqueeze(1).broadcast_to([128, 8, 2, 32])
            self.tt("pool", cs, cs.ap, self.cosp, cos4, dqk, dq4, ALU.mult)
            self.tt("pool", sn, sn.ap, self.sinp, sin4, dqk, dq4, ALU.mult)
            qk, vr, th, gr, qkT, qz = QKr[t % 2], Vr[t % 2], thr[t % 2], Gr[t % 2], QKrT[t % 2], QZ[t % 2]
            self.proj(b0, 128, [xt.tk], xt.ap, WQK)
            self.rope(b0, 128, cs, cs.ap, sn, sn.ap, t1, t2, qk, qk.ap)
            self.proj(b1, 128, [xt.tk], xt.ap, WV)
            self.cp("act", vr, vr.ap, b1, b1.ap)
            self.proj(b2, 128, [xt.tk], xt.ap, WG)
            self.gate(b2, 128, th, gr, gr.ap)
            for c in range(4):
                self.tr(bt, btv[:, c * 128:(c + 1) * 128], qk, qk.ap[:, c * 128:(c + 1) * 128], self.identb)
            tv4 = btv[:, 0:512].rearrange("p (c t) -> p c t", c=4)
            self.cp("act", qkT, qkT.ap, bt, tv4)
            self.cp("act", qz, qz.ap[0:64, 0:4:2, :], bt, tv4[0:64, 0:2, :])
            self.cp("act", qz, qz.ap[64:128, 1:4:2, :], bt, tv4[64:128, 0:2, :])

        def r_back(t):
            qk, vr, gr, qkT, qz, stm, mixrt = QKr[t % 2], Vr[t % 2], Gr[t % 2], QKrT[t % 2], QZ[t % 2], STm[t % 2], mixRt[t % 2]
            for h in range(4):
                self.mm(bs, bs.ap[:, h * 128:(h + 1) * 128], qkT, qkT.ap[:, 2 + h // 2, :], qz, qz.ap[:, h, :])
            self.tt("dve", stm, stm.ap, bs, bs.ap.rearrange("p (h i) -> p h i", h=4), cmask,
                    cmask.ap.unsqueeze(1).broadcast_to([128, 4, 128]), ALU.mult)
            for h in range(4):
                o_ap = bo.ap[:, h * 128:(h + 1) * 128]
                self.mm(bo, o_ap, stm, stm.ap[:, h, :], vr, vr.ap[:, h * 128:(h + 1) * 128], start=True, stop=False)
                self.mm(bo, o_ap, qkT, qkT.ap[:, h // 2, :], Rb, Rb.ap[:, h, :], start=False, stop=True)
            for c in range(2):
                self.mm(bkv[c], bkv[c].ap, qk, qk.ap[:, 256 + c * 128:256 + (c + 1) * 128], vr, vr.ap)
            for c in range(2):
                rv_ = Rst.ap[:, 2 * c:2 * c + 2, :]
                kv_ = bkv[c].ap[:, 2 * c * 128:(2 * c + 2) * 128].rearrange("p (h f) -> p h f", h=2)
                self.tt("dve", Rst, rv_, bkv[c], kv_, Rst, rv_, ALU.add)
                self.tt("dve", Rst, rv_, Rst, rv_, Gz, Gz.ap[:, 2 * c:2 * c + 2, :], ALU.mult)
            self.cp("act", Rb, Rb.ap, Rst, Rst.ap)
            headnorm_gate(bo, 128, gr, mixrt)

        def r_tail(t):
            mixrt = mixRt[t % 2]
            for c in range(4):
                self.tr(bt, btv[:, c * 128:(c + 1) * 128], mixrt, mixrt.ap[:, c * 128:(c + 1) * 128], self.identb)
            self.cp("act", self.mixR[t], self.mixR[t].ap, bt, btv[:, 0:512].rearrange("p (c t) -> p c t", c=4))

        r_front(0)
        for t in range(NT):
            if t + 1 < NT:
                r_front(t + 1)
            r_back(t)
            if t > 0:
                r_tail(t - 1)
            if t == 7:
                self.sample_ret(locals())
        r_tail(NT - 1)
        for h in range(4):
            rows = slice((h % 2) * 64, (h % 2) * 64 + 64)
            self.dma_out("sp", dout["retp"][h], Rst, Rst.ap[rows, h, :])
        self.Wout = []
        for ec in range(12):
            slot = (4, 5, 0)[ec // 4]
            wb = self.W[slot]
            v = self.W_t[:, slot * 4096 + (ec % 4) * 1024:slot * 4096 + (ec % 4 + 1) * 1024]
            self.Wout.append((wb, v))
            S.dma("pool", (lambda e, ec=ec, v=v: e.dma_start(out=v, in_=din["w_out"][ec * 128:(ec + 1) * 128, :])),
                  reads=[], writes=[wb.tk], prefetch=True)

    def sample_ret(self, L):
        S = self.S
        PB = self.PB
        din, dout = self.din, self.dout
        n = NS
        g4 = L["g4"]
        dqks, rmask, bmask, t1, t2 = L["dqks"], L["rmask"], L["bmask"], L["t1"], L["t2"]
        Rs, Rsb, QKsT, QZs, QZB, STs, KZ = L["Rs"], L["Rsb"], L["QKsT"], L["QZs"], L["QZB"], L["STs"], L["KZ"]
        cs, sn = L["CS"][1], L["SN"][1]
        qk, vr, th, gr, mixrt = L["QKr"][1], L["Vr"][1], L["thr"][1], L["Gr"][1], L["mixRs"]
        b0, b1, b2, bt, bs, bo = PB[0], PB[1], PB[2], PB[3], PB[4], PB[5]
        cos4 = self.coss.ap.unsqueeze(1).unsqueeze(1).broadcast_to([n, 8, 2, 32])
        sin4 = self.sins.ap.unsqueeze(1).broadcast_to([n, 8, 2, 32])
        dq4 = dqks.ap.unsqueeze(2).unsqueeze(3).broadcast_to([n, 8, 2, 32])
        self.tt("pool", cs, cs.ap[0:n], self.coss, cos4, dqks, dq4, ALU.mult)
        self.tt("pool", sn, sn.ap[0:n], self.sins, sin4, dqks, dq4, ALU.mult)
        self.proj(b0, n, [self.xsT.tk], self.xsT.ap, L["WQK"])
        self.rope(b0, n, cs, cs.ap[0:n], sn, sn.ap[0:n], t1, t2, qk, qk.ap[0:n])
        self.proj(b1, n, [self.xsT.tk], self.xsT.ap, L["WV"])
        self.cp("act", vr, vr.ap[0:n], b1, b1.ap[0:n])
        self.proj(b2, n, [self.xsT.tk], self.xsT.ap, L["WG"])
        self.gate(b2, n, th, gr, gr.ap[0:n])
        btv = bt.ap.bitcast(BF16)
        for c in range(4):
            self.tr(bt, btv[:, c * n:(c + 1) * n], qk, qk.ap[0:n, c * 128:(c + 1) * 128], self.identb)
        tv4 = btv[:, 0:4 * n].rearrange("p (c t) -> p c t", c=4)
        self.cp("act", QKsT, QKsT.ap, bt, tv4)
        self.cp("dve", QZs, QZs.ap[0:64, 0:4:2, :], bt, tv4[0:64, 0:2, :])
        self.cp("dve", QZs, QZs.ap[64:128, 1:4:2, :], bt, tv4[64:128, 0:2, :])
        for b in range(4):
            self.cp("pool", QZB, QZB.ap[:, b, :, 4 * b:4 * b + 4], QZs, QZs.ap[:, :, 4 * b:4 * b + 4])
        for h in range(4):
            self.mm(bs, bs.ap[0:n, h * n:(h + 1) * n], QKsT, QKsT.ap[:, 2 + h // 2, :], QZs, QZs.ap[:, h, :])
        self.tt("dve", STs, STs.ap, bs, bs.ap[0:n, 0:4 * n].rearrange("p (h i) -> p h i", h=4), rmask,
                rmask.ap.unsqueeze(1).broadcast_to([n, 4, n]), ALU.mult)
        for h in range(4):
            o_ap = bo.ap[0:n, h * 128:(h + 1) * 128]
            self.mm(bo, o_ap, STs, STs.ap[:, h, :], vr, vr.ap[0:n, h * 128:(h + 1) * 128], start=True, stop=False)
            for b in range(4):
                self.mm(bo, o_ap, QZB, QZB.ap[:, b, h, :], Rsb, Rsb.ap[:, b, h, :], start=False, stop=(b == 3))
        for b in range(4):
            self.ts("dve", KZ, KZ.ap[:, b, :], qk, qk.ap[0:n, 256:512], bmask.ap[:, b:b + 1], ALU.mult, extra_reads=[bmask])
        kbanks = [PB[6], PB[7]]
        i = 0
        for b in range(4):
            for c in range(2):
                kb_ = kbanks[i % 2]
                i += 1
                self.mm(kb_, kb_.ap, KZ, KZ.ap[:, b, c * 128:(c + 1) * 128], vr, vr.ap[0:n])
                for hh in range(2):
                    h = 2 * c + hh
                    rows = slice(hh * 64, hh * 64 + 64)
                    self.ts("dve", Rs, Rs.ap[rows, b, h, :], Rs, Rs.ap[rows, b, h, :], g4[h], ALU.mult)
                    self.stt(Rs, Rs.ap[rows, b, h, :], kb_, kb_.ap[rows, h * 128:(h + 1) * 128], g4[h], ALU.mult,
                             Rs, Rs.ap[rows, b, h, :], ALU.add)
        for b in range(4):
            for h in range(4):
                rows = slice((h % 2) * 64, (h % 2) * 64 + 64)
                self.dma_out("sp", dout["rets"][b, h], Rs, Rs.ap[rows, b, h, :])
        L["headnorm_gate"](bo, n, gr, mixrt)
        for c in range(4):
            self.tr(bt, btv[:, c * n:(c + 1) * n], mixrt, mixrt.ap[0:n, c * 128:(c + 1) * 128], self.identb)
        self.cp("act", self.mixsT, self.mixsT.ap[:, 0:4, :], bt, tv4)

    def phase_M(self):
        S = self.S
        PB = self.PB
        din, dco, dout = self.din, self.dco, self.dout
        n = NS
        self.arena_reset()
        A = self.arena
        mkT, mvaug = self.mkT, self.mvaug
        mqT = [A("mqT", (128, 4, 512), BF16) for _ in range(2)]
        PTm = [A("PTm", (128, 4, 2, 512), BF16) for _ in range(2)]
        thm = [A("thm", (128, 512), F32) for _ in range(2)]
        Gm = [A("Gm", (128, 512), F32) for _ in range(2)]
        rl = [A("rl", (128, 4), F32) for _ in range(2)]
        mixMt = [A("mixMt", (128, 512), BF16) for _ in range(2)]
        WMQ, WMG = 1, 2
        rot = [0]

        def nb():
            rot[0] = (rot[0] + 1) % 4
            return PB[rot[0]]

        w2 = self.W[WMQ]
        OB = [PB[4], PB[5]]
        bt = PB[6]
        bg = PB[7]
        btv = bt.ap.bitcast(BF16)
        def m_tail(t):
            mixmt = mixMt[t % 2]
            for c in range(4):
                self.tr(bt, btv[:, c * 128:(c + 1) * 128], mixmt, mixmt.ap[:, c * 128:(c + 1) * 128], self.identb)
            self.cp("act", self.mixM[t], self.mixM[t].ap, bt, btv[:, 0:512].rearrange("p (c t) -> p c t", c=4))

        for nn in range(4):
            gsl = slice(nn * 512, (nn + 1) * 512)
            mq = mqT[nn % 2]
            pt = PTm[nn % 2]
            for h in range(4):
                bank = nb()
                for k in range(8):
                    S.op("pe", (lambda e, k=k, h=h, bank=bank, gsl=gsl: e.matmul(
                        bank.ap, lhsT=w2.ap[:, k, h * 128:(h + 1) * 128], rhs=self.xT_t[:, k, gsl],
                        start=(k == 0), stop=(k == 7))), reads=self.xT_all + [w2.tk], writes=[bank.tk])
                self.cp("act", mq, mq.ap[:, h, :], bank, bank.ap)
            for h in range(4):
                for mt in range(2):
                    bank = nb()
                    self.mm(bank, bank.ap, mkT, mkT.ap[:, h, mt * 128:(mt + 1) * 128], mq, mq.ap[:, h, :])
                    self.act(pt, pt.ap[:, h, mt, :], bank, bank.ap, AF.Exp, scale=float(128.0 ** -0.5))
            for tq in range(4):
                t = 4 * nn + tq
                xt = self.xT[t]
                th, gm, r, mixmt = thm[t % 2], Gm[t % 2], rl[t % 2], mixMt[t % 2]
                self.proj(bg, 128, [xt.tk], xt.ap, WMG)
                self.gate(bg, 128, th, gm, gm.ap)
                for h in range(4):
                    ob = OB[h // 2]
                    o_ap = ob.ap[:, (h % 2) * 129:(h % 2) * 129 + 129]
                    for mt in range(2):
                        self.mm(ob, o_ap, pt, pt.ap[:, h, mt, tq * 128:(tq + 1) * 128], mvaug, mvaug.ap[:, mt, h, :],
                                start=(mt == 0), stop=(mt == 1))
                for j in range(2):
                    ov = OB[j].ap[:, 0:258].rearrange("p (h f) -> p h f", h=2)
                    S.op("dve", (lambda e, j=j, ov=ov, r=r: e.reciprocal(out=r.ap[:, 2 * j:2 * j + 2], in_=ov[:, :, 128])),
                         reads=[OB[j].tk], writes=[r.tk])
                self.ts("dve", r, r.ap, r, r.ap, 0.5, ALU.mult)
                for h in range(4):
                    hs = slice(h * 128, (h + 1) * 128)
                    ob = OB[h // 2]
                    self.stt(mixmt, mixmt.ap[:, hs], ob, ob.ap[:, (h % 2) * 129:(h % 2) * 129 + 128], r.ap[:, h:h + 1], ALU.mult,
                             gm, gm.ap[:, hs], ALU.mult, extra_reads=[r])
                if t > 0:
                    m_tail(t - 1)
            if nn == 1:
                self.sample_mem(locals())
        m_tail(NT - 1)

    def sample_mem(self, L):
        S = self.S
        PB = self.PB
        din = self.din
        n = NS
        A = self.arena
        WMQ, WMG = L["WMQ"], L["WMG"]
        nb = L["nb"]
        th, gm, rl, mixm = L["thm"][0], L["Gm"][0], L["rl"][0], L["mixMt"][0]
        mqs = A("mqs", (n, 512), BF16)
        mkld = [A("mkld", (128, 2, 512), BF16) for _ in range(2)]
        mkTs = A("mkTs", (128, 4, 4, 256), BF16)
        mvaugs = A("mvaugs", (128, 4, 2, 4, 129), BF16)
        mqsT = A("mqsT", (128, 4, n), BF16)
        PTms = A("PTms", (128, 4, 4, 2, n), BF16)
        bt = L["bt"]
        btv = L["btv"]
        bg = L["bg"]
        self.memset("pool", mvaugs, mvaugs.ap[:, :, :, :, 128:129], 1.0)
        self.memset("pool", PTms, PTms.ap, 0.0)
        for b in range(4):
            ml = mkld[b % 2]
            self.dma_in("pool", ml, ml.ap, din["cmk"][b].rearrange("(t p) c -> p t c", p=128))
            for mt in range(2):
                S.dma("pool", (lambda e, b=b, mt=mt: e.dma_start(
                    out=mvaugs.ap[:, b, mt, :, 0:128],
                    in_=din["cmv"][b][mt * 128:(mt + 1) * 128, :].rearrange("p (h f) -> p h f", h=4))),
                    reads=[], writes=[mvaugs.tk])
            for mt in range(2):
                bank = nb()
                bv = bank.ap.bitcast(BF16)
                for h in range(4):
                    self.tr(bank, bv[:, h * 128:(h + 1) * 128], ml, ml.ap[:, mt, h * 128:(h + 1) * 128], self.identb)
                self.cp("act", mkTs, mkTs.ap[:, b, :, mt * 128:(mt + 1) * 128], bank,
                        bv[:, 0:512].rearrange("p (h m) -> p h m", h=4))
        self.proj(bg, n, [self.xsT.tk], self.xsT.ap, WMQ)
        self.cp("act", mqs, mqs.ap, bg, bg.ap[0:n])
        for c in range(4):
            self.tr(bt, btv[:, c * n:(c + 1) * n], mqs, mqs.ap[:, c * 128:(c + 1) * 128], self.identb)
        tv4 = btv[:, 0:4 * n].rearrange("p (c t) -> p c t", c=4)
        self.cp("act", mqsT, mqsT.ap, bt, tv4)
        self.proj(bg, n, [self.xsT.tk], self.xsT.ap, WMG)
        self.gate(bg, n, th, gm, gm.ap[0:n])
        bsc = nb()
        for b in range(4):
            for h in range(4):
                for mt in range(2):
                    col = ((b * 4 + h) * 2 + mt) * 4
                    self.mm(bsc, bsc.ap[:, col:col + 4], mkTs, mkTs.ap[:, b, h, mt * 128:(mt + 1) * 128],
                            mqsT, mqsT.ap[:, h, 4 * b:4 * b + 4])
        for b in range(4):
            self.act(PTms, PTms.ap[:, b, :, :, 4 * b:4 * b + 4], bsc,
                     bsc.ap[:, b * 32:(b + 1) * 32].rearrange("p (h m i) -> p h m i", h=4, m=2), AF.Exp,
                     scale=float(128.0 ** -0.5))
        OB = L["OB"]
        for h in range(4):
            ob = OB[h // 2]
            o_ap = ob.ap[0:n, (h % 2) * 129:(h % 2) * 129 + 129]
            k = 0
            for b in range(4):
                for mt in range(2):
                    self.mm(ob, o_ap, PTms, PTms.ap[:, b, h, mt, :], mvaugs, mvaugs.ap[:, b, mt, h, :],
                            start=(k == 0), stop=(k == 7))
                    k += 1
        for j in range(2):
            ov = OB[j].ap[0:n, 0:258].rearrange("p (h f) -> p h f", h=2)
            S.op("dve", (lambda e, j=j, ov=ov: e.reciprocal(out=rl.ap[0:n, 2 * j:2 * j + 2], in_=ov[:, :, 128])),
                 reads=[OB[j].tk], writes=[rl.tk])
        self.ts("dve", rl, rl.ap[0:n], rl, rl.ap[0:n], 0.5, ALU.mult)
        for h in range(4):
            hs = slice(h * 128, (h + 1) * 128)
            ob = OB[h // 2]
            self.stt(mixm, mixm.ap[0:n, hs], ob, ob.ap[0:n, (h % 2) * 129:(h % 2) * 129 + 128], rl.ap[0:n, h:h + 1], ALU.mult,
                     gm, gm.ap[0:n, hs], ALU.mult, extra_reads=[rl])
        for c in range(4):
            self.tr(bt, btv[:, c * n:(c + 1) * n], mixm, mixm.ap[0:n, c * 128:(c + 1) * 128], self.identb)
        self.cp("act", self.mixsT, self.mixsT.ap[:, 8:12, :], bt, tv4)

    def phase_F(self):
        S = self.S
        PB = self.PB
        din, dco, dout = self.din, self.dco, self.dout
        n = NS
        self.arena_reset()
        A = self.arena
        Gt = A("Gt", (128, D), F32)
        Bt = A("Bt", (128, D), F32)
        mhalf = A("mhalf", (128, 4), F32)
        self.dma_in("sp", Gt, Gt.ap, din["ln_g"][0:1, :].broadcast_to([128, D]))
        self.dma_in("sp", Bt, Bt.ap, din["ln_b"][0:1, :].broadcast_to([128, D]))
        self.dma_in("sp", mhalf, mhalf.ap, dco["mhalf"])
        xf = [A("xf", (128, D), F32) for _ in range(3)]
        zz = [A("zz", (128, D), F32) for _ in range(3)]
        st = [A("st", (128, 2, 6), F32) for _ in range(2)]
        mv = [A("mv", (128, 2), F32) for _ in range(2)]
        ve = [A("ve", (128, 1), F32) for _ in range(2)]
        rstd = [A("rstd", (128, 1), F32) for _ in range(2)]
        nmr = [A("nmr", (128, 1), F32) for _ in range(2)]
        gen = self.sample_swa()
        next(gen)

        def pull(k):
            for _ in range(k):
                try:
                    next(gen)
                except StopIteration:
                    return

        for t in range(min(2, NT)):
            self.dma_in("sp", xf[t % 3], xf[t % 3].ap, din["x"][t * 128:(t + 1) * 128, :])
        for t in range(NT):
            tsl = slice(t * 128, (t + 1) * 128)
            x_, z_ = xf[t % 3], zz[t % 3]
            if t + 2 < NT:
                self.dma_in("sp", xf[(t + 2) % 3], xf[(t + 2) % 3].ap, din["x"][(t + 2) * 128:(t + 3) * 128, :])
            hb = [PB[2 * (t % 2)], PB[2 * (t % 2) + 1]]
            for half in range(2):
                pull(1)
                for ec in range(12):
                    if ec < 4:
                        mb, map_ = self.mixR[t], self.A_t[:, ec, tsl]
                    elif ec < 8:
                        mb, map_ = self.mixS, self.mixS_t[:, ec - 4, tsl]
                    else:
                        mb, map_ = self.mixM[t], self.A_t[:, 4 + ec - 8, tsl]
                    wb, wv = self.Wout[ec]
                    self.mm(hb[half], hb[half].ap, mb, map_, wb, wv[:, half * 512:(half + 1) * 512],
                            start=(ec == 0), stop=(ec == 11))
            for half in range(2):
                hs = slice(half * 512, (half + 1) * 512)
                self.stt(z_, z_.ap[:, hs], x_, x_.ap[:, hs], ALPHA, ALU.mult, hb[half], hb[half].ap, ALU.add)
            pull(1)
            self.layernorm(z_, 128, st[t % 2], mv[t % 2], ve[t % 2], rstd[t % 2], nmr[t % 2], mhalf, Gt, Bt, stage=1)
            pull(1)
            if t > 0:
                zp = zz[(t - 1) % 3]
                self.layernorm(zp, 128, None, None, None, None, None, mhalf, Gt, Bt, stage=2)
                self.dma_out("sp", dout["y"][(t - 1) * 128:t * 128, :], zp, zp.ap)
            pull(1)
        zp = zz[(NT - 1) % 3]
        self.layernorm(zp, 128, None, None, None, None, None, mhalf, Gt, Bt, stage=2)
        self.dma_out("sp", dout["y"][(NT - 1) * 128:NT * 128, :], zp, zp.ap)
        pull(1000)
        st, mv, ve, rstd, nmr = st[0], mv[0], ve[0], rstd[0], nmr[0]
        x_, z_ = xf[0], zz[0]
        self.dma_in("sp", x_, x_.ap[0:n], din["xs"])
        hb = [PB[0], PB[1]]
        for half in range(2):
            for ec in range(12):
                wb, wv = self.Wout[ec]
                self.mm(hb[half], hb[half].ap[0:n], self.mixsT, self.mixsT.ap[:, ec, :], wb,
                        wv[:, half * 512:(half + 1) * 512], start=(ec == 0), stop=(ec == 11))
        for half in range(2):
            hs = slice(half * 512, (half + 1) * 512)
            self.stt(z_, z_.ap[0:n, hs], x_, x_.ap[0:n, hs], ALPHA, ALU.mult, hb[half], hb[half].ap[0:n], ALU.add)
        self.layernorm(z_, n, st, mv, ve, rstd, nmr, mhalf, Gt, Bt)
        self.dma_out("sp", dout["ys"], z_, z_.ap[0:n])

    def layernorm(self, z_, n, st, mv, ve, rstd, nmr, mhalf, Gt, Bt, stage=0):
        S = self.S
        if stage in (0, 1):
            for half in range(2):
                hs = slice(half * 512, (half + 1) * 512)
                S.op("dve", (lambda e, half=half, hs=hs: e.bn_stats(out=st.ap[0:n, half, :], in_=z_.ap[0:n, hs])),
                     reads=[z_.tk], writes=[st.tk])
            S.op("dve", lambda e: e.bn_aggr(out=mv.ap[0:n, :], in_=st.ap[0:n, :, :].rearrange("p a b -> p (a b)")),
                 reads=[st.tk], writes=[mv.tk])
            self.ts("dve", ve, ve.ap[0:n], mv, mv.ap[0:n, 1:2], EPS, ALU.add)
            self.tt("pool", rstd, rstd.ap[0:n], ve, ve.ap[0:n], mhalf, mhalf.ap[0:n, 0:1], ALU.pow)
            self.ts("dve", nmr, nmr.ap[0:n], mv, mv.ap[0:n, 0:1], -1.0, ALU.mult, rstd.ap[0:n, 0:1], ALU.mult, extra_reads=[rstd])
            self.ts("dve", z_, z_.ap[0:n], z_, z_.ap[0:n], rstd.ap[0:n, 0:1], ALU.mult, nmr.ap[0:n, 0:1], ALU.add,
                    extra_reads=[rstd, nmr])
        if stage in (0, 2):
            self.tt("dve", z_, z_.ap[0:n], z_, z_.ap[0:n], Gt, Gt.ap[0:n], ALU.mult)
            self.tt("dve", z_, z_.ap[0:n], z_, z_.ap[0:n], Bt, Bt.ap[0:n], ALU.add)

    def sample_swa(self):
        S = self.S
        PB = self.PB
        din, dco = self.din, self.dco
        n = NS
        A = self.arena
        sqs, sks, Gss, Vnew = self.sqs, self.sks, self.Gss, self.Vnew
        smask = A("smask", (128, 9, 4), BF16)
        smaskn = A("smaskn", (n, 4, 4), BF16)
        ones = A("ones", (128, 2), BF16)
        sqsT = A("sqsT", (128, 4, n), BF16)
        sksT = A("sksT", (128, 4, n), BF16)
        GssT = A("GssT", (128, 4, n), BF16)
        Qbd = A("Qbd", (128, 4, 4, 8), BF16)
        Kc = A("Kc", (128, 9, 512), BF16)
        KcT = A("KcT", (128, 9, 4, 128), BF16)
        Vc = A("Vc", (128, 9, 512), BF16)
        PTx = [A("PTx", (128, 10, 8, 4), BF16) for _ in range(2)]
        rls = A("rlx", (4, 8), F32)
        Onb_ap = KcT.ap[0:4, 0, :, :].rearrange("p c k -> p (c k)")
        bt = PB[4]
        btv = bt.ap.bitcast(BF16)
        bsx = PB[5]
        UB = [PB[6], PB[7]]

        def issue_k(b):
            ck = din["ck"][b]
            self.dma_in("pool", Kc, Kc.ap[:, 0:4, :], ck.rearrange("(m s) c -> m s c", s=16)[:, 0:4, :], parallel=True)
            self.dma_in("pool", Kc, Kc.ap[:, 4:8, :], ck[1536:2048, :].rearrange("(m s) c -> m s c", s=4), parallel=True)
            self.dma_in("pool", Kc, Kc.ap[:, 8, :], ck[1920:2048, :], parallel=True)

        def issue_v(b):
            cv = din["cv"][b]
            self.dma_in("pool", Vc, Vc.ap[:, 0:4, :], cv.rearrange("(m s) c -> m s c", s=16)[:, 0:4, :], parallel=True)
            self.dma_in("pool", Vc, Vc.ap[:, 4:8, :], cv[1536:2048, :].rearrange("(m s) c -> m s c", s=4), parallel=True)
            self.dma_in("pool", Vc, Vc.ap[:, 8, :], cv[1920:2048, :], parallel=True)

        self.dma_in("pool", smask, smask.ap, dco["smask"])
        self.dma_in("pool", smaskn, smaskn.ap, dco["smaskn"])
        self.memset("pool", ones, ones.ap, 1.0)
        self.memset("pool", Qbd, Qbd.ap, 0.0)
        for p_ in PTx:
            self.memset("pool", p_, p_.ap, 0.0)
        issue_k(0)
        issue_v(0)
        yield
        tv4 = btv[:, 0:4 * n].rearrange("p (c t) -> p c t", c=4)
        for src, dst in ((sqs, sqsT), (sks, sksT), (Gss, GssT)):
            for c in range(4):
                self.tr(bt, btv[:, c * n:(c + 1) * n], src, src.ap[:, c * 128:(c + 1) * 128], self.identb)
            self.cp("act", dst, dst.ap, bt, tv4)
        self.cp("pool", Qbd, Qbd.ap[0:64, :, :, 0:4], sqsT, sqsT.ap[0:64, :, :].rearrange("p c (b i) -> p c b i", b=4))
        self.cp("pool", Qbd, Qbd.ap[64:128, :, :, 4:8], sqsT, sqsT.ap[64:128, :, :].rearrange("p c (b i) -> p c b i", b=4))
        yield
        for b in range(4):
            pt = PTx[b % 2]
            for tl in range(9):
                for c in range(4):
                    self.tr(bt, btv[:, c * 128:(c + 1) * 128], Kc, Kc.ap[:, tl, c * 128:(c + 1) * 128], self.identb)
                self.cp("act", KcT, KcT.ap[:, tl, :, :], bt,
                        btv[:, 0:512].rearrange("p (c k) -> p c k", c=4))
                yield
            if b + 1 < 4:
                issue_k(b + 1)
            for tl in range(9):
                for c in range(4):
                    self.mm(bsx, bsx.ap[:, tl * 32 + c * 8:tl * 32 + c * 8 + 8], KcT, KcT.ap[:, tl, c, :], Qbd, Qbd.ap[:, c, b, :])
            for c in range(4):
                self.mm(bsx, bsx.ap[0:n, 288 + c * 8:288 + c * 8 + 8], sksT, sksT.ap[:, c, :], Qbd, Qbd.ap[:, c, b, :])
            yield
            self.act(pt, pt.ap[:, 0:9, :, :], bsx, bsx.ap[:, 0:288].rearrange("p (t h i) -> p t h i", t=9, h=8), AF.Exp, scale=0.125)
            self.act(pt, pt.ap[0:n, 9, :, :], bsx, bsx.ap[0:n, 288:320].rearrange("p (h i) -> p h i", h=8), AF.Exp, scale=0.125)
            self.tt("pool", pt, pt.ap[:, 0:9, :, :], pt, pt.ap[:, 0:9, :, :], smask,
                    smask.ap.unsqueeze(2).broadcast_to([128, 9, 8, 4]), ALU.mult)
            self.tt("pool", pt, pt.ap[0:n, 9, :, :], pt, pt.ap[0:n, 9, :, :], smaskn,
                    smaskn.ap[:, b, :].unsqueeze(1).broadcast_to([n, 8, 4]), ALU.mult)
            yield
            for h in range(8):
                ub = UB[h // 4]
                o_ap = ub.ap[0:4, (h % 4) * 64:(h % 4) * 64 + 64]
                for tl in range(9):
                    self.mm(ub, o_ap, pt, pt.ap[:, tl, h, :], Vc, Vc.ap[:, tl, h * 64:(h + 1) * 64], start=(tl == 0), stop=False)
                self.mm(ub, o_ap, pt, pt.ap[0:n, 9, h, :], Vnew, Vnew.ap[:, h, 0:64], start=False, stop=True)
                l_ap = bsx.ap[0:4, 320 + h:321 + h]
                for tl in range(9):
                    self.mm(bsx, l_ap, pt, pt.ap[:, tl, h, :], ones, ones.ap[:, 0:1], start=(tl == 0), stop=False)
                self.mm(bsx, l_ap, pt, pt.ap[0:n, 9, h, :], ones, ones.ap[0:n, 0:1], start=False, stop=True)
                if h % 2 == 1:
                    yield
            if b + 1 < 4:
                issue_v(b + 1)
            S.op("dve", lambda e: e.reciprocal(out=rls.ap, in_=bsx.ap[0:4, 320:328]), reads=[bsx.tk], writes=[rls.tk])
            for j in range(2):
                uv = UB[j].ap[0:4, 0:256].rearrange("p (h f) -> p h f", h=4)
                self.tt("dve", KcT, Onb_ap[:, 256 * j:256 * (j + 1)].rearrange("p (h f) -> p h f", h=4), UB[j], uv,
                        rls, rls.ap[:, 4 * j:4 * j + 4].unsqueeze(2).broadcast_to([4, 4, 64]), ALU.mult)
            for c in range(4):
                self.tr(bt, btv[:, c * 4:(c + 1) * 4], KcT, Onb_ap[:, c * 128:(c + 1) * 128], self.identb)
            self.tt("dve", self.mixsT, self.mixsT.ap[:, 4:8, 4 * b:4 * b + 4], bt, btv[:, 0:16].rearrange("p (c i) -> p c i", c=4),
                    GssT, GssT.ap[:, :, 4 * b:4 * b + 4], ALU.mult)
            yield


_CACHE = {}


def _get_prog(phases):
    key = tuple(phases)
    if key not in _CACHE:
        p = Prog()
        _CACHE[key] = p.build(phases)
    return _CACHE[key]


PHASES = ("S", "R", "M", "F", "X")


def kernel(x_prompt, x_sample, state_ret, cache_swa_k, cache_swa_v, cache_mem_k, cache_mem_v,
           mem_prompt, w_in, w_mem_kv, w_out, ln_gain, ln_bias):
    f = lambda a: np.ascontiguousarray(np.asarray(a, dtype=np.float32))
    consts = {"c_" + k: v for k, v in _consts().items()}
    nc = _get_prog(PHASES)
    in_maps = []
    for c in range(NCORES):
        sb = slice(4 * c, 4 * c + 4)
        m = {
            "x": f(x_prompt[c]), "memx": f(mem_prompt[c]), "w_in": f(w_in[0]), "w_mem": f(w_mem_kv[0]),
            "w_out": f(w_out[0]), "ln_g": f(ln_gain), "ln_b": f(ln_bias),
            "xs": f(np.asarray(x_sample)[sb].reshape(NS, D)),
            "state": f(np.asarray(state_ret)[0, sb]),
            "ck": f(np.asarray(cache_swa_k)[0, sb].reshape(4, 2048, 512)),
            "cv": f(np.asarray(cache_swa_v)[0, sb].reshape(4, 2048, 512)),
            "cmk": f(np.asarray(cache_mem_k)[0, sb].reshape(4, 256, 512)),
            "cmv": f(np.asarray(cache_mem_v)[0, sb].reshape(4, 256, 512)),
        }
        m.update(consts)
        in_maps.append(m)
    res = run_bass_kernel_spmd(nc, in_maps, core_ids=list(range(NCORES)))
    R = res.results
    cat = lambda k: np.stack([np.asarray(R[c][k]) for c in range(NCORES)], axis=0)
    y = cat("y")
    ys = cat("ys").reshape(32, 4, D)
    retp = cat("retp")[None]
    rets = cat("rets").reshape(32, 4, 64, 128)[None]
    kp = cat("kp").reshape(8, SEQ, 8, 64)[None]
    vp = cat("vp").reshape(8, SEQ, 8, 64)[None]
    ks = cat("ks").reshape(32, 4, 8, 64)[None]
    vs = cat("vs").reshape(32, 4, 8, 64)[None]
    mkp = cat("mkp").reshape(8, 256, 4, 128)[None]
    mvp = cat("mvp").reshape(8, 256, 4, 128)[None]
    return (y, ys, retp, rets, kp, vp, ks, vs, mkp, mvp)
```

```python
from contextlib import ExitStack

import numpy as np
import concourse.bass as bass
import concourse.mybir as mybir
from concourse.bass_utils import run_bass_kernel_spmd

F32 = mybir.dt.float32
BF16 = mybir.dt.bfloat16
AF = mybir.ActivationFunctionType
ALU = mybir.AluOpType
AX = mybir.AxisListType

NCORES = 8
D = 1024
SEQ = 2048
NT = 16
DIN = 4608
DMIX = 1536
NS = 16
PAST = 8192
ALPHA = 2.0 ** 0.25
EPS = 1e-5
C_RQ, C_RK, C_RV, C_RG, C_SQ, C_SK, C_SV, C_SG, C_MQ, C_MG = 0, 256, 512, 1024, 1536, 2048, 2560, 3072, 3584, 4096

ENGS = ("pe", "act", "dve", "pool", "sp")
SAME_ENGINE_SYNC = True
SAME_ENGINE_WAR = True


class Tk:
    __slots__ = ("name", "w", "r", "excl", "wg")

    def __init__(self, name):
        self.name = name
        self.w = None
        self.r = {}
        self.wg = []
        self.excl = False


class Buf:
    __slots__ = ("ap", "tk")

    def __init__(self, ap, name):
        self.ap = ap
        self.tk = Tk(name)


class Sched:
    def __init__(self, n_dma_sems):
        self.q = {e: [] for e in ENGS}
        self.known = {e: {} for e in ENGS}
        self.dma_val = [0] * n_dma_sems
        self.rr = 0
        self.rrp = 0
        self.rrq = 0
        self.needed = {e: set() for e in ENGS}

    def _collect(self, eng, reads, writes, par=False):
        deps = []
        for t in reads:
            if t.w is not None:
                deps.append((t.w, True))
            for d in t.wg:
                deps.append((d, True))
            if t.excl:
                for d in t.r.values():
                    if not (d[0] == "e" and d[1] == eng):
                        deps.append((d, True))
        for t in writes:
            if t.w is not None and not (par and t.w[0] == "d" and not t.r):
                deps.append((t.w, True))
                for d in t.wg:
                    deps.append((d, True))
            for d in t.r.values():
                deps.append((d, False))
        kn = self.known[eng]
        best = {}
        for d, is_raw in deps:
            if d[0] == "e" and d[1] == eng:
                if eng == "pe" or not (SAME_ENGINE_SYNC and (is_raw or SAME_ENGINE_WAR)):
                    continue
            key = (d[0], d[1])
            if kn.get(key, -1) >= d[2]:
                continue
            if key not in best or best[key][2] < d[2]:
                best[key] = d
        waits = list(best.values())
        for d in waits:
            kn[(d[0], d[1])] = d[2]
            if d[0] == "e":
                self.needed[d[1]].add(d[2])
        return waits

    def op(self, eng, fn, reads=(), writes=()):
        idx = len(self.q[eng])
        waits = self._collect(eng, reads, writes)
        self.q[eng].append({"fn": fn, "waits": waits, "dma": None})
        me = ("e", eng, idx)
        for t in reads:
            t.r[("e", eng)] = me
        for t in writes:
            t.w = me
            t.wg = []
            t.r = {}
        return idx

    NPRE = 12

    def dma(self, eng, fn, reads=(), writes=(), prefetch=False, parallel=False):
        nn = len(self.dma_val) - self.NPRE
        half = nn // 2
        if prefetch:
            si = nn + self.rrp
            self.rrp = (self.rrp + 1) % self.NPRE
        elif eng == "pool":
            si = half + self.rrq
            self.rrq = (self.rrq + 1) % (nn - half)
        else:
            si = self.rr
            self.rr = (self.rr + 1) % half
        prev = self.dma_val[si]
        new = prev + 16
        self.dma_val[si] = new
        waits = self._collect(eng, reads, writes, par=parallel)
        if prev > 0:
            kn = self.known[eng]
            if kn.get(("d", si), -1) < prev:
                kn[("d", si)] = prev
                waits.append(("d", si, prev))
        self.q[eng].append({"fn": fn, "waits": waits, "dma": si})
        me = ("d", si, new)
        for t in reads:
            t.r[("d", si)] = me
        for t in writes:
            if parallel and t.w is not None and t.w[0] == "d" and not t.r:
                t.wg.append(t.w)
            else:
                t.wg = []
            t.w = me
            t.r = {}

    def barrier(self, final=False):
        last = {}
        for e in ENGS:
            for i in range(len(self.q[e]) - 1, -1, -1):
                ent = self.q[e][i]
                if ent["dma"] is None and ent["fn"] is not None:
                    last[e] = i
                    break
        for e in ENGS:
            waits = []
            kn = self.known[e]
            for e2, idx in last.items():
                if e2 == e:
                    continue
                if kn.get(("e", e2), -1) < idx:
                    kn[("e", e2)] = idx
                    waits.append(("e", e2, idx))
                    self.needed[e2].add(idx)
            for si, v in enumerate(self.dma_val):
                if si >= len(self.dma_val) - self.NPRE and not final:
                    continue
                if v > 0 and kn.get(("d", si), -1) < v:
                    kn[("d", si)] = v
                    waits.append(("d", si, v))
            self.q[e].append({"fn": None, "waits": waits, "dma": None})

    def emit(self, block, eng_sems, dma_sems):
        rank = {}
        for e in ENGS:
            for r, idx in enumerate(sorted(self.needed[e])):
                rank[(e, idx)] = r + 1

        def run(ename, eng):
            for idx, ent in enumerate(self.q[ename]):
                for d in ent["waits"]:
                    if d[0] == "e":
                        eng.wait_ge(eng_sems[d[1]], rank[(d[1], d[2])])
                    else:
                        eng.wait_ge(dma_sems[d[1]], d[2])
                if ent["fn"] is None:
                    continue
                ins = ent["fn"](eng)
                if ent["dma"] is not None:
                    ins.then_inc(dma_sems[ent["dma"]], 16)
                elif (ename, idx) in rank:
                    ins.then_inc(eng_sems[ename], 1)

        block.tensor(lambda eng: run("pe", eng))
        block.scalar(lambda eng: run("act", eng))
        block.vector(lambda eng: run("dve", eng))
        block.gpsimd(lambda eng: run("pool", eng))
        block.sync(lambda eng: run("sp", eng))


def _consts():
    f32 = np.float32
    c = {}
    c["ident"] = np.eye(128, dtype=f32)
    inv = (f32(10000.0) ** (-(np.arange(32, dtype=f32) * f32(2.0) / f32(64)))).astype(f32)
    pos = np.arange(SEQ, dtype=f32)
    ang = (pos[:, None] * inv[None, :]).astype(f32).astype(np.float64)
    cs = np.cos(ang).reshape(NT, 128, 32).transpose(1, 0, 2)
    sn = np.sin(ang).reshape(NT, 128, 32).transpose(1, 0, 2)
    c["cosp"] = np.ascontiguousarray(cs).astype(f32)
    c["sinp"] = np.ascontiguousarray(np.stack([-sn, sn], axis=2)).astype(f32)
    poss = (PAST + np.arange(4)).astype(f32)
    angs = (poss[:, None] * inv[None, :]).astype(f32).astype(np.float64)
    c["coss"] = np.tile(np.cos(angs), (4, 1)).astype(f32)
    sns = np.tile(np.sin(angs), (4, 1))
    c["sins"] = np.stack([-sns, sns], axis=1).astype(f32)
    lg = np.log1p(-np.exp2(-5.0 - np.arange(4, dtype=np.float64)))
    p1 = np.arange(1, 129, dtype=np.float64)[:, None]
    dq = np.exp(p1 * lg[None, :])
    dk = np.exp(-p1 * lg[None, :]) * 0.125
    c["dqk"] = np.concatenate([dq, dk], axis=1).astype(f32)
    g128 = np.exp(128.0 * lg)
    G = np.zeros((128, 4, 128), dtype=np.float64)
    for h in range(4):
        G[(h % 2) * 64:(h % 2) * 64 + 64, h, :] = g128[h]
    c["g128"] = G.reshape(128, 512).astype(f32)
    jj = np.arange(128)[:, None]
    ii = np.arange(128)[None, :]
    c["cmask"] = (ii >= jj).astype(f32)
    c["swamask"] = np.concatenate([(ii >= jj), (ii <= jj)], axis=1).astype(f32)
    sel = np.zeros((8, 4, 128), dtype=f32)
    for h in range(8):
        sel[h, h // 2, (h % 2) * 64:(h % 2) * 64 + 64] = 1.0
    c["sel"] = sel.reshape(8, 512)
    c["mhalf"] = np.full((128, 4), -0.5, dtype=f32)
    i4 = np.tile(np.arange(4, dtype=np.float64), 4)[:, None]
    dqs = np.exp((i4 + 1.0) * lg[None, :])
    dks = np.exp(-(i4 + 1.0) * lg[None, :]) * 0.125
    c["dqks"] = np.concatenate([dqs, dks], axis=1).astype(f32)
    r16 = np.arange(16)
    bj, ij = r16 // 4, r16 % 4
    c["rmask"] = ((bj[:, None] == bj[None, :]) & (ij[None, :] >= ij[:, None])).astype(f32)
    c["bmask"] = (bj[:, None] == np.arange(4)[None, :]).astype(f32)
    sm = np.zeros((128, 9, 4), dtype=f32)
    for i in range(4):
        sm[:, i, i] = 1.0
        sm[:, 4 + i, i] = 1.0
        sm[:, 8, i] = (np.arange(128) >= i).astype(f32)
    c["smask"] = sm
    smn = np.zeros((16, 4, 4), dtype=f32)
    for kk in range(16):
        for b in range(4):
            for i in range(4):
                if kk // 4 == b:
                    j = kk % 4
                    smn[kk, b, i] = (1.0 if j <= i else 0.0) + (2.0 if j == i else 0.0)
    c["smaskn"] = smn
    return c


CONST_SHAPES = {
    "ident": (128, 128), "cosp": (128, 16, 32), "sinp": (128, 16, 2, 32), "coss": (16, 32), "sins": (16, 2, 32),
    "dqk": (128, 8), "g128": (128, 512), "cmask": (128, 128), "swamask": (128, 256), "sel": (8, 512),
    "mhalf": (128, 4), "dqks": (16, 8), "rmask": (16, 16), "bmask": (16, 4), "smask": (128, 9, 4), "smaskn": (16, 4, 4),
}
IN_SHAPES = {
    "x": (SEQ, D), "memx": (256, D), "w_in": (D, DIN), "w_mem": (D, 1024), "w_out": (DMIX, D),
    "ln_g": (1, D), "ln_b": (1, D), "xs": (NS, D), "state": (4, 4, 64, 128),
    "ck": (4, 2048, 512), "cv": (4, 2048, 512), "cmk": (4, 256, 512), "cmv": (4, 256, 512),
}
OUT_SHAPES = {
    "y": (SEQ, D), "ys": (NS, D), "retp": (4, 64, 128), "rets": (4, 4, 64, 128),
    "kp": (SEQ, 512), "vp": (SEQ, 512), "ks": (NS, 512), "vs": (NS, 512),
    "mkp": (256, 512), "mvp": (256, 512),
}


def tok_slice(g, b, nblk=1):
    cnt = 128 * nblk
    if g == 1:
        start = 128 * b
    elif g == 4:
        r, n = divmod(b, 4)
        start = 512 * n + r
    else:
        start = b
    return slice(start, start + g * (cnt - 1) + 1, g)


class Prog:
    def __init__(self):
        self.nc = bass.Bass("TRN2", target_bir_lowering=False)
        nc = self.nc
        self.din = {k: nc.dram_tensor(k, list(s), F32, kind="ExternalInput").ap() for k, s in IN_SHAPES.items()}
        self.dco = {k: nc.dram_tensor("c_" + k, list(s), F32, kind="ExternalInput").ap() for k, s in CONST_SHAPES.items()}
        self.dout = {k: nc.dram_tensor(k, list(s), F32, kind="ExternalOutput").ap() for k, s in OUT_SHAPES.items()}
        self.vscr = nc.dram_tensor("vscr", [SEQ, 520], BF16, kind="Internal").ap()
        self.NDMA = 64
        self.S = Sched(self.NDMA)
        self.es = ExitStack()
        self.eng_sems = {e: self.es.enter_context(nc.semaphore("sem_" + e)) for e in ENGS}
        self.dma_sems = [self.es.enter_context(nc.semaphore("dsem%d" % i)) for i in range(self.NDMA)]
        self.uid = 0

    def sb(self, name, shape, dt):
        return self.es.enter_context(self.nc.sbuf_tensor(name, list(shape), dt))

    def buf(self, name, shape, dt):
        return Buf(self.sb(name, shape, dt)[:], name)

    def arena_reset(self):
        self.aoff = 0

    def arena(self, name, shape, dt):
        n = 1
        for s in shape[1:]:
            n *= s
        words = n if dt == F32 else (n + 1) // 2
        words = (words + 15) // 16 * 16
        assert self.aoff + words <= self.AW, (name, self.aoff, words, self.AW)
        v = self.arena_t[:, self.aoff:self.aoff + words]
        self.aoff += words
        if dt != F32:
            v = v.bitcast(dt)
        v = v[:, 0:n]
        if len(shape) > 2:
            names = " ".join("d%d" % i for i in range(len(shape) - 1))
            v = v.rearrange("p (%s) -> p %s" % (names, names), **{"d%d" % i: shape[i + 1] for i in range(len(shape) - 1)})
        if shape[0] < 128:
            v = v[0:shape[0]]
        self.uid += 1
        return Buf(v, "%s_%d" % (name, self.uid))

    def dma_in(self, q, dst, dst_ap, src_ap, parallel=False):
        self.S.dma(q, lambda e: e.dma_start(out=dst_ap, in_=src_ap), reads=[], writes=[dst.tk], parallel=parallel)

    def dma_out(self, q, dst_ap, src, src_ap):
        self.S.dma(q, lambda e: e.dma_start(out=dst_ap, in_=src_ap), reads=[src.tk], writes=[])

    def mm(self, out, out_ap, lhsT, lhsT_ap, rhs, rhs_ap, start=True, stop=True, extra_reads=()):
        self.S.op("pe", lambda e: e.matmul(out_ap, lhsT=lhsT_ap, rhs=rhs_ap, start=start, stop=stop),
                  reads=[lhsT.tk, rhs.tk] + list(extra_reads), writes=[out.tk])

    def tr(self, out, out_ap, in_, in_ap, ident):
        n = in_ap.shape[0]
        self.S.op("pe", lambda e: e.transpose(out=out_ap, in_=in_ap, identity=ident.ap[0:n, 0:n]),
                  reads=[in_.tk, ident.tk], writes=[out.tk])

    def act(self, out, out_ap, in_, in_ap, func, scale=1.0, bias=0.0, extra_reads=()):
        self.S.op("act", lambda e: e.activation(out=out_ap, in_=in_ap, func=func, scale=scale, bias=bias),
                  reads=[in_.tk] + [b.tk for b in extra_reads], writes=[out.tk])

    def tt(self, eng, out, out_ap, a, a_ap, b, b_ap, op, extra_reads=()):
        self.S.op(eng, lambda e: e.tensor_tensor(out=out_ap, in0=a_ap, in1=b_ap, op=op),
                  reads=[a.tk, b.tk] + list(extra_reads), writes=[out.tk])

    def ts(self, eng, out, out_ap, a, a_ap, s1, op0, s2=None, op1=None, extra_reads=()):
        if op1 is None:
            fn = lambda e: e.tensor_scalar(out=out_ap, in0=a_ap, scalar1=s1, scalar2=None, op0=op0)
        else:
            fn = lambda e: e.tensor_scalar(out=out_ap, in0=a_ap, scalar1=s1, scalar2=s2, op0=op0, op1=op1)
        self.S.op(eng, fn, reads=[a.tk] + [b.tk for b in extra_reads], writes=[out.tk])

    def stt(self, out, out_ap, a, a_ap, scalar, op0, b, b_ap, op1, extra_reads=()):
        self.S.op("dve", lambda e: e.scalar_tensor_tensor(out=out_ap, in0=a_ap, scalar=scalar, op0=op0, in1=b_ap, op1=op1),
                  reads=[a.tk, b.tk] + [x.tk for x in extra_reads], writes=[out.tk])

    def cp(self, eng, out, out_ap, in_, in_ap, extra_reads=()):
        if eng == "act":
            fn = lambda e: e.copy(out=out_ap, in_=in_ap)
        else:
            fn = lambda e: e.tensor_copy(out=out_ap, in_=in_ap)
        self.S.op(eng, fn, reads=[in_.tk] + list(extra_reads), writes=[out.tk])

    def memset(self, eng, out, out_ap, val):
        self.S.op(eng, lambda e: e.memset(out_ap, val), reads=[], writes=[out.tk])

    def load_w(self, slot, src, col0, ncols=512, row0=0):
        wb = self.W[slot]
        self.S.dma("pool", (lambda e: e.dma_start(out=wb.ap[:, :, 0:ncols],
                                                  in_=src.rearrange("(k p) c -> p k c", p=128)[:, :, col0:col0 + ncols])),
                   reads=[], writes=[wb.tk], prefetch=True)

    def load_w1(self, slot, src, col0, k, ncols=512):
        wb = self.W[slot]
        self.S.dma("pool", (lambda e: e.dma_start(out=wb.ap[:, k, 0:ncols], in_=src[k * 128:(k + 1) * 128, col0:col0 + ncols])),
                   reads=[], writes=[wb.tk], prefetch=True)

    def proj(self, bank, ntok, xt_reads, xt_ap, wslot, wc0=0, ncols=512):
        wb = self.W[wslot]
        for k in range(8):
            self.S.op("pe", (lambda e, k=k: e.matmul(bank.ap[0:ntok, 0:ncols], lhsT=xt_ap[:, k, :],
                                                     rhs=wb.ap[:, k, wc0:wc0 + ncols], start=(k == 0), stop=(k == 7))),
                      reads=list(xt_reads) + [wb.tk], writes=[bank.tk])

    def rope(self, bank, ntok, cos_b, cos_ap, sin_b, sin_ap, t1, t2, out, out_ap, nh=8):
        X = bank.ap[0:ntok, 0:nh * 64].rearrange("p (h two f) -> p h two f", h=nh, two=2)
        T1 = t1.ap[0:ntok, 0:nh * 64].rearrange("p (h two f) -> p h two f", h=nh, two=2)
        T2 = t2.ap[0:ntok, 0:nh * 64].rearrange("p (h two f) -> p h two f", h=nh, two=2)
        O = out_ap.rearrange("p (h two f) -> p h two f", h=nh, two=2)
        self.tt("dve", t1, T1, bank, X, cos_b, cos_ap, ALU.mult)
        self.tt("dve", t2, T2, bank, X[:, :, ::-1, :], sin_b, sin_ap, ALU.mult)
        self.tt("dve", out, O, t1, T1, t2, T2, ALU.add)

    def build(self, phases):
        self.phases = phases
        nc = self.nc
        S = self.S
        din, dco, dout = self.din, self.dco, self.dout
        self.ps_t = self.es.enter_context(nc.psum_tensor("psum", [128, 4096], F32))
        self.PB = [Buf(self.ps_t[:, i * 512:(i + 1) * 512], "bank%d" % i) for i in range(8)]
        for b_ in self.PB:
            b_.tk.excl = True
        PB = self.PB
        n = NS

        self.xT_t = self.sb("xT", (128, 8, SEQ), BF16)
        self.xT = [Buf(self.xT_t[:, :, t * 128:(t + 1) * 128], "xT%d" % t) for t in range(NT)]
        self.xT_all = [b.tk for b in self.xT]
        self.xsT = self.buf("xsT", (128, 8, NS), BF16)
        self.mixsT = self.buf("mixsT", (128, 12, NS), BF16)
        self.mixS_t = self.sb("mixS", (128, 4, SEQ), BF16)
        self.mixS = Buf(self.mixS_t, "mixS")
        self.NSLOT = 6
        self.W_t = self.sb("W", (128, self.NSLOT * 4096), BF16)
        self.W = [Buf(self.W_t[:, i * 4096:(i + 1) * 4096].rearrange("p (k c) -> p k c", k=8), "W%d" % i)
                  for i in range(self.NSLOT)]
        self.identb = self.buf("identb", (128, 128), BF16)
        self.identf = self.buf("identf", (128, 128), F32)
        self.cosp = self.buf("cosp", (128, 16, 32), F32)
        self.sinp = self.buf("sinp", (128, 16, 2, 32), F32)
        self.mkT = self.buf("mkT", (128, 4, 256), BF16)
        self.mvaug = self.buf("mvaug", (128, 2, 4, 129), BF16)
        self.sqs = self.buf("sqs", (n, 512), BF16)
        self.sks = self.buf("sks", (n, 512), BF16)
        self.Vnew = self.buf("Vnew", (n, 8, 65), BF16)
        self.Gss = self.buf("Gss", (n, 512), BF16)
        self.coss = self.buf("coss", (n, 32), F32)
        self.sins = self.buf("sins", (n, 2, 32), F32)
        self.A_t = self.sb("arenaA", (128, 8, SEQ), BF16)
        self.AW = 15872
        self.arena_t = self.sb("arenaB", (128, self.AW), F32)
        self.arena_reset()
        self.mixR = [Buf(self.A_t[:, 0:4, t * 128:(t + 1) * 128], "mixR%d" % t) for t in range(NT)]
        self.mixM = [Buf(self.A_t[:, 4:8, t * 128:(t + 1) * 128], "mixM%d" % t) for t in range(NT)]

        self.dma_in("pool", self.identb, self.identb.ap, dco["ident"])
        self.load_w(0, din["w_in"], C_SQ)
        self.load_w(1, din["w_in"], C_SK)
        self.load_w(2, din["w_in"], C_SV)
        self.load_w(4, din["w_mem"], 0)
        self.load_w(5, din["w_mem"], 512)
        self.load_w(3, din["w_in"], C_SG)
        self.dma_in("sp", self.identf, self.identf.ap, dco["ident"])
        self.dma_in("sp", self.cosp, self.cosp.ap, dco["cosp"])
        self.dma_in("sp", self.sinp, self.sinp.ap, dco["sinp"])
        self.dma_in("sp", self.coss, self.coss.ap, dco["coss"])
        self.dma_in("sp", self.sins, self.sins.ap, dco["sins"])

        xld = [self.arena("xld", (128, D), F32) for _ in range(3)]
        memT = self.arena("memT", (128, 8, 256), BF16)
        mf = [self.arena("mf", (128, 512), F32) for _ in range(2)]
        xsld = self.arena("xsld", (n, D), BF16)

        def load_T(src_ap, ntok, dst, dst_ap, i):
            xb = xld[i % 3]
            self.dma_in("sp", xb, xb.ap[0:ntok], src_ap)
            b0, b1 = PB[(2 * i) % 8], PB[(2 * i + 1) % 8]
            for c in range(8):
                bk = b0 if c < 4 else b1
                self.tr(bk, bk.ap[:, (c % 4) * ntok:(c % 4 + 1) * ntok], xb, xb.ap[0:ntok, c * 128:(c + 1) * 128], self.identf)
            e0, e1 = ("dve", "act") if i % 2 == 0 else ("act", "dve")
            self.cp(e0, dst, dst_ap[:, 0:4, :], b0, b0.ap[:, 0:4 * ntok].rearrange("p (c t) -> p c t", c=4))
            self.cp(e1, dst, dst_ap[:, 4:8, :], b1, b1.ap[:, 0:4 * ntok].rearrange("p (c t) -> p c t", c=4))

        i = 0
        for mt in range(2):
            load_T(din["memx"][mt * 128:(mt + 1) * 128, :], 128, memT, memT.ap[:, :, mt * 128:(mt + 1) * 128], i)
            i += 1
        for t in range(NT):
            load_T(din["x"][t * 128:(t + 1) * 128, :], 128, self.xT[t], self.xT[t].ap, i)
            i += 1
            if t == 15:
                self.dma_in("pool", xsld, xsld.ap, din["xs"])
                bk = PB[(2 * i) % 8]
                pv = bk.ap.bitcast(BF16)
                for c in range(8):
                    self.tr(bk, pv[:, c * n:(c + 1) * n], xsld, xsld.ap[:, c * 128:(c + 1) * 128], self.identb)
                self.cp("dve", self.xsT, self.xsT.ap, bk, pv[:, 0:8 * n].rearrange("p (c t) -> p c t", c=8))
                i += 1
                self.mem_setup(memT, mf)
        S.barrier()

        if "S" in phases:
            self.phase_S()
            S.barrier()
        if "R" in phases:
            self.phase_R()
            S.barrier()
        if "M" in phases:
            self.phase_M()
            S.barrier()
        if "F" in phases:
            self.phase_F()
            S.barrier()
        S.barrier(final=True)
        with nc.Block() as block:
            S.emit(block, self.eng_sems, self.dma_sems)
        self.es.close()
        return nc

    def mem_setup(self, memT, mf):
        S = self.S
        PB = self.PB
        dout = self.dout
        mkT, mvaug = self.mkT, self.mvaug
        self.memset("pool", mvaug, mvaug.ap[:, :, :, 128:129], 1.0)
        i = 0
        for mt in range(2):
            for which in range(2):
                bank = PB[4 + i % 4]
                self.proj(bank, 128, [memT.tk], memT.ap[:, :, mt * 128:(mt + 1) * 128], 4 + which)
                f = mf[i % 2]
                i += 1
                self.cp("act", f, f.ap, bank, bank.ap)
                self.dma_out("sp", dout["mkp" if which == 0 else "mvp"][mt * 128:(mt + 1) * 128, :], f, f.ap)
                if which == 1:
                    self.cp("pool", mvaug, mvaug.ap[:, mt, :, 0:128], f, f.ap.rearrange("p (h f) -> p h f", h=4))
        w0 = self.W[4]
        for h in range(4):
            bank = PB[4 + h % 4]
            for k in range(8):
                S.op("pe", (lambda e, k=k, h=h, bank=bank: e.matmul(bank.ap[:, 0:256], lhsT=w0.ap[:, k, h * 128:(h + 1) * 128],
                                                                     rhs=memT.ap[:, k, :], start=(k == 0), stop=(k == 7))),
                     reads=[w0.tk, memT.tk], writes=[bank.tk])
            self.cp("act", mkT, mkT.ap[:, h, :], bank, bank.ap[:, 0:256])

    def gate(self, bank, n, th, out, out_ap):
        self.act(th, th.ap[0:n], bank, bank.ap[0:n], AF.Tanh, scale=0.5)
        self.stt(out, out_ap, th, th.ap[0:n], 1.0, ALU.add, bank, bank.ap[0:n], ALU.mult)

    def phase_S(self):
        S = self.S
        PB = self.PB
        din, dco, dout = self.din, self.dco, self.dout
        n = NS
        self.arena_reset()
        QKT_t = self.A_t
        QKT = [Buf(QKT_t[:, :, t * 128:(t + 1) * 128], "QKT%d" % t) for t in range(NT)]
        qkt_all = [b.tk for b in QKT]
        Vaug_t = self.arena("Vaug", (128, 16, 8, 65), BF16)
        Vaug = [Buf(Vaug_t.ap[:, b], "Vaug%d" % b) for b in range(16)]
        PT = [self.arena("PT", (128, 8, 256), BF16) for _ in range(3)]
        LT = self.arena("LT", (8, SEQ), F32)
        swam = self.arena("swam", (128, 256), BF16)
        sel = self.arena("sel", (8, 512), F32)
        t1 = self.arena("t1", (128, 512), F32)
        t2 = self.arena("t2", (128, 512), F32)
        qkb = [self.arena("qkb", (128, 1024), BF16) for _ in range(2)]
        kf = [self.arena("kf", (128, 512), F32) for _ in range(2)]
        vf = [self.arena("vf", (128, 512), F32) for _ in range(2)]
        Ub = [self.arena("Ub", (128, 8, 64), BF16) for _ in range(2)]
        lf = [self.arena("lf", (128, 8), F32) for _ in range(2)]
        tha, ga, gb = [t1, t2], kf, vf

        self.dma_in("pool", swam, swam.ap, dco["swamask"])
        self.dma_in("sp", sel, sel.ap, dco["sel"])
        self.memset("pool", Vaug_t, Vaug_t.ap[:, :, :, 64:65], 1.0)
        for b in Vaug:
            b.tk.w = Vaug_t.tk.w

        vscr_tk = Tk("vscr")
        def s1_front(t):
            xt = self.xT[t]
            bq, bk, bv = PB[(2 * t) % 4], PB[(2 * t + 1) % 4], PB[4 + t % 2]
            cos4 = self.cosp.ap[:, t, :].unsqueeze(1).unsqueeze(1).broadcast_to([128, 8, 2, 32])
            sin4 = self.sinp.ap[:, t, :, :].unsqueeze(1).broadcast_to([128, 8, 2, 32])
            qk = qkb[t % 2]
            self.proj(bq, 128, [xt.tk], xt.ap, 0)
            self.rope(bq, 128, self.cosp, cos4, self.sinp, sin4, t1, t2, qk, qk.ap[:, 0:512])
            self.proj(bk, 128, [xt.tk], xt.ap, 1)
            self.rope(bk, 128, self.cosp, cos4, self.sinp, sin4, t1, t2, kf[t % 2], kf[t % 2].ap)
            self.dma_out("sp", dout["kp"][t * 128:(t + 1) * 128, :], kf[t % 2], kf[t % 2].ap)
            self.cp("pool", qk, qk.ap[:, 512:1024], kf[t % 2], kf[t % 2].ap)
            self.proj(bv, 128, [xt.tk], xt.ap, 2)
            self.cp("act", vf[t % 2], vf[t % 2].ap, bv, bv.ap)
            self.dma_out("sp", dout["vp"][t * 128:(t + 1) * 128, :], vf[t % 2], vf[t % 2].ap)
            self.cp("pool", Vaug[t], Vaug[t].ap[:, :, 0:64], vf[t % 2], vf[t % 2].ap.rearrange("p (h f) -> p h f", h=8))
            S.dma("sp", (lambda e, t=t: e.dma_start(out=self.vscr[t * 128:(t + 1) * 128, :],
                                                    in_=Vaug[t].ap.rearrange("p h f -> p (h f)"))),
                  reads=[Vaug[t].tk], writes=[vscr_tk])

        def s1_back(t):
            bt = PB[6 + t % 2]
            qk = qkb[t % 2]
            btv = bt.ap.bitcast(BF16)
            for c in range(8):
                self.tr(bt, btv[:, c * 128:(c + 1) * 128], qk, qk.ap[:, c * 128:(c + 1) * 128], self.identb)
            self.cp("act", QKT[t], QKT[t].ap, bt, btv.rearrange("p (c t) -> p c t", c=8))

        for t in range(NT):
            s1_front(t)
            if t > 0:
                s1_back(t - 1)
        s1_back(NT - 1)
        cos4 = self.coss.ap.unsqueeze(1).unsqueeze(1).broadcast_to([n, 8, 2, 32])
        sin4 = self.sins.ap.unsqueeze(1).broadcast_to([n, 8, 2, 32])
        self.proj(PB[0], n, [self.xsT.tk], self.xsT.ap, 0)
        self.rope(PB[0], n, self.coss, cos4, self.sins, sin4, t1, t2, self.sqs, self.sqs.ap)
        self.proj(PB[1], n, [self.xsT.tk], self.xsT.ap, 1)
        self.rope(PB[1], n, self.coss, cos4, self.sins, sin4, t1, t2, kf[0], kf[0].ap[0:n])
        self.dma_out("sp", dout["ks"], kf[0], kf[0].ap[0:n])
        self.cp("pool", self.sks, self.sks.ap, kf[0], kf[0].ap[0:n])
        self.proj(PB[4], n, [self.xsT.tk], self.xsT.ap, 2)
        self.cp("act", vf[0], vf[0].ap[0:n], PB[4], PB[4].ap[0:n])
        self.dma_out("sp", dout["vs"], vf[0], vf[0].ap[0:n])
        self.memset("pool", self.Vnew, self.Vnew.ap[:, :, 64:65], 1.0)
        self.cp("pool", self.Vnew, self.Vnew.ap[:, :, 0:64], vf[0], vf[0].ap[0:n].rearrange("p (h f) -> p h f", h=8))
        self.proj(PB[5], n, [self.xsT.tk], self.xsT.ap, 3)
        self.gate(PB[5], n, t1, t2, t2.ap[0:n])
        self.ts("dve", self.Gss, self.Gss.ap, t2, t2.ap[0:n], 0.5, ALU.mult)
        pre = []
        for slot, col in ((4, C_RQ), (5, C_RV), (0, C_RG), (1, C_MQ)):
            for k in range(8):
                pre.append((slot, col, k))

        STB = PB[0:4]
        UB = [PB[4], PB[5]]
        TBa = PB[6]
        VB = PB[7]
        TBl = VB
        entries = []
        for g in (1, 4, 16):
            seqs = {1: [list(range(16))], 4: [[4 * r + nn for nn in range(4)] for r in range(4)],
                    16: [[r] for r in range(16)]}[g]
            for seq in seqs:
                for si, kb in enumerate(seq):
                    entries.append((g, seq, si, kb))
        NE = len(entries)
        vdone = {1: True}

        nextg = {1: 4, 4: 16}

        def vreload(g, b):
            tsl = tok_slice(g, b)
            S.dma("sp", (lambda e, b=b, tsl=tsl: e.dma_start(out=Vaug[b].ap.rearrange("p h f -> p (h f)"),
                                                            in_=self.vscr[tsl, :])),
                  reads=[vscr_tk], writes=[Vaug[b].tk])

        def stage_A(i):
            g, seq, si, kb = entries[i]
            last = (si == len(seq) - 1)
            nq = 128 if last else 256
            ksl = tok_slice(g, kb)
            qsl = tok_slice(g, kb, 1 if last else 2)
            pt = PT[i % 3]
            for h in range(8):
                c, half = h // 2, h % 2
                bank = STB[2 * (h // 4) + (h % 2)]
                slot = (h // 2) % 2
                self.mm(bank, bank.ap[:, slot * 256:slot * 256 + nq],
                        QKT[0], QKT_t[half * 64:(half + 1) * 64, 4 + c, ksl],
                        QKT[0], QKT_t[half * 64:(half + 1) * 64, c, qsl], extra_reads=qkt_all)
            for bi in range(4):
                bank = STB[bi]
                hs = 4 * (bi // 2) + (bi % 2)
                self.act(pt, pt.ap[:, hs:hs + 3:2, 0:nq], bank,
                         bank.ap.rearrange("p (s q) -> p s q", s=2)[:, :, 0:nq], AF.Exp, scale=0.125)
            self.tt("pool", pt, pt.ap[:, :, 0:nq], pt, pt.ap[:, :, 0:nq], swam,
                    swam.ap[:, 0:nq].unsqueeze(1).broadcast_to([128, 8, nq]), ALU.mult)

        def stage_B(i):
            g, seq, si, kb = entries[i]
            pt = PT[i % 3]
            ptp = PT[(i - 1) % 3]
            for h in range(8):
                ub = UB[h // 4]
                o_ap = ub.ap[:, (h % 4) * 65:(h % 4) * 65 + 65]
                if si > 0:
                    self.mm(ub, o_ap, ptp, ptp.ap[:, h, 128:256], Vaug[seq[si - 1]], Vaug[seq[si - 1]].ap[:, h, :],
                            start=True, stop=False)
                    self.mm(ub, o_ap, pt, pt.ap[:, h, 0:128], Vaug[kb], Vaug[kb].ap[:, h, :], start=False, stop=True)
                else:
                    self.mm(ub, o_ap, pt, pt.ap[:, h, 0:128], Vaug[kb], Vaug[kb].ap[:, h, :], start=True, stop=True)
            u = Ub[i % 2]
            l = lf[i % 2]
            for j in range(2):
                uv = UB[j].ap[:, 0:260].rearrange("p (h f) -> p h f", h=4)
                self.cp("dve", u, u.ap[:, 4 * j:4 * j + 4, :], UB[j], uv[:, :, 0:64])
                self.cp("dve", l, l.ap[:, 4 * j:4 * j + 4], UB[j], uv[:, :, 64])
            if g in nextg:
                if si > 0:
                    vreload(nextg[g], seq[si - 1])
                if si == len(seq) - 1:
                    vreload(nextg[g], kb)
            if g == 1 and kb == 0:
                self.load_w(2, din["w_in"], C_MG)
            if pre:
                slot, col, k = pre.pop(0)
                self.load_w1(slot, din["w_in"], col, k)

        def stage_C(i):
            g, seq, si, kb = entries[i]
            ksl = tok_slice(g, kb)
            u = Ub[i % 2]
            l = lf[i % 2]
            tav = TBa.ap.bitcast(BF16)
            for c in range(4):
                self.tr(TBa, tav[:, c * 128:(c + 1) * 128], u,
                        u.ap[:, 2 * c:2 * c + 2, :].rearrange("p h f -> p (h f)"), self.identb)
            self.tr(TBl, TBl.ap[0:8, 256:384], l, l.ap, self.identf)
            dstm = self.mixS_t[:, :, ksl]
            dstl = LT.ap[:, ksl]
            srcm = tav[:, 0:512].rearrange("p (c q) -> p c q", c=4)
            if g == 1:
                self.cp("dve", self.mixS, dstm, TBa, srcm)
                self.cp("dve", LT, dstl, TBl, TBl.ap[0:8, 256:384])
            else:
                self.tt("dve", self.mixS, dstm, TBa, srcm, self.mixS, dstm, ALU.add)
                self.tt("dve", LT, dstl, TBl, TBl.ap[0:8, 256:384], LT, dstl, ALU.add)

        for i in range(NE + 2):
            if i < NE:
                stage_A(i)
            if 0 <= i - 1 < NE:
                stage_B(i - 1)
            if 0 <= i - 2 < NE:
                stage_C(i - 2)

        S.op("dve", lambda e: e.reciprocal(out=LT.ap, in_=LT.ap), reads=[LT.tk], writes=[LT.tk])
        i = 0
        for nn in range(4):
            gsl = slice(nn * 512, (nn + 1) * 512)
            for c in range(4):
                bb, bg = PB[(2 * i) % 8], PB[(2 * i + 1) % 8]
                self.mm(bb, bb.ap, sel, sel.ap[:, c * 128:(c + 1) * 128], LT, LT.ap[:, gsl])
                wb = self.W[3]
                for k in range(8):
                    S.op("pe", (lambda e, k=k, c=c, gsl=gsl, bg=bg, wb=wb: e.matmul(
                        bg.ap, lhsT=wb.ap[:, k, c * 128:(c + 1) * 128], rhs=self.xT_t[:, k, gsl],
                        start=(k == 0), stop=(k == 7))), reads=self.xT_all + [wb.tk], writes=[bg.tk])
                th, a, b = tha[i % 2], ga[i % 2], gb[i % 2]
                self.act(th, th.ap, bg, bg.ap, AF.Tanh, scale=0.5)
                self.stt(a, a.ap, th, th.ap, 1.0, ALU.add, bg, bg.ap, ALU.mult)
                self.stt(b, b.ap, self.mixS, self.mixS_t[:, c, gsl], 0.5, ALU.mult, bb, bb.ap, ALU.mult)
                self.tt("dve", self.mixS, self.mixS_t[:, c, gsl], a, a.ap, b, b.ap, ALU.mult)
                i += 1

    def phase_R(self):
        S = self.S
        PB = self.PB
        din, dco, dout = self.din, self.dco, self.dout
        n = NS
        self.arena_reset()
        A = self.arena
        lgs = np.log1p(-np.exp2(-5.0 - np.arange(4, dtype=np.float64)))
        g128 = [float(np.exp(128.0 * v)) for v in lgs]
        g4 = [float(np.exp(4.0 * v)) for v in lgs]
        dqk = A("dqk", (128, 8), F32)
        dqks = A("dqks", (n, 8), F32)
        cmask = A("cmask", (128, 128), BF16)
        rmask = A("rmask", (n, 16), BF16)
        bmask = A("bmask", (n, 4), F32)
        mhalf = A("mhalf", (128, 4), F32)
        Rst = A("Rst", (128, 4, 128), F32)
        Rb = A("Rb", (128, 4, 128), BF16)
        CS = [A("CS", (128, 8, 2, 32), F32) for _ in range(2)]
        SN = [A("SN", (128, 8, 2, 32), F32) for _ in range(2)]
        t1 = A("t1", (128, 512), F32)
        t2 = A("t2", (128, 512), F32)
        QKr = [A("QKr", (128, 512), BF16) for _ in range(2)]
        Vr = [A("Vr", (128, 512), BF16) for _ in range(2)]
        thr = [A("thr", (128, 512), F32) for _ in range(2)]
        Gr = [A("Gr", (128, 512), F32) for _ in range(2)]
        QKrT = [A("QKrT", (128, 4, 128), BF16) for _ in range(2)]
        QZ = [A("QZ", (128, 4, 128), BF16) for _ in range(2)]
        STm = [A("STm", (128, 4, 128), BF16) for _ in range(2)]
        s12 = A("s12", (128, 2, 4), F32)
        st6 = A("st6", (128, 4, 6), F32)
        mvr = A("mvr", (128, 4, 2), F32)
        mean = A("mean", (128, 4), F32)
        msq = A("msq", (128, 4), F32)
        va = A("va", (128, 4), F32)
        ve = A("ve", (128, 4), F32)
        rstd = A("rstd", (128, 4), F32)
        junk = A("junk", (128, 512), BF16)
        mixRs = A("mixRs", (n, 512), BF16)
        On = A("On", (128, 512), F32)
        mixRt = [A("mixRt", (128, 512), BF16) for _ in range(2)]
        Rs = A("Rs", (128, 4, 4, 128), F32)
        Rsb = A("Rsb", (128, 4, 4, 128), BF16)
        QKsT = A("QKsT", (128, 4, n), BF16)
        QZs = A("QZs", (128, 4, n), BF16)
        QZB = A("QZB", (128, 4, 4, n), BF16)
        STs = A("STs", (n, 4, n), BF16)
        KZ = A("KZ", (n, 4, 256), BF16)

        for b_, k_ in ((dqk, "dqk"), (mhalf, "mhalf"), (dqks, "dqks"), (bmask, "bmask")):
            self.dma_in("sp", b_, b_.ap, dco[k_])
        self.dma_in("pool", cmask, cmask.ap, dco["cmask"])
        self.dma_in("pool", rmask, rmask.ap, dco["rmask"])
        Gz = A("Gz", (128, 4, 128), F32)
        self.dma_in("sp", Gz, Gz.ap, dco["g128"].rearrange("p (h f) -> p h f", h=4))
        self.memset("pool", Rst, Rst.ap, 0.0)
        self.memset("pool", Rb, Rb.ap, 0.0)
        for z in QZ:
            self.memset("pool", z, z.ap, 0.0)
        self.memset("pool", Rs, Rs.ap, 0.0)
        self.memset("pool", QZs, QZs.ap, 0.0)
        self.memset("pool", QZB, QZB.ap, 0.0)
        self.memset("pool", Rsb, Rsb.ap, 0.0)
        st_src = din["state"].rearrange("b h k v -> k (b h) v")
        for hp in range(2):
            rows = slice(hp * 64, hp * 64 + 64)
            self.dma_in("sp", Rs, Rs.ap.rearrange("p b h v -> p (b h) v")[rows, hp:16:2, :], st_src[:, hp:16:2, :], parallel=True)
            self.dma_in("pool", Rsb, Rsb.ap.rearrange("p b h v -> p (b h) v")[rows, hp:16:2, :], st_src[:, hp:16:2, :], parallel=True)
        dq4 = dqk.ap.unsqueeze(2).unsqueeze(3).broadcast_to([128, 8, 2, 32])
        WQK, WV, WG = 4, 5, 0

        def headnorm_gate(bo, nn_, gr, mixrt):
            for h in range(4):
                S.op("dve", (lambda e, h=h: e.bn_stats(out=st6.ap[0:nn_, h, :], in_=bo.ap[0:nn_, h * 128:(h + 1) * 128])),
                     reads=[bo.tk], writes=[st6.tk])
            for h in range(4):
                S.op("dve", (lambda e, h=h: e.bn_aggr(out=mvr.ap[0:nn_, h, :], in_=st6.ap[0:nn_, h, :])),
                     reads=[st6.tk], writes=[mvr.tk])
            self.cp("dve", mean, mean.ap[0:nn_], mvr, mvr.ap[0:nn_, :, 0])
            self.ts("dve", ve, ve.ap[0:nn_], mvr, mvr.ap[0:nn_, :, 1], 4.0, ALU.mult, 4.0 * EPS, ALU.add)
            self.tt("pool", rstd, rstd.ap[0:nn_], ve, ve.ap[0:nn_], mhalf, mhalf.ap[0:nn_], ALU.pow)
            for h in range(4):
                hs = slice(h * 128, (h + 1) * 128)
                self.stt(On, On.ap[0:nn_, hs], bo, bo.ap[0:nn_, hs], mean.ap[0:nn_, h:h + 1], ALU.subtract,
                         gr, gr.ap[0:nn_, hs], ALU.mult, extra_reads=[mean])
            for h in range(4):
                hs = slice(h * 128, (h + 1) * 128)
                S.op("act", (lambda e, h=h, hs=hs: e.activation(out=mixrt.ap[0:nn_, hs], in_=On.ap[0:nn_, hs], func=AF.Identity,
                                                               scale=rstd.ap[0:nn_, h:h + 1])),
                     reads=[On.tk, rstd.tk], writes=[mixrt.tk])

        b0, b1, b2, bt, bs, bo = PB[0], PB[1], PB[2], PB[3], PB[4], PB[5]
        bkv = [PB[6], PB[7]]
        btv = bt.ap.bitcast(BF16)
        bsv = bs.ap.bitcast(BF16)

        def r_front(t):
            xt = self.xT[t]
            cs, sn = CS[t % 2], SN[t % 2]
            cos4 = self.cosp.ap[:, t, :].unsqueeze(1).unsqueeze(1).broadcast_to([128, 8, 2, 32])
            sin4 = self.sinp.ap[:, t, :, :].unsqueeze(1).broadcast_to([128, 8, 2, 32])
            self.tt("pool", cs, cs.ap, self.cosp, cos4, dqk, dq4, ALU.mult)
            self.tt("pool", sn, sn.ap, self.sinp, sin4, dqk, dq4, ALU.mult)
            qk, vr, th, gr, qkT, qz = QKr[t % 2], Vr[t % 2], thr[t % 2], Gr[t % 2], QKrT[t % 2], QZ[t % 2]
            self.proj(b0, 128, [xt.tk], xt.ap, WQK)
            self.rope(b0, 128, cs, cs.ap, sn, sn.ap, t1, t2, qk, qk.ap)
            self.proj(b1, 128, [xt.tk], xt.ap, WV)
            self.cp("act", vr, vr.ap, b1, b1.ap)
            self.proj(b2, 128, [xt.tk], xt.ap, WG)
            self.gate(b2, 128, th, gr, gr.ap)
            for c in range(4):
                self.tr(bt, btv[:, c * 128:(c + 1) * 128], qk, qk.ap[:, c * 128:(c + 1) * 128], self.identb)
            tv4 = btv[:, 0:512].rearrange("p (c t) -> p c t", c=4)
            self.cp("act", qkT, qkT.ap, bt, tv4)
            self.cp("act", qz, qz.ap[0:64, 0:4:2, :], bt, tv4[0:64, 0:2, :])
            self.cp("act", qz, qz.ap[64:128, 1:4:2, :], bt, tv4[64:128, 0:2, :])

        def r_back(t):
            qk, vr, gr, qkT, qz, stm, mixrt = QKr[t % 2], Vr[t % 2], Gr[t % 2], QKrT[t % 2], QZ[t % 2], STm[t % 2], mixRt[t % 2]
            for h in range(4):
                self.mm(bs, bs.ap[:, h * 128:(h + 1) * 128], qkT, qkT.ap[:, 2 + h // 2, :], qz, qz.ap[:, h, :])
            self.tt("dve", stm, stm.ap, bs, bs.ap.rearrange("p (h i) -> p h i", h=4), cmask,
                    cmask.ap.unsqueeze(1).broadcast_to([128, 4, 128]), ALU.mult)
            for h in range(4):
                o_ap = bo.ap[:, h * 128:(h + 1) * 128]
                self.mm(bo, o_ap, stm, stm.ap[:, h, :], vr, vr.ap[:, h * 128:(h + 1) * 128], start=True, stop=False)
                self.mm(bo, o_ap, qkT, qkT.ap[:, h // 2, :], Rb, Rb.ap[:, h, :], start=False, stop=True)
            for c in range(2):
                self.mm(bkv[c], bkv[c].ap, qk, qk.ap[:, 256 + c * 128:256 + (c + 1) * 128], vr, vr.ap)
            for c in range(2):
                rv_ = Rst.ap[:, 2 * c:2 * c + 2, :]
                kv_ = bkv[c].ap[:, 2 * c * 128:(2 * c + 2) * 128].rearrange("p (h f) -> p h f", h=2)
                self.tt("dve", Rst, rv_, bkv[c], kv_, Rst, rv_, ALU.add)
                self.tt("dve", Rst, rv_, Rst, rv_, Gz, Gz.ap[:, 2 * c:2 * c + 2, :], ALU.mult)
            self.cp("act", Rb, Rb.ap, Rst, Rst.ap)
            headnorm_gate(bo, 128, gr, mixrt)

        def r_tail(t):
            mixrt = mixRt[t % 2]
            for c in range(4):
                self.tr(bt, btv[:, c * 128:(c + 1) * 128], mixrt, mixrt.ap[:, c * 128:(c + 1) * 128], self.identb)
            self.cp("act", self.mixR[t], self.mixR[t].ap, bt, btv[:, 0:512].rearrange("p (c t) -> p c t", c=4))

        r_front(0)
        for t in range(NT):
            if t + 1 < NT:
                r_front(t + 1)
            r_back(t)
            if t > 0:
                r_tail(t - 1)
            if t == 7:
                self.sample_ret(locals())
        r_tail(NT - 1)
        for h in range(4):
            rows = slice((h % 2) * 64, (h % 2) * 64 + 64)
            self.dma_out("sp", dout["retp"][h], Rst, Rst.ap[rows, h, :])
        self.Wout = []
        for ec in range(12):
            slot = (4, 5, 0)[ec // 4]
            wb = self.W[slot]
            v = self.W_t[:, slot * 4096 + (ec % 4) * 1024:slot * 4096 + (ec % 4 + 1) * 1024]
            self.Wout.append((wb, v))
            S.dma("pool", (lambda e, ec=ec, v=v: e.dma_start(out=v, in_=din["w_out"][ec * 128:(ec + 1) * 128, :])),
                  reads=[], writes=[wb.tk], prefetch=True)

    def sample_ret(self, L):
        S = self.S
        PB = self.PB
        din, dout = self.din, self.dout
        n = NS
        g4 = L["g4"]
        dqks, rmask, bmask, t1, t2 = L["dqks"], L["rmask"], L["bmask"], L["t1"], L["t2"]
        Rs, Rsb, QKsT, QZs, QZB, STs, KZ = L["Rs"], L["Rsb"], L["QKsT"], L["QZs"], L["QZB"], L["STs"], L["KZ"]
        cs, sn = L["CS"][1], L["SN"][1]
        qk, vr, th, gr, mixrt = L["QKr"][1], L["Vr"][1], L["thr"][1], L["Gr"][1], L["mixRs"]
        b0, b1, b2, bt, bs, bo = PB[0], PB[1], PB[2], PB[3], PB[4], PB[5]
        cos4 = self.coss.ap.unsqueeze(1).unsqueeze(1).broadcast_to([n, 8, 2, 32])
        sin4 = self.sins.ap.unsqueeze(1).broadcast_to([n, 8, 2, 32])
        dq4 = dqks.ap.unsqueeze(2).unsqueeze(3).broadcast_to([n, 8, 2, 32])
        self.tt("pool", cs, cs.ap[0:n], self.coss, cos4, dqks, dq4, ALU.mult)
        self.tt("pool", sn, sn.ap[0:n], self.sins, sin4, dqks, dq4, ALU.mult)
        self.proj(b0, n, [self.xsT.tk], self.xsT.ap, L["WQK"])
        self.rope(b0, n, cs, cs.ap[0:n], sn, sn.ap[0:n], t1, t2, qk, qk.ap[0:n])
        self.proj(b1, n, [self.xsT.tk], self.xsT.ap, L["WV"])
        self.cp("act", vr, vr.ap[0:n], b1, b1.ap[0:n])
        self.proj(b2, n, [self.xsT.tk], self.xsT.ap, L["WG"])
        self.gate(b2, n, th, gr, gr.ap[0:n])
        btv = bt.ap.bitcast(BF16)
        for c in range(4):
            self.tr(bt, btv[:, c * n:(c + 1) * n], qk, qk.ap[0:n, c * 128:(c + 1) * 128], self.identb)
        tv4 = btv[:, 0:4 * n].rearrange("p (c t) -> p c t", c=4)
        self.cp("act", QKsT, QKsT.ap, bt, tv4)
        self.cp("dve", QZs, QZs.ap[0:64, 0:4:2, :], bt, tv4[0:64, 0:2, :])
        self.cp("dve", QZs, QZs.ap[64:128, 1:4:2, :], bt, tv4[64:128, 0:2, :])
        for b in range(4):
            self.cp("pool", QZB, QZB.ap[:, b, :, 4 * b:4 * b + 4], QZs, QZs.ap[:, :, 4 * b:4 * b + 4])
        for h in range(4):
            self.mm(bs, bs.ap[0:n, h * n:(h + 1) * n], QKsT, QKsT.ap[:, 2 + h // 2, :], QZs, QZs.ap[:, h, :])
        self.tt("dve", STs, STs.ap, bs, bs.ap[0:n, 0:4 * n].rearrange("p (h i) -> p h i", h=4), rmask,
                rmask.ap.unsqueeze(1).broadcast_to([n, 4, n]), ALU.mult)
        for h in range(4):
            o_ap = bo.ap[0:n, h * 128:(h + 1) * 128]
            self.mm(bo, o_ap, STs, STs.ap[:, h, :], vr, vr.ap[0:n, h * 128:(h + 1) * 128], start=True, stop=False)
            for b in range(4):
                self.mm(bo, o_ap, QZB, QZB.ap[:, b, h, :], Rsb, Rsb.ap[:, b, h, :], start=False, stop=(b == 3))
        for b in range(4):
            self.ts("dve", KZ, KZ.ap[:, b, :], qk, qk.ap[0:n, 256:512], bmask.ap[:, b:b + 1], ALU.mult, extra_reads=[bmask])
        kbanks = [PB[6], PB[7]]
        i = 0
        for b in range(4):
            for c in range(2):
                kb_ = kbanks[i % 2]
                i += 1
                self.mm(kb_, kb_.ap, KZ, KZ.ap[:, b, c * 128:(c + 1) * 128], vr, vr.ap[0:n])
                for hh in range(2):
                    h = 2 * c + hh
                    rows = slice(hh * 64, hh * 64 + 64)
                    self.ts("dve", Rs, Rs.ap[rows, b, h, :], Rs, Rs.ap[rows, b, h, :], g4[h], ALU.mult)
                    self.stt(Rs, Rs.ap[rows, b, h, :], kb_, kb_.ap[rows, h * 128:(h + 1) * 128], g4[h], ALU.mult,
                             Rs, Rs.ap[rows, b, h, :], ALU.add)
        for b in range(4):
            for h in range(4):
                rows = slice((h % 2) * 64, (h % 2) * 64 + 64)
                self.dma_out("sp", dout["rets"][b, h], Rs, Rs.ap[rows, b, h, :])
        L["headnorm_gate"](bo, n, gr, mixrt)
        for c in range(4):
            self.tr(bt, btv[:, c * n:(c + 1) * n], mixrt, mixrt.ap[0:n, c * 128:(c + 1) * 128], self.identb)
        self.cp("act", self.mixsT, self.mixsT.ap[:, 0:4, :], bt, tv4)

    def phase_M(self):
        S = self.S
        PB = self.PB
        din, dco, dout = self.din, self.dco, self.dout
        n = NS
        self.arena_reset()
        A = self.arena
        mkT, mvaug = self.mkT, self.mvaug
        mqT = [A("mqT", (128, 4, 512), BF16) for _ in range(2)]
        PTm = [A("PTm", (128, 4, 2, 512), BF16) for _ in range(2)]
        thm = [A("thm", (128, 512), F32) for _ in range(2)]
        Gm = [A("Gm", (128, 512), F32) for _ in range(2)]
        rl = [A("rl", (128, 4), F32) for _ in range(2)]
        mixMt = [A("mixMt", (128, 512), BF16) for _ in range(2)]
        WMQ, WMG = 1, 2
        rot = [0]

        def nb():
            rot[0] = (rot[0] + 1) % 4
            return PB[rot[0]]

        w2 = self.W[WMQ]
        OB = [PB[4], PB[5]]
        bt = PB[6]
        bg = PB[7]
        btv = bt.ap.bitcast(BF16)
        def m_tail(t):
            mixmt = mixMt[t % 2]
            for c in range(4):
                self.tr(bt, btv[:, c * 128:(c + 1) * 128], mixmt, mixmt.ap[:, c * 128:(c + 1) * 128], self.identb)
            self.cp("act", self.mixM[t], self.mixM[t].ap, bt, btv[:, 0:512].rearrange("p (c t) -> p c t", c=4))

        for nn in range(4):
            gsl = slice(nn * 512, (nn + 1) * 512)
            mq = mqT[nn % 2]
            pt = PTm[nn % 2]
            for h in range(4):
                bank = nb()
                for k in range(8):
                    S.op("pe", (lambda e, k=k, h=h, bank=bank, gsl=gsl: e.matmul(
                        bank.ap, lhsT=w2.ap[:, k, h * 128:(h + 1) * 128], rhs=self.xT_t[:, k, gsl],
                        start=(k == 0), stop=(k == 7))), reads=self.xT_all + [w2.tk], writes=[bank.tk])
                self.cp("act", mq, mq.ap[:, h, :], bank, bank.ap)
            for h in range(4):
                for mt in range(2):
                    bank = nb()
                    self.mm(bank, bank.ap, mkT, mkT.ap[:, h, mt * 128:(mt + 1) * 128], mq, mq.ap[:, h, :])
                    self.act(pt, pt.ap[:, h, mt, :], bank, bank.ap, AF.Exp, scale=float(128.0 ** -0.5))
            for tq in range(4):
                t = 4 * nn + tq
                xt = self.xT[t]
                th, gm, r, mixmt = thm[t % 2], Gm[t % 2], rl[t % 2], mixMt[t % 2]
                self.proj(bg, 128, [xt.tk], xt.ap, WMG)
                self.gate(bg, 128, th, gm, gm.ap)
                for h in range(4):
                    ob = OB[h // 2]
                    o_ap = ob.ap[:, (h % 2) * 129:(h % 2) * 129 + 129]
                    for mt in range(2):
                        self.mm(ob, o_ap, pt, pt.ap[:, h, mt, tq * 128:(tq + 1) * 128], mvaug, mvaug.ap[:, mt, h, :],
                                start=(mt == 0), stop=(mt == 1))
                for j in range(2):
                    ov = OB[j].ap[:, 0:258].rearrange("p (h f) -> p h f", h=2)
                    S.op("dve", (lambda e, j=j, ov=ov, r=r: e.reciprocal(out=r.ap[:, 2 * j:2 * j + 2], in_=ov[:, :, 128])),
                         reads=[OB[j].tk], writes=[r.tk])
                self.ts("dve", r, r.ap, r, r.ap, 0.5, ALU.mult)
                for h in range(4):
                    hs = slice(h * 128, (h + 1) * 128)
                    ob = OB[h // 2]
                    self.stt(mixmt, mixmt.ap[:, hs], ob, ob.ap[:, (h % 2) * 129:(h % 2) * 129 + 128], r.ap[:, h:h + 1], ALU.mult,
                             gm, gm.ap[:, hs], ALU.mult, extra_reads=[r])
                if t > 0:
                    m_tail(t - 1)
            if nn == 1:
                self.sample_mem(locals())
        m_tail(NT - 1)

    def sample_mem(self, L):
        S = self.S
        PB = self.PB
        din = self.din
        n = NS
        A = self.arena
        WMQ, WMG = L["WMQ"], L["WMG"]
        nb = L["nb"]
        th, gm, rl, mixm = L["thm"][0], L["Gm"][0], L["rl"][0], L["mixMt"][0]
        mqs = A("mqs", (n, 512), BF16)
        mkld = [A("mkld", (128, 2, 512), BF16) for _ in range(2)]
        mkTs = A("mkTs", (128, 4, 4, 256), BF16)
        mvaugs = A("mvaugs", (128, 4, 2, 4, 129), BF16)
        mqsT = A("mqsT", (128, 4, n), BF16)
        PTms = A("PTms", (128, 4, 4, 2, n), BF16)
        bt = L["bt"]
        btv = L["btv"]
        bg = L["bg"]
        self.memset("pool", mvaugs, mvaugs.ap[:, :, :, :, 128:129], 1.0)
        self.memset("pool", PTms, PTms.ap, 0.0)
        for b in range(4):
            ml = mkld[b % 2]
            self.dma_in("pool", ml, ml.ap, din["cmk"][b].rearrange("(t p) c -> p t c", p=128))
            for mt in range(2):
                S.dma("pool", (lambda e, b=b, mt=mt: e.dma_start(
                    out=mvaugs.ap[:, b, mt, :, 0:128],
                    in_=din["cmv"][b][mt * 128:(mt + 1) * 128, :].rearrange("p (h f) -> p h f", h=4))),
                    reads=[], writes=[mvaugs.tk])
            for mt in range(2):
                bank = nb()
                bv = bank.ap.bitcast(BF16)
                for h in range(4):
                    self.tr(bank, bv[:, h * 128:(h + 1) * 128], ml, ml.ap[:, mt, h * 128:(h + 1) * 128], self.identb)
                self.cp("act", mkTs, mkTs.ap[:, b, :, mt * 128:(mt + 1) * 128], bank,
                        bv[:, 0:512].rearrange("p (h m) -> p h m", h=4))
        self.proj(bg, n, [self.xsT.tk], self.xsT.ap, WMQ)
        self.cp("act", mqs, mqs.ap, bg, bg.ap[0:n])
        for c in range(4):
            self.tr(bt, btv[:, c * n:(c + 1) * n], mqs, mqs.ap[:, c * 128:(c + 1) * 128], self.identb)
        tv4 = btv[:, 0:4 * n].rearrange("p (c t) -> p c t", c=4)
        self.cp("act", mqsT, mqsT.ap, bt, tv4)
        self.proj(bg, n, [self.xsT.tk], self.xsT.ap, WMG)
        self.gate(bg, n, th, gm, gm.ap[0:n])
        bsc = nb()
        for b in range(4):
            for h in range(4):
                for mt in range(2):
                    col = ((b * 4 + h) * 2 + mt) * 4
                    self.mm(bsc, bsc.ap[:, col:col + 4], mkTs, mkTs.ap[:, b, h, mt * 128:(mt + 1) * 128],
                            mqsT, mqsT.ap[:, h, 4 * b:4 * b + 4])
        for b in range(4):
            self.act(PTms, PTms.ap[:, b, :, :, 4 * b:4 * b + 4], bsc,
                     bsc.ap[:, b * 32:(b + 1) * 32].rearrange("p (h m i) -> p h m i", h=4, m=2), AF.Exp,
                     scale=float(128.0 ** -0.5))
        OB = L["OB"]
        for h in range(4):
            ob = OB[h // 2]
            o_ap = ob.ap[0:n, (h % 2) * 129:(h % 2) * 129 + 129]
            k = 0
            for b in range(4):
                for mt in range(2):
                    self.mm(ob, o_ap, PTms, PTms.ap[:, b, h, mt, :], mvaugs, mvaugs.ap[:, b, mt, h, :],
                            start=(k == 0), stop=(k == 7))
                    k += 1
        for j in range(2):
            ov = OB[j].ap[0:n, 0:258].rearrange("p (h f) -> p h f", h=2)
            S.op("dve", (lambda e, j=j, ov=ov: e.reciprocal(out=rl.ap[0:n, 2 * j:2 * j + 2], in_=ov[:, :, 128])),
                 reads=[OB[j].tk], writes=[rl.tk])
        self.ts("dve", rl, rl.ap[0:n], rl, rl.ap[0:n], 0.5, ALU.mult)
        for h in range(4):
            hs = slice(h * 128, (h + 1) * 128)
            ob = OB[h // 2]
            self.stt(mixm, mixm.ap[0:n, hs], ob, ob.ap[0:n, (h % 2) * 129:(h % 2) * 129 + 128], rl.ap[0:n, h:h + 1], ALU.mult,
                     gm, gm.ap[0:n, hs], ALU.mult, extra_reads=[rl])
        for c in range(4):
            self.tr(bt, btv[:, c * n:(c + 1) * n], mixm, mixm.ap[0:n, c * 128:(c + 1) * 128], self.identb)
        self.cp("act", self.mixsT, self.mixsT.ap[:, 8:12, :], bt, tv4)

    def phase_F(self):
        S = self.S
        PB = self.PB
        din, dco, dout = self.din, self.dco, self.dout
        n = NS
        self.arena_reset()
        A = self.arena
        Gt = A("Gt", (128, D), F32)
        Bt = A("Bt", (128, D), F32)
        mhalf = A("mhalf", (128, 4), F32)
        self.dma_in("sp", Gt, Gt.ap, din["ln_g"][0:1, :].broadcast_to([128, D]))
        self.dma_in("sp", Bt, Bt.ap, din["ln_b"][0:1, :].broadcast_to([128, D]))
        self.dma_in("sp", mhalf, mhalf.ap, dco["mhalf"])
        xf = [A("xf", (128, D), F32) for _ in range(3)]
        zz = [A("zz", (128, D), F32) for _ in range(3)]
        st = [A("st", (128, 2, 6), F32) for _ in range(2)]
        mv = [A("mv", (128, 2), F32) for _ in range(2)]
        ve = [A("ve", (128, 1), F32) for _ in range(2)]
        rstd = [A("rstd", (128, 1), F32) for _ in range(2)]
        nmr = [A("nmr", (128, 1), F32) for _ in range(2)]
        gen = self.sample_swa()
        next(gen)

        def pull(k):
            for _ in range(k):
                try:
                    next(gen)
                except StopIteration:
                    return

        def sample_finish(x_, z_, k):
            hbs = [PB[0], PB[1]]
            for half in range(2):
                for ec in range(12):
                    wb, wv = self.Wout[ec]
                    self.mm(hbs[half], hbs[half].ap[0:n], self.mixsT, self.mixsT.ap[:, ec, :], wb,
                            wv[:, half * 512:(half + 1) * 512], start=(ec == 0), stop=(ec == 11))
            for half in range(2):
                hs = slice(half * 512, (half + 1) * 512)
                self.stt(z_, z_.ap[0:n, hs], x_, x_.ap[0:n, hs], ALPHA, ALU.mult, hbs[half], hbs[half].ap[0:n], ALU.add)
            self.layernorm(z_, n, st[k], mv[k], ve[k], rstd[k], nmr[k], mhalf, Gt, Bt)
            self.dma_out("sp", dout["ys"], z_, z_.ap[0:n])

        for t in range(min(2, NT)):
            self.dma_in("sp", xf[t % 3], xf[t % 3].ap, din["x"][t * 128:(t + 1) * 128, :])
        for t in range(NT):
            tsl = slice(t * 128, (t + 1) * 128)
            x_, z_ = xf[t % 3], zz[t % 3]
            if t + 2 < NT:
                self.dma_in("sp", xf[(t + 2) % 3], xf[(t + 2) % 3].ap, din["x"][(t + 2) * 128:(t + 3) * 128, :])
            if t == NT - 2:
                self.dma_in("sp", xf[NT % 3], xf[NT % 3].ap[0:n], din["xs"])
            hb = [PB[2 * (t % 2)], PB[2 * (t % 2) + 1]]
            for half in range(2):
                pull(1)
                for ec in range(12):
                    if ec < 4:
                        mb, map_ = self.mixR[t], self.A_t[:, ec, tsl]
                    elif ec < 8:
                        mb, map_ = self.mixS, self.mixS_t[:, ec - 4, tsl]
                    else:
                        mb, map_ = self.mixM[t], self.A_t[:, 4 + ec - 8, tsl]
                    wb, wv = self.Wout[ec]
                    self.mm(hb[half], hb[half].ap, mb, map_, wb, wv[:, half * 512:(half + 1) * 512],
                            start=(ec == 0), stop=(ec == 11))
            for half in range(2):
                hs = slice(half * 512, (half + 1) * 512)
                self.stt(z_, z_.ap[:, hs], x_, x_.ap[:, hs], ALPHA, ALU.mult, hb[half], hb[half].ap, ALU.add)
            pull(1)
            self.layernorm(z_, 128, st[t % 2], mv[t % 2], ve[t % 2], rstd[t % 2], nmr[t % 2], mhalf, Gt, Bt, stage=1)
            pull(2)
            if t > 0:
                zp = zz[(t - 1) % 3]
                self.layernorm(zp, 128, None, None, None, None, None, mhalf, Gt, Bt, stage=2)
                self.dma_out("sp", dout["y"][(t - 1) * 128:t * 128, :], zp, zp.ap)
            pull(1)
            if t == NT - 1:
                pull(1000)
                sample_finish(xf[NT % 3], zz[(NT - 2) % 3], NT % 2)
        zp = zz[(NT - 1) % 3]
        self.layernorm(zp, 128, None, None, None, None, None, mhalf, Gt, Bt, stage=2)
        self.dma_out("sp", dout["y"][(NT - 1) * 128:NT * 128, :], zp, zp.ap)

    def layernorm(self, z_, n, st, mv, ve, rstd, nmr, mhalf, Gt, Bt, stage=0):
        S = self.S
        if stage in (0, 1):
            for half in range(2):
                hs = slice(half * 512, (half + 1) * 512)
                S.op("dve", (lambda e, half=half, hs=hs: e.bn_stats(out=st.ap[0:n, half, :], in_=z_.ap[0:n, hs])),
                     reads=[z_.tk], writes=[st.tk])
            S.op("dve", lambda e: e.bn_aggr(out=mv.ap[0:n, :], in_=st.ap[0:n, :, :].rearrange("p a b -> p (a b)")),
                 reads=[st.tk], writes=[mv.tk])
            self.ts("dve", ve, ve.ap[0:n], mv, mv.ap[0:n, 1:2], EPS, ALU.add)
            self.tt("pool", rstd, rstd.ap[0:n], ve, ve.ap[0:n], mhalf, mhalf.ap[0:n, 0:1], ALU.pow)
            self.ts("dve", nmr, nmr.ap[0:n], mv, mv.ap[0:n, 0:1], -1.0, ALU.mult, rstd.ap[0:n, 0:1], ALU.mult, extra_reads=[rstd])
            self.ts("dve", z_, z_.ap[0:n], z_, z_.ap[0:n], rstd.ap[0:n, 0:1], ALU.mult, nmr.ap[0:n, 0:1], ALU.add,
                    extra_reads=[rstd, nmr])
        if stage in (0, 2):
            self.tt("dve", z_, z_.ap[0:n], z_, z_.ap[0:n], Gt, Gt.ap[0:n], ALU.mult)
            self.tt("dve", z_, z_.ap[0:n], z_, z_.ap[0:n], Bt, Bt.ap[0:n], ALU.add)

    def sample_swa(self):
        S = self.S
        PB = self.PB
        din, dco = self.din, self.dco
        n = NS
        A = self.arena
        sqs, sks, Gss, Vnew = self.sqs, self.sks, self.Gss, self.Vnew
        smask = A("smask", (128, 9, 4), BF16)
        smaskn = A("smaskn", (n, 4, 4), BF16)
        ones = A("ones", (128, 2), BF16)
        sqsT = A("sqsT", (128, 4, n), BF16)
        sksT = A("sksT", (128, 4, n), BF16)
        GssT = A("GssT", (128, 4, n), BF16)
        Qbd = A("Qbd", (128, 4, 4, 8), BF16)
        Kc = A("Kc", (128, 9, 512), BF16)
        KcT = A("KcT", (128, 9, 4, 128), BF16)
        Vc = A("Vc", (128, 9, 512), BF16)
        PTx = [A("PTx", (128, 10, 8, 4), BF16) for _ in range(2)]
        rls = A("rlx", (4, 8), F32)
        Onb_ap = KcT.ap[0:4, 0, :, :].rearrange("p c k -> p (c k)")
        bt = PB[4]
        btv = bt.ap.bitcast(BF16)
        bsx = PB[5]
        UB = [PB[6], PB[7]]

        def issue_k(b):
            ck = din["ck"][b]
            self.dma_in("pool", Kc, Kc.ap[:, 0:4, :], ck.rearrange("(m s) c -> m s c", s=16)[:, 0:4, :], parallel=True)
            self.dma_in("pool", Kc, Kc.ap[:, 4:8, :], ck[1536:2048, :].rearrange("(m s) c -> m s c", s=4), parallel=True)
            self.dma_in("pool", Kc, Kc.ap[:, 8, :], ck[1920:2048, :], parallel=True)

        def issue_v(b):
            cv = din["cv"][b]
            self.dma_in("pool", Vc, Vc.ap[:, 0:4, :], cv.rearrange("(m s) c -> m s c", s=16)[:, 0:4, :], parallel=True)
            self.dma_in("pool", Vc, Vc.ap[:, 4:8, :], cv[1536:2048, :].rearrange("(m s) c -> m s c", s=4), parallel=True)
            self.dma_in("pool", Vc, Vc.ap[:, 8, :], cv[1920:2048, :], parallel=True)

        self.dma_in("pool", smask, smask.ap, dco["smask"])
        self.dma_in("pool", smaskn, smaskn.ap, dco["smaskn"])
        self.memset("pool", ones, ones.ap, 1.0)
        self.memset("pool", Qbd, Qbd.ap, 0.0)
        for p_ in PTx:
            self.memset("pool", p_, p_.ap, 0.0)
        issue_k(0)
        issue_v(0)
        yield
        tv4 = btv[:, 0:4 * n].rearrange("p (c t) -> p c t", c=4)
        for src, dst in ((sqs, sqsT), (sks, sksT), (Gss, GssT)):
            for c in range(4):
                self.tr(bt, btv[:, c * n:(c + 1) * n], src, src.ap[:, c * 128:(c + 1) * 128], self.identb)
            self.cp("act", dst, dst.ap, bt, tv4)
        self.cp("pool", Qbd, Qbd.ap[0:64, :, :, 0:4], sqsT, sqsT.ap[0:64, :, :].rearrange("p c (b i) -> p c b i", b=4))
        self.cp("pool", Qbd, Qbd.ap[64:128, :, :, 4:8], sqsT, sqsT.ap[64:128, :, :].rearrange("p c (b i) -> p c b i", b=4))
        yield
        for b in range(4):
            pt = PTx[b % 2]
            for tl in range(9):
                for c in range(4):
                    self.tr(bt, btv[:, c * 128:(c + 1) * 128], Kc, Kc.ap[:, tl, c * 128:(c + 1) * 128], self.identb)
                self.cp("act", KcT, KcT.ap[:, tl, :, :], bt,
                        btv[:, 0:512].rearrange("p (c k) -> p c k", c=4))
                yield
            if b + 1 < 4:
                issue_k(b + 1)
            for tl in range(9):
                for c in range(4):
                    self.mm(bsx, bsx.ap[:, tl * 32 + c * 8:tl * 32 + c * 8 + 8], KcT, KcT.ap[:, tl, c, :], Qbd, Qbd.ap[:, c, b, :])
            for c in range(4):
                self.mm(bsx, bsx.ap[0:n, 288 + c * 8:288 + c * 8 + 8], sksT, sksT.ap[:, c, :], Qbd, Qbd.ap[:, c, b, :])
            yield
            self.act(pt, pt.ap[:, 0:9, :, :], bsx, bsx.ap[:, 0:288].rearrange("p (t h i) -> p t h i", t=9, h=8), AF.Exp, scale=0.125)
            self.act(pt, pt.ap[0:n, 9, :, :], bsx, bsx.ap[0:n, 288:320].rearrange("p (h i) -> p h i", h=8), AF.Exp, scale=0.125)
            self.tt("pool", pt, pt.ap[:, 0:9, :, :], pt, pt.ap[:, 0:9, :, :], smask,
                    smask.ap.unsqueeze(2).broadcast_to([128, 9, 8, 4]), ALU.mult)
            self.tt("pool", pt, pt.ap[0:n, 9, :, :], pt, pt.ap[0:n, 9, :, :], smaskn,
                    smaskn.ap[:, b, :].unsqueeze(1).broadcast_to([n, 8, 4]), ALU.mult)
            yield
            for h in range(8):
                ub = UB[h // 4]
                o_ap = ub.ap[0:4, (h % 4) * 64:(h % 4) * 64 + 64]
                for tl in range(9):
                    self.mm(ub, o_ap, pt, pt.ap[:, tl, h, :], Vc, Vc.ap[:, tl, h * 64:(h + 1) * 64], start=(tl == 0), stop=False)
                self.mm(ub, o_ap, pt, pt.ap[0:n, 9, h, :], Vnew, Vnew.ap[:, h, 0:64], start=False, stop=True)
                l_ap = bsx.ap[0:4, 320 + h:321 + h]
                for tl in range(9):
                    self.mm(bsx, l_ap, pt, pt.ap[:, tl, h, :], ones, ones.ap[:, 0:1], start=(tl == 0), stop=False)
                self.mm(bsx, l_ap, pt, pt.ap[0:n, 9, h, :], ones, ones.ap[0:n, 0:1], start=False, stop=True)
                if h % 2 == 1:
                    yield
            if b + 1 < 4:
                issue_v(b + 1)
            S.op("dve", lambda e: e.reciprocal(out=rls.ap, in_=bsx.ap[0:4, 320:328]), reads=[bsx.tk], writes=[rls.tk])
            for j in range(2):
                uv = UB[j].ap[0:4, 0:256].rearrange("p (h f) -> p h f", h=4)
                self.tt("dve", KcT, Onb_ap[:, 256 * j:256 * (j + 1)].rearrange("p (h f) -> p h f", h=4), UB[j], uv,
                        rls, rls.ap[:, 4 * j:4 * j + 4].unsqueeze(2).broadcast_to([4, 4, 64]), ALU.mult)
            for c in range(4):
                self.tr(bt, btv[:, c * 4:(c + 1) * 4], KcT, Onb_ap[:, c * 128:(c + 1) * 128], self.identb)
            self.tt("dve", self.mixsT, self.mixsT.ap[:, 4:8, 4 * b:4 * b + 4], bt, btv[:, 0:16].rearrange("p (c i) -> p c i", c=4),
                    GssT, GssT.ap[:, :, 4 * b:4 * b + 4], ALU.mult)
            yield


_CACHE = {}


def _get_prog(phases):
    key = tuple(phases)
    if key not in _CACHE:
        p = Prog()
        _CACHE[key] = p.build(phases)
    return _CACHE[key]


PHASES = ("S", "R", "M", "F", "X")


def kernel(x_prompt, x_sample, state_ret, cache_swa_k, cache_swa_v, cache_mem_k, cache_mem_v,
           mem_prompt, w_in, w_mem_kv, w_out, ln_gain, ln_bias):
    f = lambda a: np.ascontiguousarray(np.asarray(a, dtype=np.float32))
    consts = {"c_" + k: v for k, v in _consts().items()}
    nc = _get_prog(PHASES)
    in_maps = []
    for c in range(NCORES):
        sb = slice(4 * c, 4 * c + 4)
        m = {
            "x": f(x_prompt[c]), "memx": f(mem_prompt[c]), "w_in": f(w_in[0]), "w_mem": f(w_mem_kv[0]),
            "w_out": f(w_out[0]), "ln_g": f(ln_gain), "ln_b": f(ln_bias),
            "xs": f(np.asarray(x_sample)[sb].reshape(NS, D)),
            "state": f(np.asarray(state_ret)[0, sb]),
            "ck": f(np.asarray(cache_swa_k)[0, sb].reshape(4, 2048, 512)),
            "cv": f(np.asarray(cache_swa_v)[0, sb].reshape(4, 2048, 512)),
            "cmk": f(np.asarray(cache_mem_k)[0, sb].reshape(4, 256, 512)),
            "cmv": f(np.asarray(cache_mem_v)[0, sb].reshape(4, 256, 512)),
        }
        m.update(consts)
        in_maps.append(m)
    res = run_bass_kernel_spmd(nc, in_maps, core_ids=list(range(NCORES)))
    R = res.results
    cat = lambda k: np.stack([np.asarray(R[c][k]) for c in range(NCORES)], axis=0)
    y = cat("y")
    ys = cat("ys").reshape(32, 4, D)
    retp = cat("retp")[None]
    rets = cat("rets").reshape(32, 4, 64, 128)[None]
    kp = cat("kp").reshape(8, SEQ, 8, 64)[None]
    vp = cat("vp").reshape(8, SEQ, 8, 64)[None]
    ks = cat("ks").reshape(32, 4, 8, 64)[None]
    vs = cat("vs").reshape(32, 4, 8, 64)[None]
    mkp = cat("mkp").reshape(8, 256, 4, 128)[None]
    mvp = cat("mvp").reshape(8, 256, 4, 128)[None]
    return (y, ys, retp, rets, kp, vp, ks, vs, mkp, mvp)
```

```python
from contextlib import ExitStack

import numpy as np
import concourse.bass as bass
import concourse.mybir as mybir
from concourse.bass_utils import run_bass_kernel_spmd

F32 = mybir.dt.float32
BF16 = mybir.dt.bfloat16
AF = mybir.ActivationFunctionType
ALU = mybir.AluOpType
AX = mybir.AxisListType

NCORES = 8
D = 1024
SEQ = 2048
NT = 16
DIN = 4608
DMIX = 1536
NS = 16
PAST = 8192
ALPHA = 2.0 ** 0.25
EPS = 1e-5
C_RQ, C_RK, C_RV, C_RG, C_SQ, C_SK, C_SV, C_SG, C_MQ, C_MG = 0, 256, 512, 1024, 1536, 2048, 2560, 3072, 3584, 4096

ENGS = ("pe", "act", "dve", "pool", "sp")
SAME_ENGINE_SYNC = True
SAME_ENGINE_WAR = True


class Tk:
    __slots__ = ("name", "w", "r", "excl", "wg")

    def __init__(self, name):
        self.name = name
        self.w = None
        self.r = {}
        self.wg = []
        self.excl = False


class Buf:
    __slots__ = ("ap", "tk")

    def __init__(self, ap, name):
        self.ap = ap
        self.tk = Tk(name)


class Sched:
    def __init__(self, n_dma_sems):
        self.q = {e: [] for e in ENGS}
        self.known = {e: {} for e in ENGS}
        self.dma_val = [0] * n_dma_sems
        self.rr = 0
        self.rrp = 0
        self.rrq = 0
        self.needed = {e: set() for e in ENGS}

    def _collect(self, eng, reads, writes, par=False):
        deps = []
        for t in reads:
            if t.w is not None:
                deps.append((t.w, True))
            for d in t.wg:
                deps.append((d, True))
            if t.excl:
                for d in t.r.values():
                    if not (d[0] == "e" and d[1] == eng):
                        deps.append((d, True))
        for t in writes:
            if t.w is not None and not (par and t.w[0] == "d" and not t.r):
                deps.append((t.w, True))
                for d in t.wg:
                    deps.append((d, True))
            for d in t.r.values():
                deps.append((d, False))
        kn = self.known[eng]
        best = {}
        for d, is_raw in deps:
            if d[0] == "e" and d[1] == eng:
                if eng == "pe" or not (SAME_ENGINE_SYNC and (is_raw or SAME_ENGINE_WAR)):
                    continue
            key = (d[0], d[1])
            if kn.get(key, -1) >= d[2]:
                continue
            if key not in best or best[key][2] < d[2]:
                best[key] = d
        waits = list(best.values())
        for d in waits:
            kn[(d[0], d[1])] = d[2]
            if d[0] == "e":
                self.needed[d[1]].add(d[2])
        return waits

    def op(self, eng, fn, reads=(), writes=()):
        idx = len(self.q[eng])
        waits = self._collect(eng, reads, writes)
        self.q[eng].append({"fn": fn, "waits": waits, "dma": None})
        me = ("e", eng, idx)
        for t in reads:
            t.r[("e", eng)] = me
        for t in writes:
            t.w = me
            t.wg = []
            t.r = {}
        return idx

    NPRE = 12

    def dma(self, eng, fn, reads=(), writes=(), prefetch=False, parallel=False):
        nn = len(self.dma_val) - self.NPRE
        half = nn // 2
        if prefetch:
            si = nn + self.rrp
            self.rrp = (self.rrp + 1) % self.NPRE
        elif eng == "pool":
            si = half + self.rrq
            self.rrq = (self.rrq + 1) % (nn - half)
        else:
            si = self.rr
            self.rr = (self.rr + 1) % half
        prev = self.dma_val[si]
        new = prev + 16
        self.dma_val[si] = new
        waits = self._collect(eng, reads, writes, par=parallel)
        if prev > 0:
            kn = self.known[eng]
            if kn.get(("d", si), -1) < prev:
                kn[("d", si)] = prev
                waits.append(("d", si, prev))
        self.q[eng].append({"fn": fn, "waits": waits, "dma": si})
        me = ("d", si, new)
        for t in reads:
            t.r[("d", si)] = me
        for t in writes:
            if parallel and t.w is not None and t.w[0] == "d" and not t.r:
                t.wg.append(t.w)
            else:
                t.wg = []
            t.w = me
            t.r = {}

    def barrier(self, final=False):
        last = {}
        for e in ENGS:
            for i in range(len(self.q[e]) - 1, -1, -1):
                ent = self.q[e][i]
                if ent["dma"] is None and ent["fn"] is not None:
                    last[e] = i
                    break
        for e in ENGS:
            waits = []
            kn = self.known[e]
            for e2, idx in last.items():
                if e2 == e:
                    continue
                if kn.get(("e", e2), -1) < idx:
                    kn[("e", e2)] = idx
                    waits.append(("e", e2, idx))
                    self.needed[e2].add(idx)
            for si, v in enumerate(self.dma_val):
                if si >= len(self.dma_val) - self.NPRE and not final:
                    continue
                if v > 0 and kn.get(("d", si), -1) < v:
                    kn[("d", si)] = v
                    waits.append(("d", si, v))
            self.q[e].append({"fn": None, "waits": waits, "dma": None})

    def emit(self, block, eng_sems, dma_sems):
        rank = {}
        for e in ENGS:
            for r, idx in enumerate(sorted(self.needed[e])):
                rank[(e, idx)] = r + 1

        def run(ename, eng):
            for idx, ent in enumerate(self.q[ename]):
                for d in ent["waits"]:
                    if d[0] == "e":
                        eng.wait_ge(eng_sems[d[1]], rank[(d[1], d[2])])
                    else:
                        eng.wait_ge(dma_sems[d[1]], d[2])
                if ent["fn"] is None:
                    continue
                ins = ent["fn"](eng)
                if ent["dma"] is not None:
                    ins.then_inc(dma_sems[ent["dma"]], 16)
                elif (ename, idx) in rank:
                    ins.then_inc(eng_sems[ename], 1)

        block.tensor(lambda eng: run("pe", eng))
        block.scalar(lambda eng: run("act", eng))
        block.vector(lambda eng: run("dve", eng))
        block.gpsimd(lambda eng: run("pool", eng))
        block.sync(lambda eng: run("sp", eng))


def _consts():
    f32 = np.float32
    c = {}
    c["ident"] = np.eye(128, dtype=f32)
    inv = (f32(10000.0) ** (-(np.arange(32, dtype=f32) * f32(2.0) / f32(64)))).astype(f32)
    pos = np.arange(SEQ, dtype=f32)
    ang = (pos[:, None] * inv[None, :]).astype(f32).astype(np.float64)
    cs = np.cos(ang).reshape(NT, 128, 32).transpose(1, 0, 2)
    sn = np.sin(ang).reshape(NT, 128, 32).transpose(1, 0, 2)
    c["cosp"] = np.ascontiguousarray(cs).astype(f32)
    c["sinp"] = np.ascontiguousarray(np.stack([-sn, sn], axis=2)).astype(f32)
    poss = (PAST + np.arange(4)).astype(f32)
    angs = (poss[:, None] * inv[None, :]).astype(f32).astype(np.float64)
    c["coss"] = np.tile(np.cos(angs), (4, 1)).astype(f32)
    sns = np.tile(np.sin(angs), (4, 1))
    c["sins"] = np.stack([-sns, sns], axis=1).astype(f32)
    lg = np.log1p(-np.exp2(-5.0 - np.arange(4, dtype=np.float64)))
    p1 = np.arange(1, 129, dtype=np.float64)[:, None]
    dq = np.exp(p1 * lg[None, :])
    dk = np.exp(-p1 * lg[None, :]) * 0.125
    c["dqk"] = np.concatenate([dq, dk], axis=1).astype(f32)
    g128 = np.exp(128.0 * lg)
    G = np.zeros((128, 4, 128), dtype=np.float64)
    for h in range(4):
        G[(h % 2) * 64:(h % 2) * 64 + 64, h, :] = g128[h]
    c["g128"] = G.reshape(128, 512).astype(f32)
    jj = np.arange(128)[:, None]
    ii = np.arange(128)[None, :]
    c["cmask"] = (ii >= jj).astype(f32)
    c["swamask"] = np.concatenate([(ii >= jj), (ii <= jj)], axis=1).astype(f32)
    sel = np.zeros((8, 4, 128), dtype=f32)
    for h in range(8):
        sel[h, h // 2, (h % 2) * 64:(h % 2) * 64 + 64] = 1.0
    c["sel"] = sel.reshape(8, 512)
    c["mhalf"] = np.full((128, 4), -0.5, dtype=f32)
    i4 = np.tile(np.arange(4, dtype=np.float64), 4)[:, None]
    dqs = np.exp((i4 + 1.0) * lg[None, :])
    dks = np.exp(-(i4 + 1.0) * lg[None, :]) * 0.125
    c["dqks"] = np.concatenate([dqs, dks], axis=1).astype(f32)
    r16 = np.arange(16)
    bj, ij = r16 // 4, r16 % 4
    c["rmask"] = ((bj[:, None] == bj[None, :]) & (ij[None, :] >= ij[:, None])).astype(f32)
    c["bmask"] = (bj[:, None] == np.arange(4)[None, :]).astype(f32)
    sm = np.zeros((128, 9, 4), dtype=f32)
    for i in range(4):
        sm[:, i, i] = 1.0
        sm[:, 4 + i, i] = 1.0
        sm[:, 8, i] = (np.arange(128) >= i).astype(f32)
    c["smask"] = sm
    smn = np.zeros((16, 4, 4), dtype=f32)
    for kk in range(16):
        for b in range(4):
            for i in range(4):
                if kk // 4 == b:
                    j = kk % 4
                    smn[kk, b, i] = (1.0 if j <= i else 0.0) + (2.0 if j == i else 0.0)
    c["smaskn"] = smn
    return c


CONST_SHAPES = {
    "ident": (128, 128), "cosp": (128, 16, 32), "sinp": (128, 16, 2, 32), "coss": (16, 32), "sins": (16, 2, 32),
    "dqk": (128, 8), "g128": (128, 512), "cmask": (128, 128), "swamask": (128, 256), "sel": (8, 512),
    "mhalf": (128, 4), "dqks": (16, 8), "rmask": (16, 16), "bmask": (16, 4), "smask": (128, 9, 4), "smaskn": (16, 4, 4),
}
IN_SHAPES = {
    "x": (SEQ, D), "memx": (256, D), "w_in": (D, DIN), "w_mem": (D, 1024), "w_out": (DMIX, D),
    "ln_g": (1, D), "ln_b": (1, D), "xs": (NS, D), "state": (4, 4, 64, 128),
    "ck": (4, 2048, 512), "cv": (4, 2048, 512), "cmk": (4, 256, 512), "cmv": (4, 256, 512),
}
OUT_SHAPES = {
    "y": (SEQ, D), "ys": (NS, D), "retp": (4, 64, 128), "rets": (4, 4, 64, 128),
    "kp": (SEQ, 512), "vp": (SEQ, 512), "ks": (NS, 512), "vs": (NS, 512),
    "mkp": (256, 512), "mvp": (256, 512),
}


def tok_slice(g, b, nblk=1):
    cnt = 128 * nblk
    if g == 1:
        start = 128 * b
    elif g == 4:
        r, n = divmod(b, 4)
        start = 512 * n + r
    else:
        start = b
    return slice(start, start + g * (cnt - 1) + 1, g)


class Prog:
    def __init__(self):
        self.nc = bass.Bass("TRN2", target_bir_lowering=False)
        nc = self.nc
        self.din = {k: nc.dram_tensor(k, list(s), F32, kind="ExternalInput").ap() for k, s in IN_SHAPES.items()}
        self.dco = {k: nc.dram_tensor("c_" + k, list(s), F32, kind="ExternalInput").ap() for k, s in CONST_SHAPES.items()}
        self.dout = {k: nc.dram_tensor(k, list(s), F32, kind="ExternalOutput").ap() for k, s in OUT_SHAPES.items()}
        self.vscr = nc.dram_tensor("vscr", [SEQ, 520], BF16, kind="Internal").ap()
        self.NDMA = 64
        self.S = Sched(self.NDMA)
        self.es = ExitStack()
        self.eng_sems = {e: self.es.enter_context(nc.semaphore("sem_" + e)) for e in ENGS}
        self.dma_sems = [self.es.enter_context(nc.semaphore("dsem%d" % i)) for i in range(self.NDMA)]
        self.uid = 0

    def sb(self, name, shape, dt):
        return self.es.enter_context(self.nc.sbuf_tensor(name, list(shape), dt))

    def buf(self, name, shape, dt):
        return Buf(self.sb(name, shape, dt)[:], name)

    def arena_reset(self):
        self.aoff = 0

    def arena(self, name, shape, dt):
        n = 1
        for s in shape[1:]:
            n *= s
        words = n if dt == F32 else (n + 1) // 2
        words = (words + 15) // 16 * 16
        assert self.aoff + words <= self.AW, (name, self.aoff, words, self.AW)
        v = self.arena_t[:, self.aoff:self.aoff + words]
        self.aoff += words
        if dt != F32:
            v = v.bitcast(dt)
        v = v[:, 0:n]
        if len(shape) > 2:
            names = " ".join("d%d" % i for i in range(len(shape) - 1))
            v = v.rearrange("p (%s) -> p %s" % (names, names), **{"d%d" % i: shape[i + 1] for i in range(len(shape) - 1)})
        if shape[0] < 128:
            v = v[0:shape[0]]
        self.uid += 1
        return Buf(v, "%s_%d" % (name, self.uid))

    def dma_in(self, q, dst, dst_ap, src_ap, parallel=False):
        self.S.dma(q, lambda e: e.dma_start(out=dst_ap, in_=src_ap), reads=[], writes=[dst.tk], parallel=parallel)

    def dma_out(self, q, dst_ap, src, src_ap):
        self.S.dma(q, lambda e: e.dma_start(out=dst_ap, in_=src_ap), reads=[src.tk], writes=[])

    def mm(self, out, out_ap, lhsT, lhsT_ap, rhs, rhs_ap, start=True, stop=True, extra_reads=()):
        self.S.op("pe", lambda e: e.matmul(out_ap, lhsT=lhsT_ap, rhs=rhs_ap, start=start, stop=stop),
                  reads=[lhsT.tk, rhs.tk] + list(extra_reads), writes=[out.tk])

    def tr(self, out, out_ap, in_, in_ap, ident):
        n = in_ap.shape[0]
        self.S.op("pe", lambda e: e.transpose(out=out_ap, in_=in_ap, identity=ident.ap[0:n, 0:n]),
                  reads=[in_.tk, ident.tk], writes=[out.tk])

    def act(self, out, out_ap, in_, in_ap, func, scale=1.0, bias=0.0, extra_reads=()):
        self.S.op("act", lambda e: e.activation(out=out_ap, in_=in_ap, func=func, scale=scale, bias=bias),
                  reads=[in_.tk] + [b.tk for b in extra_reads], writes=[out.tk])

    def tt(self, eng, out, out_ap, a, a_ap, b, b_ap, op, extra_reads=()):
        self.S.op(eng, lambda e: e.tensor_tensor(out=out_ap, in0=a_ap, in1=b_ap, op=op),
                  reads=[a.tk, b.tk] + list(extra_reads), writes=[out.tk])

    def ts(self, eng, out, out_ap, a, a_ap, s1, op0, s2=None, op1=None, extra_reads=()):
        if op1 is None:
            fn = lambda e: e.tensor_scalar(out=out_ap, in0=a_ap, scalar1=s1, scalar2=None, op0=op0)
        else:
            fn = lambda e: e.tensor_scalar(out=out_ap, in0=a_ap, scalar1=s1, scalar2=s2, op0=op0, op1=op1)
        self.S.op(eng, fn, reads=[a.tk] + [b.tk for b in extra_reads], writes=[out.tk])

    def stt(self, out, out_ap, a, a_ap, scalar, op0, b, b_ap, op1, extra_reads=()):
        self.S.op("dve", lambda e: e.scalar_tensor_tensor(out=out_ap, in0=a_ap, scalar=scalar, op0=op0, in1=b_ap, op1=op1),
                  reads=[a.tk, b.tk] + [x.tk for x in extra_reads], writes=[out.tk])

    def cp(self, eng, out, out_ap, in_, in_ap, extra_reads=()):
        if eng == "act":
            fn = lambda e: e.copy(out=out_ap, in_=in_ap)
        else:
            fn = lambda e: e.tensor_copy(out=out_ap, in_=in_ap)
        self.S.op(eng, fn, reads=[in_.tk] + list(extra_reads), writes=[out.tk])

    def memset(self, eng, out, out_ap, val):
        self.S.op(eng, lambda e: e.memset(out_ap, val), reads=[], writes=[out.tk])

    def load_w(self, slot, src, col0, ncols=512, row0=0):
        wb = self.W[slot]
        self.S.dma("pool", (lambda e: e.dma_start(out=wb.ap[:, :, 0:ncols],
                                                  in_=src.rearrange("(k p) c -> p k c", p=128)[:, :, col0:col0 + ncols])),
                   reads=[], writes=[wb.tk], prefetch=True)

    def load_w1(self, slot, src, col0, k, ncols=512):
        wb = self.W[slot]
        self.S.dma("pool", (lambda e: e.dma_start(out=wb.ap[:, k, 0:ncols], in_=src[k * 128:(k + 1) * 128, col0:col0 + ncols])),
                   reads=[], writes=[wb.tk], prefetch=True)

    def proj(self, bank, ntok, xt_reads, xt_ap, wslot, wc0=0, ncols=512):
        wb = self.W[wslot]
        for k in range(8):
            self.S.op("pe", (lambda e, k=k: e.matmul(bank.ap[0:ntok, 0:ncols], lhsT=xt_ap[:, k, :],
                                                     rhs=wb.ap[:, k, wc0:wc0 + ncols], start=(k == 0), stop=(k == 7))),
                      reads=list(xt_reads) + [wb.tk], writes=[bank.tk])

    def rope(self, bank, ntok, cos_b, cos_ap, sin_b, sin_ap, t1, t2, out, out_ap, nh=8):
        X = bank.ap[0:ntok, 0:nh * 64].rearrange("p (h two f) -> p h two f", h=nh, two=2)
        T1 = t1.ap[0:ntok, 0:nh * 64].rearrange("p (h two f) -> p h two f", h=nh, two=2)
        T2 = t2.ap[0:ntok, 0:nh * 64].rearrange("p (h two f) -> p h two f", h=nh, two=2)
        O = out_ap.rearrange("p (h two f) -> p h two f", h=nh, two=2)
        self.tt("dve", t1, T1, bank, X, cos_b, cos_ap, ALU.mult)
        self.tt("dve", t2, T2, bank, X[:, :, ::-1, :], sin_b, sin_ap, ALU.mult)
        self.tt("dve", out, O, t1, T1, t2, T2, ALU.add)

    def build(self, phases):
        self.phases = phases
        nc = self.nc
        S = self.S
        din, dco, dout = self.din, self.dco, self.dout
        self.ps_t = self.es.enter_context(nc.psum_tensor("psum", [128, 4096], F32))
        self.PB = [Buf(self.ps_t[:, i * 512:(i + 1) * 512], "bank%d" % i) for i in range(8)]
        for b_ in self.PB:
            b_.tk.excl = True
        PB = self.PB
        n = NS

        self.xT_t = self.sb("xT", (128, 8, SEQ), BF16)
        self.xT = [Buf(self.xT_t[:, :, t * 128:(t + 1) * 128], "xT%d" % t) for t in range(NT)]
        self.xT_all = [b.tk for b in self.xT]
        self.xsT = self.buf("xsT", (128, 8, NS), BF16)
        self.mixsT = self.buf("mixsT", (128, 12, NS), BF16)
        self.mixS_t = self.sb("mixS", (128, 4, SEQ), BF16)
        self.mixS = Buf(self.mixS_t, "mixS")
        self.NSLOT = 6
        self.W_t = self.sb("W", (128, self.NSLOT * 4096), BF16)
        self.W = [Buf(self.W_t[:, i * 4096:(i + 1) * 4096].rearrange("p (k c) -> p k c", k=8), "W%d" % i)
                  for i in range(self.NSLOT)]
        self.identb = self.buf("identb", (128, 128), BF16)
        self.identf = self.buf("identf", (128, 128), F32)
        self.cosp = self.buf("cosp", (128, 16, 32), F32)
        self.sinp = self.buf("sinp", (128, 16, 2, 32), F32)
        self.mkT = self.buf("mkT", (128, 4, 256), BF16)
        self.mvaug = self.buf("mvaug", (128, 2, 4, 129), BF16)
        self.sqs = self.buf("sqs", (n, 512), BF16)
        self.sks = self.buf("sks", (n, 512), BF16)
        self.Vnew = self.buf("Vnew", (n, 8, 65), BF16)
        self.Gss = self.buf("Gss", (n, 512), BF16)
        self.coss = self.buf("coss", (n, 32), F32)
        self.sins = self.buf("sins", (n, 2, 32), F32)
        self.A_t = self.sb("arenaA", (128, 8, SEQ), BF16)
        self.AW = 15872
        self.arena_t = self.sb("arenaB", (128, self.AW), F32)
        self.arena_reset()
        self.mixR = [Buf(self.A_t[:, 0:4, t * 128:(t + 1) * 128], "mixR%d" % t) for t in range(NT)]
        self.mixM = [Buf(self.A_t[:, 4:8, t * 128:(t + 1) * 128], "mixM%d" % t) for t in range(NT)]

        self.dma_in("pool", self.identb, self.identb.ap, dco["ident"])
        self.load_w(0, din["w_in"], C_SQ)
        self.load_w(1, din["w_in"], C_SK)
        self.load_w(2, din["w_in"], C_SV)
        self.load_w(4, din["w_mem"], 0)
        self.load_w(5, din["w_mem"], 512)
        self.load_w(3, din["w_in"], C_SG)
        self.dma_in("sp", self.identf, self.identf.ap, dco["ident"])
        self.dma_in("sp", self.cosp, self.cosp.ap, dco["cosp"])
        self.dma_in("sp", self.sinp, self.sinp.ap, dco["sinp"])
        self.dma_in("sp", self.coss, self.coss.ap, dco["coss"])
        self.dma_in("sp", self.sins, self.sins.ap, dco["sins"])

        xld = [self.arena("xld", (128, D), F32) for _ in range(3)]
        memT = self.arena("memT", (128, 8, 256), BF16)
        mf = [self.arena("mf", (128, 512), F32) for _ in range(2)]
        xsld = self.arena("xsld", (n, D), BF16)

        def load_T(src_ap, ntok, dst, dst_ap, i):
            xb = xld[i % 3]
            self.dma_in("sp", xb, xb.ap[0:ntok], src_ap)
            b0, b1 = PB[(2 * i) % 8], PB[(2 * i + 1) % 8]
            for c in range(8):
                bk = b0 if c < 4 else b1
                self.tr(bk, bk.ap[:, (c % 4) * ntok:(c % 4 + 1) * ntok], xb, xb.ap[0:ntok, c * 128:(c + 1) * 128], self.identf)
            e0, e1 = ("dve", "act") if i % 2 == 0 else ("act", "dve")
            self.cp(e0, dst, dst_ap[:, 0:4, :], b0, b0.ap[:, 0:4 * ntok].rearrange("p (c t) -> p c t", c=4))
            self.cp(e1, dst, dst_ap[:, 4:8, :], b1, b1.ap[:, 0:4 * ntok].rearrange("p (c t) -> p c t", c=4))

        i = 0
        for mt in range(2):
            load_T(din["memx"][mt * 128:(mt + 1) * 128, :], 128, memT, memT.ap[:, :, mt * 128:(mt + 1) * 128], i)
            i += 1
        for t in range(NT):
            load_T(din["x"][t * 128:(t + 1) * 128, :], 128, self.xT[t], self.xT[t].ap, i)
            i += 1
            if t == 15:
                self.dma_in("pool", xsld, xsld.ap, din["xs"])
                bk = PB[(2 * i) % 8]
                pv = bk.ap.bitcast(BF16)
                for c in range(8):
                    self.tr(bk, pv[:, c * n:(c + 1) * n], xsld, xsld.ap[:, c * 128:(c + 1) * 128], self.identb)
                self.cp("dve", self.xsT, self.xsT.ap, bk, pv[:, 0:8 * n].rearrange("p (c t) -> p c t", c=8))
                i += 1
                self.mem_setup(memT, mf)
        S.barrier()

        if "S" in phases:
            self.phase_S()
            S.barrier()
        if "R" in phases:
            self.phase_R()
            S.barrier()
        if "M" in phases:
            self.phase_M()
            S.barrier()
        if "F" in phases:
            self.phase_F()
            S.barrier()
        S.barrier(final=True)
        with nc.Block() as block:
            S.emit(block, self.eng_sems, self.dma_sems)
        self.es.close()
        return nc

    def mem_setup(self, memT, mf):
        S = self.S
        PB = self.PB
        dout = self.dout
        mkT, mvaug = self.mkT, self.mvaug
        self.memset("pool", mvaug, mvaug.ap[:, :, :, 128:129], 1.0)
        i = 0
        for mt in range(2):
            for which in range(2):
                bank = PB[4 + i % 4]
                self.proj(bank, 128, [memT.tk], memT.ap[:, :, mt * 128:(mt + 1) * 128], 4 + which)
                f = mf[i % 2]
                i += 1
                self.cp("act", f, f.ap, bank, bank.ap)
                self.dma_out("sp", dout["mkp" if which == 0 else "mvp"][mt * 128:(mt + 1) * 128, :], f, f.ap)
                if which == 1:
                    self.cp("pool", mvaug, mvaug.ap[:, mt, :, 0:128], f, f.ap.rearrange("p (h f) -> p h f", h=4))
        w0 = self.W[4]
        for h in range(4):
            bank = PB[4 + h % 4]
            for k in range(8):
                S.op("pe", (lambda e, k=k, h=h, bank=bank: e.matmul(bank.ap[:, 0:256], lhsT=w0.ap[:, k, h * 128:(h + 1) * 128],
                                                                     rhs=memT.ap[:, k, :], start=(k == 0), stop=(k == 7))),
                     reads=[w0.tk, memT.tk], writes=[bank.tk])
            self.cp("act", mkT, mkT.ap[:, h, :], bank, bank.ap[:, 0:256])

    def gate(self, bank, n, th, out, out_ap):
        self.act(th, th.ap[0:n], bank, bank.ap[0:n], AF.Tanh, scale=0.5)
        self.stt(out, out_ap, th, th.ap[0:n], 1.0, ALU.add, bank, bank.ap[0:n], ALU.mult)

    def phase_S(self):
        S = self.S
        PB = self.PB
        din, dco, dout = self.din, self.dco, self.dout
        n = NS
        self.arena_reset()
        QKT_t = self.A_t
        QKT = [Buf(QKT_t[:, :, t * 128:(t + 1) * 128], "QKT%d" % t) for t in range(NT)]
        qkt_all = [b.tk for b in QKT]
        Vaug_t = self.arena("Vaug", (128, 16, 8, 65), BF16)
        Vaug = [Buf(Vaug_t.ap[:, b], "Vaug%d" % b) for b in range(16)]
        PT = [self.arena("PT", (128, 8, 256), BF16) for _ in range(3)]
        LT = self.arena("LT", (8, SEQ), F32)
        swam = self.arena("swam", (128, 256), BF16)
        sel = self.arena("sel", (8, 512), F32)
        t1 = self.arena("t1", (128, 512), F32)
        t2 = self.arena("t2", (128, 512), F32)
        qkb = [self.arena("qkb", (128, 1024), BF16) for _ in range(2)]
        kf = [self.arena("kf", (128, 512), F32) for _ in range(2)]
        vf = [self.arena("vf", (128, 512), F32) for _ in range(2)]
        Ub = [self.arena("Ub", (128, 8, 64), BF16) for _ in range(2)]
        lf = [self.arena("lf", (128, 8), F32) for _ in range(2)]
        tha, ga, gb = [t1, t2], kf, vf

        self.dma_in("pool", swam, swam.ap, dco["swamask"])
        self.dma_in("sp", sel, sel.ap, dco["sel"])
        self.memset("pool", Vaug_t, Vaug_t.ap[:, :, :, 64:65], 1.0)
        for b in Vaug:
            b.tk.w = Vaug_t.tk.w

        vscr_tk = Tk("vscr")
        def s1_front(t):
            xt = self.xT[t]
            bq, bk, bv = PB[(2 * t) % 4], PB[(2 * t + 1) % 4], PB[4 + t % 2]
            cos4 = self.cosp.ap[:, t, :].unsqueeze(1).unsqueeze(1).broadcast_to([128, 8, 2, 32])
            sin4 = self.sinp.ap[:, t, :, :].unsqueeze(1).broadcast_to([128, 8, 2, 32])
            qk = qkb[t % 2]
            self.proj(bq, 128, [xt.tk], xt.ap, 0)
            self.rope(bq, 128, self.cosp, cos4, self.sinp, sin4, t1, t2, qk, qk.ap[:, 0:512])
            self.proj(bk, 128, [xt.tk], xt.ap, 1)
            self.rope(bk, 128, self.cosp, cos4, self.sinp, sin4, t1, t2, kf[t % 2], kf[t % 2].ap)
            self.dma_out("sp", dout["kp"][t * 128:(t + 1) * 128, :], kf[t % 2], kf[t % 2].ap)
            self.cp("pool", qk, qk.ap[:, 512:1024], kf[t % 2], kf[t % 2].ap)
            self.proj(bv, 128, [xt.tk], xt.ap, 2)
            self.cp("act", vf[t % 2], vf[t % 2].ap, bv, bv.ap)
            self.dma_out("sp", dout["vp"][t * 128:(t + 1) * 128, :], vf[t % 2], vf[t % 2].ap)
            self.cp("pool", Vaug[t], Vaug[t].ap[:, :, 0:64], vf[t % 2], vf[t % 2].ap.rearrange("p (h f) -> p h f", h=8))
            S.dma("sp", (lambda e, t=t: e.dma_start(out=self.vscr[t * 128:(t + 1) * 128, :],
                                                    in_=Vaug[t].ap.rearrange("p h f -> p (h f)"))),
                  reads=[Vaug[t].tk], writes=[vscr_tk])

        def s1_back(t):
            bt = PB[6 + t % 2]
            qk = qkb[t % 2]
            btv = bt.ap.bitcast(BF16)
            for c in range(8):
                self.tr(bt, btv[:, c * 128:(c + 1) * 128], qk, qk.ap[:, c * 128:(c + 1) * 128], self.identb)
            self.cp("act", QKT[t], QKT[t].ap, bt, btv.rearrange("p (c t) -> p c t", c=8))

        for t in range(NT):
            s1_front(t)
            if t > 0:
                s1_back(t - 1)
        s1_back(NT - 1)
        cos4 = self.coss.ap.unsqueeze(1).unsqueeze(1).broadcast_to([n, 8, 2, 32])
        sin4 = self.sins.ap.unsqueeze(1).broadcast_to([n, 8, 2, 32])
        self.proj(PB[0], n, [self.xsT.tk], self.xsT.ap, 0)
        self.rope(PB[0], n, self.coss, cos4, self.sins, sin4, t1, t2, self.sqs, self.sqs.ap)
        self.proj(PB[1], n, [self.xsT.tk], self.xsT.ap, 1)
        self.rope(PB[1], n, self.coss, cos4, self.sins, sin4, t1, t2, kf[0], kf[0].ap[0:n])
        self.dma_out("sp", dout["ks"], kf[0], kf[0].ap[0:n])
        self.cp("pool", self.sks, self.sks.ap, kf[0], kf[0].ap[0:n])
        self.proj(PB[4], n, [self.xsT.tk], self.xsT.ap, 2)
        self.cp("act", vf[0], vf[0].ap[0:n], PB[4], PB[4].ap[0:n])
        self.dma_out("sp", dout["vs"], vf[0], vf[0].ap[0:n])
        self.memset("pool", self.Vnew, self.Vnew.ap[:, :, 64:65], 1.0)
        self.cp("pool", self.Vnew, self.Vnew.ap[:, :, 0:64], vf[0], vf[0].ap[0:n].rearrange("p (h f) -> p h f", h=8))
        self.proj(PB[5], n, [self.xsT.tk], self.xsT.ap, 3)
        self.gate(PB[5], n, t1, t2, t2.ap[0:n])
        self.ts("dve", self.Gss, self.Gss.ap, t2, t2.ap[0:n], 0.5, ALU.mult)
        pre = []
        for slot, col in ((4, C_RQ), (5, C_RV), (0, C_RG), (1, C_MQ)):
            for k in range(8):
                pre.append((slot, col, k))

        STB = PB[0:4]
        UB = [PB[4], PB[5]]
        TBa = PB[6]
        VB = PB[7]
        TBl = VB
        entries = []
        for g in (1, 4, 16):
            seqs = {1: [list(range(16))], 4: [[4 * r + nn for nn in range(4)] for r in range(4)],
                    16: [[r] for r in range(16)]}[g]
            for seq in seqs:
                for si, kb in enumerate(seq):
                    entries.append((g, seq, si, kb))
        NE = len(entries)
        vdone = {1: True}

        nextg = {1: 4, 4: 16}

        def vreload(g, b):
            tsl = tok_slice(g, b)
            S.dma("sp", (lambda e, b=b, tsl=tsl: e.dma_start(out=Vaug[b].ap.rearrange("p h f -> p (h f)"),
                                                            in_=self.vscr[tsl, :])),
                  reads=[vscr_tk], writes=[Vaug[b].tk])

        def stage_A(i):
            g, seq, si, kb = entries[i]
            last = (si == len(seq) - 1)
            nq = 128 if last else 256
            ksl = tok_slice(g, kb)
            qsl = tok_slice(g, kb, 1 if last else 2)
            pt = PT[i % 3]
            for h in range(8):
                c, half = h // 2, h % 2
                bank = STB[2 * (h // 4) + (h % 2)]
                slot = (h // 2) % 2
                self.mm(bank, bank.ap[:, slot * 256:slot * 256 + nq],
                        QKT[0], QKT_t[half * 64:(half + 1) * 64, 4 + c, ksl],
                        QKT[0], QKT_t[half * 64:(half + 1) * 64, c, qsl], extra_reads=qkt_all)
            for bi in range(4):
                bank = STB[bi]
                hs = 4 * (bi // 2) + (bi % 2)
                self.act(pt, pt.ap[:, hs:hs + 3:2, 0:nq], bank,
                         bank.ap.rearrange("p (s q) -> p s q", s=2)[:, :, 0:nq], AF.Exp, scale=0.125)
            self.tt("pool", pt, pt.ap[:, :, 0:nq], pt, pt.ap[:, :, 0:nq], swam,
                    swam.ap[:, 0:nq].unsqueeze(1).broadcast_to([128, 8, nq]), ALU.mult)

        def stage_B(i):
            g, seq, si, kb = entries[i]
            pt = PT[i % 3]
            ptp = PT[(i - 1) % 3]
            for h in range(8):
                ub = UB[h // 4]
                o_ap = ub.ap[:, (h % 4) * 65:(h % 4) * 65 + 65]
                if si > 0:
                    self.mm(ub, o_ap, ptp, ptp.ap[:, h, 128:256], Vaug[seq[si - 1]], Vaug[seq[si - 1]].ap[:, h, :],
                            start=True, stop=False)
                    self.mm(ub, o_ap, pt, pt.ap[:, h, 0:128], Vaug[kb], Vaug[kb].ap[:, h, :], start=False, stop=True)
                else:
                    self.mm(ub, o_ap, pt, pt.ap[:, h, 0:128], Vaug[kb], Vaug[kb].ap[:, h, :], start=True, stop=True)
            u = Ub[i % 2]
            l = lf[i % 2]
            for j in range(2):
                uv = UB[j].ap[:, 0:260].rearrange("p (h f) -> p h f", h=4)
                self.cp("dve", u, u.ap[:, 4 * j:4 * j + 4, :], UB[j], uv[:, :, 0:64])
                self.cp("dve", l, l.ap[:, 4 * j:4 * j + 4], UB[j], uv[:, :, 64])
            if g in nextg:
                if si > 0:
                    vreload(nextg[g], seq[si - 1])
                if si == len(seq) - 1:
                    vreload(nextg[g], kb)
            if g == 1 and kb == 0:
                self.load_w(2, din["w_in"], C_MG)
            if pre:
                slot, col, k = pre.pop(0)
                self.load_w1(slot, din["w_in"], col, k)

        def stage_C(i):
            g, seq, si, kb = entries[i]
            ksl = tok_slice(g, kb)
            u = Ub[i % 2]
            l = lf[i % 2]
            tav = TBa.ap.bitcast(BF16)
            for c in range(4):
                self.tr(TBa, tav[:, c * 128:(c + 1) * 128], u,
                        u.ap[:, 2 * c:2 * c + 2, :].rearrange("p h f -> p (h f)"), self.identb)
            self.tr(TBl, TBl.ap[0:8, 256:384], l, l.ap, self.identf)
            dstm = self.mixS_t[:, :, ksl]
            dstl = LT.ap[:, ksl]
            srcm = tav[:, 0:512].rearrange("p (c q) -> p c q", c=4)
            if g == 1:
                self.cp("dve", self.mixS, dstm, TBa, srcm)
                self.cp("dve", LT, dstl, TBl, TBl.ap[0:8, 256:384])
            else:
                self.tt("dve", self.mixS, dstm, TBa, srcm, self.mixS, dstm, ALU.add)
                self.tt("dve", LT, dstl, TBl, TBl.ap[0:8, 256:384], LT, dstl, ALU.add)

        for i in range(NE + 2):
            if i < NE:
                stage_A(i)
            if 0 <= i - 1 < NE:
                stage_B(i - 1)
            if 0 <= i - 2 < NE:
                stage_C(i - 2)

        S.op("dve", lambda e: e.reciprocal(out=LT.ap, in_=LT.ap), reads=[LT.tk], writes=[LT.tk])
        i = 0
        for nn in range(4):
            gsl = slice(nn * 512, (nn + 1) * 512)
            for c in range(4):
                bb, bg = PB[(2 * i) % 8], PB[(2 * i + 1) % 8]
                self.mm(bb, bb.ap, sel, sel.ap[:, c * 128:(c + 1) * 128], LT, LT.ap[:, gsl])
                wb = self.W[3]
                for k in range(8):
                    S.op("pe", (lambda e, k=k, c=c, gsl=gsl, bg=bg, wb=wb: e.matmul(
                        bg.ap, lhsT=wb.ap[:, k, c * 128:(c + 1) * 128], rhs=self.xT_t[:, k, gsl],
                        start=(k == 0), stop=(k == 7))), reads=self.xT_all + [wb.tk], writes=[bg.tk])
                th, a, b = tha[i % 2], ga[i % 2], gb[i % 2]
                self.act(th, th.ap, bg, bg.ap, AF.Tanh, scale=0.5)
                self.stt(a, a.ap, th, th.ap, 1.0, ALU.add, bg, bg.ap, ALU.mult)
                self.stt(b, b.ap, self.mixS, self.mixS_t[:, c, gsl], 0.5, ALU.mult, bb, bb.ap, ALU.mult)
                self.tt("dve", self.mixS, self.mixS_t[:, c, gsl], a, a.ap, b, b.ap, ALU.mult)
                i += 1

    def phase_R(self):
        S = self.S
        PB = self.PB
        din, dco, dout = self.din, self.dco, self.dout
        n = NS
        self.arena_reset()
        A = self.arena
        lgs = np.log1p(-np.exp2(-5.0 - np.arange(4, dtype=np.float64)))
        g128 = [float(np.exp(128.0 * v)) for v in lgs]
        g4 = [float(np.exp(4.0 * v)) for v in lgs]
        dqk = A("dqk", (128, 8), F32)
        dqks = A("dqks", (n, 8), F32)
        cmask = A("cmask", (128, 128), BF16)
        rmask = A("rmask", (n, 16), BF16)
        bmask = A("bmask", (n, 4), F32)
        mhalf = A("mhalf", (128, 4), F32)
        Rst = A("Rst", (128, 4, 128), F32)
        Rb = A("Rb", (128, 4, 128), BF16)
        CS = [A("CS", (128, 8, 2, 32), F32) for _ in range(2)]
        SN = [A("SN", (128, 8, 2, 32), F32) for _ in range(2)]
        t1 = A("t1", (128, 512), F32)
        t2 = A("t2", (128, 512), F32)
        QKr = [A("QKr", (128, 512), BF16) for _ in range(2)]
        Vr = [A("Vr", (128, 512), BF16) for _ in range(2)]
        thr = [A("thr", (128, 512), F32) for _ in range(2)]
        Gr = [A("Gr", (128, 512), F32) for _ in range(2)]
        QKrT = [A("QKrT", (128, 4, 128), BF16) for _ in range(2)]
        QZ = [A("QZ", (128, 4, 128), BF16) for _ in range(2)]
        STm = [A("STm", (128, 4, 128), BF16) for _ in range(2)]
        s12 = A("s12", (128, 2, 4), F32)
        st6 = A("st6", (128, 4, 6), F32)
        mvr = A("mvr", (128, 4, 2), F32)
        mean = A("mean", (128, 4), F32)
        msq = A("msq", (128, 4), F32)
        va = A("va", (128, 4), F32)
        ve = A("ve", (128, 4), F32)
        rstd = A("rstd", (128, 4), F32)
        junk = A("junk", (128, 512), BF16)
        mixRs = A("mixRs", (n, 512), BF16)
        On = A("On", (128, 512), F32)
        mixRt = [A("mixRt", (128, 512), BF16) for _ in range(2)]
        Rs = A("Rs", (128, 4, 4, 128), F32)
        Rsb = A("Rsb", (128, 4, 4, 128), BF16)
        QKsT = A("QKsT", (128, 4, n), BF16)
        QZs = A("QZs", (128, 4, n), BF16)
        QZB = A("QZB", (128, 4, 4, n), BF16)
        STs = A("STs", (n, 4, n), BF16)
        KZ = A("KZ", (n, 4, 256), BF16)

        for b_, k_ in ((dqk, "dqk"), (mhalf, "mhalf"), (dqks, "dqks"), (bmask, "bmask")):
            self.dma_in("sp", b_, b_.ap, dco[k_])
        self.dma_in("pool", cmask, cmask.ap, dco["cmask"])
        self.dma_in("pool", rmask, rmask.ap, dco["rmask"])
        Gz = A("Gz", (128, 4, 128), F32)
        self.dma_in("sp", Gz, Gz.ap, dco["g128"].rearrange("p (h f) -> p h f", h=4))
        self.memset("pool", Rst, Rst.ap, 0.0)
        self.memset("pool", Rb, Rb.ap, 0.0)
        for z in QZ:
            self.memset("pool", z, z.ap, 0.0)
        self.memset("pool", Rs, Rs.ap, 0.0)
        self.memset("pool", QZs, QZs.ap, 0.0)
        self.memset("pool", QZB, QZB.ap, 0.0)
        self.memset("pool", Rsb, Rsb.ap, 0.0)
        st_src = din["state"].rearrange("b h k v -> k (b h) v")
        for hp in range(2):
            rows = slice(hp * 64, hp * 64 + 64)
            self.dma_in("sp", Rs, Rs.ap.rearrange("p b h v -> p (b h) v")[rows, hp:16:2, :], st_src[:, hp:16:2, :], parallel=True)
            self.dma_in("pool", Rsb, Rsb.ap.rearrange("p b h v -> p (b h) v")[rows, hp:16:2, :], st_src[:, hp:16:2, :], parallel=True)
        dq4 = dqk.ap.unsqueeze(2).unsqueeze(3).broadcast_to([128, 8, 2, 32])
        WQK, WV, WG = 4, 5, 0

        def headnorm_gate(bo, nn_, gr, mixrt):
            for h in range(4):
                S.op("dve", (lambda e, h=h: e.bn_stats(out=st6.ap[0:nn_, h, :], in_=bo.ap[0:nn_, h * 128:(h + 1) * 128])),
                     reads=[bo.tk], writes=[st6.tk])
            for h in range(4):
                S.op("dve", (lambda e, h=h: e.bn_aggr(out=mvr.ap[0:nn_, h, :], in_=st6.ap[0:nn_, h, :])),
                     reads=[st6.tk], writes=[mvr.tk])
            self.cp("dve", mean, mean.ap[0:nn_], mvr, mvr.ap[0:nn_, :, 0])
            self.ts("dve", ve, ve.ap[0:nn_], mvr, mvr.ap[0:nn_, :, 1], 4.0, ALU.mult, 4.0 * EPS, ALU.add)
            self.tt("pool", rstd, rstd.ap[0:nn_], ve, ve.ap[0:nn_], mhalf, mhalf.ap[0:nn_], ALU.pow)
            for h in range(4):
                hs = slice(h * 128, (h + 1) * 128)
                self.stt(On, On.ap[0:nn_, hs], bo, bo.ap[0:nn_, hs], mean.ap[0:nn_, h:h + 1], ALU.subtract,
                         gr, gr.ap[0:nn_, hs], ALU.mult, extra_reads=[mean])
            for h in range(4):
                hs = slice(h * 128, (h + 1) * 128)
                S.op("act", (lambda e, h=h, hs=hs: e.activation(out=mixrt.ap[0:nn_, hs], in_=On.ap[0:nn_, hs], func=AF.Identity,
                                                               scale=rstd.ap[0:nn_, h:h + 1])),
                     reads=[On.tk, rstd.tk], writes=[mixrt.tk])

        b0, b1, b2, bt, bs, bo = PB[0], PB[1], PB[2], PB[3], PB[4], PB[5]
        bkv = [PB[6], PB[7]]
        btv = bt.ap.bitcast(BF16)
        bsv = bs.ap.bitcast(BF16)

        def r_front(t):
            xt = self.xT[t]
            cs, sn = CS[t % 2], SN[t % 2]
            cos4 = self.cosp.ap[:, t, :].unsqueeze(1).unsqueeze(1).broadcast_to([128, 8, 2, 32])
            sin4 = self.sinp.ap[:, t, :, :].unsqueeze(1).broadcast_to([128, 8, 2, 32])
            self.tt("pool", cs, cs.ap, self.cosp, cos4, dqk, dq4, ALU.mult)
            self.tt("pool", sn, sn.ap, self.sinp, sin4, dqk, dq4, ALU.mult)
            qk, vr, th, gr, qkT, qz = QKr[t % 2], Vr[t % 2], thr[t % 2], Gr[t % 2], QKrT[t % 2], QZ[t % 2]
            self.proj(b0, 128, [xt.tk], xt.ap, WQK)
            self.rope(b0, 128, cs, cs.ap, sn, sn.ap, t1, t2, qk, qk.ap)
            self.proj(b1, 128, [xt.tk], xt.ap, WV)
            self.cp("act", vr, vr.ap, b1, b1.ap)
            self.proj(b2, 128, [xt.tk], xt.ap, WG)
            self.gate(b2, 128, th, gr, gr.ap)
            for c in range(4):
                self.tr(bt, btv[:, c * 128:(c + 1) * 128], qk, qk.ap[:, c * 128:(c + 1) * 128], self.identb)
            tv4 = btv[:, 0:512].rearrange("p (c t) -> p c t", c=4)
            self.cp("act", qkT, qkT.ap, bt, tv4)
            self.cp("act", qz, qz.ap[0:64, 0:4:2, :], bt, tv4[0:64, 0:2, :])
            self.cp("act", qz, qz.ap[64:128, 1:4:2, :], bt, tv4[64:128, 0:2, :])

        def r_back(t):
            qk, vr, gr, qkT, qz, stm, mixrt = QKr[t % 2], Vr[t % 2], Gr[t % 2], QKrT[t % 2], QZ[t % 2], STm[t % 2], mixRt[t % 2]
            for h in range(4):
                self.mm(bs, bs.ap[:, h * 128:(h + 1) * 128], qkT, qkT.ap[:, 2 + h // 2, :], qz, qz.ap[:, h, :])
            self.tt("dve", stm, stm.ap, bs, bs.ap.rearrange("p (h i) -> p h i", h=4), cmask,
                    cmask.ap.unsqueeze(1).broadcast_to([128, 4, 128]), ALU.mult)
            for h in range(4):
                o_ap = bo.ap[:, h * 128:(h + 1) * 128]
                self.mm(bo, o_ap, stm, stm.ap[:, h, :], vr, vr.ap[:, h * 128:(h + 1) * 128], start=True, stop=False)
                self.mm(bo, o_ap, qkT, qkT.ap[:, h // 2, :], Rb, Rb.ap[:, h, :], start=False, stop=True)
            for c in range(2):
                self.mm(bkv[c], bkv[c].ap, qk, qk.ap[:, 256 + c * 128:256 + (c + 1) * 128], vr, vr.ap)
            for c in range(2):
                rv_ = Rst.ap[:, 2 * c:2 * c + 2, :]
                kv_ = bkv[c].ap[:, 2 * c * 128:(2 * c + 2) * 128].rearrange("p (h f) -> p h f", h=2)
                self.tt("dve", Rst, rv_, bkv[c], kv_, Rst, rv_, ALU.add)
                self.tt("dve", Rst, rv_, Rst, rv_, Gz, Gz.ap[:, 2 * c:2 * c + 2, :], ALU.mult)
            self.cp("act", Rb, Rb.ap, Rst, Rst.ap)
            headnorm_gate(bo, 128, gr, mixrt)

        def r_tail(t):
            mixrt = mixRt[t % 2]
            for c in range(4):
                self.tr(bt, btv[:, c * 128:(c + 1) * 128], mixrt, mixrt.ap[:, c * 128:(c + 1) * 128], self.identb)
            self.cp("act", self.mixR[t], self.mixR[t].ap, bt, btv[:, 0:512].rearrange("p (c t) -> p c t", c=4))

        r_front(0)
        for t in range(NT):
            if t + 1 < NT:
                r_front(t + 1)
            r_back(t)
            if t > 0:
                r_tail(t - 1)
            if t == 7:
                self.sample_ret(locals())
        r_tail(NT - 1)
        for h in range(4):
            rows = slice((h % 2) * 64, (h % 2) * 64 + 64)
            self.dma_out("sp", dout["retp"][h], Rst, Rst.ap[rows, h, :])
        self.Wout = []
        for ec in range(12):
            slot = (4, 5, 0)[ec // 4]
            wb = self.W[slot]
            v = self.W_t[:, slot * 4096 + (ec % 4) * 1024:slot * 4096 + (ec % 4 + 1) * 1024]
            self.Wout.append((wb, v))
            S.dma("pool", (lambda e, ec=ec, v=v: e.dma_start(out=v, in_=din["w_out"][ec * 128:(ec + 1) * 128, :])),
                  reads=[], writes=[wb.tk], prefetch=True)

    def sample_ret(self, L):
        S = self.S
        PB = self.PB
        din, dout = self.din, self.dout
        n = NS
        g4 = L["g4"]
        dqks, rmask, bmask, t1, t2 = L["dqks"], L["rmask"], L["bmask"], L["t1"], L["t2"]
        Rs, Rsb, QKsT, QZs, QZB, STs, KZ = L["Rs"], L["Rsb"], L["QKsT"], L["QZs"], L["QZB"], L["STs"], L["KZ"]
        cs, sn = L["CS"][1], L["SN"][1]
        qk, vr, th, gr, mixrt = L["QKr"][1], L["Vr"][1], L["thr"][1], L["Gr"][1], L["mixRs"]
        b0, b1, b2, bt, bs, bo = PB[0], PB[1], PB[2], PB[3], PB[4], PB[5]
        cos4 = self.coss.ap.unsqueeze(1).unsqueeze(1).broadcast_to([n, 8, 2, 32])
        sin4 = self.sins.ap.unsqueeze(1).broadcast_to([n, 8, 2, 32])
        dq4 = dqks.ap.unsqueeze(2).unsqueeze(3).broadcast_to([n, 8, 2, 32])
        self.tt("pool", cs, cs.ap[0:n], self.coss, cos4, dqks, dq4, ALU.mult)
        self.tt("pool", sn, sn.ap[0:n], self.sins, sin4, dqks, dq4, ALU.mult)
        self.proj(b0, n, [self.xsT.tk], self.xsT.ap, L["WQK"])
        self.rope(b0, n, cs, cs.ap[0:n], sn, sn.ap[0:n], t1, t2, qk, qk.ap[0:n])
        self.proj(b1, n, [self.xsT.tk], self.xsT.ap, L["WV"])
        self.cp("act", vr, vr.ap[0:n], b1, b1.ap[0:n])
        self.proj(b2, n, [self.xsT.tk], self.xsT.ap, L["WG"])
        self.gate(b2, n, th, gr, gr.ap[0:n])
        btv = bt.ap.bitcast(BF16)
        for c in range(4):
            self.tr(bt, btv[:, c * n:(c + 1) * n], qk, qk.ap[0:n, c * 128:(c + 1) * 128], self.identb)
        tv4 = btv[:, 0:4 * n].rearrange("p (c t) -> p c t", c=4)
        self.cp("act", QKsT, QKsT.ap, bt, tv4)
        self.cp("dve", QZs, QZs.ap[0:64, 0:4:2, :], bt, tv4[0:64, 0:2, :])
        self.cp("dve", QZs, QZs.ap[64:128, 1:4:2, :], bt, tv4[64:128, 0:2, :])
        for b in range(4):
            self.cp("pool", QZB, QZB.ap[:, b, :, 4 * b:4 * b + 4], QZs, QZs.ap[:, :, 4 * b:4 * b + 4])
        for h in range(4):
            self.mm(bs, bs.ap[0:n, h * n:(h + 1) * n], QKsT, QKsT.ap[:, 2 + h // 2, :], QZs, QZs.ap[:, h, :])
        self.tt("dve", STs, STs.ap, bs, bs.ap[0:n, 0:4 * n].rearrange("p (h i) -> p h i", h=4), rmask,
                rmask.ap.unsqueeze(1).broadcast_to([n, 4, n]), ALU.mult)
        for h in range(4):
            o_ap = bo.ap[0:n, h * 128:(h + 1) * 128]
            self.mm(bo, o_ap, STs, STs.ap[:, h, :], vr, vr.ap[0:n, h * 128:(h + 1) * 128], start=True, stop=False)
            for b in range(4):
                self.mm(bo, o_ap, QZB, QZB.ap[:, b, h, :], Rsb, Rsb.ap[:, b, h, :], start=False, stop=(b == 3))
        for b in range(4):
            self.ts("dve", KZ, KZ.ap[:, b, :], qk, qk.ap[0:n, 256:512], bmask.ap[:, b:b + 1], ALU.mult, extra_reads=[bmask])
        kbanks = [PB[6], PB[7]]
        i = 0
        for b in range(4):
            for c in range(2):
                kb_ = kbanks[i % 2]
                i += 1
                self.mm(kb_, kb_.ap, KZ, KZ.ap[:, b, c * 128:(c + 1) * 128], vr, vr.ap[0:n])
                for hh in range(2):
                    h = 2 * c + hh
                    rows = slice(hh * 64, hh * 64 + 64)
                    self.ts("dve", Rs, Rs.ap[rows, b, h, :], Rs, Rs.ap[rows, b, h, :], g4[h], ALU.mult)
                    self.stt(Rs, Rs.ap[rows, b, h, :], kb_, kb_.ap[rows, h * 128:(h + 1) * 128], g4[h], ALU.mult,
                             Rs, Rs.ap[rows, b, h, :], ALU.add)
        for b in range(4):
            for h in range(4):
                rows = slice((h % 2) * 64, (h % 2) * 64 + 64)
                self.dma_out("sp", dout["rets"][b, h], Rs, Rs.ap[rows, b, h, :])
        L["headnorm_gate"](bo, n, gr, mixrt)
        for c in range(4):
            self.tr(bt, btv[:, c * n:(c + 1) * n], mixrt, mixrt.ap[0:n, c * 128:(c + 1) * 128], self.identb)
        self.cp("act", self.mixsT, self.mixsT.ap[:, 0:4, :], bt, tv4)

    def phase_M(self):
        S = self.S
        PB = self.PB
        din, dco, dout = self.din, self.dco, self.dout
        n = NS
        self.arena_reset()
        A = self.arena
        mkT, mvaug = self.mkT, self.mvaug
        mqT = [A("mqT", (128, 4, 512), BF16) for _ in range(2)]
        PTm = [A("PTm", (128, 4, 2, 512), BF16) for _ in range(2)]
        thm = [A("thm", (128, 512), F32) for _ in range(2)]
        Gm = [A("Gm", (128, 512), F32) for _ in range(2)]
        rl = [A("rl", (128, 4), F32) for _ in range(2)]
        mixMt = [A("mixMt", (128, 512), BF16) for _ in range(2)]
        WMQ, WMG = 1, 2
        rot = [0]

        def nb():
            rot[0] = (rot[0] + 1) % 2
            return PB[rot[0]]

        w2 = self.W[WMQ]
        OBs = [[PB[4], PB[5]], [PB[2], PB[3]]]
        OB = OBs[0]
        bt = PB[6]
        bg = PB[7]
        btv = bt.ap.bitcast(BF16)
        def m_tail(t):
            mixmt = mixMt[t % 2]
            for c in range(4):
                self.tr(bt, btv[:, c * 128:(c + 1) * 128], mixmt, mixmt.ap[:, c * 128:(c + 1) * 128], self.identb)
            self.cp("act", self.mixM[t], self.mixM[t].ap, bt, btv[:, 0:512].rearrange("p (c t) -> p c t", c=4))

        for nn in range(4):
            gsl = slice(nn * 512, (nn + 1) * 512)
            mq = mqT[nn % 2]
            pt = PTm[nn % 2]
            for h in range(4):
                bank = nb()
                for k in range(8):
                    S.op("pe", (lambda e, k=k, h=h, bank=bank, gsl=gsl: e.matmul(
                        bank.ap, lhsT=w2.ap[:, k, h * 128:(h + 1) * 128], rhs=self.xT_t[:, k, gsl],
                        start=(k == 0), stop=(k == 7))), reads=self.xT_all + [w2.tk], writes=[bank.tk])
                self.cp("act", mq, mq.ap[:, h, :], bank, bank.ap)
            for h in range(4):
                for mt in range(2):
                    bank = nb()
                    self.mm(bank, bank.ap, mkT, mkT.ap[:, h, mt * 128:(mt + 1) * 128], mq, mq.ap[:, h, :])
                    self.act(pt, pt.ap[:, h, mt, :], bank, bank.ap, AF.Exp, scale=float(128.0 ** -0.5))
            for tq in range(4):
                t = 4 * nn + tq
                xt = self.xT[t]
                th, gm, r, mixmt = thm[t % 2], Gm[t % 2], rl[t % 2], mixMt[t % 2]
                OB = OBs[t % 2]
                self.proj(bg, 128, [xt.tk], xt.ap, WMG)
                self.gate(bg, 128, th, gm, gm.ap)
                for h in range(4):
                    ob = OB[h // 2]
                    o_ap = ob.ap[:, (h % 2) * 129:(h % 2) * 129 + 129]
                    for mt in range(2):
                        self.mm(ob, o_ap, pt, pt.ap[:, h, mt, tq * 128:(tq + 1) * 128], mvaug, mvaug.ap[:, mt, h, :],
                                start=(mt == 0), stop=(mt == 1))
                for j in range(2):
                    ov = OB[j].ap[:, 0:258].rearrange("p (h f) -> p h f", h=2)
                    S.op("dve", (lambda e, j=j, ov=ov, r=r: e.reciprocal(out=r.ap[:, 2 * j:2 * j + 2], in_=ov[:, :, 128])),
                         reads=[OB[j].tk], writes=[r.tk])
                self.ts("dve", r, r.ap, r, r.ap, 0.5, ALU.mult)
                for h in range(4):
                    hs = slice(h * 128, (h + 1) * 128)
                    ob = OB[h // 2]
                    self.stt(mixmt, mixmt.ap[:, hs], ob, ob.ap[:, (h % 2) * 129:(h % 2) * 129 + 128], r.ap[:, h:h + 1], ALU.mult,
                             gm, gm.ap[:, hs], ALU.mult, extra_reads=[r])
                if t > 0:
                    m_tail(t - 1)
            if nn == 1:
                self.sample_mem(locals())
        m_tail(NT - 1)

    def sample_mem(self, L):
        S = self.S
        PB = self.PB
        din = self.din
        n = NS
        A = self.arena
        WMQ, WMG = L["WMQ"], L["WMG"]
        nb = L["nb"]
        th, gm, rl, mixm = L["thm"][0], L["Gm"][0], L["rl"][0], L["mixMt"][0]
        mqs = A("mqs", (n, 512), BF16)
        mkld = [A("mkld", (128, 2, 512), BF16) for _ in range(2)]
        mkTs = A("mkTs", (128, 4, 4, 256), BF16)
        mvaugs = A("mvaugs", (128, 4, 2, 4, 129), BF16)
        mqsT = A("mqsT", (128, 4, n), BF16)
        PTms = A("PTms", (128, 4, 4, 2, n), BF16)
        bt = L["bt"]
        btv = L["btv"]
        bg = L["bg"]
        self.memset("pool", mvaugs, mvaugs.ap[:, :, :, :, 128:129], 1.0)
        self.memset("pool", PTms, PTms.ap, 0.0)
        for b in range(4):
            ml = mkld[b % 2]
            self.dma_in("pool", ml, ml.ap, din["cmk"][b].rearrange("(t p) c -> p t c", p=128))
            for mt in range(2):
                S.dma("pool", (lambda e, b=b, mt=mt: e.dma_start(
                    out=mvaugs.ap[:, b, mt, :, 0:128],
                    in_=din["cmv"][b][mt * 128:(mt + 1) * 128, :].rearrange("p (h f) -> p h f", h=4))),
                    reads=[], writes=[mvaugs.tk])
            for mt in range(2):
                bank = nb()
                bv = bank.ap.bitcast(BF16)
                for h in range(4):
                    self.tr(bank, bv[:, h * 128:(h + 1) * 128], ml, ml.ap[:, mt, h * 128:(h + 1) * 128], self.identb)
                self.cp("act", mkTs, mkTs.ap[:, b, :, mt * 128:(mt + 1) * 128], bank,
                        bv[:, 0:512].rearrange("p (h m) -> p h m", h=4))
        self.proj(bg, n, [self.xsT.tk], self.xsT.ap, WMQ)
        self.cp("act", mqs, mqs.ap, bg, bg.ap[0:n])
        for c in range(4):
            self.tr(bt, btv[:, c * n:(c + 1) * n], mqs, mqs.ap[:, c * 128:(c + 1) * 128], self.identb)
        tv4 = btv[:, 0:4 * n].rearrange("p (c t) -> p c t", c=4)
        self.cp("act", mqsT, mqsT.ap, bt, tv4)
        self.proj(bg, n, [self.xsT.tk], self.xsT.ap, WMG)
        self.gate(bg, n, th, gm, gm.ap[0:n])
        bsc = nb()
        for b in range(4):
            for h in range(4):
                for mt in range(2):
                    col = ((b * 4 + h) * 2 + mt) * 4
                    self.mm(bsc, bsc.ap[:, col:col + 4], mkTs, mkTs.ap[:, b, h, mt * 128:(mt + 1) * 128],
                            mqsT, mqsT.ap[:, h, 4 * b:4 * b + 4])
        for b in range(4):
            self.act(PTms, PTms.ap[:, b, :, :, 4 * b:4 * b + 4], bsc,
                     bsc.ap[:, b * 32:(b + 1) * 32].rearrange("p (h m i) -> p h m i", h=4, m=2), AF.Exp,
                     scale=float(128.0 ** -0.5))
        OB = L["OB"]
        for h in range(4):
            ob = OB[h // 2]
            o_ap = ob.ap[0:n, (h % 2) * 129:(h % 2) * 129 + 129]
            k = 0
            for b in range(4):
                for mt in range(2):
                    self.mm(ob, o_ap, PTms, PTms.ap[:, b, h, mt, :], mvaugs, mvaugs.ap[:, b, mt, h, :],
                            start=(k == 0), stop=(k == 7))
                    k += 1
        for j in range(2):
            ov = OB[j].ap[0:n, 0:258].rearrange("p (h f) -> p h f", h=2)
            S.op("dve", (lambda e, j=j, ov=ov: e.reciprocal(out=rl.ap[0:n, 2 * j:2 * j + 2], in_=ov[:, :, 128])),
                 reads=[OB[j].tk], writes=[rl.tk])
        self.ts("dve", rl, rl.ap[0:n], rl, rl.ap[0:n], 0.5, ALU.mult)
        for h in range(4):
            hs = slice(h * 128, (h + 1) * 128)
            ob = OB[h // 2]
            self.stt(mixm, mixm.ap[0:n, hs], ob, ob.ap[0:n, (h % 2) * 129:(h % 2) * 129 + 128], rl.ap[0:n, h:h + 1], ALU.mult,
                     gm, gm.ap[0:n, hs], ALU.mult, extra_reads=[rl])
        for c in range(4):
            self.tr(bt, btv[:, c * n:(c + 1) * n], mixm, mixm.ap[0:n, c * 128:(c + 1) * 128], self.identb)
        self.cp("act", self.mixsT, self.mixsT.ap[:, 8:12, :], bt, tv4)

    def phase_F(self):
        S = self.S
        PB = self.PB
        din, dco, dout = self.din, self.dco, self.dout
        n = NS
        self.arena_reset()
        A = self.arena
        Gt = A("Gt", (128, D), F32)
        Bt = A("Bt", (128, D), F32)
        mhalf = A("mhalf", (128, 4), F32)
        self.dma_in("sp", Gt, Gt.ap, din["ln_g"][0:1, :].broadcast_to([128, D]))
        self.dma_in("sp", Bt, Bt.ap, din["ln_b"][0:1, :].broadcast_to([128, D]))
        self.dma_in("sp", mhalf, mhalf.ap, dco["mhalf"])
        xf = [A("xf", (128, D), F32) for _ in range(3)]
        zz = [A("zz", (128, D), F32) for _ in range(3)]
        st = [A("st", (128, 2, 6), F32) for _ in range(2)]
        mv = [A("mv", (128, 2), F32) for _ in range(2)]
        ve = [A("ve", (128, 1), F32) for _ in range(2)]
        rstd = [A("rstd", (128, 1), F32) for _ in range(2)]
        nmr = [A("nmr", (128, 1), F32) for _ in range(2)]
        gen = self.sample_swa()
        next(gen)

        def pull(k):
            for _ in range(k):
                try:
                    next(gen)
                except StopIteration:
                    return

        for t in range(min(2, NT)):
            self.dma_in("sp", xf[t % 3], xf[t % 3].ap, din["x"][t * 128:(t + 1) * 128, :])
        for t in range(NT):
            tsl = slice(t * 128, (t + 1) * 128)
            x_, z_ = xf[t % 3], zz[t % 3]
            if t + 2 < NT:
                self.dma_in("sp", xf[(t + 2) % 3], xf[(t + 2) % 3].ap, din["x"][(t + 2) * 128:(t + 3) * 128, :])
            hb = [PB[2 * (t % 2)], PB[2 * (t % 2) + 1]]
            for half in range(2):
                pull(1)
                for ec in range(12):
                    if ec < 4:
                        mb, map_ = self.mixR[t], self.A_t[:, ec, tsl]
                    elif ec < 8:
                        mb, map_ = self.mixS, self.mixS_t[:, ec - 4, tsl]
                    else:
                        mb, map_ = self.mixM[t], self.A_t[:, 4 + ec - 8, tsl]
                    wb, wv = self.Wout[ec]
                    self.mm(hb[half], hb[half].ap, mb, map_, wb, wv[:, half * 512:(half + 1) * 512],
                            start=(ec == 0), stop=(ec == 11))
            for half in range(2):
                hs = slice(half * 512, (half + 1) * 512)
                self.stt(z_, z_.ap[:, hs], x_, x_.ap[:, hs], ALPHA, ALU.mult, hb[half], hb[half].ap, ALU.add)
            pull(1)
            self.layernorm(z_, 128, st[t % 2], mv[t % 2], ve[t % 2], rstd[t % 2], nmr[t % 2], mhalf, Gt, Bt, stage=1)
            pull(1)
            if t > 0:
                zp = zz[(t - 1) % 3]
                self.layernorm(zp, 128, None, None, None, None, None, mhalf, Gt, Bt, stage=2)
                self.dma_out("sp", dout["y"][(t - 1) * 128:t * 128, :], zp, zp.ap)
            pull(1)
        zp = zz[(NT - 1) % 3]
        self.layernorm(zp, 128, None, None, None, None, None, mhalf, Gt, Bt, stage=2)
        self.dma_out("sp", dout["y"][(NT - 1) * 128:NT * 128, :], zp, zp.ap)
        pull(1000)
        st, mv, ve, rstd, nmr = st[0], mv[0], ve[0], rstd[0], nmr[0]
        x_, z_ = xf[0], zz[0]
        self.dma_in("sp", x_, x_.ap[0:n], din["xs"])
        hb = [PB[0], PB[1]]
        for half in range(2):
            for ec in range(12):
                wb, wv = self.Wout[ec]
                self.mm(hb[half], hb[half].ap[0:n], self.mixsT, self.mixsT.ap[:, ec, :], wb,
                        wv[:, half * 512:(half + 1) * 512], start=(ec == 0), stop=(ec == 11))
        for half in range(2):
            hs = slice(half * 512, (half + 1) * 512)
            self.stt(z_, z_.ap[0:n, hs], x_, x_.ap[0:n, hs], ALPHA, ALU.mult, hb[half], hb[half].ap[0:n], ALU.add)
        self.layernorm(z_, n, st, mv, ve, rstd, nmr, mhalf, Gt, Bt)
        self.dma_out("sp", dout["ys"], z_, z_.ap[0:n])

    def layernorm(self, z_, n, st, mv, ve, rstd, nmr, mhalf, Gt, Bt, stage=0):
        S = self.S
        if stage in (0, 1):
            for half in range(2):
                hs = slice(half * 512, (half + 1) * 512)
                S.op("dve", (lambda e, half=half, hs=hs: e.bn_stats(out=st.ap[0:n, half, :], in_=z_.ap[0:n, hs])),
                     reads=[z_.tk], writes=[st.tk])
            S.op("dve", lambda e: e.bn_aggr(out=mv.ap[0:n, :], in_=st.ap[0:n, :, :].rearrange("p a b -> p (a b)")),
                 reads=[st.tk], writes=[mv.tk])
            self.ts("dve", ve, ve.ap[0:n], mv, mv.ap[0:n, 1:2], EPS, ALU.add)
            self.tt("pool", rstd, rstd.ap[0:n], ve, ve.ap[0:n], mhalf, mhalf.ap[0:n, 0:1], ALU.pow)
            self.ts("dve", nmr, nmr.ap[0:n], mv, mv.ap[0:n, 0:1], -1.0, ALU.mult, rstd.ap[0:n, 0:1], ALU.mult, extra_reads=[rstd])
            self.ts("dve", z_, z_.ap[0:n], z_, z_.ap[0:n], rstd.ap[0:n, 0:1], ALU.mult, nmr.ap[0:n, 0:1], ALU.add,
                    extra_reads=[rstd, nmr])
        if stage in (0, 2):
            self.tt("dve", z_, z_.ap[0:n], z_, z_.ap[0:n], Gt, Gt.ap[0:n], ALU.mult)
            self.tt("dve", z_, z_.ap[0:n], z_, z_.ap[0:n], Bt, Bt.ap[0:n], ALU.add)

    def sample_swa(self):
        S = self.S
        PB = self.PB
        din, dco = self.din, self.dco
        n = NS
        A = self.arena
        sqs, sks, Gss, Vnew = self.sqs, self.sks, self.Gss, self.Vnew
        smask = A("smask", (128, 9, 4), BF16)
        smaskn = A("smaskn", (n, 4, 4), BF16)
        ones = A("ones", (128, 2), BF16)
        sqsT = A("sqsT", (128, 4, n), BF16)
        sksT = A("sksT", (128, 4, n), BF16)
        GssT = A("GssT", (128, 4, n), BF16)
        Qbd = A("Qbd", (128, 4, 4, 8), BF16)
        Kc = A("Kc", (128, 9, 512), BF16)
        KcT = A("KcT", (128, 9, 4, 128), BF16)
        Vc = A("Vc", (128, 9, 512), BF16)
        PTx = [A("PTx", (128, 10, 8, 4), BF16) for _ in range(2)]
        rls = A("rlx", (4, 8), F32)
        Onb_ap = KcT.ap[0:4, 0, :, :].rearrange("p c k -> p (c k)")
        bt = PB[4]
        btv = bt.ap.bitcast(BF16)
        bsx = PB[5]
        UB = [PB[6], PB[7]]

        def issue_k(b):
            ck = din["ck"][b]
            self.dma_in("pool", Kc, Kc.ap[:, 0:4, :], ck.rearrange("(m s) c -> m s c", s=16)[:, 0:4, :], parallel=True)
            self.dma_in("pool", Kc, Kc.ap[:, 4:8, :], ck[1536:2048, :].rearrange("(m s) c -> m s c", s=4), parallel=True)
            self.dma_in("pool", Kc, Kc.ap[:, 8, :], ck[1920:2048, :], parallel=True)

        def issue_v(b):
            cv = din["cv"][b]
            self.dma_in("pool", Vc, Vc.ap[:, 0:4, :], cv.rearrange("(m s) c -> m s c", s=16)[:, 0:4, :], parallel=True)
            self.dma_in("pool", Vc, Vc.ap[:, 4:8, :], cv[1536:2048, :].rearrange("(m s) c -> m s c", s=4), parallel=True)
            self.dma_in("pool", Vc, Vc.ap[:, 8, :], cv[1920:2048, :], parallel=True)

        self.dma_in("pool", smask, smask.ap, dco["smask"])
        self.dma_in("pool", smaskn, smaskn.ap, dco["smaskn"])
        self.memset("pool", ones, ones.ap, 1.0)
        self.memset("pool", Qbd, Qbd.ap, 0.0)
        for p_ in PTx:
            self.memset("pool", p_, p_.ap, 0.0)
        issue_k(0)
        issue_v(0)
        yield
        tv4 = btv[:, 0:4 * n].rearrange("p (c t) -> p c t", c=4)
        for src, dst in ((sqs, sqsT), (sks, sksT), (Gss, GssT)):
            for c in range(4):
                self.tr(bt, btv[:, c * n:(c + 1) * n], src, src.ap[:, c * 128:(c + 1) * 128], self.identb)
            self.cp("act", dst, dst.ap, bt, tv4)
        self.cp("pool", Qbd, Qbd.ap[0:64, :, :, 0:4], sqsT, sqsT.ap[0:64, :, :].rearrange("p c (b i) -> p c b i", b=4))
        self.cp("pool", Qbd, Qbd.ap[64:128, :, :, 4:8], sqsT, sqsT.ap[64:128, :, :].rearrange("p c (b i) -> p c b i", b=4))
        yield
        for b in range(4):
            pt = PTx[b % 2]
            for tl in range(9):
                for c in range(4):
                    self.tr(bt, btv[:, c * 128:(c + 1) * 128], Kc, Kc.ap[:, tl, c * 128:(c + 1) * 128], self.identb)
                self.cp("act", KcT, KcT.ap[:, tl, :, :], bt,
                        btv[:, 0:512].rearrange("p (c k) -> p c k", c=4))
                yield
            if b + 1 < 4:
                issue_k(b + 1)
            for tl in range(9):
                for c in range(4):
                    self.mm(bsx, bsx.ap[:, tl * 32 + c * 8:tl * 32 + c * 8 + 8], KcT, KcT.ap[:, tl, c, :], Qbd, Qbd.ap[:, c, b, :])
            for c in range(4):
                self.mm(bsx, bsx.ap[0:n, 288 + c * 8:288 + c * 8 + 8], sksT, sksT.ap[:, c, :], Qbd, Qbd.ap[:, c, b, :])
            yield
            self.act(pt, pt.ap[:, 0:9, :, :], bsx, bsx.ap[:, 0:288].rearrange("p (t h i) -> p t h i", t=9, h=8), AF.Exp, scale=0.125)
            self.act(pt, pt.ap[0:n, 9, :, :], bsx, bsx.ap[0:n, 288:320].rearrange("p (h i) -> p h i", h=8), AF.Exp, scale=0.125)
            self.tt("pool", pt, pt.ap[:, 0:9, :, :], pt, pt.ap[:, 0:9, :, :], smask,
                    smask.ap.unsqueeze(2).broadcast_to([128, 9, 8, 4]), ALU.mult)
            self.tt("pool", pt, pt.ap[0:n, 9, :, :], pt, pt.ap[0:n, 9, :, :], smaskn,
                    smaskn.ap[:, b, :].unsqueeze(1).broadcast_to([n, 8, 4]), ALU.mult)
            yield
            for h in range(8):
                ub = UB[h // 4]
                o_ap = ub.ap[0:4, (h % 4) * 64:(h % 4) * 64 + 64]
                for tl in range(9):
                    self.mm(ub, o_ap, pt, pt.ap[:, tl, h, :], Vc, Vc.ap[:, tl, h * 64:(h + 1) * 64], start=(tl == 0), stop=False)
                self.mm(ub, o_ap, pt, pt.ap[0:n, 9, h, :], Vnew, Vnew.ap[:, h, 0:64], start=False, stop=True)
                l_ap = bsx.ap[0:4, 320 + h:321 + h]
                for tl in range(9):
                    self.mm(bsx, l_ap, pt, pt.ap[:, tl, h, :], ones, ones.ap[:, 0:1], start=(tl == 0), stop=False)
                self.mm(bsx, l_ap, pt, pt.ap[0:n, 9, h, :], ones, ones.ap[0:n, 0:1], start=False, stop=True)
                if h % 2 == 1:
                    yield
            if b + 1 < 4:
                issue_v(b + 1)
            S.op("dve", lambda e: e.reciprocal(out=rls.ap, in_=bsx.ap[0:4, 320:328]), reads=[bsx.tk], writes=[rls.tk])
            for j in range(2):
                uv = UB[j].ap[0:4, 0:256].rearrange("p (h f) -> p h f", h=4)
                self.tt("dve", KcT, Onb_ap[:, 256 * j:256 * (j + 1)].rearrange("p (h f) -> p h f", h=4), UB[j], uv,
                        rls, rls.ap[:, 4 * j:4 * j + 4].unsqueeze(2).broadcast_to([4, 4, 64]), ALU.mult)
            for c in range(4):
                self.tr(bt, btv[:, c * 4:(c + 1) * 4], KcT, Onb_ap[:, c * 128:(c + 1) * 128], self.identb)
            self.tt("dve", self.mixsT, self.mixsT.ap[:, 4:8, 4 * b:4 * b + 4], bt, btv[:, 0:16].rearrange("p (c i) -> p c i", c=4),
                    GssT, GssT.ap[:, :, 4 * b:4 * b + 4], ALU.mult)
            yield


_CACHE = {}


def _get_prog(phases):
    key = tuple(phases)
    if key not in _CACHE:
        p = Prog()
        _CACHE[key] = p.build(phases)
    return _CACHE[key]


PHASES = ("S", "R", "M", "F", "X")


def kernel(x_prompt, x_sample, state_ret, cache_swa_k, cache_swa_v, cache_mem_k, cache_mem_v,
           mem_prompt, w_in, w_mem_kv, w_out, ln_gain, ln_bias):
    f = lambda a: np.ascontiguousarray(np.asarray(a, dtype=np.float32))
    consts = {"c_" + k: v for k, v in _consts().items()}
    nc = _get_prog(PHASES)
    in_maps = []
    for c in range(NCORES):
        sb = slice(4 * c, 4 * c + 4)
        m = {
            "x": f(x_prompt[c]), "memx": f(mem_prompt[c]), "w_in": f(w_in[0]), "w_mem": f(w_mem_kv[0]),
            "w_out": f(w_out[0]), "ln_g": f(ln_gain), "ln_b": f(ln_bias),
            "xs": f(np.asarray(x_sample)[sb].reshape(NS, D)),
            "state": f(np.asarray(state_ret)[0, sb]),
            "ck": f(np.asarray(cache_swa_k)[0, sb].reshape(4, 2048, 512)),
            "cv": f(np.asarray(cache_swa_v)[0, sb].reshape(4, 2048, 512)),
            "cmk": f(np.asarray(cache_mem_k)[0, sb].reshape(4, 256, 512)),
            "cmv": f(np.asarray(cache_mem_v)[0, sb].reshape(4, 256, 512)),
        }
        m.update(consts)
        in_maps.append(m)
    res = run_bass_kernel_spmd(nc, in_maps, core_ids=list(range(NCORES)))
    R = res.results
    cat = lambda k: np.stack([np.asarray(R[c][k]) for c in range(NCORES)], axis=0)
    y = cat("y")
    ys = cat("ys").reshape(32, 4, D)
    retp = cat("retp")[None]
    rets = cat("rets").reshape(32, 4, 64, 128)[None]
    kp = cat("kp").reshape(8, SEQ, 8, 64)[None]
    vp = cat("vp").reshape(8, SEQ, 8, 64)[None]
    ks = cat("ks").reshape(32, 4, 8, 64)[None]
    vs = cat("vs").reshape(32, 4, 8, 64)[None]
    mkp = cat("mkp").reshape(8, 256, 4, 128)[None]
    mvp = cat("mvp").reshape(8, 256, 4, 128)[None]
    return (y, ys, retp, rets, kp, vp, ks, vs, mkp, mvp)
```

```python
from contextlib import ExitStack

import numpy as np
import concourse.bass as bass
import concourse.mybir as mybir
from concourse.bass_utils import run_bass_kernel_spmd

F32 = mybir.dt.float32
BF16 = mybir.dt.bfloat16
AF = mybir.ActivationFunctionType
ALU = mybir.AluOpType
AX = mybir.AxisListType

NCORES = 8
D = 1024
SEQ = 2048
NT = 16
DIN = 4608
DMIX = 1536
NS = 16
PAST = 8192
ALPHA = 2.0 ** 0.25
EPS = 1e-5
C_RQ, C_RK, C_RV, C_RG, C_SQ, C_SK, C_SV, C_SG, C_MQ, C_MG = 0, 256, 512, 1024, 1536, 2048, 2560, 3072, 3584, 4096

ENGS = ("pe", "act", "dve", "pool", "sp")
SAME_ENGINE_SYNC = True
SAME_ENGINE_WAR = True


class Tk:
    __slots__ = ("name", "w", "r", "excl", "wg")

    def __init__(self, name):
        self.name = name
        self.w = None
        self.r = {}
        self.wg = []
        self.excl = False


class Buf:
    __slots__ = ("ap", "tk")

    def __init__(self, ap, name):
        self.ap = ap
        self.tk = Tk(name)


class Sched:
    def __init__(self, n_dma_sems):
        self.q = {e: [] for e in ENGS}
        self.known = {e: {} for e in ENGS}
        self.dma_val = [0] * n_dma_sems
        self.rr = 0
        self.rrp = 0
        self.rrq = 0
        self.needed = {e: set() for e in ENGS}

    def _collect(self, eng, reads, writes, par=False):
        deps = []
        for t in reads:
            if t.w is not None:
                deps.append((t.w, True))
            for d in t.wg:
                deps.append((d, True))
            if t.excl:
                for d in t.r.values():
                    if not (d[0] == "e" and d[1] == eng):
                        deps.append((d, True))
        for t in writes:
            if t.w is not None and not (par and t.w[0] == "d" and not t.r):
                deps.append((t.w, True))
                for d in t.wg:
                    deps.append((d, True))
            for d in t.r.values():
                deps.append((d, False))
        kn = self.known[eng]
        best = {}
        for d, is_raw in deps:
            if d[0] == "e" and d[1] == eng:
                if eng == "pe" or not (SAME_ENGINE_SYNC and (is_raw or SAME_ENGINE_WAR)):
                    continue
            key = (d[0], d[1])
            if kn.get(key, -1) >= d[2]:
                continue
            if key not in best or best[key][2] < d[2]:
                best[key] = d
        waits = list(best.values())
        for d in waits:
            kn[(d[0], d[1])] = d[2]
            if d[0] == "e":
                self.needed[d[1]].add(d[2])
        return waits

    def op(self, eng, fn, reads=(), writes=()):
        idx = len(self.q[eng])
        waits = self._collect(eng, reads, writes)
        self.q[eng].append({"fn": fn, "waits": waits, "dma": None})
        me = ("e", eng, idx)
        for t in reads:
            t.r[("e", eng)] = me
        for t in writes:
            t.w = me
            t.wg = []
            t.r = {}
        return idx

    NPRE = 12

    def dma(self, eng, fn, reads=(), writes=(), prefetch=False, parallel=False):
        nn = len(self.dma_val) - self.NPRE
        half = nn // 2
        if prefetch:
            si = nn + self.rrp
            self.rrp = (self.rrp + 1) % self.NPRE
        elif eng == "pool":
            si = half + self.rrq
            self.rrq = (self.rrq + 1) % (nn - half)
        else:
            si = self.rr
            self.rr = (self.rr + 1) % half
        prev = self.dma_val[si]
        new = prev + 16
        self.dma_val[si] = new
        waits = self._collect(eng, reads, writes, par=parallel)
        if prev > 0:
            kn = self.known[eng]
            if kn.get(("d", si), -1) < prev:
                kn[("d", si)] = prev
                waits.append(("d", si, prev))
        self.q[eng].append({"fn": fn, "waits": waits, "dma": si})
        me = ("d", si, new)
        for t in reads:
            t.r[("d", si)] = me
        for t in writes:
            if parallel and t.w is not None and t.w[0] == "d" and not t.r:
                t.wg.append(t.w)
            else:
                t.wg = []
            t.w = me
            t.r = {}

    def barrier(self, final=False):
        last = {}
        for e in ENGS:
            for i in range(len(self.q[e]) - 1, -1, -1):
                ent = self.q[e][i]
                if ent["dma"] is None and ent["fn"] is not None:
                    last[e] = i
                    break
        for e in ENGS:
            waits = []
            kn = self.known[e]
            for e2, idx in last.items():
                if e2 == e:
                    continue
                if kn.get(("e", e2), -1) < idx:
                    kn[("e", e2)] = idx
                    waits.append(("e", e2, idx))
                    self.needed[e2].add(idx)
            for si, v in enumerate(self.dma_val):
                if si >= len(self.dma_val) - self.NPRE and not final:
                    continue
                if v > 0 and kn.get(("d", si), -1) < v:
                    kn[("d", si)] = v
                    waits.append(("d", si, v))
            self.q[e].append({"fn": None, "waits": waits, "dma": None})

    def emit(self, block, eng_sems, dma_sems):
        rank = {}
        for e in ENGS:
            for r, idx in enumerate(sorted(self.needed[e])):
                rank[(e, idx)] = r + 1

        def run(ename, eng):
            for idx, ent in enumerate(self.q[ename]):
                for d in ent["waits"]:
                    if d[0] == "e":
                        eng.wait_ge(eng_sems[d[1]], rank[(d[1], d[2])])
                    else:
                        eng.wait_ge(dma_sems[d[1]], d[2])
                if ent["fn"] is None:
                    continue
                ins = ent["fn"](eng)
                if ent["dma"] is not None:
                    ins.then_inc(dma_sems[ent["dma"]], 16)
                elif (ename, idx) in rank:
                    ins.then_inc(eng_sems[ename], 1)

        block.tensor(lambda eng: run("pe", eng))
        block.scalar(lambda eng: run("act", eng))
        block.vector(lambda eng: run("dve", eng))
        block.gpsimd(lambda eng: run("pool", eng))
        block.sync(lambda eng: run("sp", eng))


def _consts():
    f32 = np.float32
    c = {}
    c["ident"] = np.eye(128, dtype=f32)
    inv = (f32(10000.0) ** (-(np.arange(32, dtype=f32) * f32(2.0) / f32(64)))).astype(f32)
    pos = np.arange(SEQ, dtype=f32)
    ang = (pos[:, None] * inv[None, :]).astype(f32).astype(np.float64)
    cs = np.cos(ang).reshape(NT, 128, 32).transpose(1, 0, 2)
    sn = np.sin(ang).reshape(NT, 128, 32).transpose(1, 0, 2)
    c["cosp"] = np.ascontiguousarray(cs).astype(f32)
    c["sinp"] = np.ascontiguousarray(np.stack([-sn, sn], axis=2)).astype(f32)
    poss = (PAST + np.arange(4)).astype(f32)
    angs = (poss[:, None] * inv[None, :]).astype(f32).astype(np.float64)
    c["coss"] = np.tile(np.cos(angs), (4, 1)).astype(f32)
    sns = np.tile(np.sin(angs), (4, 1))
    c["sins"] = np.stack([-sns, sns], axis=1).astype(f32)
    lg = np.log1p(-np.exp2(-5.0 - np.arange(4, dtype=np.float64)))
    p1 = np.arange(1, 129, dtype=np.float64)[:, None]
    dq = np.exp(p1 * lg[None, :])
    dk = np.exp(-p1 * lg[None, :]) * 0.125
    c["dqk"] = np.concatenate([dq, dk], axis=1).astype(f32)
    g128 = np.exp(128.0 * lg)
    G = np.zeros((128, 4, 128), dtype=np.float64)
    for h in range(4):
        G[(h % 2) * 64:(h % 2) * 64 + 64, h, :] = g128[h]
    c["g128"] = G.reshape(128, 512).astype(f32)
    jj = np.arange(128)[:, None]
    ii = np.arange(128)[None, :]
    c["cmask"] = (ii >= jj).astype(f32)
    c["swamask"] = np.concatenate([(ii >= jj), (ii <= jj)], axis=1).astype(f32)
    sel = np.zeros((8, 4, 128), dtype=f32)
    for h in range(8):
        sel[h, h // 2, (h % 2) * 64:(h % 2) * 64 + 64] = 1.0
    c["sel"] = sel.reshape(8, 512)
    c["mhalf"] = np.full((128, 4), -0.5, dtype=f32)
    i4 = np.tile(np.arange(4, dtype=np.float64), 4)[:, None]
    dqs = np.exp((i4 + 1.0) * lg[None, :])
    dks = np.exp(-(i4 + 1.0) * lg[None, :]) * 0.125
    c["dqks"] = np.concatenate([dqs, dks], axis=1).astype(f32)
    r16 = np.arange(16)
    bj, ij = r16 // 4, r16 % 4
    c["rmask"] = ((bj[:, None] == bj[None, :]) & (ij[None, :] >= ij[:, None])).astype(f32)
    c["bmask"] = (bj[:, None] == np.arange(4)[None, :]).astype(f32)
    sm = np.zeros((128, 9, 4), dtype=f32)
    for i in range(4):
        sm[:, i, i] = 1.0
        sm[:, 4 + i, i] = 1.0
        sm[:, 8, i] = (np.arange(128) >= i).astype(f32)
    c["smask"] = sm
    smn = np.zeros((16, 4, 4), dtype=f32)
    for kk in range(16):
        for b in range(4):
            for i in range(4):
                if kk // 4 == b:
                    j = kk % 4
                    smn[kk, b, i] = (1.0 if j <= i else 0.0) + (2.0 if j == i else 0.0)
    c["smaskn"] = smn
    return c


CONST_SHAPES = {
    "ident": (128, 128), "cosp": (128, 16, 32), "sinp": (128, 16, 2, 32), "coss": (16, 32), "sins": (16, 2, 32),
    "dqk": (128, 8), "g128": (128, 512), "cmask": (128, 128), "swamask": (128, 256), "sel": (8, 512),
    "mhalf": (128, 4), "dqks": (16, 8), "rmask": (16, 16), "bmask": (16, 4), "smask": (128, 9, 4), "smaskn": (16, 4, 4),
}
IN_SHAPES = {
    "x": (SEQ, D), "memx": (256, D), "w_in": (D, DIN), "w_mem": (D, 1024), "w_out": (DMIX, D),
    "ln_g": (1, D), "ln_b": (1, D), "xs": (NS, D), "state": (4, 4, 64, 128),
    "ck": (4, 2048, 512), "cv": (4, 2048, 512), "cmk": (4, 256, 512), "cmv": (4, 256, 512),
}
OUT_SHAPES = {
    "y": (SEQ, D), "ys": (NS, D), "retp": (4, 64, 128), "rets": (4, 4, 64, 128),
    "kp": (SEQ, 512), "vp": (SEQ, 512), "ks": (NS, 512), "vs": (NS, 512),
    "mkp": (256, 512), "mvp": (256, 512),
}


def tok_slice(g, b, nblk=1):
    cnt = 128 * nblk
    if g == 1:
        start = 128 * b
    elif g == 4:
        r, n = divmod(b, 4)
        start = 512 * n + r
    else:
        start = b
    return slice(start, start + g * (cnt - 1) + 1, g)


class Prog:
    def __init__(self):
        self.nc = bass.Bass("TRN2", target_bir_lowering=False)
        nc = self.nc
        self.din = {k: nc.dram_tensor(k, list(s), F32, kind="ExternalInput").ap() for k, s in IN_SHAPES.items()}
        self.dco = {k: nc.dram_tensor("c_" + k, list(s), F32, kind="ExternalInput").ap() for k, s in CONST_SHAPES.items()}
        self.dout = {k: nc.dram_tensor(k, list(s), F32, kind="ExternalOutput").ap() for k, s in OUT_SHAPES.items()}
        self.vscr = nc.dram_tensor("vscr", [SEQ, 520], BF16, kind="Internal").ap()
        self.NDMA = 64
        self.S = Sched(self.NDMA)
        self.es = ExitStack()
        self.eng_sems = {e: self.es.enter_context(nc.semaphore("sem_" + e)) for e in ENGS}
        self.dma_sems = [self.es.enter_context(nc.semaphore("dsem%d" % i)) for i in range(self.NDMA)]
        self.uid = 0

    def sb(self, name, shape, dt):
        return self.es.enter_context(self.nc.sbuf_tensor(name, list(shape), dt))

    def buf(self, name, shape, dt):
        return Buf(self.sb(name, shape, dt)[:], name)

    def arena_reset(self):
        self.aoff = 0

    def arena(self, name, shape, dt):
        n = 1
        for s in shape[1:]:
            n *= s
        words = n if dt == F32 else (n + 1) // 2
        words = (words + 15) // 16 * 16
        assert self.aoff + words <= self.AW, (name, self.aoff, words, self.AW)
        v = self.arena_t[:, self.aoff:self.aoff + words]
        self.aoff += words
        if dt != F32:
            v = v.bitcast(dt)
        v = v[:, 0:n]
        if len(shape) > 2:
            names = " ".join("d%d" % i for i in range(len(shape) - 1))
            v = v.rearrange("p (%s) -> p %s" % (names, names), **{"d%d" % i: shape[i + 1] for i in range(len(shape) - 1)})
        if shape[0] < 128:
            v = v[0:shape[0]]
        self.uid += 1
        return Buf(v, "%s_%d" % (name, self.uid))

    def dma_in(self, q, dst, dst_ap, src_ap, parallel=False):
        self.S.dma(q, lambda e: e.dma_start(out=dst_ap, in_=src_ap), reads=[], writes=[dst.tk], parallel=parallel)

    def dma_out(self, q, dst_ap, src, src_ap):
        self.S.dma(q, lambda e: e.dma_start(out=dst_ap, in_=src_ap), reads=[src.tk], writes=[])

    def mm(self, out, out_ap, lhsT, lhsT_ap, rhs, rhs_ap, start=True, stop=True, extra_reads=()):
        self.S.op("pe", lambda e: e.matmul(out_ap, lhsT=lhsT_ap, rhs=rhs_ap, start=start, stop=stop),
                  reads=[lhsT.tk, rhs.tk] + list(extra_reads), writes=[out.tk])

    def tr(self, out, out_ap, in_, in_ap, ident):
        n = in_ap.shape[0]
        self.S.op("pe", lambda e: e.transpose(out=out_ap, in_=in_ap, identity=ident.ap[0:n, 0:n]),
                  reads=[in_.tk, ident.tk], writes=[out.tk])

    def act(self, out, out_ap, in_, in_ap, func, scale=1.0, bias=0.0, extra_reads=()):
        self.S.op("act", lambda e: e.activation(out=out_ap, in_=in_ap, func=func, scale=scale, bias=bias),
                  reads=[in_.tk] + [b.tk for b in extra_reads], writes=[out.tk])

    def tt(self, eng, out, out_ap, a, a_ap, b, b_ap, op, extra_reads=()):
        self.S.op(eng, lambda e: e.tensor_tensor(out=out_ap, in0=a_ap, in1=b_ap, op=op),
                  reads=[a.tk, b.tk] + list(extra_reads), writes=[out.tk])

    def ts(self, eng, out, out_ap, a, a_ap, s1, op0, s2=None, op1=None, extra_reads=()):
        if op1 is None:
            fn = lambda e: e.tensor_scalar(out=out_ap, in0=a_ap, scalar1=s1, scalar2=None, op0=op0)
        else:
            fn = lambda e: e.tensor_scalar(out=out_ap, in0=a_ap, scalar1=s1, scalar2=s2, op0=op0, op1=op1)
        self.S.op(eng, fn, reads=[a.tk] + [b.tk for b in extra_reads], writes=[out.tk])

    def stt(self, out, out_ap, a, a_ap, scalar, op0, b, b_ap, op1, extra_reads=()):
        self.S.op("dve", lambda e: e.scalar_tensor_tensor(out=out_ap, in0=a_ap, scalar=scalar, op0=op0, in1=b_ap, op1=op1),
                  reads=[a.tk, b.tk] + [x.tk for x in extra_reads], writes=[out.tk])

    def cp(self, eng, out, out_ap, in_, in_ap, extra_reads=()):
        if eng == "act":
            fn = lambda e: e.copy(out=out_ap, in_=in_ap)
        else:
            fn = lambda e: e.tensor_copy(out=out_ap, in_=in_ap)
        self.S.op(eng, fn, reads=[in_.tk] + list(extra_reads), writes=[out.tk])

    def memset(self, eng, out, out_ap, val):
        self.S.op(eng, lambda e: e.memset(out_ap, val), reads=[], writes=[out.tk])

    def load_w(self, slot, src, col0, ncols=512, row0=0):
        wb = self.W[slot]
        self.S.dma("pool", (lambda e: e.dma_start(out=wb.ap[:, :, 0:ncols],
                                                  in_=src.rearrange("(k p) c -> p k c", p=128)[:, :, col0:col0 + ncols])),
                   reads=[], writes=[wb.tk], prefetch=True)

    def load_w1(self, slot, src, col0, k, ncols=512):
        wb = self.W[slot]
        self.S.dma("pool", (lambda e: e.dma_start(out=wb.ap[:, k, 0:ncols], in_=src[k * 128:(k + 1) * 128, col0:col0 + ncols])),
                   reads=[], writes=[wb.tk], prefetch=True)

    def proj(self, bank, ntok, xt_reads, xt_ap, wslot, wc0=0, ncols=512):
        wb = self.W[wslot]
        for k in range(8):
            self.S.op("pe", (lambda e, k=k: e.matmul(bank.ap[0:ntok, 0:ncols], lhsT=xt_ap[:, k, :],
                                                     rhs=wb.ap[:, k, wc0:wc0 + ncols], start=(k == 0), stop=(k == 7))),
                      reads=list(xt_reads) + [wb.tk], writes=[bank.tk])

    def rope(self, bank, ntok, cos_b, cos_ap, sin_b, sin_ap, t1, t2, out, out_ap, nh=8):
        X = bank.ap[0:ntok, 0:nh * 64].rearrange("p (h two f) -> p h two f", h=nh, two=2)
        T1 = t1.ap[0:ntok, 0:nh * 64].rearrange("p (h two f) -> p h two f", h=nh, two=2)
        T2 = t2.ap[0:ntok, 0:nh * 64].rearrange("p (h two f) -> p h two f", h=nh, two=2)
        O = out_ap.rearrange("p (h two f) -> p h two f", h=nh, two=2)
        self.tt("dve", t1, T1, bank, X, cos_b, cos_ap, ALU.mult)
        self.tt("dve", t2, T2, bank, X[:, :, ::-1, :], sin_b, sin_ap, ALU.mult)
        self.tt("dve", out, O, t1, T1, t2, T2, ALU.add)

    def build(self, phases):
        self.phases = phases
        nc = self.nc
        S = self.S
        din, dco, dout = self.din, self.dco, self.dout
        self.ps_t = self.es.enter_context(nc.psum_tensor("psum", [128, 4096], F32))
        self.PB = [Buf(self.ps_t[:, i * 512:(i + 1) * 512], "bank%d" % i) for i in range(8)]
        for b_ in self.PB:
            b_.tk.excl = True
        PB = self.PB
        n = NS

        self.xT_t = self.sb("xT", (128, 8, SEQ), BF16)
        self.xT = [Buf(self.xT_t[:, :, t * 128:(t + 1) * 128], "xT%d" % t) for t in range(NT)]
        self.xT_all = [b.tk for b in self.xT]
        self.xsT = self.buf("xsT", (128, 8, NS), BF16)
        self.mixsT = self.buf("mixsT", (128, 12, NS), BF16)
        self.mixS_t = self.sb("mixS", (128, 4, SEQ), BF16)
        self.mixS = Buf(self.mixS_t, "mixS")
        self.NSLOT = 6
        self.W_t = self.sb("W", (128, self.NSLOT * 4096), BF16)
        self.W = [Buf(self.W_t[:, i * 4096:(i + 1) * 4096].rearrange("p (k c) -> p k c", k=8), "W%d" % i)
                  for i in range(self.NSLOT)]
        self.identb = self.buf("identb", (128, 128), BF16)
        self.identf = self.buf("identf", (128, 128), F32)
        self.cosp = self.buf("cosp", (128, 16, 32), F32)
        self.sinp = self.buf("sinp", (128, 16, 2, 32), F32)
        self.mkT = self.buf("mkT", (128, 4, 256), BF16)
        self.mvaug = self.buf("mvaug", (128, 2, 4, 129), BF16)
        self.sqs = self.buf("sqs", (n, 512), BF16)
        self.sks = self.buf("sks", (n, 512), BF16)
        self.Vnew = self.buf("Vnew", (n, 8, 65), BF16)
        self.Gss = self.buf("Gss", (n, 512), BF16)
        self.coss = self.buf("coss", (n, 32), F32)
        self.sins = self.buf("sins", (n, 2, 32), F32)
        self.A_t = self.sb("arenaA", (128, 8, SEQ), BF16)
        self.AW = 15872
        self.arena_t = self.sb("arenaB", (128, self.AW), F32)
        self.arena_reset()
        self.mixR = [Buf(self.A_t[:, 0:4, t * 128:(t + 1) * 128], "mixR%d" % t) for t in range(NT)]
        self.mixM = [Buf(self.A_t[:, 4:8, t * 128:(t + 1) * 128], "mixM%d" % t) for t in range(NT)]

        self.dma_in("pool", self.identb, self.identb.ap, dco["ident"])
        self.load_w(0, din["w_in"], C_SQ)
        self.load_w(1, din["w_in"], C_SK)
        self.load_w(2, din["w_in"], C_SV)
        self.load_w(4, din["w_mem"], 0)
        self.load_w(5, din["w_mem"], 512)
        self.load_w(3, din["w_in"], C_SG)
        self.dma_in("sp", self.identf, self.identf.ap, dco["ident"])
        self.dma_in("sp", self.cosp, self.cosp.ap, dco["cosp"])
        self.dma_in("sp", self.sinp, self.sinp.ap, dco["sinp"])
        self.dma_in("sp", self.coss, self.coss.ap, dco["coss"])
        self.dma_in("sp", self.sins, self.sins.ap, dco["sins"])

        xld = [self.arena("xld", (128, D), F32) for _ in range(3)]
        memT = self.arena("memT", (128, 8, 256), BF16)
        mf = [self.arena("mf", (128, 512), F32) for _ in range(2)]
        xsld = self.arena("xsld", (n, D), BF16)

        def load_T(src_ap, ntok, dst, dst_ap, i):
            xb = xld[i % 3]
            self.dma_in("sp", xb, xb.ap[0:ntok], src_ap)
            b0, b1 = PB[(2 * i) % 8], PB[(2 * i + 1) % 8]
            for c in range(8):
                bk = b0 if c < 4 else b1
                self.tr(bk, bk.ap[:, (c % 4) * ntok:(c % 4 + 1) * ntok], xb, xb.ap[0:ntok, c * 128:(c + 1) * 128], self.identf)
            e0, e1 = ("dve", "act") if i % 2 == 0 else ("act", "dve")
            self.cp(e0, dst, dst_ap[:, 0:4, :], b0, b0.ap[:, 0:4 * ntok].rearrange("p (c t) -> p c t", c=4))
            self.cp(e1, dst, dst_ap[:, 4:8, :], b1, b1.ap[:, 0:4 * ntok].rearrange("p (c t) -> p c t", c=4))

        i = 0
        for mt in range(2):
            load_T(din["memx"][mt * 128:(mt + 1) * 128, :], 128, memT, memT.ap[:, :, mt * 128:(mt + 1) * 128], i)
            i += 1
        for t in range(NT):
            load_T(din["x"][t * 128:(t + 1) * 128, :], 128, self.xT[t], self.xT[t].ap, i)
            i += 1
            if t == 15:
                self.dma_in("pool", xsld, xsld.ap, din["xs"])
                bk = PB[(2 * i) % 8]
                pv = bk.ap.bitcast(BF16)
                for c in range(8):
                    self.tr(bk, pv[:, c * n:(c + 1) * n], xsld, xsld.ap[:, c * 128:(c + 1) * 128], self.identb)
                self.cp("dve", self.xsT, self.xsT.ap, bk, pv[:, 0:8 * n].rearrange("p (c t) -> p c t", c=8))
                i += 1
                self.mem_setup(memT, mf)
        S.barrier()

        if "S" in phases:
            self.phase_S()
            S.barrier()
        if "R" in phases:
            self.phase_R()
            S.barrier()
        if "M" in phases:
            self.phase_M()
            S.barrier()
        if "F" in phases:
            self.phase_F()
            S.barrier()
        S.barrier(final=True)
        with nc.Block() as block:
            S.emit(block, self.eng_sems, self.dma_sems)
        self.es.close()
        return nc

    def mem_setup(self, memT, mf):
        S = self.S
        PB = self.PB
        dout = self.dout
        mkT, mvaug = self.mkT, self.mvaug
        self.memset("pool", mvaug, mvaug.ap[:, :, :, 128:129], 1.0)
        i = 0
        for mt in range(2):
            for which in range(2):
                bank = PB[4 + i % 4]
                self.proj(bank, 128, [memT.tk], memT.ap[:, :, mt * 128:(mt + 1) * 128], 4 + which)
                f = mf[i % 2]
                i += 1
                self.cp("act", f, f.ap, bank, bank.ap)
                self.dma_out("sp", dout["mkp" if which == 0 else "mvp"][mt * 128:(mt + 1) * 128, :], f, f.ap)
                if which == 1:
                    self.cp("pool", mvaug, mvaug.ap[:, mt, :, 0:128], f, f.ap.rearrange("p (h f) -> p h f", h=4))
        w0 = self.W[4]
        for h in range(4):
            bank = PB[4 + h % 4]
            for k in range(8):
                S.op("pe", (lambda e, k=k, h=h, bank=bank: e.matmul(bank.ap[:, 0:256], lhsT=w0.ap[:, k, h * 128:(h + 1) * 128],
                                                                     rhs=memT.ap[:, k, :], start=(k == 0), stop=(k == 7))),
                     reads=[w0.tk, memT.tk], writes=[bank.tk])
            self.cp("act", mkT, mkT.ap[:, h, :], bank, bank.ap[:, 0:256])

    def gate(self, bank, n, th, out, out_ap):
        self.act(th, th.ap[0:n], bank, bank.ap[0:n], AF.Tanh, scale=0.5)
        self.stt(out, out_ap, th, th.ap[0:n], 1.0, ALU.add, bank, bank.ap[0:n], ALU.mult)

    def phase_S(self):
        S = self.S
        PB = self.PB
        din, dco, dout = self.din, self.dco, self.dout
        n = NS
        self.arena_reset()
        QKT_t = self.A_t
        QKT = [Buf(QKT_t[:, :, t * 128:(t + 1) * 128], "QKT%d" % t) for t in range(NT)]
        qkt_all = [b.tk for b in QKT]
        Vaug_t = self.arena("Vaug", (128, 16, 8, 65), BF16)
        Vaug = [Buf(Vaug_t.ap[:, b], "Vaug%d" % b) for b in range(16)]
        PT = [self.arena("PT", (128, 8, 256), BF16) for _ in range(3)]
        LT = self.arena("LT", (8, SEQ), F32)
        swam = self.arena("swam", (128, 256), BF16)
        sel = self.arena("sel", (8, 512), F32)
        t1 = self.arena("t1", (128, 512), F32)
        t2 = self.arena("t2", (128, 512), F32)
        qkb = [self.arena("qkb", (128, 1024), BF16) for _ in range(2)]
        kf = [self.arena("kf", (128, 512), F32) for _ in range(2)]
        vf = [self.arena("vf", (128, 512), F32) for _ in range(2)]
        Ub = [self.arena("Ub", (128, 8, 64), BF16) for _ in range(2)]
        lf = [self.arena("lf", (128, 8), F32) for _ in range(2)]
        tha, ga, gb = [t1, t2], kf, vf

        self.dma_in("pool", swam, swam.ap, dco["swamask"])
        self.dma_in("sp", sel, sel.ap, dco["sel"])
        self.memset("pool", Vaug_t, Vaug_t.ap[:, :, :, 64:65], 1.0)
        for b in Vaug:
            b.tk.w = Vaug_t.tk.w

        vscr_tk = Tk("vscr")
        def s1_front(t):
            xt = self.xT[t]
            bq, bk, bv = PB[(2 * t) % 4], PB[(2 * t + 1) % 4], PB[4 + t % 2]
            cos4 = self.cosp.ap[:, t, :].unsqueeze(1).unsqueeze(1).broadcast_to([128, 8, 2, 32])
            sin4 = self.sinp.ap[:, t, :, :].unsqueeze(1).broadcast_to([128, 8, 2, 32])
            qk = qkb[t % 2]
            self.proj(bq, 128, [xt.tk], xt.ap, 0)
            self.rope(bq, 128, self.cosp, cos4, self.sinp, sin4, t1, t2, qk, qk.ap[:, 0:512])
            self.proj(bk, 128, [xt.tk], xt.ap, 1)
            self.rope(bk, 128, self.cosp, cos4, self.sinp, sin4, t1, t2, kf[t % 2], kf[t % 2].ap)
            self.dma_out("sp", dout["kp"][t * 128:(t + 1) * 128, :], kf[t % 2], kf[t % 2].ap)
            self.cp("pool", qk, qk.ap[:, 512:1024], kf[t % 2], kf[t % 2].ap)
            self.proj(bv, 128, [xt.tk], xt.ap, 2)
            self.cp("act", vf[t % 2], vf[t % 2].ap, bv, bv.ap)
            self.dma_out("sp", dout["vp"][t * 128:(t + 1) * 128, :], vf[t % 2], vf[t % 2].ap)
            self.cp("pool", Vaug[t], Vaug[t].ap[:, :, 0:64], vf[t % 2], vf[t % 2].ap.rearrange("p (h f) -> p h f", h=8))
            S.dma("sp", (lambda e, t=t: e.dma_start(out=self.vscr[t * 128:(t + 1) * 128, :],
                                                    in_=Vaug[t].ap.rearrange("p h f -> p (h f)"))),
                  reads=[Vaug[t].tk], writes=[vscr_tk])

        def s1_back(t):
            bt = PB[6 + t % 2]
            qk = qkb[t % 2]
            btv = bt.ap.bitcast(BF16)
            for c in range(8):
                self.tr(bt, btv[:, c * 128:(c + 1) * 128], qk, qk.ap[:, c * 128:(c + 1) * 128], self.identb)
            self.cp("act", QKT[t], QKT[t].ap, bt, btv.rearrange("p (c t) -> p c t", c=8))

        for t in range(NT):
            s1_front(t)
            if t > 0:
                s1_back(t - 1)
        s1_back(NT - 1)
        cos4 = self.coss.ap.unsqueeze(1).unsqueeze(1).broadcast_to([n, 8, 2, 32])
        sin4 = self.sins.ap.unsqueeze(1).broadcast_to([n, 8, 2, 32])
        self.proj(PB[0], n, [self.xsT.tk], self.xsT.ap, 0)
        self.rope(PB[0], n, self.coss, cos4, self.sins, sin4, t1, t2, self.sqs, self.sqs.ap)
        self.proj(PB[1], n, [self.xsT.tk], self.xsT.ap, 1)
        self.rope(PB[1], n, self.coss, cos4, self.sins, sin4, t1, t2, kf[0], kf[0].ap[0:n])
        self.dma_out("sp", dout["ks"], kf[0], kf[0].ap[0:n])
        self.cp("pool", self.sks, self.sks.ap, kf[0], kf[0].ap[0:n])
        self.proj(PB[4], n, [self.xsT.tk], self.xsT.ap, 2)
        self.cp("act", vf[0], vf[0].ap[0:n], PB[4], PB[4].ap[0:n])
        self.dma_out("sp", dout["vs"], vf[0], vf[0].ap[0:n])
        self.memset("pool", self.Vnew, self.Vnew.ap[:, :, 64:65], 1.0)
        self.cp("pool", self.Vnew, self.Vnew.ap[:, :, 0:64], vf[0], vf[0].ap[0:n].rearrange("p (h f) -> p h f", h=8))
        self.proj(PB[5], n, [self.xsT.tk], self.xsT.ap, 3)
        self.gate(PB[5], n, t1, t2, t2.ap[0:n])
        self.ts("dve", self.Gss, self.Gss.ap, t2, t2.ap[0:n], 0.5, ALU.mult)
        pre = []
        for slot, col in ((4, C_RQ), (5, C_RV), (0, C_RG), (1, C_MQ)):
            for k in range(8):
                pre.append((slot, col, k))

        STB = PB[0:4]
        UB = [PB[4], PB[5]]
        TBa = PB[6]
        VB = PB[7]
        TBl = VB
        entries = []
        for g in (1, 4, 16):
            seqs = {1: [list(range(16))], 4: [[4 * r + nn for nn in range(4)] for r in range(4)],
                    16: [[r] for r in range(16)]}[g]
            for seq in seqs:
                for si, kb in enumerate(seq):
                    entries.append((g, seq, si, kb))
        NE = len(entries)
        vdone = {1: True}

        nextg = {1: 4, 4: 16}

        def vreload(g, b):
            tsl = tok_slice(g, b)
            S.dma("sp", (lambda e, b=b, tsl=tsl: e.dma_start(out=Vaug[b].ap.rearrange("p h f -> p (h f)"),
                                                            in_=self.vscr[tsl, :])),
                  reads=[vscr_tk], writes=[Vaug[b].tk])

        def stage_A(i):
            g, seq, si, kb = entries[i]
            last = (si == len(seq) - 1)
            nq = 128 if last else 256
            ksl = tok_slice(g, kb)
            qsl = tok_slice(g, kb, 1 if last else 2)
            pt = PT[i % 3]
            for h in range(8):
                c, half = h // 2, h % 2
                bank = STB[2 * (h // 4) + (h % 2)]
                slot = (h // 2) % 2
                self.mm(bank, bank.ap[:, slot * 256:slot * 256 + nq],
                        QKT[0], QKT_t[half * 64:(half + 1) * 64, 4 + c, ksl],
                        QKT[0], QKT_t[half * 64:(half + 1) * 64, c, qsl], extra_reads=qkt_all)
            for bi in range(4):
                bank = STB[bi]
                hs = 4 * (bi // 2) + (bi % 2)
                self.act(pt, pt.ap[:, hs:hs + 3:2, 0:nq], bank,
                         bank.ap.rearrange("p (s q) -> p s q", s=2)[:, :, 0:nq], AF.Exp, scale=0.125)
            self.tt("pool", pt, pt.ap[:, :, 0:nq], pt, pt.ap[:, :, 0:nq], swam,
                    swam.ap[:, 0:nq].unsqueeze(1).broadcast_to([128, 8, nq]), ALU.mult)

        def stage_B(i):
            g, seq, si, kb = entries[i]
            pt = PT[i % 3]
            ptp = PT[(i - 1) % 3]
            for h in range(8):
                ub = UB[h // 4]
                o_ap = ub.ap[:, (h % 4) * 65:(h % 4) * 65 + 65]
                if si > 0:
                    self.mm(ub, o_ap, ptp, ptp.ap[:, h, 128:256], Vaug[seq[si - 1]], Vaug[seq[si - 1]].ap[:, h, :],
                            start=True, stop=False)
                    self.mm(ub, o_ap, pt, pt.ap[:, h, 0:128], Vaug[kb], Vaug[kb].ap[:, h, :], start=False, stop=True)
                else:
                    self.mm(ub, o_ap, pt, pt.ap[:, h, 0:128], Vaug[kb], Vaug[kb].ap[:, h, :], start=True, stop=True)
            u = Ub[i % 2]
            l = lf[i % 2]
            for j in range(2):
                uv = UB[j].ap[:, 0:260].rearrange("p (h f) -> p h f", h=4)
                self.cp("dve", u, u.ap[:, 4 * j:4 * j + 4, :], UB[j], uv[:, :, 0:64])
                self.cp("dve", l, l.ap[:, 4 * j:4 * j + 4], UB[j], uv[:, :, 64])
            if g in nextg:
                if si > 0:
                    vreload(nextg[g], seq[si - 1])
                if si == len(seq) - 1:
                    vreload(nextg[g], kb)
            if g == 1 and kb == 0:
                self.load_w(2, din["w_in"], C_MG)
            if pre:
                slot, col, k = pre.pop(0)
                self.load_w1(slot, din["w_in"], col, k)

        def stage_C(i):
            g, seq, si, kb = entries[i]
            ksl = tok_slice(g, kb)
            u = Ub[i % 2]
            l = lf[i % 2]
            TBa = PB[6 + i % 2]
            TBl = TBa
            tav = TBa.ap.bitcast(BF16)
            for c in range(4):
                self.tr(TBa, tav[:, c * 128:(c + 1) * 128], u,
                        u.ap[:, 2 * c:2 * c + 2, :].rearrange("p h f -> p (h f)"), self.identb)
            self.tr(TBl, TBl.ap[0:8, 256:384], l, l.ap, self.identf)
            dstm = self.mixS_t[:, :, ksl]
            dstl = LT.ap[:, ksl]
            srcm = tav[:, 0:512].rearrange("p (c q) -> p c q", c=4)
            if g == 1:
                self.cp("dve", self.mixS, dstm, TBa, srcm)
                self.cp("dve", LT, dstl, TBl, TBl.ap[0:8, 256:384])
            else:
                self.tt("dve", self.mixS, dstm, TBa, srcm, self.mixS, dstm, ALU.add)
                self.tt("dve", LT, dstl, TBl, TBl.ap[0:8, 256:384], LT, dstl, ALU.add)

        for i in range(NE + 2):
            if i < NE:
                stage_A(i)
            if 0 <= i - 1 < NE:
                stage_B(i - 1)
            if 0 <= i - 2 < NE:
                stage_C(i - 2)

        S.op("dve", lambda e: e.reciprocal(out=LT.ap, in_=LT.ap), reads=[LT.tk], writes=[LT.tk])
        i = 0
        for nn in range(4):
            gsl = slice(nn * 512, (nn + 1) * 512)
            for c in range(4):
                bb, bg = PB[(2 * i) % 8], PB[(2 * i + 1) % 8]
                self.mm(bb, bb.ap, sel, sel.ap[:, c * 128:(c + 1) * 128], LT, LT.ap[:, gsl])
                wb = self.W[3]
                for k in range(8):
                    S.op("pe", (lambda e, k=k, c=c, gsl=gsl, bg=bg, wb=wb: e.matmul(
                        bg.ap, lhsT=wb.ap[:, k, c * 128:(c + 1) * 128], rhs=self.xT_t[:, k, gsl],
                        start=(k == 0), stop=(k == 7))), reads=self.xT_all + [wb.tk], writes=[bg.tk])
                th, a, b = tha[i % 2], ga[i % 2], gb[i % 2]
                self.act(th, th.ap, bg, bg.ap, AF.Tanh, scale=0.5)
                self.stt(a, a.ap, th, th.ap, 1.0, ALU.add, bg, bg.ap, ALU.mult)
                self.stt(b, b.ap, self.mixS, self.mixS_t[:, c, gsl], 0.5, ALU.mult, bb, bb.ap, ALU.mult)
                self.tt("dve", self.mixS, self.mixS_t[:, c, gsl], a, a.ap, b, b.ap, ALU.mult)
                i += 1

    def phase_R(self):
        S = self.S
        PB = self.PB
        din, dco, dout = self.din, self.dco, self.dout
        n = NS
        self.arena_reset()
        A = self.arena
        lgs = np.log1p(-np.exp2(-5.0 - np.arange(4, dtype=np.float64)))
        g128 = [float(np.exp(128.0 * v)) for v in lgs]
        g4 = [float(np.exp(4.0 * v)) for v in lgs]
        dqk = A("dqk", (128, 8), F32)
        dqks = A("dqks", (n, 8), F32)
        cmask = A("cmask", (128, 128), BF16)
        rmask = A("rmask", (n, 16), BF16)
        bmask = A("bmask", (n, 4), F32)
        mhalf = A("mhalf", (128, 4), F32)
        Rst = A("Rst", (128, 4, 128), F32)
        Rb = A("Rb", (128, 4, 128), BF16)
        CS = [A("CS", (128, 8, 2, 32), F32) for _ in range(2)]
        SN = [A("SN", (128, 8, 2, 32), F32) for _ in range(2)]
        t1 = A("t1", (128, 512), F32)
        t2 = A("t2", (128, 512), F32)
        QKr = [A("QKr", (128, 512), BF16) for _ in range(2)]
        Vr = [A("Vr", (128, 512), BF16) for _ in range(2)]
        thr = [A("thr", (128, 512), F32) for _ in range(2)]
        Gr = [A("Gr", (128, 512), F32) for _ in range(2)]
        QKrT = [A("QKrT", (128, 4, 128), BF16) for _ in range(2)]
        QZ = [A("QZ", (128, 4, 128), BF16) for _ in range(2)]
        STm = [A("STm", (128, 4, 128), BF16) for _ in range(2)]
        s12 = A("s12", (128, 2, 4), F32)
        st6 = A("st6", (128, 4, 6), F32)
        mvr = A("mvr", (128, 4, 2), F32)
        mean = A("mean", (128, 4), F32)
        msq = A("msq", (128, 4), F32)
        va = A("va", (128, 4), F32)
        ve = A("ve", (128, 4), F32)
        rstd = A("rstd", (128, 4), F32)
        junk = A("junk", (128, 512), BF16)
        mixRs = A("mixRs", (n, 512), BF16)
        On = A("On", (128, 512), F32)
        mixRt = [A("mixRt", (128, 512), BF16) for _ in range(2)]
        Rs = A("Rs", (128, 4, 4, 128), F32)
        Rsb = A("Rsb", (128, 4, 4, 128), BF16)
        QKsT = A("QKsT", (128, 4, n), BF16)
        QZs = A("QZs", (128, 4, n), BF16)
        QZB = A("QZB", (128, 4, 4, n), BF16)
        STs = A("STs", (n, 4, n), BF16)
        KZ = A("KZ", (n, 4, 256), BF16)

        for b_, k_ in ((dqk, "dqk"), (mhalf, "mhalf"), (dqks, "dqks"), (bmask, "bmask")):
            self.dma_in("sp", b_, b_.ap, dco[k_])
        self.dma_in("pool", cmask, cmask.ap, dco["cmask"])
        self.dma_in("pool", rmask, rmask.ap, dco["rmask"])
        Gz = A("Gz", (128, 4, 128), F32)
        self.dma_in("sp", Gz, Gz.ap, dco["g128"].rearrange("p (h f) -> p h f", h=4))
        self.memset("pool", Rst, Rst.ap, 0.0)
        self.memset("pool", Rb, Rb.ap, 0.0)
        for z in QZ:
            self.memset("pool", z, z.ap, 0.0)
        self.memset("pool", Rs, Rs.ap, 0.0)
        self.memset("pool", QZs, QZs.ap, 0.0)
        self.memset("pool", QZB, QZB.ap, 0.0)
        self.memset("pool", Rsb, Rsb.ap, 0.0)
        st_src = din["state"].rearrange("b h k v -> k (b h) v")
        for hp in range(2):
            rows = slice(hp * 64, hp * 64 + 64)
            self.dma_in("sp", Rs, Rs.ap.rearrange("p b h v -> p (b h) v")[rows, hp:16:2, :], st_src[:, hp:16:2, :], parallel=True)
            self.dma_in("pool", Rsb, Rsb.ap.rearrange("p b h v -> p (b h) v")[rows, hp:16:2, :], st_src[:, hp:16:2, :], parallel=True)
        dq4 = dqk.ap.unsqueeze(2).unsqueeze(3).broadcast_to([128, 8, 2, 32])
        WQK, WV, WG = 4, 5, 0

        def headnorm_gate(bo, nn_, gr, mixrt):
            for h in range(4):
                S.op("dve", (lambda e, h=h: e.bn_stats(out=st6.ap[0:nn_, h, :], in_=bo.ap[0:nn_, h * 128:(h + 1) * 128])),
                     reads=[bo.tk], writes=[st6.tk])
            for h in range(4):
                S.op("dve", (lambda e, h=h: e.bn_aggr(out=mvr.ap[0:nn_, h, :], in_=st6.ap[0:nn_, h, :])),
                     reads=[st6.tk], writes=[mvr.tk])
            self.cp("dve", mean, mean.ap[0:nn_], mvr, mvr.ap[0:nn_, :, 0])
            self.ts("dve", ve, ve.ap[0:nn_], mvr, mvr.ap[0:nn_, :, 1], 4.0, ALU.mult, 4.0 * EPS, ALU.add)
            self.tt("pool", rstd, rstd.ap[0:nn_], ve, ve.ap[0:nn_], mhalf, mhalf.ap[0:nn_], ALU.pow)
            for h in range(4):
                hs = slice(h * 128, (h + 1) * 128)
                self.stt(On, On.ap[0:nn_, hs], bo, bo.ap[0:nn_, hs], mean.ap[0:nn_, h:h + 1], ALU.subtract,
                         gr, gr.ap[0:nn_, hs], ALU.mult, extra_reads=[mean])
            for h in range(4):
                hs = slice(h * 128, (h + 1) * 128)
                S.op("act", (lambda e, h=h, hs=hs: e.activation(out=mixrt.ap[0:nn_, hs], in_=On.ap[0:nn_, hs], func=AF.Identity,
                                                               scale=rstd.ap[0:nn_, h:h + 1])),
                     reads=[On.tk, rstd.tk], writes=[mixrt.tk])

        b0, b1, b2, bt, bs, bo = PB[0], PB[1], PB[2], PB[3], PB[4], PB[5]
        bkv = [PB[6], PB[7]]
        btv = bt.ap.bitcast(BF16)
        bsv = bs.ap.bitcast(BF16)

        def r_front(t):
            xt = self.xT[t]
            cs, sn = CS[t % 2], SN[t % 2]
            cos4 = self.cosp.ap[:, t, :].unsqueeze(1).unsqueeze(1).broadcast_to([128, 8, 2, 32])
            sin4 = self.sinp.ap[:, t, :, :].unsqueeze(1).broadcast_to([128, 8, 2, 32])
            self.tt("pool", cs, cs.ap, self.cosp, cos4, dqk, dq4, ALU.mult)
            self.tt("pool", sn, sn.ap, self.sinp, sin4, dqk, dq4, ALU.mult)
            qk, vr, th, gr, qkT, qz = QKr[t % 2], Vr[t % 2], thr[t % 2], Gr[t % 2], QKrT[t % 2], QZ[t % 2]
            self.proj(b0, 128, [xt.tk], xt.ap, WQK)
            self.rope(b0, 128, cs, cs.ap, sn, sn.ap, t1, t2, qk, qk.ap)
            self.proj(b1, 128, [xt.tk], xt.ap, WV)
            self.cp("act", vr, vr.ap, b1, b1.ap)
            self.proj(b2, 128, [xt.tk], xt.ap, WG)
            self.gate(b2, 128, th, gr, gr.ap)
            for c in range(4):
                self.tr(bt, btv[:, c * 128:(c + 1) * 128], qk, qk.ap[:, c * 128:(c + 1) * 128], self.identb)
            tv4 = btv[:, 0:512].rearrange("p (c t) -> p c t", c=4)
            self.cp("act", qkT, qkT.ap, bt, tv4)
            self.cp("act", qz, qz.ap[0:64, 0:4:2, :], bt, tv4[0:64, 0:2, :])
            self.cp("act", qz, qz.ap[64:128, 1:4:2, :], bt, tv4[64:128, 0:2, :])

        def r_back(t):
            qk, vr, gr, qkT, qz, stm, mixrt = QKr[t % 2], Vr[t % 2], Gr[t % 2], QKrT[t % 2], QZ[t % 2], STm[t % 2], mixRt[t % 2]
            for h in range(4):
                self.mm(bs, bs.ap[:, h * 128:(h + 1) * 128], qkT, qkT.ap[:, 2 + h // 2, :], qz, qz.ap[:, h, :])
            self.tt("dve", stm, stm.ap, bs, bs.ap.rearrange("p (h i) -> p h i", h=4), cmask,
                    cmask.ap.unsqueeze(1).broadcast_to([128, 4, 128]), ALU.mult)
            for h in range(4):
                o_ap = bo.ap[:, h * 128:(h + 1) * 128]
                self.mm(bo, o_ap, stm, stm.ap[:, h, :], vr, vr.ap[:, h * 128:(h + 1) * 128], start=True, stop=False)
                self.mm(bo, o_ap, qkT, qkT.ap[:, h // 2, :], Rb, Rb.ap[:, h, :], start=False, stop=True)
            for c in range(2):
                self.mm(bkv[c], bkv[c].ap, qk, qk.ap[:, 256 + c * 128:256 + (c + 1) * 128], vr, vr.ap)
            for c in range(2):
                rv_ = Rst.ap[:, 2 * c:2 * c + 2, :]
                kv_ = bkv[c].ap[:, 2 * c * 128:(2 * c + 2) * 128].rearrange("p (h f) -> p h f", h=2)
                self.tt("dve", Rst, rv_, bkv[c], kv_, Rst, rv_, ALU.add)
                self.tt("dve", Rst, rv_, Rst, rv_, Gz, Gz.ap[:, 2 * c:2 * c + 2, :], ALU.mult)
            self.cp("act", Rb, Rb.ap, Rst, Rst.ap)
            headnorm_gate(bo, 128, gr, mixrt)

        def r_tail(t):
            mixrt = mixRt[t % 2]
            for c in range(4):
                self.tr(bt, btv[:, c * 128:(c + 1) * 128], mixrt, mixrt.ap[:, c * 128:(c + 1) * 128], self.identb)
            self.cp("act", self.mixR[t], self.mixR[t].ap, bt, btv[:, 0:512].rearrange("p (c t) -> p c t", c=4))

        r_front(0)
        for t in range(NT):
            if t + 1 < NT:
                r_front(t + 1)
            r_back(t)
            if t > 0:
                r_tail(t - 1)
            if t == 7:
                self.sample_ret(locals())
        r_tail(NT - 1)
        for h in range(4):
            rows = slice((h % 2) * 64, (h % 2) * 64 + 64)
            self.dma_out("sp", dout["retp"][h], Rst, Rst.ap[rows, h, :])
        self.Wout = []
        for ec in range(12):
            slot = (4, 5, 0)[ec // 4]
            wb = self.W[slot]
            v = self.W_t[:, slot * 4096 + (ec % 4) * 1024:slot * 4096 + (ec % 4 + 1) * 1024]
            self.Wout.append((wb, v))
            S.dma("pool", (lambda e, ec=ec, v=v: e.dma_start(out=v, in_=din["w_out"][ec * 128:(ec + 1) * 128, :])),
                  reads=[], writes=[wb.tk], prefetch=True)

    def sample_ret(self, L):
        S = self.S
        PB = self.PB
        din, dout = self.din, self.dout
        n = NS
        g4 = L["g4"]
        dqks, rmask, bmask, t1, t2 = L["dqks"], L["rmask"], L["bmask"], L["t1"], L["t2"]
        Rs, Rsb, QKsT, QZs, QZB, STs, KZ = L["Rs"], L["Rsb"], L["QKsT"], L["QZs"], L["QZB"], L["STs"], L["KZ"]
        cs, sn = L["CS"][1], L["SN"][1]
        qk, vr, th, gr, mixrt = L["QKr"][1], L["Vr"][1], L["thr"][1], L["Gr"][1], L["mixRs"]
        b0, b1, b2, bt, bs, bo = PB[0], PB[1], PB[2], PB[3], PB[4], PB[5]
        cos4 = self.coss.ap.unsqueeze(1).unsqueeze(1).broadcast_to([n, 8, 2, 32])
        sin4 = self.sins.ap.unsqueeze(1).broadcast_to([n, 8, 2, 32])
        dq4 = dqks.ap.unsqueeze(2).unsqueeze(3).broadcast_to([n, 8, 2, 32])
        self.tt("pool", cs, cs.ap[0:n], self.coss, cos4, dqks, dq4, ALU.mult)
        self.tt("pool", sn, sn.ap[0:n], self.sins, sin4, dqks, dq4, ALU.mult)
        self.proj(b0, n, [self.xsT.tk], self.xsT.ap, L["WQK"])
        self.rope(b0, n, cs, cs.ap[0:n], sn, sn.ap[0:n], t1, t2, qk, qk.ap[0:n])
        self.proj(b1, n, [self.xsT.tk], self.xsT.ap, L["WV"])
        self.cp("act", vr, vr.ap[0:n], b1, b1.ap[0:n])
        self.proj(b2, n, [self.xsT.tk], self.xsT.ap, L["WG"])
        self.gate(b2, n, th, gr, gr.ap[0:n])
        btv = bt.ap.bitcast(BF16)
        for c in range(4):
            self.tr(bt, btv[:, c * n:(c + 1) * n], qk, qk.ap[0:n, c * 128:(c + 1) * 128], self.identb)
        tv4 = btv[:, 0:4 * n].rearrange("p (c t) -> p c t", c=4)
        self.cp("act", QKsT, QKsT.ap, bt, tv4)
        self.cp("dve", QZs, QZs.ap[0:64, 0:4:2, :], bt, tv4[0:64, 0:2, :])
        self.cp("dve", QZs, QZs.ap[64:128, 1:4:2, :], bt, tv4[64:128, 0:2, :])
        for b in range(4):
            self.cp("pool", QZB, QZB.ap[:, b, :, 4 * b:4 * b + 4], QZs, QZs.ap[:, :, 4 * b:4 * b + 4])
        for h in range(4):
            self.mm(bs, bs.ap[0:n, h * n:(h + 1) * n], QKsT, QKsT.ap[:, 2 + h // 2, :], QZs, QZs.ap[:, h, :])
        self.tt("dve", STs, STs.ap, bs, bs.ap[0:n, 0:4 * n].rearrange("p (h i) -> p h i", h=4), rmask,
                rmask.ap.unsqueeze(1).broadcast_to([n, 4, n]), ALU.mult)
        for h in range(4):
            o_ap = bo.ap[0:n, h * 128:(h + 1) * 128]
            self.mm(bo, o_ap, STs, STs.ap[:, h, :], vr, vr.ap[0:n, h * 128:(h + 1) * 128], start=True, stop=False)
            for b in range(4):
                self.mm(bo, o_ap, QZB, QZB.ap[:, b, h, :], Rsb, Rsb.ap[:, b, h, :], start=False, stop=(b == 3))
        for b in range(4):
            self.ts("dve", KZ, KZ.ap[:, b, :], qk, qk.ap[0:n, 256:512], bmask.ap[:, b:b + 1], ALU.mult, extra_reads=[bmask])
        kbanks = [PB[6], PB[7]]
        i = 0
        for b in range(4):
            for c in range(2):
                kb_ = kbanks[i % 2]
                i += 1
                self.mm(kb_, kb_.ap, KZ, KZ.ap[:, b, c * 128:(c + 1) * 128], vr, vr.ap[0:n])
                for hh in range(2):
                    h = 2 * c + hh
                    rows = slice(hh * 64, hh * 64 + 64)
                    self.ts("dve", Rs, Rs.ap[rows, b, h, :], Rs, Rs.ap[rows, b, h, :], g4[h], ALU.mult)
                    self.stt(Rs, Rs.ap[rows, b, h, :], kb_, kb_.ap[rows, h * 128:(h + 1) * 128], g4[h], ALU.mult,
                             Rs, Rs.ap[rows, b, h, :], ALU.add)
        for b in range(4):
            for h in range(4):
                rows = slice((h % 2) * 64, (h % 2) * 64 + 64)
                self.dma_out("sp", dout["rets"][b, h], Rs, Rs.ap[rows, b, h, :])
        L["headnorm_gate"](bo, n, gr, mixrt)
        for c in range(4):
            self.tr(bt, btv[:, c * n:(c + 1) * n], mixrt, mixrt.ap[0:n, c * 128:(c + 1) * 128], self.identb)
        self.cp("act", self.mixsT, self.mixsT.ap[:, 0:4, :], bt, tv4)

    def phase_M(self):
        S = self.S
        PB = self.PB
        din, dco, dout = self.din, self.dco, self.dout
        n = NS
        self.arena_reset()
        A = self.arena
        mkT, mvaug = self.mkT, self.mvaug
        mqT = [A("mqT", (128, 4, 512), BF16) for _ in range(2)]
        PTm = [A("PTm", (128, 4, 2, 512), BF16) for _ in range(2)]
        thm = [A("thm", (128, 512), F32) for _ in range(2)]
        Gm = [A("Gm", (128, 512), F32) for _ in range(2)]
        rl = [A("rl", (128, 4), F32) for _ in range(2)]
        mixMt = [A("mixMt", (128, 512), BF16) for _ in range(2)]
        WMQ, WMG = 1, 2
        rot = [0]

        def nb():
            rot[0] = (rot[0] + 1) % 2
            return PB[rot[0]]

        w2 = self.W[WMQ]
        OBs = [[PB[4], PB[5]], [PB[2], PB[3]]]
        OB = OBs[0]
        bt = PB[6]
        bg = PB[7]
        btv = bt.ap.bitcast(BF16)
        def m_tail(t):
            mixmt = mixMt[t % 2]
            for c in range(4):
                self.tr(bt, btv[:, c * 128:(c + 1) * 128], mixmt, mixmt.ap[:, c * 128:(c + 1) * 128], self.identb)
            self.cp("act", self.mixM[t], self.mixM[t].ap, bt, btv[:, 0:512].rearrange("p (c t) -> p c t", c=4))

        for nn in range(4):
            gsl = slice(nn * 512, (nn + 1) * 512)
            mq = mqT[nn % 2]
            pt = PTm[nn % 2]
            for h in range(4):
                bank = nb()
                for k in range(8):
                    S.op("pe", (lambda e, k=k, h=h, bank=bank, gsl=gsl: e.matmul(
                        bank.ap, lhsT=w2.ap[:, k, h * 128:(h + 1) * 128], rhs=self.xT_t[:, k, gsl],
                        start=(k == 0), stop=(k == 7))), reads=self.xT_all + [w2.tk], writes=[bank.tk])
                self.cp("act", mq, mq.ap[:, h, :], bank, bank.ap)
            for h in range(4):
                for mt in range(2):
                    bank = nb()
                    self.mm(bank, bank.ap, mkT, mkT.ap[:, h, mt * 128:(mt + 1) * 128], mq, mq.ap[:, h, :])
                    self.act(pt, pt.ap[:, h, mt, :], bank, bank.ap, AF.Exp, scale=float(128.0 ** -0.5))
            for tq in range(4):
                t = 4 * nn + tq
                xt = self.xT[t]
                th, gm, r, mixmt = thm[t % 2], Gm[t % 2], rl[t % 2], mixMt[t % 2]
                OB = OBs[t % 2]
                self.proj(bg, 128, [xt.tk], xt.ap, WMG)
                self.gate(bg, 128, th, gm, gm.ap)
                for h in range(4):
                    ob = OB[h // 2]
                    o_ap = ob.ap[:, (h % 2) * 129:(h % 2) * 129 + 129]
                    for mt in range(2):
                        self.mm(ob, o_ap, pt, pt.ap[:, h, mt, tq * 128:(tq + 1) * 128], mvaug, mvaug.ap[:, mt, h, :],
                                start=(mt == 0), stop=(mt == 1))
                for j in range(2):
                    ov = OB[j].ap[:, 0:258].rearrange("p (h f) -> p h f", h=2)
                    S.op("dve", (lambda e, j=j, ov=ov, r=r: e.reciprocal(out=r.ap[:, 2 * j:2 * j + 2], in_=ov[:, :, 128])),
                         reads=[OB[j].tk], writes=[r.tk])
                self.ts("dve", r, r.ap, r, r.ap, 0.5, ALU.mult)
                for h in range(4):
                    hs = slice(h * 128, (h + 1) * 128)
                    ob = OB[h // 2]
                    self.stt(mixmt, mixmt.ap[:, hs], ob, ob.ap[:, (h % 2) * 129:(h % 2) * 129 + 128], r.ap[:, h:h + 1], ALU.mult,
                             gm, gm.ap[:, hs], ALU.mult, extra_reads=[r])
                if t > 0:
                    m_tail(t - 1)
            if nn == 1:
                self.sample_mem(locals())
        m_tail(NT - 1)

    def sample_mem(self, L):
        S = self.S
        PB = self.PB
        din = self.din
        n = NS
        A = self.arena
        WMQ, WMG = L["WMQ"], L["WMG"]
        nb = L["nb"]
        th, gm, rl, mixm = L["thm"][0], L["Gm"][0], L["rl"][0], L["mixMt"][0]
        mqs = A("mqs", (n, 512), BF16)
        mkld = [A("mkld", (128, 2, 512), BF16) for _ in range(2)]
        mkTs = A("mkTs", (128, 4, 4, 256), BF16)
        mvaugs = A("mvaugs", (128, 4, 2, 4, 129), BF16)
        mqsT = A("mqsT", (128, 4, n), BF16)
        PTms = A("PTms", (128, 4, 4, 2, n), BF16)
        bt = L["bt"]
        btv = L["btv"]
        bg = L["bg"]
        self.memset("pool", mvaugs, mvaugs.ap[:, :, :, :, 128:129], 1.0)
        self.memset("pool", PTms, PTms.ap, 0.0)
        for b in range(4):
            ml = mkld[b % 2]
            self.dma_in("pool", ml, ml.ap, din["cmk"][b].rearrange("(t p) c -> p t c", p=128))
            for mt in range(2):
                S.dma("pool", (lambda e, b=b, mt=mt: e.dma_start(
                    out=mvaugs.ap[:, b, mt, :, 0:128],
                    in_=din["cmv"][b][mt * 128:(mt + 1) * 128, :].rearrange("p (h f) -> p h f", h=4))),
                    reads=[], writes=[mvaugs.tk])
            for mt in range(2):
                bank = nb()
                bv = bank.ap.bitcast(BF16)
                for h in range(4):
                    self.tr(bank, bv[:, h * 128:(h + 1) * 128], ml, ml.ap[:, mt, h * 128:(h + 1) * 128], self.identb)
                self.cp("act", mkTs, mkTs.ap[:, b, :, mt * 128:(mt + 1) * 128], bank,
                        bv[:, 0:512].rearrange("p (h m) -> p h m", h=4))
        self.proj(bg, n, [self.xsT.tk], self.xsT.ap, WMQ)
        self.cp("act", mqs, mqs.ap, bg, bg.ap[0:n])
        for c in range(4):
            self.tr(bt, btv[:, c * n:(c + 1) * n], mqs, mqs.ap[:, c * 128:(c + 1) * 128], self.identb)
        tv4 = btv[:, 0:4 * n].rearrange("p (c t) -> p c t", c=4)
        self.cp("act", mqsT, mqsT.ap, bt, tv4)
        self.proj(bg, n, [self.xsT.tk], self.xsT.ap, WMG)
        self.gate(bg, n, th, gm, gm.ap[0:n])
        bsc = nb()
        for b in range(4):
            for h in range(4):
                for mt in range(2):
                    col = ((b * 4 + h) * 2 + mt) * 4
                    self.mm(bsc, bsc.ap[:, col:col + 4], mkTs, mkTs.ap[:, b, h, mt * 128:(mt + 1) * 128],
                            mqsT, mqsT.ap[:, h, 4 * b:4 * b + 4])
        for b in range(4):
            self.act(PTms, PTms.ap[:, b, :, :, 4 * b:4 * b + 4], bsc,
                     bsc.ap[:, b * 32:(b + 1) * 32].rearrange("p (h m i) -> p h m i", h=4, m=2), AF.Exp,
                     scale=float(128.0 ** -0.5))
        OB = L["OB"]
        for h in range(4):
            ob = OB[h // 2]
            o_ap = ob.ap[0:n, (h % 2) * 129:(h % 2) * 129 + 129]
            k = 0
            for b in range(4):
                for mt in range(2):
                    self.mm(ob, o_ap, PTms, PTms.ap[:, b, h, mt, :], mvaugs, mvaugs.ap[:, b, mt, h, :],
                            start=(k == 0), stop=(k == 7))
                    k += 1
        for j in range(2):
            ov = OB[j].ap[0:n, 0:258].rearrange("p (h f) -> p h f", h=2)
            S.op("dve", (lambda e, j=j, ov=ov: e.reciprocal(out=rl.ap[0:n, 2 * j:2 * j + 2], in_=ov[:, :, 128])),
                 reads=[OB[j].tk], writes=[rl.tk])
        self.ts("dve", rl, rl.ap[0:n], rl, rl.ap[0:n], 0.5, ALU.mult)
        for h in range(4):
            hs = slice(h * 128, (h + 1) * 128)
            ob = OB[h // 2]
            self.stt(mixm, mixm.ap[0:n, hs], ob, ob.ap[0:n, (h % 2) * 129:(h % 2) * 129 + 128], rl.ap[0:n, h:h + 1], ALU.mult,
                     gm, gm.ap[0:n, hs], ALU.mult, extra_reads=[rl])
        for c in range(4):
            self.tr(bt, btv[:, c * n:(c + 1) * n], mixm, mixm.ap[0:n, c * 128:(c + 1) * 128], self.identb)
        self.cp("act", self.mixsT, self.mixsT.ap[:, 8:12, :], bt, tv4)

    def phase_F(self):
        S = self.S
        PB = self.PB
        din, dco, dout = self.din, self.dco, self.dout
        n = NS
        self.arena_reset()
        A = self.arena
        Gt = A("Gt", (128, D), F32)
        Bt = A("Bt", (128, D), F32)
        mhalf = A("mhalf", (128, 4), F32)
        self.dma_in("sp", Gt, Gt.ap, din["ln_g"][0:1, :].broadcast_to([128, D]))
        self.dma_in("sp", Bt, Bt.ap, din["ln_b"][0:1, :].broadcast_to([128, D]))
        self.dma_in("sp", mhalf, mhalf.ap, dco["mhalf"])
        xf = [A("xf", (128, D), F32) for _ in range(3)]
        zz = [A("zz", (128, D), F32) for _ in range(3)]
        st = [A("st", (128, 2, 6), F32) for _ in range(2)]
        mv = [A("mv", (128, 2), F32) for _ in range(2)]
        ve = [A("ve", (128, 1), F32) for _ in range(2)]
        rstd = [A("rstd", (128, 1), F32) for _ in range(2)]
        nmr = [A("nmr", (128, 1), F32) for _ in range(2)]
        gen = self.sample_swa()
        next(gen)

        def pull(k):
            for _ in range(k):
                try:
                    next(gen)
                except StopIteration:
                    return

        for t in range(min(2, NT)):
            self.dma_in("sp", xf[t % 3], xf[t % 3].ap, din["x"][t * 128:(t + 1) * 128, :])
        for t in range(NT):
            tsl = slice(t * 128, (t + 1) * 128)
            x_, z_ = xf[t % 3], zz[t % 3]
            if t + 2 < NT:
                self.dma_in("sp", xf[(t + 2) % 3], xf[(t + 2) % 3].ap, din["x"][(t + 2) * 128:(t + 3) * 128, :])
            hb = [PB[2 * (t % 2)], PB[2 * (t % 2) + 1]]
            for half in range(2):
                pull(1)
                for ec in range(12):
                    if ec < 4:
                        mb, map_ = self.mixR[t], self.A_t[:, ec, tsl]
                    elif ec < 8:
                        mb, map_ = self.mixS, self.mixS_t[:, ec - 4, tsl]
                    else:
                        mb, map_ = self.mixM[t], self.A_t[:, 4 + ec - 8, tsl]
                    wb, wv = self.Wout[ec]
                    self.mm(hb[half], hb[half].ap, mb, map_, wb, wv[:, half * 512:(half + 1) * 512],
                            start=(ec == 0), stop=(ec == 11))
            for half in range(2):
                hs = slice(half * 512, (half + 1) * 512)
                self.stt(z_, z_.ap[:, hs], x_, x_.ap[:, hs], ALPHA, ALU.mult, hb[half], hb[half].ap, ALU.add)
            pull(1)
            self.layernorm(z_, 128, st[t % 2], mv[t % 2], ve[t % 2], rstd[t % 2], nmr[t % 2], mhalf, Gt, Bt, stage=1)
            pull(1)
            if t > 0:
                zp = zz[(t - 1) % 3]
                self.layernorm(zp, 128, None, None, None, None, None, mhalf, Gt, Bt, stage=2)
                self.dma_out("sp", dout["y"][(t - 1) * 128:t * 128, :], zp, zp.ap)
            pull(1)
        zp = zz[(NT - 1) % 3]
        self.layernorm(zp, 128, None, None, None, None, None, mhalf, Gt, Bt, stage=2)
        self.dma_out("sp", dout["y"][(NT - 1) * 128:NT * 128, :], zp, zp.ap)
        pull(1000)
        st, mv, ve, rstd, nmr = st[0], mv[0], ve[0], rstd[0], nmr[0]
        x_, z_ = xf[0], zz[0]
        self.dma_in("sp", x_, x_.ap[0:n], din["xs"])
        hb = [PB[0], PB[1]]
        for half in range(2):
            for ec in range(12):
                wb, wv = self.Wout[ec]
                self.mm(hb[half], hb[half].ap[0:n], self.mixsT, self.mixsT.ap[:, ec, :], wb,
                        wv[:, half * 512:(half + 1) * 512], start=(ec == 0), stop=(ec == 11))
        for half in range(2):
            hs = slice(half * 512, (half + 1) * 512)
            self.stt(z_, z_.ap[0:n, hs], x_, x_.ap[0:n, hs], ALPHA, ALU.mult, hb[half], hb[half].ap[0:n], ALU.add)
        self.layernorm(z_, n, st, mv, ve, rstd, nmr, mhalf, Gt, Bt)
        self.dma_out("sp", dout["ys"], z_, z_.ap[0:n])

    def layernorm(self, z_, n, st, mv, ve, rstd, nmr, mhalf, Gt, Bt, stage=0):
        S = self.S
        if stage in (0, 1):
            for half in range(2):
                hs = slice(half * 512, (half + 1) * 512)
                S.op("dve", (lambda e, half=half, hs=hs: e.bn_stats(out=st.ap[0:n, half, :], in_=z_.ap[0:n, hs])),
                     reads=[z_.tk], writes=[st.tk])
            S.op("dve", lambda e: e.bn_aggr(out=mv.ap[0:n, :], in_=st.ap[0:n, :, :].rearrange("p a b -> p (a b)")),
                 reads=[st.tk], writes=[mv.tk])
            self.ts("dve", ve, ve.ap[0:n], mv, mv.ap[0:n, 1:2], EPS, ALU.add)
            self.tt("pool", rstd, rstd.ap[0:n], ve, ve.ap[0:n], mhalf, mhalf.ap[0:n, 0:1], ALU.pow)
            self.ts("dve", nmr, nmr.ap[0:n], mv, mv.ap[0:n, 0:1], -1.0, ALU.mult, rstd.ap[0:n, 0:1], ALU.mult, extra_reads=[rstd])
            self.ts("dve", z_, z_.ap[0:n], z_, z_.ap[0:n], rstd.ap[0:n, 0:1], ALU.mult, nmr.ap[0:n, 0:1], ALU.add,
                    extra_reads=[rstd, nmr])
        if stage in (0, 2):
            self.tt("dve", z_, z_.ap[0:n], z_, z_.ap[0:n], Gt, Gt.ap[0:n], ALU.mult)
            self.tt("dve", z_, z_.ap[0:n], z_, z_.ap[0:n], Bt, Bt.ap[0:n], ALU.add)

    def sample_swa(self):
        S = self.S
        PB = self.PB
        din, dco = self.din, self.dco
        n = NS
        A = self.arena
        sqs, sks, Gss, Vnew = self.sqs, self.sks, self.Gss, self.Vnew
        smask = A("smask", (128, 9, 4), BF16)
        smaskn = A("smaskn", (n, 4, 4), BF16)
        ones = A("ones", (128, 2), BF16)
        sqsT = A("sqsT", (128, 4, n), BF16)
        sksT = A("sksT", (128, 4, n), BF16)
        GssT = A("GssT", (128, 4, n), BF16)
        Qbd = A("Qbd", (128, 4, 4, 8), BF16)
        Kc = A("Kc", (128, 9, 512), BF16)
        KcT = A("KcT", (128, 9, 4, 128), BF16)
        Vc = A("Vc", (128, 9, 512), BF16)
        PTx = [A("PTx", (128, 10, 8, 4), BF16) for _ in range(2)]
        rls = A("rlx", (4, 8), F32)
        Onb_ap = KcT.ap[0:4, 0, :, :].rearrange("p c k -> p (c k)")
        bt = PB[4]
        btv = bt.ap.bitcast(BF16)
        bsx = PB[5]
        UB = [PB[6], PB[7]]

        def issue_k(b):
            ck = din["ck"][b]
            self.dma_in("pool", Kc, Kc.ap[:, 0:4, :], ck.rearrange("(m s) c -> m s c", s=16)[:, 0:4, :], parallel=True)
            self.dma_in("pool", Kc, Kc.ap[:, 4:8, :], ck[1536:2048, :].rearrange("(m s) c -> m s c", s=4), parallel=True)
            self.dma_in("pool", Kc, Kc.ap[:, 8, :], ck[1920:2048, :], parallel=True)

        def issue_v(b):
            cv = din["cv"][b]
            self.dma_in("pool", Vc, Vc.ap[:, 0:4, :], cv.rearrange("(m s) c -> m s c", s=16)[:, 0:4, :], parallel=True)
            self.dma_in("pool", Vc, Vc.ap[:, 4:8, :], cv[1536:2048, :].rearrange("(m s) c -> m s c", s=4), parallel=True)
            self.dma_in("pool", Vc, Vc.ap[:, 8, :], cv[1920:2048, :], parallel=True)

        self.dma_in("pool", smask, smask.ap, dco["smask"])
        self.dma_in("pool", smaskn, smaskn.ap, dco["smaskn"])
        self.memset("pool", ones, ones.ap, 1.0)
        self.memset("pool", Qbd, Qbd.ap, 0.0)
        for p_ in PTx:
            self.memset("pool", p_, p_.ap, 0.0)
        issue_k(0)
        issue_v(0)
        yield
        tv4 = btv[:, 0:4 * n].rearrange("p (c t) -> p c t", c=4)
        for src, dst in ((sqs, sqsT), (sks, sksT), (Gss, GssT)):
            for c in range(4):
                self.tr(bt, btv[:, c * n:(c + 1) * n], src, src.ap[:, c * 128:(c + 1) * 128], self.identb)
            self.cp("act", dst, dst.ap, bt, tv4)
        self.cp("pool", Qbd, Qbd.ap[0:64, :, :, 0:4], sqsT, sqsT.ap[0:64, :, :].rearrange("p c (b i) -> p c b i", b=4))
        self.cp("pool", Qbd, Qbd.ap[64:128, :, :, 4:8], sqsT, sqsT.ap[64:128, :, :].rearrange("p c (b i) -> p c b i", b=4))
        yield
        for b in range(4):
            pt = PTx[b % 2]
            for tl in range(9):
                for c in range(4):
                    self.tr(bt, btv[:, c * 128:(c + 1) * 128], Kc, Kc.ap[:, tl, c * 128:(c + 1) * 128], self.identb)
                self.cp("act", KcT, KcT.ap[:, tl, :, :], bt,
                        btv[:, 0:512].rearrange("p (c k) -> p c k", c=4))
                yield
            if b + 1 < 4:
                issue_k(b + 1)
            for tl in range(9):
                for c in range(4):
                    self.mm(bsx, bsx.ap[:, tl * 32 + c * 8:tl * 32 + c * 8 + 8], KcT, KcT.ap[:, tl, c, :], Qbd, Qbd.ap[:, c, b, :])
            for c in range(4):
                self.mm(bsx, bsx.ap[0:n, 288 + c * 8:288 + c * 8 + 8], sksT, sksT.ap[:, c, :], Qbd, Qbd.ap[:, c, b, :])
            yield
            self.act(pt, pt.ap[:, 0:9, :, :], bsx, bsx.ap[:, 0:288].rearrange("p (t h i) -> p t h i", t=9, h=8), AF.Exp, scale=0.125)
            self.act(pt, pt.ap[0:n, 9, :, :], bsx, bsx.ap[0:n, 288:320].rearrange("p (h i) -> p h i", h=8), AF.Exp, scale=0.125)
            self.tt("pool", pt, pt.ap[:, 0:9, :, :], pt, pt.ap[:, 0:9, :, :], smask,
                    smask.ap.unsqueeze(2).broadcast_to([128, 9, 8, 4]), ALU.mult)
            self.tt("pool", pt, pt.ap[0:n, 9, :, :], pt, pt.ap[0:n, 9, :, :], smaskn,
                    smaskn.ap[:, b, :].unsqueeze(1).broadcast_to([n, 8, 4]), ALU.mult)
            yield
            for h in range(8):
                ub = UB[h // 4]
                o_ap = ub.ap[0:4, (h % 4) * 64:(h % 4) * 64 + 64]
                for tl in range(9):
                    self.mm(ub, o_ap, pt, pt.ap[:, tl, h, :], Vc, Vc.ap[:, tl, h * 64:(h + 1) * 64], start=(tl == 0), stop=False)
                self.mm(ub, o_ap, pt, pt.ap[0:n, 9, h, :], Vnew, Vnew.ap[:, h, 0:64], start=False, stop=True)
                l_ap = bsx.ap[0:4, 320 + h:321 + h]
                for tl in range(9):
                    self.mm(bsx, l_ap, pt, pt.ap[:, tl, h, :], ones, ones.ap[:, 0:1], start=(tl == 0), stop=False)
                self.mm(bsx, l_ap, pt, pt.ap[0:n, 9, h, :], ones, ones.ap[0:n, 0:1], start=False, stop=True)
                if h % 2 == 1:
                    yield
            if b + 1 < 4:
                issue_v(b + 1)
            S.op("dve", lambda e: e.reciprocal(out=rls.ap, in_=bsx.ap[0:4, 320:328]), reads=[bsx.tk], writes=[rls.tk])
            for j in range(2):
                uv = UB[j].ap[0:4, 0:256].rearrange("p (h f) -> p h f", h=4)
                self.tt("dve", KcT, Onb_ap[:, 256 * j:256 * (j + 1)].rearrange("p (h f) -> p h f", h=4), UB[j], uv,
                        rls, rls.ap[:, 4 * j:4 * j + 4].unsqueeze(2).broadcast_to([4, 4, 64]), ALU.mult)
            for c in range(4):
                self.tr(bt, btv[:, c * 4:(c + 1) * 4], KcT, Onb_ap[:, c * 128:(c + 1) * 128], self.identb)
            self.tt("dve", self.mixsT, self.mixsT.ap[:, 4:8, 4 * b:4 * b + 4], bt, btv[:, 0:16].rearrange("p (c i) -> p c i", c=4),
                    GssT, GssT.ap[:, :, 4 * b:4 * b + 4], ALU.mult)
            yield


_CACHE = {}


def _get_prog(phases):
    key = tuple(phases)
    if key not in _CACHE:
        p = Prog()
        _CACHE[key] = p.build(phases)
    return _CACHE[key]


PHASES = ("S", "R", "M", "F", "X")


def kernel(x_prompt, x_sample, state_ret, cache_swa_k, cache_swa_v, cache_mem_k, cache_mem_v,
           mem_prompt, w_in, w_mem_kv, w_out, ln_gain, ln_bias):
    f = lambda a: np.ascontiguousarray(np.asarray(a, dtype=np.float32))
    consts = {"c_" + k: v for k, v in _consts().items()}
    nc = _get_prog(PHASES)
    in_maps = []
    for c in range(NCORES):
        sb = slice(4 * c, 4 * c + 4)
        m = {
            "x": f(x_prompt[c]), "memx": f(mem_prompt[c]), "w_in": f(w_in[0]), "w_mem": f(w_mem_kv[0]),
            "w_out": f(w_out[0]), "ln_g": f(ln_gain), "ln_b": f(ln_bias),
            "xs": f(np.asarray(x_sample)[sb].reshape(NS, D)),
            "state": f(np.asarray(state_ret)[0, sb]),
            "ck": f(np.asarray(cache_swa_k)[0, sb].reshape(4, 2048, 512)),
            "cv": f(np.asarray(cache_swa_v)[0, sb].reshape(4, 2048, 512)),
            "cmk": f(np.asarray(cache_mem_k)[0, sb].reshape(4, 256, 512)),
            "cmv": f(np.asarray(cache_mem_v)[0, sb].reshape(4, 256, 512)),
        }
        m.update(consts)
        in_maps.append(m)
    res = run_bass_kernel_spmd(nc, in_maps, core_ids=list(range(NCORES)))
    R = res.results
    cat = lambda k: np.stack([np.asarray(R[c][k]) for c in range(NCORES)], axis=0)
    y = cat("y")
    ys = cat("ys").reshape(32, 4, D)
    retp = cat("retp")[None]
    rets = cat("rets").reshape(32, 4, 64, 128)[None]
    kp = cat("kp").reshape(8, SEQ, 8, 64)[None]
    vp = cat("vp").reshape(8, SEQ, 8, 64)[None]
    ks = cat("ks").reshape(32, 4, 8, 64)[None]
    vs = cat("vs").reshape(32, 4, 8, 64)[None]
    mkp = cat("mkp").reshape(8, 256, 4, 128)[None]
    mvp = cat("mvp").reshape(8, 256, 4, 128)[None]
    return (y, ys, retp, rets, kp, vp, ks, vs, mkp, mvp)
```

```python
from contextlib import ExitStack

import numpy as np
import concourse.bass as bass
import concourse.mybir as mybir
from concourse.bass_utils import run_bass_kernel_spmd

F32 = mybir.dt.float32
BF16 = mybir.dt.bfloat16
AF = mybir.ActivationFunctionType
ALU = mybir.AluOpType
AX = mybir.AxisListType

NCORES = 8
D = 1024
SEQ = 2048
NT = 16
DIN = 4608
DMIX = 1536
NS = 16
PAST = 8192
ALPHA = 2.0 ** 0.25
EPS = 1e-5
C_RQ, C_RK, C_RV, C_RG, C_SQ, C_SK, C_SV, C_SG, C_MQ, C_MG = 0, 256, 512, 1024, 1536, 2048, 2560, 3072, 3584, 4096

ENGS = ("pe", "act", "dve", "pool", "sp")
SAME_ENGINE_SYNC = True
SAME_ENGINE_WAR = True


class Tk:
    __slots__ = ("name", "w", "r", "excl", "wg")

    def __init__(self, name):
        self.name = name
        self.w = None
        self.r = {}
        self.wg = []
        self.excl = False


class Buf:
    __slots__ = ("ap", "tk")

    def __init__(self, ap, name):
        self.ap = ap
        self.tk = Tk(name)


class Sched:
    def __init__(self, n_dma_sems):
        self.q = {e: [] for e in ENGS}
        self.known = {e: {} for e in ENGS}
        self.dma_val = [0] * n_dma_sems
        self.rr = 0
        self.rrp = 0
        self.rrq = 0
        self.needed = {e: set() for e in ENGS}

    def _collect(self, eng, reads, writes, par=False):
        deps = []
        for t in reads:
            if t.w is not None:
                deps.append((t.w, True))
            for d in t.wg:
                deps.append((d, True))
            if t.excl:
                for d in t.r.values():
                    if not (d[0] == "e" and d[1] == eng):
                        deps.append((d, True))
        for t in writes:
            if t.w is not None and not (par and t.w[0] == "d" and not t.r):
                deps.append((t.w, True))
                for d in t.wg:
                    deps.append((d, True))
            for d in t.r.values():
                deps.append((d, False))
        kn = self.known[eng]
        best = {}
        for d, is_raw in deps:
            if d[0] == "e" and d[1] == eng:
                if eng == "pe" or not (SAME_ENGINE_SYNC and (is_raw or SAME_ENGINE_WAR)):
                    continue
            key = (d[0], d[1])
            if kn.get(key, -1) >= d[2]:
                continue
            if key not in best or best[key][2] < d[2]:
                best[key] = d
        waits = list(best.values())
        for d in waits:
            kn[(d[0], d[1])] = d[2]
            if d[0] == "e":
                self.needed[d[1]].add(d[2])
        return waits

    def op(self, eng, fn, reads=(), writes=()):
        idx = len(self.q[eng])
        waits = self._collect(eng, reads, writes)
        self.q[eng].append({"fn": fn, "waits": waits, "dma": None})
        me = ("e", eng, idx)
        for t in reads:
            t.r[("e", eng)] = me
        for t in writes:
            t.w = me
            t.wg = []
            t.r = {}
        return idx

    NPRE = 12

    def dma(self, eng, fn, reads=(), writes=(), prefetch=False, parallel=False):
        nn = len(self.dma_val) - self.NPRE
        half = nn // 2
        if prefetch:
            si = nn + self.rrp
            self.rrp = (self.rrp + 1) % self.NPRE
        elif eng == "pool":
            si = half + self.rrq
            self.rrq = (self.rrq + 1) % (nn - half)
        else:
            si = self.rr
            self.rr = (self.rr + 1) % half
        prev = self.dma_val[si]
        new = prev + 16
        self.dma_val[si] = new
        waits = self._collect(eng, reads, writes, par=parallel)
        if prev > 0:
            kn = self.known[eng]
            if kn.get(("d", si), -1) < prev:
                kn[("d", si)] = prev
                waits.append(("d", si, prev))
        self.q[eng].append({"fn": fn, "waits": waits, "dma": si})
        me = ("d", si, new)
        for t in reads:
            t.r[("d", si)] = me
        for t in writes:
            if parallel and t.w is not None and t.w[0] == "d" and not t.r:
                t.wg.append(t.w)
            else:
                t.wg = []
            t.w = me
            t.r = {}

    def barrier(self, final=False):
        last = {}
        for e in ENGS:
            for i in range(len(self.q[e]) - 1, -1, -1):
                ent = self.q[e][i]
                if ent["dma"] is None and ent["fn"] is not None:
                    last[e] = i
                    break
        for e in ENGS:
            waits = []
            kn = self.known[e]
            for e2, idx in last.items():
                if e2 == e:
                    continue
                if kn.get(("e", e2), -1) < idx:
                    kn[("e", e2)] = idx
                    waits.append(("e", e2, idx))
                    self.needed[e2].add(idx)
            for si, v in enumerate(self.dma_val):
                if si >= len(self.dma_val) - self.NPRE and not final:
                    continue
                if v > 0 and kn.get(("d", si), -1) < v:
                    kn[("d", si)] = v
                    waits.append(("d", si, v))
            self.q[e].append({"fn": None, "waits": waits, "dma": None})

    def emit(self, block, eng_sems, dma_sems):
        rank = {}
        for e in ENGS:
            for r, idx in enumerate(sorted(self.needed[e])):
                rank[(e, idx)] = r + 1

        def run(ename, eng):
            for idx, ent in enumerate(self.q[ename]):
                for d in ent["waits"]:
                    if d[0] == "e":
                        eng.wait_ge(eng_sems[d[1]], rank[(d[1], d[2])])
                    else:
                        eng.wait_ge(dma_sems[d[1]], d[2])
                if ent["fn"] is None:
                    continue
                ins = ent["fn"](eng)
                if ent["dma"] is not None:
                    ins.then_inc(dma_sems[ent["dma"]], 16)
                elif (ename, idx) in rank:
                    ins.then_inc(eng_sems[ename], 1)

        block.tensor(lambda eng: run("pe", eng))
        block.scalar(lambda eng: run("act", eng))
        block.vector(lambda eng: run("dve", eng))
        block.gpsimd(lambda eng: run("pool", eng))
        block.sync(lambda eng: run("sp", eng))


def _consts():
    f32 = np.float32
    c = {}
    c["ident"] = np.eye(128, dtype=f32)
    inv = (f32(10000.0) ** (-(np.arange(32, dtype=f32) * f32(2.0) / f32(64)))).astype(f32)
    pos = np.arange(SEQ, dtype=f32)
    ang = (pos[:, None] * inv[None, :]).astype(f32).astype(np.float64)
    cs = np.cos(ang).reshape(NT, 128, 32).transpose(1, 0, 2)
    sn = np.sin(ang).reshape(NT, 128, 32).transpose(1, 0, 2)
    c["cosp"] = np.ascontiguousarray(cs).astype(f32)
    c["sinp"] = np.ascontiguousarray(np.stack([-sn, sn], axis=2)).astype(f32)
    poss = (PAST + np.arange(4)).astype(f32)
    angs = (poss[:, None] * inv[None, :]).astype(f32).astype(np.float64)
    c["coss"] = np.tile(np.cos(angs), (4, 1)).astype(f32)
    sns = np.tile(np.sin(angs), (4, 1))
    c["sins"] = np.stack([-sns, sns], axis=1).astype(f32)
    lg = np.log1p(-np.exp2(-5.0 - np.arange(4, dtype=np.float64)))
    p1 = np.arange(1, 129, dtype=np.float64)[:, None]
    dq = np.exp(p1 * lg[None, :])
    dk = np.exp(-p1 * lg[None, :]) * 0.125
    c["dqk"] = np.concatenate([dq, dk], axis=1).astype(f32)
    g128 = np.exp(128.0 * lg)
    G = np.zeros((128, 4, 128), dtype=np.float64)
    for h in range(4):
        G[(h % 2) * 64:(h % 2) * 64 + 64, h, :] = g128[h]
    c["g128"] = G.reshape(128, 512).astype(f32)
    jj = np.arange(128)[:, None]
    ii = np.arange(128)[None, :]
    c["cmask"] = (ii >= jj).astype(f32)
    c["swamask"] = np.concatenate([(ii >= jj), (ii <= jj)], axis=1).astype(f32)
    sel = np.zeros((8, 4, 128), dtype=f32)
    for h in range(8):
        sel[h, h // 2, (h % 2) * 64:(h % 2) * 64 + 64] = 1.0
    c["sel"] = sel.reshape(8, 512)
    c["mhalf"] = np.full((128, 4), -0.5, dtype=f32)
    i4 = np.tile(np.arange(4, dtype=np.float64), 4)[:, None]
    dqs = np.exp((i4 + 1.0) * lg[None, :])
    dks = np.exp(-(i4 + 1.0) * lg[None, :]) * 0.125
    c["dqks"] = np.concatenate([dqs, dks], axis=1).astype(f32)
    r16 = np.arange(16)
    bj, ij = r16 // 4, r16 % 4
    c["rmask"] = ((bj[:, None] == bj[None, :]) & (ij[None, :] >= ij[:, None])).astype(f32)
    c["bmask"] = (bj[:, None] == np.arange(4)[None, :]).astype(f32)
    sm = np.zeros((128, 9, 4), dtype=f32)
    for i in range(4):
        sm[:, i, i] = 1.0
        sm[:, 4 + i, i] = 1.0
        sm[:, 8, i] = (np.arange(128) >= i).astype(f32)
    c["smask"] = sm
    smn = np.zeros((16, 4, 4), dtype=f32)
    for kk in range(16):
        for b in range(4):
            for i in range(4):
                if kk // 4 == b:
                    j = kk % 4
                    smn[kk, b, i] = (1.0 if j <= i else 0.0) + (2.0 if j == i else 0.0)
    c["smaskn"] = smn
    return c


CONST_SHAPES = {
    "ident": (128, 128), "cosp": (128, 16, 32), "sinp": (128, 16, 2, 32), "coss": (16, 32), "sins": (16, 2, 32),
    "dqk": (128, 8), "g128": (128, 512), "cmask": (128, 128), "swamask": (128, 256), "sel": (8, 512),
    "mhalf": (128, 4), "dqks": (16, 8), "rmask": (16, 16), "bmask": (16, 4), "smask": (128, 9, 4), "smaskn": (16, 4, 4),
}
IN_SHAPES = {
    "x": (SEQ, D), "memx": (256, D), "w_in": (D, DIN), "w_mem": (D, 1024), "w_out": (DMIX, D),
    "ln_g": (1, D), "ln_b": (1, D), "xs": (NS, D), "state": (4, 4, 64, 128),
    "ck": (4, 2048, 512), "cv": (4, 2048, 512), "cmk": (4, 256, 512), "cmv": (4, 256, 512),
}
OUT_SHAPES = {
    "y": (SEQ, D), "ys": (NS, D), "retp": (4, 64, 128), "rets": (4, 4, 64, 128),
    "kp": (SEQ, 512), "vp": (SEQ, 512), "ks": (NS, 512), "vs": (NS, 512),
    "mkp": (256, 512), "mvp": (256, 512),
}


def tok_slice(g, b, nblk=1):
    cnt = 128 * nblk
    if g == 1:
        start = 128 * b
    elif g == 4:
        r, n = divmod(b, 4)
        start = 512 * n + r
    else:
        start = b
    return slice(start, start + g * (cnt - 1) + 1, g)


class Prog:
    def __init__(self):
        self.nc = bass.Bass("TRN2", target_bir_lowering=False)
        nc = self.nc
        self.din = {k: nc.dram_tensor(k, list(s), F32, kind="ExternalInput").ap() for k, s in IN_SHAPES.items()}
        self.dco = {k: nc.dram_tensor("c_" + k, list(s), F32, kind="ExternalInput").ap() for k, s in CONST_SHAPES.items()}
        self.dout = {k: nc.dram_tensor(k, list(s), F32, kind="ExternalOutput").ap() for k, s in OUT_SHAPES.items()}
        self.vscr = nc.dram_tensor("vscr", [SEQ, 520], BF16, kind="Internal").ap()
        self.NDMA = 64
        self.S = Sched(self.NDMA)
        self.es = ExitStack()
        self.eng_sems = {e: self.es.enter_context(nc.semaphore("sem_" + e)) for e in ENGS}
        self.dma_sems = [self.es.enter_context(nc.semaphore("dsem%d" % i)) for i in range(self.NDMA)]
        self.uid = 0

    def sb(self, name, shape, dt):
        return self.es.enter_context(self.nc.sbuf_tensor(name, list(shape), dt))

    def buf(self, name, shape, dt):
        return Buf(self.sb(name, shape, dt)[:], name)

    def arena_reset(self):
        self.aoff = 0

    def arena(self, name, shape, dt):
        n = 1
        for s in shape[1:]:
            n *= s
        words = n if dt == F32 else (n + 1) // 2
        words = (words + 15) // 16 * 16
        assert self.aoff + words <= self.AW, (name, self.aoff, words, self.AW)
        v = self.arena_t[:, self.aoff:self.aoff + words]
        self.aoff += words
        if dt != F32:
            v = v.bitcast(dt)
        v = v[:, 0:n]
        if len(shape) > 2:
            names = " ".join("d%d" % i for i in range(len(shape) - 1))
            v = v.rearrange("p (%s) -> p %s" % (names, names), **{"d%d" % i: shape[i + 1] for i in range(len(shape) - 1)})
        if shape[0] < 128:
            v = v[0:shape[0]]
        self.uid += 1
        return Buf(v, "%s_%d" % (name, self.uid))

    def dma_in(self, q, dst, dst_ap, src_ap, parallel=False):
        self.S.dma(q, lambda e: e.dma_start(out=dst_ap, in_=src_ap), reads=[], writes=[dst.tk], parallel=parallel)

    def dma_out(self, q, dst_ap, src, src_ap):
        self.S.dma(q, lambda e: e.dma_start(out=dst_ap, in_=src_ap), reads=[src.tk], writes=[])

    def mm(self, out, out_ap, lhsT, lhsT_ap, rhs, rhs_ap, start=True, stop=True, extra_reads=()):
        self.S.op("pe", lambda e: e.matmul(out_ap, lhsT=lhsT_ap, rhs=rhs_ap, start=start, stop=stop),
                  reads=[lhsT.tk, rhs.tk] + list(extra_reads), writes=[out.tk])

    def tr(self, out, out_ap, in_, in_ap, ident):
        n = in_ap.shape[0]
        self.S.op("pe", lambda e: e.transpose(out=out_ap, in_=in_ap, identity=ident.ap[0:n, 0:n]),
                  reads=[in_.tk, ident.tk], writes=[out.tk])

    def act(self, out, out_ap, in_, in_ap, func, scale=1.0, bias=0.0, extra_reads=()):
        self.S.op("act", lambda e: e.activation(out=out_ap, in_=in_ap, func=func, scale=scale, bias=bias),
                  reads=[in_.tk] + [b.tk for b in extra_reads], writes=[out.tk])

    def tt(self, eng, out, out_ap, a, a_ap, b, b_ap, op, extra_reads=()):
        self.S.op(eng, lambda e: e.tensor_tensor(out=out_ap, in0=a_ap, in1=b_ap, op=op),
                  reads=[a.tk, b.tk] + list(extra_reads), writes=[out.tk])

    def ts(self, eng, out, out_ap, a, a_ap, s1, op0, s2=None, op1=None, extra_reads=()):
        if op1 is None:
            fn = lambda e: e.tensor_scalar(out=out_ap, in0=a_ap, scalar1=s1, scalar2=None, op0=op0)
        else:
            fn = lambda e: e.tensor_scalar(out=out_ap, in0=a_ap, scalar1=s1, scalar2=s2, op0=op0, op1=op1)
        self.S.op(eng, fn, reads=[a.tk] + [b.tk for b in extra_reads], writes=[out.tk])

    def stt(self, out, out_ap, a, a_ap, scalar, op0, b, b_ap, op1, extra_reads=()):
        self.S.op("dve", lambda e: e.scalar_tensor_tensor(out=out_ap, in0=a_ap, scalar=scalar, op0=op0, in1=b_ap, op1=op1),
                  reads=[a.tk, b.tk] + [x.tk for x in extra_reads], writes=[out.tk])

    def cp(self, eng, out, out_ap, in_, in_ap, extra_reads=()):
        if eng == "act":
            fn = lambda e: e.copy(out=out_ap, in_=in_ap)
        else:
            fn = lambda e: e.tensor_copy(out=out_ap, in_=in_ap)
        self.S.op(eng, fn, reads=[in_.tk] + list(extra_reads), writes=[out.tk])

    def memset(self, eng, out, out_ap, val):
        self.S.op(eng, lambda e: e.memset(out_ap, val), reads=[], writes=[out.tk])

    def load_w(self, slot, src, col0, ncols=512, row0=0):
        wb = self.W[slot]
        self.S.dma("pool", (lambda e: e.dma_start(out=wb.ap[:, :, 0:ncols],
                                                  in_=src.rearrange("(k p) c -> p k c", p=128)[:, :, col0:col0 + ncols])),
                   reads=[], writes=[wb.tk], prefetch=True)

    def load_w1(self, slot, src, col0, k, ncols=512):
        wb = self.W[slot]
        self.S.dma("pool", (lambda e: e.dma_start(out=wb.ap[:, k, 0:ncols], in_=src[k * 128:(k + 1) * 128, col0:col0 + ncols])),
                   reads=[], writes=[wb.tk], prefetch=True)

    def proj(self, bank, ntok, xt_reads, xt_ap, wslot, wc0=0, ncols=512):
        wb = self.W[wslot]
        for k in range(8):
            self.S.op("pe", (lambda e, k=k: e.matmul(bank.ap[0:ntok, 0:ncols], lhsT=xt_ap[:, k, :],
                                                     rhs=wb.ap[:, k, wc0:wc0 + ncols], start=(k == 0), stop=(k == 7))),
                      reads=list(xt_reads) + [wb.tk], writes=[bank.tk])

    def rope(self, bank, ntok, cos_b, cos_ap, sin_b, sin_ap, t1, t2, out, out_ap, nh=8):
        X = bank.ap[0:ntok, 0:nh * 64].rearrange("p (h two f) -> p h two f", h=nh, two=2)
        T1 = t1.ap[0:ntok, 0:nh * 64].rearrange("p (h two f) -> p h two f", h=nh, two=2)
        T2 = t2.ap[0:ntok, 0:nh * 64].rearrange("p (h two f) -> p h two f", h=nh, two=2)
        O = out_ap.rearrange("p (h two f) -> p h two f", h=nh, two=2)
        self.tt("dve", t1, T1, bank, X, cos_b, cos_ap, ALU.mult)
        self.tt("dve", t2, T2, bank, X[:, :, ::-1, :], sin_b, sin_ap, ALU.mult)
        self.tt("dve", out, O, t1, T1, t2, T2, ALU.add)

    def build(self, phases):
        self.phases = phases
        nc = self.nc
        S = self.S
        din, dco, dout = self.din, self.dco, self.dout
        self.ps_t = self.es.enter_context(nc.psum_tensor("psum", [128, 4096], F32))
        self.PB = [Buf(self.ps_t[:, i * 512:(i + 1) * 512], "bank%d" % i) for i in range(8)]
        for b_ in self.PB:
            b_.tk.excl = True
        PB = self.PB
        n = NS

        self.xT_t = self.sb("xT", (128, 8, SEQ), BF16)
        self.xT = [Buf(self.xT_t[:, :, t * 128:(t + 1) * 128], "xT%d" % t) for t in range(NT)]
        self.xT_all = [b.tk for b in self.xT]
        self.xsT = self.buf("xsT", (128, 8, NS), BF16)
        self.mixsT = self.buf("mixsT", (128, 12, NS), BF16)
        self.mixS_t = self.sb("mixS", (128, 4, SEQ), BF16)
        self.mixS = Buf(self.mixS_t, "mixS")
        self.NSLOT = 6
        self.W_t = self.sb("W", (128, self.NSLOT * 4096), BF16)
        self.W = [Buf(self.W_t[:, i * 4096:(i + 1) * 4096].rearrange("p (k c) -> p k c", k=8), "W%d" % i)
                  for i in range(self.NSLOT)]
        self.identb = self.buf("identb", (128, 128), BF16)
        self.identf = self.buf("identf", (128, 128), F32)
        self.cosp = self.buf("cosp", (128, 16, 32), F32)
        self.sinp = self.buf("sinp", (128, 16, 2, 32), F32)
        self.mkT = self.buf("mkT", (128, 4, 256), BF16)
        self.mvaug = self.buf("mvaug", (128, 2, 4, 129), BF16)
        self.sqs = self.buf("sqs", (n, 512), BF16)
        self.sks = self.buf("sks", (n, 512), BF16)
        self.Vnew = self.buf("Vnew", (n, 8, 65), BF16)
        self.Gss = self.buf("Gss", (n, 512), BF16)
        self.coss = self.buf("coss", (n, 32), F32)
        self.sins = self.buf("sins", (n, 2, 32), F32)
        self.A_t = self.sb("arenaA", (128, 8, SEQ), BF16)
        self.AW = 15872
        self.arena_t = self.sb("arenaB", (128, self.AW), F32)
        self.arena_reset()
        self.mixR = [Buf(self.A_t[:, 0:4, t * 128:(t + 1) * 128], "mixR%d" % t) for t in range(NT)]
        self.mixM = [Buf(self.A_t[:, 4:8, t * 128:(t + 1) * 128], "mixM%d" % t) for t in range(NT)]

        self.dma_in("pool", self.identb, self.identb.ap, dco["ident"])
        self.load_w(0, din["w_in"], C_SQ)
        self.load_w(1, din["w_in"], C_SK)
        self.load_w(2, din["w_in"], C_SV)
        self.load_w(4, din["w_mem"], 0)
        self.load_w(5, din["w_mem"], 512)
        self.load_w(3, din["w_in"], C_SG)
        self.dma_in("sp", self.identf, self.identf.ap, dco["ident"])
        self.dma_in("sp", self.cosp, self.cosp.ap, dco["cosp"])
        self.dma_in("sp", self.sinp, self.sinp.ap, dco["sinp"])
        self.dma_in("sp", self.coss, self.coss.ap, dco["coss"])
        self.dma_in("sp", self.sins, self.sins.ap, dco["sins"])

        xld = [self.arena("xld", (128, D), F32) for _ in range(3)]
        memT = self.arena("memT", (128, 8, 256), BF16)
        mf = [self.arena("mf", (128, 512), F32) for _ in range(2)]
        xsld = self.arena("xsld", (n, D), BF16)

        def load_T(src_ap, ntok, dst, dst_ap, i):
            xb = xld[i % 3]
            self.dma_in("sp", xb, xb.ap[0:ntok], src_ap)
            b0, b1 = PB[(2 * i) % 8], PB[(2 * i + 1) % 8]
            for c in range(8):
                bk = b0 if c < 4 else b1
                self.tr(bk, bk.ap[:, (c % 4) * ntok:(c % 4 + 1) * ntok], xb, xb.ap[0:ntok, c * 128:(c + 1) * 128], self.identf)
            e0, e1 = ("dve", "act") if i % 2 == 0 else ("act", "dve")
            self.cp(e0, dst, dst_ap[:, 0:4, :], b0, b0.ap[:, 0:4 * ntok].rearrange("p (c t) -> p c t", c=4))
            self.cp(e1, dst, dst_ap[:, 4:8, :], b1, b1.ap[:, 0:4 * ntok].rearrange("p (c t) -> p c t", c=4))

        i = 0
        for mt in range(2):
            load_T(din["memx"][mt * 128:(mt + 1) * 128, :], 128, memT, memT.ap[:, :, mt * 128:(mt + 1) * 128], i)
            i += 1
        for t in range(NT):
            load_T(din["x"][t * 128:(t + 1) * 128, :], 128, self.xT[t], self.xT[t].ap, i)
            i += 1
            if t == 15:
                self.dma_in("pool", xsld, xsld.ap, din["xs"])
                bk = PB[(2 * i) % 8]
                pv = bk.ap.bitcast(BF16)
                for c in range(8):
                    self.tr(bk, pv[:, c * n:(c + 1) * n], xsld, xsld.ap[:, c * 128:(c + 1) * 128], self.identb)
                self.cp("dve", self.xsT, self.xsT.ap, bk, pv[:, 0:8 * n].rearrange("p (c t) -> p c t", c=8))
                i += 1
                self.mem_setup(memT, mf)
        S.barrier()

        if "S" in phases:
            self.phase_S()
            S.barrier()
        if "R" in phases:
            self.phase_R()
            S.barrier()
        if "M" in phases:
            self.phase_M()
            S.barrier()
        if "F" in phases:
            self.phase_F()
            S.barrier()
        S.barrier(final=True)
        with nc.Block() as block:
            S.emit(block, self.eng_sems, self.dma_sems)
        self.es.close()
        return nc

    def mem_setup(self, memT, mf):
        S = self.S
        PB = self.PB
        dout = self.dout
        mkT, mvaug = self.mkT, self.mvaug
        self.memset("pool", mvaug, mvaug.ap[:, :, :, 128:129], 1.0)
        i = 0
        for mt in range(2):
            for which in range(2):
                bank = PB[4 + i % 4]
                self.proj(bank, 128, [memT.tk], memT.ap[:, :, mt * 128:(mt + 1) * 128], 4 + which)
                f = mf[i % 2]
                i += 1
                self.cp("act", f, f.ap, bank, bank.ap)
                self.dma_out("sp", dout["mkp" if which == 0 else "mvp"][mt * 128:(mt + 1) * 128, :], f, f.ap)
                if which == 1:
                    self.cp("pool", mvaug, mvaug.ap[:, mt, :, 0:128], f, f.ap.rearrange("p (h f) -> p h f", h=4))
        w0 = self.W[4]
        for h in range(4):
            bank = PB[4 + h % 4]
            for k in range(8):
                S.op("pe", (lambda e, k=k, h=h, bank=bank: e.matmul(bank.ap[:, 0:256], lhsT=w0.ap[:, k, h * 128:(h + 1) * 128],
                                                                     rhs=memT.ap[:, k, :], start=(k == 0), stop=(k == 7))),
                     reads=[w0.tk, memT.tk], writes=[bank.tk])
            self.cp("act", mkT, mkT.ap[:, h, :], bank, bank.ap[:, 0:256])

    def gate(self, bank, n, th, out, out_ap):
        self.act(th, th.ap[0:n], bank, bank.ap[0:n], AF.Tanh, scale=0.5)
        self.stt(out, out_ap, th, th.ap[0:n], 1.0, ALU.add, bank, bank.ap[0:n], ALU.mult)

    def phase_S(self):
        S = self.S
        PB = self.PB
        din, dco, dout = self.din, self.dco, self.dout
        n = NS
        self.arena_reset()
        QKT_t = self.A_t
        QKT = [Buf(QKT_t[:, :, t * 128:(t + 1) * 128], "QKT%d" % t) for t in range(NT)]
        qkt_all = [b.tk for b in QKT]
        Vaug_t = self.arena("Vaug", (128, 16, 8, 65), BF16)
        Vaug = [Buf(Vaug_t.ap[:, b], "Vaug%d" % b) for b in range(16)]
        PT = [self.arena("PT", (128, 8, 256), BF16) for _ in range(3)]
        LT = self.arena("LT", (8, SEQ), F32)
        swam = self.arena("swam", (128, 256), BF16)
        sel = self.arena("sel", (8, 512), F32)
        t1 = self.arena("t1", (128, 512), F32)
        t2 = self.arena("t2", (128, 512), F32)
        qkb = [self.arena("qkb", (128, 1024), BF16) for _ in range(2)]
        kf = [self.arena("kf", (128, 512), F32) for _ in range(2)]
        vf = [self.arena("vf", (128, 512), F32) for _ in range(2)]
        Ub = [self.arena("Ub", (128, 8, 64), BF16) for _ in range(2)]
        lf = [self.arena("lf", (128, 8), F32) for _ in range(2)]
        tha, ga, gb = [t1, t2], kf, vf

        self.dma_in("pool", swam, swam.ap, dco["swamask"])
        self.dma_in("sp", sel, sel.ap, dco["sel"])
        self.memset("pool", Vaug_t, Vaug_t.ap[:, :, :, 64:65], 1.0)
        for b in Vaug:
            b.tk.w = Vaug_t.tk.w

        vscr_tk = Tk("vscr")
        def s1_front(t):
            xt = self.xT[t]
            bq, bk, bv = PB[(2 * t) % 4], PB[(2 * t + 1) % 4], PB[4 + t % 2]
            cos4 = self.cosp.ap[:, t, :].unsqueeze(1).unsqueeze(1).broadcast_to([128, 8, 2, 32])
            sin4 = self.sinp.ap[:, t, :, :].unsqueeze(1).broadcast_to([128, 8, 2, 32])
            qk = qkb[t % 2]
            self.proj(bq, 128, [xt.tk], xt.ap, 0)
            self.rope(bq, 128, self.cosp, cos4, self.sinp, sin4, t1, t2, qk, qk.ap[:, 0:512])
            self.proj(bk, 128, [xt.tk], xt.ap, 1)
            self.rope(bk, 128, self.cosp, cos4, self.sinp, sin4, t1, t2, kf[t % 2], kf[t % 2].ap)
            self.dma_out("sp", dout["kp"][t * 128:(t + 1) * 128, :], kf[t % 2], kf[t % 2].ap)
            self.cp("pool", qk, qk.ap[:, 512:1024], kf[t % 2], kf[t % 2].ap)
            self.proj(bv, 128, [xt.tk], xt.ap, 2)
            self.cp("act", vf[t % 2], vf[t % 2].ap, bv, bv.ap)
            self.dma_out("sp", dout["vp"][t * 128:(t + 1) * 128, :], vf[t % 2], vf[t % 2].ap)
            self.cp("pool", Vaug[t], Vaug[t].ap[:, :, 0:64], vf[t % 2], vf[t % 2].ap.rearrange("p (h f) -> p h f", h=8))
            S.dma("sp", (lambda e, t=t: e.dma_start(out=self.vscr[t * 128:(t + 1) * 128, :],
                                                    in_=Vaug[t].ap.rearrange("p h f -> p (h f)"))),
                  reads=[Vaug[t].tk], writes=[vscr_tk])

        def s1_back(t):
            bt = PB[6 + t % 2]
            qk = qkb[t % 2]
            btv = bt.ap.bitcast(BF16)
            for c in range(8):
                self.tr(bt, btv[:, c * 128:(c + 1) * 128], qk, qk.ap[:, c * 128:(c + 1) * 128], self.identb)
            self.cp("act", QKT[t], QKT[t].ap, bt, btv.rearrange("p (c t) -> p c t", c=8))

        for t in range(NT):
            s1_front(t)
            if t > 0:
                s1_back(t - 1)
        s1_back(NT - 1)
        cos4 = self.coss.ap.unsqueeze(1).unsqueeze(1).broadcast_to([n, 8, 2, 32])
        sin4 = self.sins.ap.unsqueeze(1).broadcast_to([n, 8, 2, 32])
        self.proj(PB[0], n, [self.xsT.tk], self.xsT.ap, 0)
        self.rope(PB[0], n, self.coss, cos4, self.sins, sin4, t1, t2, self.sqs, self.sqs.ap)
        self.proj(PB[1], n, [self.xsT.tk], self.xsT.ap, 1)
        self.rope(PB[1], n, self.coss, cos4, self.sins, sin4, t1, t2, kf[0], kf[0].ap[0:n])
        self.dma_out("sp", dout["ks"], kf[0], kf[0].ap[0:n])
        self.cp("pool", self.sks, self.sks.ap, kf[0], kf[0].ap[0:n])
        self.proj(PB[4], n, [self.xsT.tk], self.xsT.ap, 2)
        self.cp("act", vf[0], vf[0].ap[0:n], PB[4], PB[4].ap[0:n])
        self.dma_out("sp", dout["vs"], vf[0], vf[0].ap[0:n])
        self.memset("pool", self.Vnew, self.Vnew.ap[:, :, 64:65], 1.0)
        self.cp("pool", self.Vnew, self.Vnew.ap[:, :, 0:64], vf[0], vf[0].ap[0:n].rearrange("p (h f) -> p h f", h=8))
        self.proj(PB[5], n, [self.xsT.tk], self.xsT.ap, 3)
        self.gate(PB[5], n, t1, t2, t2.ap[0:n])
        self.ts("dve", self.Gss, self.Gss.ap, t2, t2.ap[0:n], 0.5, ALU.mult)
        pre = []
        for slot, col in ((4, C_RQ), (5, C_RV), (0, C_RG), (1, C_MQ)):
            for k in range(8):
                pre.append((slot, col, k))

        STB = PB[0:4]
        UB = [PB[4], PB[5]]
        TBa = PB[6]
        VB = PB[7]
        TBl = VB
        entries = []
        for g in (1, 4, 16):
            seqs = {1: [list(range(16))], 4: [[4 * r + nn for nn in range(4)] for r in range(4)],
                    16: [[r] for r in range(16)]}[g]
            for seq in seqs:
                for si, kb in enumerate(seq):
                    entries.append((g, seq, si, kb))
        NE = len(entries)
        vdone = {1: True}

        nextg = {1: 4, 4: 16}

        def vreload(g, b):
            tsl = tok_slice(g, b)
            S.dma("sp", (lambda e, b=b, tsl=tsl: e.dma_start(out=Vaug[b].ap.rearrange("p h f -> p (h f)"),
                                                            in_=self.vscr[tsl, :])),
                  reads=[vscr_tk], writes=[Vaug[b].tk])

        def stage_A(i):
            g, seq, si, kb = entries[i]
            last = (si == len(seq) - 1)
            nq = 128 if last else 256
            ksl = tok_slice(g, kb)
            qsl = tok_slice(g, kb, 1 if last else 2)
            pt = PT[i % 3]
            for h in range(8):
                c, half = h // 2, h % 2
                bank = STB[2 * (h // 4) + (h % 2)]
                slot = (h // 2) % 2
                self.mm(bank, bank.ap[:, slot * 256:slot * 256 + nq],
                        QKT[0], QKT_t[half * 64:(half + 1) * 64, 4 + c, ksl],
                        QKT[0], QKT_t[half * 64:(half + 1) * 64, c, qsl], extra_reads=qkt_all)
            for j in range(2):
                i_ap = self.ps_t[:, (2 * j) * 512:(2 * j + 2) * 512].rearrange("p (b s q) -> p b s q", b=2, s=2)[:, :, :, 0:nq]
                o_ap = pt.ap[:, 4 * j:4 * j + 4, 0:nq].rearrange("p (s b) q -> p b s q", b=2)
                S.op("act", (lambda e, i_ap=i_ap, o_ap=o_ap: e.activation(out=o_ap, in_=i_ap, func=AF.Exp, scale=0.125)),
                     reads=[STB[2 * j].tk, STB[2 * j + 1].tk], writes=[pt.tk])
            self.tt("pool", pt, pt.ap[:, :, 0:nq], pt, pt.ap[:, :, 0:nq], swam,
                    swam.ap[:, 0:nq].unsqueeze(1).broadcast_to([128, 8, nq]), ALU.mult)

        def stage_B(i):
            g, seq, si, kb = entries[i]
            pt = PT[i % 3]
            ptp = PT[(i - 1) % 3]
            for h in range(8):
                ub = UB[h // 4]
                o_ap = ub.ap[:, (h % 4) * 65:(h % 4) * 65 + 65]
                if si > 0:
                    self.mm(ub, o_ap, ptp, ptp.ap[:, h, 128:256], Vaug[seq[si - 1]], Vaug[seq[si - 1]].ap[:, h, :],
                            start=True, stop=False)
                    self.mm(ub, o_ap, pt, pt.ap[:, h, 0:128], Vaug[kb], Vaug[kb].ap[:, h, :], start=False, stop=True)
                else:
                    self.mm(ub, o_ap, pt, pt.ap[:, h, 0:128], Vaug[kb], Vaug[kb].ap[:, h, :], start=True, stop=True)
            u = Ub[i % 2]
            l = lf[i % 2]
            for j in range(2):
                uv = UB[j].ap[:, 0:260].rearrange("p (h f) -> p h f", h=4)
                self.cp("dve", u, u.ap[:, 4 * j:4 * j + 4, :], UB[j], uv[:, :, 0:64])
                self.cp("dve", l, l.ap[:, 4 * j:4 * j + 4], UB[j], uv[:, :, 64])
            if g in nextg:
                if si > 0:
                    vreload(nextg[g], seq[si - 1])
                if si == len(seq) - 1:
                    vreload(nextg[g], kb)
            if g == 1 and kb == 0:
                self.load_w(2, din["w_in"], C_MG)
            if pre:
                slot, col, k = pre.pop(0)
                self.load_w1(slot, din["w_in"], col, k)

        def stage_C(i):
            g, seq, si, kb = entries[i]
            ksl = tok_slice(g, kb)
            u = Ub[i % 2]
            l = lf[i % 2]
            TBa = PB[6 + i % 2]
            TBl = TBa
            tav = TBa.ap.bitcast(BF16)
            for c in range(4):
                self.tr(TBa, tav[:, c * 128:(c + 1) * 128], u,
                        u.ap[:, 2 * c:2 * c + 2, :].rearrange("p h f -> p (h f)"), self.identb)
            self.tr(TBl, TBl.ap[0:8, 256:384], l, l.ap, self.identf)
            dstm = self.mixS_t[:, :, ksl]
            dstl = LT.ap[:, ksl]
            srcm = tav[:, 0:512].rearrange("p (c q) -> p c q", c=4)
            if g == 1:
                self.cp("dve", self.mixS, dstm, TBa, srcm)
                self.cp("dve", LT, dstl, TBl, TBl.ap[0:8, 256:384])
            else:
                self.tt("dve", self.mixS, dstm, TBa, srcm, self.mixS, dstm, ALU.add)
                self.tt("dve", LT, dstl, TBl, TBl.ap[0:8, 256:384], LT, dstl, ALU.add)

        for i in range(NE + 2):
            if i < NE:
                stage_A(i)
            if 0 <= i - 1 < NE:
                stage_B(i - 1)
            if 0 <= i - 2 < NE:
                stage_C(i - 2)

        S.op("dve", lambda e: e.reciprocal(out=LT.ap, in_=LT.ap), reads=[LT.tk], writes=[LT.tk])
        i = 0
        for nn in range(4):
            gsl = slice(nn * 512, (nn + 1) * 512)
            for c in range(4):
                bb, bg = PB[(2 * i) % 8], PB[(2 * i + 1) % 8]
                self.mm(bb, bb.ap, sel, sel.ap[:, c * 128:(c + 1) * 128], LT, LT.ap[:, gsl])
                wb = self.W[3]
                for k in range(8):
                    S.op("pe", (lambda e, k=k, c=c, gsl=gsl, bg=bg, wb=wb: e.matmul(
                        bg.ap, lhsT=wb.ap[:, k, c * 128:(c + 1) * 128], rhs=self.xT_t[:, k, gsl],
                        start=(k == 0), stop=(k == 7))), reads=self.xT_all + [wb.tk], writes=[bg.tk])
                th, a, b = tha[i % 2], ga[i % 2], gb[i % 2]
                self.act(th, th.ap, bg, bg.ap, AF.Tanh, scale=0.5)
                self.stt(a, a.ap, th, th.ap, 1.0, ALU.add, bg, bg.ap, ALU.mult)
                self.stt(b, b.ap, self.mixS, self.mixS_t[:, c, gsl], 0.5, ALU.mult, bb, bb.ap, ALU.mult)
                self.tt("dve", self.mixS, self.mixS_t[:, c, gsl], a, a.ap, b, b.ap, ALU.mult)
                i += 1

    def phase_R(self):
        S = self.S
        PB = self.PB
        din, dco, dout = self.din, self.dco, self.dout
        n = NS
        self.arena_reset()
        A = self.arena
        lgs = np.log1p(-np.exp2(-5.0 - np.arange(4, dtype=np.float64)))
        g128 = [float(np.exp(128.0 * v)) for v in lgs]
        g4 = [float(np.exp(4.0 * v)) for v in lgs]
        dqk = A("dqk", (128, 8), F32)
        dqks = A("dqks", (n, 8), F32)
        cmask = A("cmask", (128, 128), BF16)
        rmask = A("rmask", (n, 16), BF16)
        bmask = A("bmask", (n, 4), F32)
        mhalf = A("mhalf", (128, 4), F32)
        Rst = A("Rst", (128, 4, 128), F32)
        Rb = A("Rb", (128, 4, 128), BF16)
        CS = [A("CS", (128, 8, 2, 32), F32) for _ in range(2)]
        SN = [A("SN", (128, 8, 2, 32), F32) for _ in range(2)]
        t1 = A("t1", (128, 512), F32)
        t2 = A("t2", (128, 512), F32)
        QKr = [A("QKr", (128, 512), BF16) for _ in range(2)]
        Vr = [A("Vr", (128, 512), BF16) for _ in range(2)]
        thr = [A("thr", (128, 512), F32) for _ in range(2)]
        Gr = [A("Gr", (128, 512), F32) for _ in range(2)]
        QKrT = [A("QKrT", (128, 4, 128), BF16) for _ in range(2)]
        QZ = [A("QZ", (128, 4, 128), BF16) for _ in range(2)]
        STm = [A("STm", (128, 4, 128), BF16) for _ in range(2)]
        s12 = A("s12", (128, 2, 4), F32)
        st6 = A("st6", (128, 4, 6), F32)
        mvr = A("mvr", (128, 4, 2), F32)
        mean = A("mean", (128, 4), F32)
        msq = A("msq", (128, 4), F32)
        va = A("va", (128, 4), F32)
        ve = A("ve", (128, 4), F32)
        rstd = A("rstd", (128, 4), F32)
        junk = A("junk", (128, 512), BF16)
        mixRs = A("mixRs", (n, 512), BF16)
        On = A("On", (128, 512), F32)
        mixRt = [A("mixRt", (128, 512), BF16) for _ in range(2)]
        Rs = A("Rs", (128, 4, 4, 128), F32)
        Rsb = A("Rsb", (128, 4, 4, 128), BF16)
        QKsT = A("QKsT", (128, 4, n), BF16)
        QZs = A("QZs", (128, 4, n), BF16)
        QZB = A("QZB", (128, 4, 4, n), BF16)
        STs = A("STs", (n, 4, n), BF16)
        KZ = A("KZ", (n, 4, 256), BF16)

        for b_, k_ in ((dqk, "dqk"), (mhalf, "mhalf"), (dqks, "dqks"), (bmask, "bmask")):
            self.dma_in("sp", b_, b_.ap, dco[k_])
        self.dma_in("pool", cmask, cmask.ap, dco["cmask"])
        self.dma_in("pool", rmask, rmask.ap, dco["rmask"])
        Gz = A("Gz", (128, 4, 128), F32)
        self.dma_in("sp", Gz, Gz.ap, dco["g128"].rearrange("p (h f) -> p h f", h=4))
        self.memset("pool", Rst, Rst.ap, 0.0)
        self.memset("pool", Rb, Rb.ap, 0.0)
        for z in QZ:
            self.memset("pool", z, z.ap, 0.0)
        self.memset("pool", Rs, Rs.ap, 0.0)
        self.memset("pool", QZs, QZs.ap, 0.0)
        self.memset("pool", QZB, QZB.ap, 0.0)
        self.memset("pool", Rsb, Rsb.ap, 0.0)
        st_src = din["state"].rearrange("b h k v -> k (b h) v")
        for hp in range(2):
            rows = slice(hp * 64, hp * 64 + 64)
            self.dma_in("sp", Rs, Rs.ap.rearrange("p b h v -> p (b h) v")[rows, hp:16:2, :], st_src[:, hp:16:2, :], parallel=True)
            self.dma_in("pool", Rsb, Rsb.ap.rearrange("p b h v -> p (b h) v")[rows, hp:16:2, :], st_src[:, hp:16:2, :], parallel=True)
        dq4 = dqk.ap.unsqueeze(2).unsqueeze(3).broadcast_to([128, 8, 2, 32])
        WQK, WV, WG = 4, 5, 0

        def headnorm_gate(bo, nn_, gr, mixrt):
            for h in range(4):
                S.op("dve", (lambda e, h=h: e.bn_stats(out=st6.ap[0:nn_, h, :], in_=bo.ap[0:nn_, h * 128:(h + 1) * 128])),
                     reads=[bo.tk], writes=[st6.tk])
            for h in range(4):
                S.op("dve", (lambda e, h=h: e.bn_aggr(out=mvr.ap[0:nn_, h, :], in_=st6.ap[0:nn_, h, :])),
                     reads=[st6.tk], writes=[mvr.tk])
            self.cp("dve", mean, mean.ap[0:nn_], mvr, mvr.ap[0:nn_, :, 0])
            self.ts("dve", ve, ve.ap[0:nn_], mvr, mvr.ap[0:nn_, :, 1], 4.0, ALU.mult, 4.0 * EPS, ALU.add)
            self.tt("pool", rstd, rstd.ap[0:nn_], ve, ve.ap[0:nn_], mhalf, mhalf.ap[0:nn_], ALU.pow)
            for h in range(4):
                hs = slice(h * 128, (h + 1) * 128)
                self.stt(On, On.ap[0:nn_, hs], bo, bo.ap[0:nn_, hs], mean.ap[0:nn_, h:h + 1], ALU.subtract,
                         gr, gr.ap[0:nn_, hs], ALU.mult, extra_reads=[mean])
            for h in range(4):
                hs = slice(h * 128, (h + 1) * 128)
                S.op("act", (lambda e, h=h, hs=hs: e.activation(out=mixrt.ap[0:nn_, hs], in_=On.ap[0:nn_, hs], func=AF.Identity,
                                                               scale=rstd.ap[0:nn_, h:h + 1])),
                     reads=[On.tk, rstd.tk], writes=[mixrt.tk])

        b0, b1, b2, bt, bs, bo = PB[0], PB[1], PB[2], PB[3], PB[4], PB[5]
        bkv = [PB[6], PB[7]]
        btv = bt.ap.bitcast(BF16)
        bsv = bs.ap.bitcast(BF16)

        def r_front(t):
            xt = self.xT[t]
            cs, sn = CS[t % 2], SN[t % 2]
            cos4 = self.cosp.ap[:, t, :].unsqueeze(1).unsqueeze(1).broadcast_to([128, 8, 2, 32])
            sin4 = self.sinp.ap[:, t, :, :].unsqueeze(1).broadcast_to([128, 8, 2, 32])
            self.tt("pool", cs, cs.ap, self.cosp, cos4, dqk, dq4, ALU.mult)
            self.tt("pool", sn, sn.ap, self.sinp, sin4, dqk, dq4, ALU.mult)
            qk, vr, th, gr, qkT, qz = QKr[t % 2], Vr[t % 2], thr[t % 2], Gr[t % 2], QKrT[t % 2], QZ[t % 2]
            self.proj(b0, 128, [xt.tk], xt.ap, WQK)
            self.rope(b0, 128, cs, cs.ap, sn, sn.ap, t1, t2, qk, qk.ap)
            self.proj(b1, 128, [xt.tk], xt.ap, WV)
            self.cp("act", vr, vr.ap, b1, b1.ap)
            self.proj(b2, 128, [xt.tk], xt.ap, WG)
            self.gate(b2, 128, th, gr, gr.ap)
            for c in range(4):
                self.tr(bt, btv[:, c * 128:(c + 1) * 128], qk, qk.ap[:, c * 128:(c + 1) * 128], self.identb)
            tv4 = btv[:, 0:512].rearrange("p (c t) -> p c t", c=4)
            self.cp("act", qkT, qkT.ap, bt, tv4)
            self.cp("act", qz, qz.ap[0:64, 0:4:2, :], bt, tv4[0:64, 0:2, :])
            self.cp("act", qz, qz.ap[64:128, 1:4:2, :], bt, tv4[64:128, 0:2, :])

        def r_back(t):
            qk, vr, gr, qkT, qz, stm, mixrt = QKr[t % 2], Vr[t % 2], Gr[t % 2], QKrT[t % 2], QZ[t % 2], STm[t % 2], mixRt[t % 2]
            for h in range(4):
                self.mm(bs, bs.ap[:, h * 128:(h + 1) * 128], qkT, qkT.ap[:, 2 + h // 2, :], qz, qz.ap[:, h, :])
            self.tt("dve", stm, stm.ap, bs, bs.ap.rearrange("p (h i) -> p h i", h=4), cmask,
                    cmask.ap.unsqueeze(1).broadcast_to([128, 4, 128]), ALU.mult)
            for h in range(4):
                o_ap = bo.ap[:, h * 128:(h + 1) * 128]
                self.mm(bo, o_ap, stm, stm.ap[:, h, :], vr, vr.ap[:, h * 128:(h + 1) * 128], start=True, stop=False)
                self.mm(bo, o_ap, qkT, qkT.ap[:, h // 2, :], Rb, Rb.ap[:, h, :], start=False, stop=True)
            for c in range(2):
                self.mm(bkv[c], bkv[c].ap, qk, qk.ap[:, 256 + c * 128:256 + (c + 1) * 128], vr, vr.ap)
            for c in range(2):
                rv_ = Rst.ap[:, 2 * c:2 * c + 2, :]
                kv_ = bkv[c].ap[:, 2 * c * 128:(2 * c + 2) * 128].rearrange("p (h f) -> p h f", h=2)
                self.tt("dve", Rst, rv_, bkv[c], kv_, Rst, rv_, ALU.add)
                self.tt("dve", Rst, rv_, Rst, rv_, Gz, Gz.ap[:, 2 * c:2 * c + 2, :], ALU.mult)
            self.cp("act", Rb, Rb.ap, Rst, Rst.ap)
            headnorm_gate(bo, 128, gr, mixrt)

        def r_tail(t):
            mixrt = mixRt[t % 2]
            for c in range(4):
                self.tr(bt, btv[:, c * 128:(c + 1) * 128], mixrt, mixrt.ap[:, c * 128:(c + 1) * 128], self.identb)
            self.cp("act", self.mixR[t], self.mixR[t].ap, bt, btv[:, 0:512].rearrange("p (c t) -> p c t", c=4))

        r_front(0)
        for t in range(NT):
            if t + 1 < NT:
                r_front(t + 1)
            r_back(t)
            if t > 0:
                r_tail(t - 1)
            if t == 7:
                self.sample_ret(locals())
        r_tail(NT - 1)
        for h in range(4):
            rows = slice((h % 2) * 64, (h % 2) * 64 + 64)
            self.dma_out("sp", dout["retp"][h], Rst, Rst.ap[rows, h, :])
        self.Wout = []
        for ec in range(12):
            slot = (4, 5, 0)[ec // 4]
            wb = self.W[slot]
            v = self.W_t[:, slot * 4096 + (ec % 4) * 1024:slot * 4096 + (ec % 4 + 1) * 1024]
            self.Wout.append((wb, v))
            S.dma("pool", (lambda e, ec=ec, v=v: e.dma_start(out=v, in_=din["w_out"][ec * 128:(ec + 1) * 128, :])),
                  reads=[], writes=[wb.tk], prefetch=True)

    def sample_ret(self, L):
        S = self.S
        PB = self.PB
        din, dout = self.din, self.dout
        n = NS
        g4 = L["g4"]
        dqks, rmask, bmask, t1, t2 = L["dqks"], L["rmask"], L["bmask"], L["t1"], L["t2"]
        Rs, Rsb, QKsT, QZs, QZB, STs, KZ = L["Rs"], L["Rsb"], L["QKsT"], L["QZs"], L["QZB"], L["STs"], L["KZ"]
        cs, sn = L["CS"][1], L["SN"][1]
        qk, vr, th, gr, mixrt = L["QKr"][1], L["Vr"][1], L["thr"][1], L["Gr"][1], L["mixRs"]
        b0, b1, b2, bt, bs, bo = PB[0], PB[1], PB[2], PB[3], PB[4], PB[5]
        cos4 = self.coss.ap.unsqueeze(1).unsqueeze(1).broadcast_to([n, 8, 2, 32])
        sin4 = self.sins.ap.unsqueeze(1).broadcast_to([n, 8, 2, 32])
        dq4 = dqks.ap.unsqueeze(2).unsqueeze(3).broadcast_to([n, 8, 2, 32])
        self.tt("pool", cs, cs.ap[0:n], self.coss, cos4, dqks, dq4, ALU.mult)
        self.tt("pool", sn, sn.ap[0:n], self.sins, sin4, dqks, dq4, ALU.mult)
        self.proj(b0, n, [self.xsT.tk], self.xsT.ap, L["WQK"])
        self.rope(b0, n, cs, cs.ap[0:n], sn, sn.ap[0:n], t1, t2, qk, qk.ap[0:n])
        self.proj(b1, n, [self.xsT.tk], self.xsT.ap, L["WV"])
        self.cp("act", vr, vr.ap[0:n], b1, b1.ap[0:n])
        self.proj(b2, n, [self.xsT.tk], self.xsT.ap, L["WG"])
        self.gate(b2, n, th, gr, gr.ap[0:n])
        btv = bt.ap.bitcast(BF16)
        for c in range(4):
            self.tr(bt, btv[:, c * n:(c + 1) * n], qk, qk.ap[0:n, c * 128:(c + 1) * 128], self.identb)
        tv4 = btv[:, 0:4 * n].rearrange("p (c t) -> p c t", c=4)
        self.cp("act", QKsT, QKsT.ap, bt, tv4)
        self.cp("dve", QZs, QZs.ap[0:64, 0:4:2, :], bt, tv4[0:64, 0:2, :])
        self.cp("dve", QZs, QZs.ap[64:128, 1:4:2, :], bt, tv4[64:128, 0:2, :])
        for b in range(4):
            self.cp("pool", QZB, QZB.ap[:, b, :, 4 * b:4 * b + 4], QZs, QZs.ap[:, :, 4 * b:4 * b + 4])
        for h in range(4):
            self.mm(bs, bs.ap[0:n, h * n:(h + 1) * n], QKsT, QKsT.ap[:, 2 + h // 2, :], QZs, QZs.ap[:, h, :])
        self.tt("dve", STs, STs.ap, bs, bs.ap[0:n, 0:4 * n].rearrange("p (h i) -> p h i", h=4), rmask,
                rmask.ap.unsqueeze(1).broadcast_to([n, 4, n]), ALU.mult)
        for h in range(4):
            o_ap = bo.ap[0:n, h * 128:(h + 1) * 128]
            self.mm(bo, o_ap, STs, STs.ap[:, h, :], vr, vr.ap[0:n, h * 128:(h + 1) * 128], start=True, stop=False)
            for b in range(4):
                self.mm(bo, o_ap, QZB, QZB.ap[:, b, h, :], Rsb, Rsb.ap[:, b, h, :], start=False, stop=(b == 3))
        for b in range(4):
            self.ts("dve", KZ, KZ.ap[:, b, :], qk, qk.ap[0:n, 256:512], bmask.ap[:, b:b + 1], ALU.mult, extra_reads=[bmask])
        kbanks = [PB[6], PB[7]]
        i = 0
        for b in range(4):
            for c in range(2):
                kb_ = kbanks[i % 2]
                i += 1
                self.mm(kb_, kb_.ap, KZ, KZ.ap[:, b, c * 128:(c + 1) * 128], vr, vr.ap[0:n])
                for hh in range(2):
                    h = 2 * c + hh
                    rows = slice(hh * 64, hh * 64 + 64)
                    self.ts("dve", Rs, Rs.ap[rows, b, h, :], Rs, Rs.ap[rows, b, h, :], g4[h], ALU.mult)
                    self.stt(Rs, Rs.ap[rows, b, h, :], kb_, kb_.ap[rows, h * 128:(h + 1) * 128], g4[h], ALU.mult,
                             Rs, Rs.ap[rows, b, h, :], ALU.add)
        for b in range(4):
            for h in range(4):
                rows = slice((h % 2) * 64, (h % 2) * 64 + 64)
                self.dma_out("sp", dout["rets"][b, h], Rs, Rs.ap[rows, b, h, :])
        L["headnorm_gate"](bo, n, gr, mixrt)
        for c in range(4):
            self.tr(bt, btv[:, c * n:(c + 1) * n], mixrt, mixrt.ap[0:n, c * 128:(c + 1) * 128], self.identb)
        self.cp("act", self.mixsT, self.mixsT.ap[:, 0:4, :], bt, tv4)

    def phase_M(self):
        S = self.S
        PB = self.PB
        din, dco, dout = self.din, self.dco, self.dout
        n = NS
        self.arena_reset()
        A = self.arena
        mkT, mvaug = self.mkT, self.mvaug
        mqT = [A("mqT", (128, 4, 512), BF16) for _ in range(2)]
        PTm = [A("PTm", (128, 4, 2, 512), BF16) for _ in range(2)]
        thm = [A("thm", (128, 512), F32) for _ in range(2)]
        Gm = [A("Gm", (128, 512), F32) for _ in range(2)]
        rl = [A("rl", (128, 4), F32) for _ in range(2)]
        mixMt = [A("mixMt", (128, 512), BF16) for _ in range(2)]
        WMQ, WMG = 1, 2
        rot = [0]

        def nb():
            rot[0] = (rot[0] + 1) % 2
            return PB[rot[0]]

        w2 = self.W[WMQ]
        OBs = [[PB[4], PB[5]], [PB[2], PB[3]]]
        OB = OBs[0]
        bt = PB[6]
        bg = PB[7]
        btv = bt.ap.bitcast(BF16)
        def m_tail(t):
            mixmt = mixMt[t % 2]
            for c in range(4):
                self.tr(bt, btv[:, c * 128:(c + 1) * 128], mixmt, mixmt.ap[:, c * 128:(c + 1) * 128], self.identb)
            self.cp("act", self.mixM[t], self.mixM[t].ap, bt, btv[:, 0:512].rearrange("p (c t) -> p c t", c=4))

        for nn in range(4):
            gsl = slice(nn * 512, (nn + 1) * 512)
            mq = mqT[nn % 2]
            pt = PTm[nn % 2]
            for h in range(4):
                bank = nb()
                for k in range(8):
                    S.op("pe", (lambda e, k=k, h=h, bank=bank, gsl=gsl: e.matmul(
                        bank.ap, lhsT=w2.ap[:, k, h * 128:(h + 1) * 128], rhs=self.xT_t[:, k, gsl],
                        start=(k == 0), stop=(k == 7))), reads=self.xT_all + [w2.tk], writes=[bank.tk])
                self.cp("act", mq, mq.ap[:, h, :], bank, bank.ap)
            for h in range(4):
                for mt in range(2):
                    bank = nb()
                    self.mm(bank, bank.ap, mkT, mkT.ap[:, h, mt * 128:(mt + 1) * 128], mq, mq.ap[:, h, :])
                    self.act(pt, pt.ap[:, h, mt, :], bank, bank.ap, AF.Exp, scale=float(128.0 ** -0.5))
            for tq in range(4):
                t = 4 * nn + tq
                xt = self.xT[t]
                th, gm, r, mixmt = thm[t % 2], Gm[t % 2], rl[t % 2], mixMt[t % 2]
                OB = OBs[t % 2]
                self.proj(bg, 128, [xt.tk], xt.ap, WMG)
                self.gate(bg, 128, th, gm, gm.ap)
                for h in range(4):
                    ob = OB[h // 2]
                    o_ap = ob.ap[:, (h % 2) * 129:(h % 2) * 129 + 129]
                    for mt in range(2):
                        self.mm(ob, o_ap, pt, pt.ap[:, h, mt, tq * 128:(tq + 1) * 128], mvaug, mvaug.ap[:, mt, h, :],
                                start=(mt == 0), stop=(mt == 1))
                for j in range(2):
                    ov = OB[j].ap[:, 0:258].rearrange("p (h f) -> p h f", h=2)
                    S.op("dve", (lambda e, j=j, ov=ov, r=r: e.reciprocal(out=r.ap[:, 2 * j:2 * j + 2], in_=ov[:, :, 128])),
                         reads=[OB[j].tk], writes=[r.tk])
                self.ts("dve", r, r.ap, r, r.ap, 0.5, ALU.mult)
                for h in range(4):
                    hs = slice(h * 128, (h + 1) * 128)
                    ob = OB[h // 2]
                    self.stt(mixmt, mixmt.ap[:, hs], ob, ob.ap[:, (h % 2) * 129:(h % 2) * 129 + 128], r.ap[:, h:h + 1], ALU.mult,
                             gm, gm.ap[:, hs], ALU.mult, extra_reads=[r])
                if t > 0:
                    m_tail(t - 1)
            if nn == 1:
                self.sample_mem(locals())
        m_tail(NT - 1)

    def sample_mem(self, L):
        S = self.S
        PB = self.PB
        din = self.din
        n = NS
        A = self.arena
        WMQ, WMG = L["WMQ"], L["WMG"]
        nb = L["nb"]
        th, gm, rl, mixm = L["thm"][0], L["Gm"][0], L["rl"][0], L["mixMt"][0]
        mqs = A("mqs", (n, 512), BF16)
        mkld = [A("mkld", (128, 2, 512), BF16) for _ in range(2)]
        mkTs = A("mkTs", (128, 4, 4, 256), BF16)
        mvaugs = A("mvaugs", (128, 4, 2, 4, 129), BF16)
        mqsT = A("mqsT", (128, 4, n), BF16)
        PTms = A("PTms", (128, 4, 4, 2, n), BF16)
        bt = L["bt"]
        btv = L["btv"]
        bg = L["bg"]
        self.memset("pool", mvaugs, mvaugs.ap[:, :, :, :, 128:129], 1.0)
        self.memset("pool", PTms, PTms.ap, 0.0)
        for b in range(4):
            ml = mkld[b % 2]
            self.dma_in("pool", ml, ml.ap, din["cmk"][b].rearrange("(t p) c -> p t c", p=128))
            for mt in range(2):
                S.dma("pool", (lambda e, b=b, mt=mt: e.dma_start(
                    out=mvaugs.ap[:, b, mt, :, 0:128],
                    in_=din["cmv"][b][mt * 128:(mt + 1) * 128, :].rearrange("p (h f) -> p h f", h=4))),
                    reads=[], writes=[mvaugs.tk])
            for mt in range(2):
                bank = nb()
                bv = bank.ap.bitcast(BF16)
                for h in range(4):
                    self.tr(bank, bv[:, h * 128:(h + 1) * 128], ml, ml.ap[:, mt, h * 128:(h + 1) * 128], self.identb)
                self.cp("act", mkTs, mkTs.ap[:, b, :, mt * 128:(mt + 1) * 128], bank,
                        bv[:, 0:512].rearrange("p (h m) -> p h m", h=4))
        self.proj(bg, n, [self.xsT.tk], self.xsT.ap, WMQ)
        self.cp("act", mqs, mqs.ap, bg, bg.ap[0:n])
        for c in range(4):
            self.tr(bt, btv[:, c * n:(c + 1) * n], mqs, mqs.ap[:, c * 128:(c + 1) * 128], self.identb)
        tv4 = btv[:, 0:4 * n].rearrange("p (c t) -> p c t", c=4)
        self.cp("act", mqsT, mqsT.ap, bt, tv4)
        self.proj(bg, n, [self.xsT.tk], self.xsT.ap, WMG)
        self.gate(bg, n, th, gm, gm.ap[0:n])
        bsc = nb()
        for b in range(4):
            for h in range(4):
                for mt in range(2):
                    col = ((b * 4 + h) * 2 + mt) * 4
                    self.mm(bsc, bsc.ap[:, col:col + 4], mkTs, mkTs.ap[:, b, h, mt * 128:(mt + 1) * 128],
                            mqsT, mqsT.ap[:, h, 4 * b:4 * b + 4])
        for b in range(4):
            self.act(PTms, PTms.ap[:, b, :, :, 4 * b:4 * b + 4], bsc,
                     bsc.ap[:, b * 32:(b + 1) * 32].rearrange("p (h m i) -> p h m i", h=4, m=2), AF.Exp,
                     scale=float(128.0 ** -0.5))
        OB = L["OB"]
        for h in range(4):
            ob = OB[h // 2]
            o_ap = ob.ap[0:n, (h % 2) * 129:(h % 2) * 129 + 129]
            k = 0
            for b in range(4):
                for mt in range(2):
                    self.mm(ob, o_ap, PTms, PTms.ap[:, b, h, mt, :], mvaugs, mvaugs.ap[:, b, mt, h, :],
                            start=(k == 0), stop=(k == 7))
                    k += 1
        for j in range(2):
            ov = OB[j].ap[0:n, 0:258].rearrange("p (h f) -> p h f", h=2)
            S.op("dve", (lambda e, j=j, ov=ov: e.reciprocal(out=rl.ap[0:n, 2 * j:2 * j + 2], in_=ov[:, :, 128])),
                 reads=[OB[j].tk], writes=[rl.tk])
        self.ts("dve", rl, rl.ap[0:n], rl, rl.ap[0:n], 0.5, ALU.mult)
        for h in range(4):
            hs = slice(h * 128, (h + 1) * 128)
            ob = OB[h // 2]
            self.stt(mixm, mixm.ap[0:n, hs], ob, ob.ap[0:n, (h % 2) * 129:(h % 2) * 129 + 128], rl.ap[0:n, h:h + 1], ALU.mult,
                     gm, gm.ap[0:n, hs], ALU.mult, extra_reads=[rl])
        for c in range(4):
            self.tr(bt, btv[:, c * n:(c + 1) * n], mixm, mixm.ap[0:n, c * 128:(c + 1) * 128], self.identb)
        self.cp("act", self.mixsT, self.mixsT.ap[:, 8:12, :], bt, tv4)

    def phase_F(self):
        S = self.S
        PB = self.PB
        din, dco, dout = self.din, self.dco, self.dout
        n = NS
        self.arena_reset()
        A = self.arena
        Gt = A("Gt", (128, D), F32)
        Bt = A("Bt", (128, D), F32)
        mhalf = A("mhalf", (128, 4), F32)
        self.dma_in("sp", Gt, Gt.ap, din["ln_g"][0:1, :].broadcast_to([128, D]))
        self.dma_in("sp", Bt, Bt.ap, din["ln_b"][0:1, :].broadcast_to([128, D]))
        self.dma_in("sp", mhalf, mhalf.ap, dco["mhalf"])
        xf = [A("xf", (128, D), F32) for _ in range(3)]
        zz = [A("zz", (128, D), F32) for _ in range(3)]
        st = [A("st", (128, 2, 6), F32) for _ in range(2)]
        mv = [A("mv", (128, 2), F32) for _ in range(2)]
        ve = [A("ve", (128, 1), F32) for _ in range(2)]
        rstd = [A("rstd", (128, 1), F32) for _ in range(2)]
        nmr = [A("nmr", (128, 1), F32) for _ in range(2)]
        gen = self.sample_swa()
        next(gen)

        def pull(k):
            for _ in range(k):
                try:
                    next(gen)
                except StopIteration:
                    return

        for t in range(min(2, NT)):
            self.dma_in("sp", xf[t % 3], xf[t % 3].ap, din["x"][t * 128:(t + 1) * 128, :])
        for t in range(NT):
            tsl = slice(t * 128, (t + 1) * 128)
            x_, z_ = xf[t % 3], zz[t % 3]
            if t + 2 < NT:
                self.dma_in("sp", xf[(t + 2) % 3], xf[(t + 2) % 3].ap, din["x"][(t + 2) * 128:(t + 3) * 128, :])
            hb = [PB[2 * (t % 2)], PB[2 * (t % 2) + 1]]
            for half in range(2):
                pull(1)
                for ec in range(12):
                    if ec < 4:
                        mb, map_ = self.mixR[t], self.A_t[:, ec, tsl]
                    elif ec < 8:
                        mb, map_ = self.mixS, self.mixS_t[:, ec - 4, tsl]
                    else:
                        mb, map_ = self.mixM[t], self.A_t[:, 4 + ec - 8, tsl]
                    wb, wv = self.Wout[ec]
                    self.mm(hb[half], hb[half].ap, mb, map_, wb, wv[:, half * 512:(half + 1) * 512],
                            start=(ec == 0), stop=(ec == 11))
            for half in range(2):
                hs = slice(half * 512, (half + 1) * 512)
                self.stt(z_, z_.ap[:, hs], x_, x_.ap[:, hs], ALPHA, ALU.mult, hb[half], hb[half].ap, ALU.add)
            pull(1)
            self.layernorm(z_, 128, st[t % 2], mv[t % 2], ve[t % 2], rstd[t % 2], nmr[t % 2], mhalf, Gt, Bt, stage=1)
            pull(1)
            if t > 0:
                zp = zz[(t - 1) % 3]
                self.layernorm(zp, 128, None, None, None, None, None, mhalf, Gt, Bt, stage=2)
                self.dma_out("sp", dout["y"][(t - 1) * 128:t * 128, :], zp, zp.ap)
            pull(1)
        zp = zz[(NT - 1) % 3]
        self.layernorm(zp, 128, None, None, None, None, None, mhalf, Gt, Bt, stage=2)
        self.dma_out("sp", dout["y"][(NT - 1) * 128:NT * 128, :], zp, zp.ap)
        pull(1000)
        st, mv, ve, rstd, nmr = st[0], mv[0], ve[0], rstd[0], nmr[0]
        x_, z_ = xf[0], zz[0]
        self.dma_in("sp", x_, x_.ap[0:n], din["xs"])
        hb = [PB[0], PB[1]]
        for half in range(2):
            for ec in range(12):
                wb, wv = self.Wout[ec]
                self.mm(hb[half], hb[half].ap[0:n], self.mixsT, self.mixsT.ap[:, ec, :], wb,
                        wv[:, half * 512:(half + 1) * 512], start=(ec == 0), stop=(ec == 11))
        for half in range(2):
            hs = slice(half * 512, (half + 1) * 512)
            self.stt(z_, z_.ap[0:n, hs], x_, x_.ap[0:n, hs], ALPHA, ALU.mult, hb[half], hb[half].ap[0:n], ALU.add)
        self.layernorm(z_, n, st, mv, ve, rstd, nmr, mhalf, Gt, Bt)
        self.dma_out("sp", dout["ys"], z_, z_.ap[0:n])

    def layernorm(self, z_, n, st, mv, ve, rstd, nmr, mhalf, Gt, Bt, stage=0):
        S = self.S
        if stage in (0, 1):
            for half in range(2):
                hs = slice(half * 512, (half + 1) * 512)
                S.op("dve", (lambda e, half=half, hs=hs: e.bn_stats(out=st.ap[0:n, half, :], in_=z_.ap[0:n, hs])),
                     reads=[z_.tk], writes=[st.tk])
            S.op("dve", lambda e: e.bn_aggr(out=mv.ap[0:n, :], in_=st.ap[0:n, :, :].rearrange("p a b -> p (a b)")),
                 reads=[st.tk], writes=[mv.tk])
            self.ts("dve", ve, ve.ap[0:n], mv, mv.ap[0:n, 1:2], EPS, ALU.add)
            self.tt("pool", rstd, rstd.ap[0:n], ve, ve.ap[0:n], mhalf, mhalf.ap[0:n, 0:1], ALU.pow)
            self.ts("dve", nmr, nmr.ap[0:n], mv, mv.ap[0:n, 0:1], -1.0, ALU.mult, rstd.ap[0:n, 0:1], ALU.mult, extra_reads=[rstd])
            self.ts("dve", z_, z_.ap[0:n], z_, z_.ap[0:n], rstd.ap[0:n, 0:1], ALU.mult, nmr.ap[0:n, 0:1], ALU.add,
                    extra_reads=[rstd, nmr])
        if stage in (0, 2):
            self.tt("dve", z_, z_.ap[0:n], z_, z_.ap[0:n], Gt, Gt.ap[0:n], ALU.mult)
            self.tt("dve", z_, z_.ap[0:n], z_, z_.ap[0:n], Bt, Bt.ap[0:n], ALU.add)

    def sample_swa(self):
        S = self.S
        PB = self.PB
        din, dco = self.din, self.dco
        n = NS
        A = self.arena
        sqs, sks, Gss, Vnew = self.sqs, self.sks, self.Gss, self.Vnew
        smask = A("smask", (128, 9, 4), BF16)
        smaskn = A("smaskn", (n, 4, 4), BF16)
        ones = A("ones", (128, 2), BF16)
        sqsT = A("sqsT", (128, 4, n), BF16)
        sksT = A("sksT", (128, 4, n), BF16)
        GssT = A("GssT", (128, 4, n), BF16)
        Qbd = A("Qbd", (128, 4, 4, 8), BF16)
        Kc = A("Kc", (128, 9, 512), BF16)
        KcT = A("KcT", (128, 9, 4, 128), BF16)
        Vc = A("Vc", (128, 9, 512), BF16)
        PTx = [A("PTx", (128, 10, 8, 4), BF16) for _ in range(2)]
        rls = A("rlx", (4, 8), F32)
        Onb_ap = KcT.ap[0:4, 0, :, :].rearrange("p c k -> p (c k)")
        bt = PB[4]
        btv = bt.ap.bitcast(BF16)
        bsx = PB[5]
        UB = [PB[6], PB[7]]

        def issue_k(b):
            ck = din["ck"][b]
            self.dma_in("pool", Kc, Kc.ap[:, 0:4, :], ck.rearrange("(m s) c -> m s c", s=16)[:, 0:4, :], parallel=True)
            self.dma_in("pool", Kc, Kc.ap[:, 4:8, :], ck[1536:2048, :].rearrange("(m s) c -> m s c", s=4), parallel=True)
            self.dma_in("pool", Kc, Kc.ap[:, 8, :], ck[1920:2048, :], parallel=True)

        def issue_v(b):
            cv = din["cv"][b]
            self.dma_in("pool", Vc, Vc.ap[:, 0:4, :], cv.rearrange("(m s) c -> m s c", s=16)[:, 0:4, :], parallel=True)
            self.dma_in("pool", Vc, Vc.ap[:, 4:8, :], cv[1536:2048, :].rearrange("(m s) c -> m s c", s=4), parallel=True)
            self.dma_in("pool", Vc, Vc.ap[:, 8, :], cv[1920:2048, :], parallel=True)

        self.dma_in("pool", smask, smask.ap, dco["smask"])
        self.dma_in("pool", smaskn, smaskn.ap, dco["smaskn"])
        self.memset("pool", ones, ones.ap, 1.0)
        self.memset("pool", Qbd, Qbd.ap, 0.0)
        for p_ in PTx:
            self.memset("pool", p_, p_.ap, 0.0)
        issue_k(0)
        issue_v(0)
        yield
        tv4 = btv[:, 0:4 * n].rearrange("p (c t) -> p c t", c=4)
        for src, dst in ((sqs, sqsT), (sks, sksT), (Gss, GssT)):
            for c in range(4):
                self.tr(bt, btv[:, c * n:(c + 1) * n], src, src.ap[:, c * 128:(c + 1) * 128], self.identb)
            self.cp("act", dst, dst.ap, bt, tv4)
        self.cp("pool", Qbd, Qbd.ap[0:64, :, :, 0:4], sqsT, sqsT.ap[0:64, :, :].rearrange("p c (b i) -> p c b i", b=4))
        self.cp("pool", Qbd, Qbd.ap[64:128, :, :, 4:8], sqsT, sqsT.ap[64:128, :, :].rearrange("p c (b i) -> p c b i", b=4))
        yield
        for b in range(4):
            pt = PTx[b % 2]
            for tl in range(9):
                for c in range(4):
                    self.tr(bt, btv[:, c * 128:(c + 1) * 128], Kc, Kc.ap[:, tl, c * 128:(c + 1) * 128], self.identb)
                self.cp("act", KcT, KcT.ap[:, tl, :, :], bt,
                        btv[:, 0:512].rearrange("p (c k) -> p c k", c=4))
                yield
            if b + 1 < 4:
                issue_k(b + 1)
            for tl in range(9):
                for c in range(4):
                    self.mm(bsx, bsx.ap[:, tl * 32 + c * 8:tl * 32 + c * 8 + 8], KcT, KcT.ap[:, tl, c, :], Qbd, Qbd.ap[:, c, b, :])
            for c in range(4):
                self.mm(bsx, bsx.ap[0:n, 288 + c * 8:288 + c * 8 + 8], sksT, sksT.ap[:, c, :], Qbd, Qbd.ap[:, c, b, :])
            yield
            self.act(pt, pt.ap[:, 0:9, :, :], bsx, bsx.ap[:, 0:288].rearrange("p (t h i) -> p t h i", t=9, h=8), AF.Exp, scale=0.125)
            self.act(pt, pt.ap[0:n, 9, :, :], bsx, bsx.ap[0:n, 288:320].rearrange("p (h i) -> p h i", h=8), AF.Exp, scale=0.125)
            self.tt("pool", pt, pt.ap[:, 0:9, :, :], pt, pt.ap[:, 0:9, :, :], smask,
                    smask.ap.unsqueeze(2).broadcast_to([128, 9, 8, 4]), ALU.mult)
            self.tt("pool", pt, pt.ap[0:n, 9, :, :], pt, pt.ap[0:n, 9, :, :], smaskn,
                    smaskn.ap[:, b, :].unsqueeze(1).broadcast_to([n, 8, 4]), ALU.mult)
            yield
            for h in range(8):
                ub = UB[h // 4]
                o_ap = ub.ap[0:4, (h % 4) * 64:(h % 4) * 64 + 64]
                for tl in range(9):
                    self.mm(ub, o_ap, pt, pt.ap[:, tl, h, :], Vc, Vc.ap[:, tl, h * 64:(h + 1) * 64], start=(tl == 0), stop=False)
                self.mm(ub, o_ap, pt, pt.ap[0:n, 9, h, :], Vnew, Vnew.ap[:, h, 0:64], start=False, stop=True)
                l_ap = bsx.ap[0:4, 320 + h:321 + h]
                for tl in range(9):
                    self.mm(bsx, l_ap, pt, pt.ap[:, tl, h, :], ones, ones.ap[:, 0:1], start=(tl == 0), stop=False)
                self.mm(bsx, l_ap, pt, pt.ap[0:n, 9, h, :], ones, ones.ap[0:n, 0:1], start=False, stop=True)
                if h % 2 == 1:
                    yield
            if b + 1 < 4:
                issue_v(b + 1)
            S.op("dve", lambda e: e.reciprocal(out=rls.ap, in_=bsx.ap[0:4, 320:328]), reads=[bsx.tk], writes=[rls.tk])
            for j in range(2):
                uv = UB[j].ap[0:4, 0:256].rearrange("p (h f) -> p h f", h=4)
                self.tt("dve", KcT, Onb_ap[:, 256 * j:256 * (j + 1)].rearrange("p (h f) -> p h f", h=4), UB[j], uv,
                        rls, rls.ap[:, 4 * j:4 * j + 4].unsqueeze(2).broadcast_to([4, 4, 64]), ALU.mult)
            for c in range(4):
                self.tr(bt, btv[:, c * 4:(c + 1) * 4], KcT, Onb_ap[:, c * 128:(c + 1) * 128], self.identb)
            self.tt("dve", self.mixsT, self.mixsT.ap[:, 4:8, 4 * b:4 * b + 4], bt, btv[:, 0:16].rearrange("p (c i) -> p c i", c=4),
                    GssT, GssT.ap[:, :, 4 * b:4 * b + 4], ALU.mult)
            yield


_CACHE = {}


def _get_prog(phases):
    key = tuple(phases)
    if key not in _CACHE:
        p = Prog()
        _CACHE[key] = p.build(phases)
    return _CACHE[key]


PHASES = ("S", "R", "M", "F", "X")


def kernel(x_prompt, x_sample, state_ret, cache_swa_k, cache_swa_v, cache_mem_k, cache_mem_v,
           mem_prompt, w_in, w_mem_kv, w_out, ln_gain, ln_bias):
    f = lambda a: np.ascontiguousarray(np.asarray(a, dtype=np.float32))
    consts = {"c_" + k: v for k, v in _consts().items()}
    nc = _get_prog(PHASES)
    in_maps = []
    for c in range(NCORES):
        sb = slice(4 * c, 4 * c + 4)
        m = {
            "x": f(x_prompt[c]), "memx": f(mem_prompt[c]), "w_in": f(w_in[0]), "w_mem": f(w_mem_kv[0]),
            "w_out": f(w_out[0]), "ln_g": f(ln_gain), "ln_b": f(ln_bias),
            "xs": f(np.asarray(x_sample)[sb].reshape(NS, D)),
            "state": f(np.asarray(state_ret)[0, sb]),
            "ck": f(np.asarray(cache_swa_k)[0, sb].reshape(4, 2048, 512)),
            "cv": f(np.asarray(cache_swa_v)[0, sb].reshape(4, 2048, 512)),
            "cmk": f(np.asarray(cache_mem_k)[0, sb].reshape(4, 256, 512)),
            "cmv": f(np.asarray(cache_mem_v)[0, sb].reshape(4, 256, 512)),
        }
        m.update(consts)
        in_maps.append(m)
    res = run_bass_kernel_spmd(nc, in_maps, core_ids=list(range(NCORES)))
    R = res.results
    cat = lambda k: np.stack([np.asarray(R[c][k]) for c in range(NCORES)], axis=0)
    y = cat("y")
    ys = cat("ys").reshape(32, 4, D)
    retp = cat("retp")[None]
    rets = cat("rets").reshape(32, 4, 64, 128)[None]
    kp = cat("kp").reshape(8, SEQ, 8, 64)[None]
    vp = cat("vp").reshape(8, SEQ, 8, 64)[None]
    ks = cat("ks").reshape(32, 4, 8, 64)[None]
    vs = cat("vs").reshape(32, 4, 8, 64)[None]
    mkp = cat("mkp").reshape(8, 256, 4, 128)[None]
    mvp = cat("mvp").reshape(8, 256, 4, 128)[None]
    return (y, ys, retp, rets, kp, vp, ks, vs, mkp, mvp)
```
